# Optimizing a Trainium2 kernel written in Bass

```python
import math
import jax
import jax.numpy as jnp
from jax import lax
import numpy as np

D_MODEL = 2048
BATCH = 16
SEQ = 256
DEPTH = 2
DEC_BATCH = 2
DEC_SEQ = 1024
PAST_LEN = 512

GRID_W = 64
N_BRANCH = 4
BRANCH_W = 512
GDN_H = 4
GDN_DK = 128
GDN_DV = 128
GDN_CONV = 5
DELTA_CHUNK = 64
GLA_H = 4
GLA_DK = 64
GLA_DV = 128
GLA_RANK = 16
GLA_TAU = 16.0
GLA_CHUNK = 16
HG_H = 4
HG_DK = 128
HG_DV = 128
MLA_H = 4
MLA_NOPE = 128
MLA_ROPE = 64
MLA_V = 128
Q_LORA = 512
KV_LORA = 512
ATTN_QBLOCK = 128
ROPE_BASE = 10000.0
D_FF = ((8 * D_MODEL // 3 + 255) // 256) * 256
NORM_EPS = 1e-6

IN_SPLITS = (
    ('gdn_q', GDN_H * GDN_DK), ('gdn_k', GDN_H * GDN_DK), ('gdn_v', GDN_H * GDN_DV),
    ('gdn_z', GDN_H * GDN_DV), ('gdn_b', 2 * GDN_H), ('gdn_a', 2 * GDN_H),
    ('gla_q', GLA_H * GLA_DK), ('gla_k', GLA_H * GLA_DK), ('gla_v', GLA_H * GLA_DV),
    ('gla_r', GLA_H * GLA_DV), ('gla_g', 2 * GLA_RANK),
    ('hg_q', HG_H * HG_DK), ('hg_f', 2 * HG_H * HG_DK), ('hg_i', HG_H * HG_DV),
    ('hg_g', HG_H * HG_DV),
    ('mla_qa', Q_LORA), ('mla_kva', KV_LORA), ('mla_kpe', MLA_ROPE),
    ('gates', N_BRANCH * D_MODEL),
)
IN_WIDTH = sum(n for _, n in IN_SPLITS)
GDN_QKV = 2 * GDN_H * GDN_DK + GDN_H * GDN_DV

kernel_name = 'hybrid_diffusion_step'


def _split_in(p):
    out, off = {}, 0
    for name, n in IN_SPLITS:
        out[name] = p[..., off:off + n]
        off += n
    return out


def _rmsnorm(x, w):
    xf = x.astype(jnp.float32)
    y = xf * lax.rsqrt(jnp.mean(xf * xf, axis=-1, keepdims=True) + NORM_EPS)
    return (y * w.astype(jnp.float32)).astype(x.dtype)


def _layernorm(x, g, b):
    xf = x.astype(jnp.float32)
    mu = jnp.mean(xf, axis=-1, keepdims=True)
    var = jnp.mean(jnp.square(xf - mu), axis=-1, keepdims=True)
    y = (xf - mu) * lax.rsqrt(var + NORM_EPS) * g.astype(jnp.float32) + b.astype(jnp.float32)
    return y.astype(x.dtype)


def _l2norm(x):
    xf = x.astype(jnp.float32)
    return (xf * lax.rsqrt(jnp.sum(xf * xf, axis=-1, keepdims=True) + NORM_EPS)).astype(x.dtype)


def _heads(x, h):
    b, l, _ = x.shape
    return x.reshape(b, l, h, -1).transpose(0, 2, 1, 3)


def _merge_heads(x):
    b, h, l, d = x.shape
    return x.transpose(0, 2, 1, 3).reshape(b, l, h * d)


def _flip(x):
    return jnp.flip(x, axis=2)


def _centred_conv(x, w):
    pad = w.shape[0] // 2
    return lax.conv_general_dilated(
        x, w[:, None, :].astype(x.dtype), window_strides=(1,), padding=((pad, pad),),
        dimension_numbers=('NWC', 'WIO', 'NWC'), feature_group_count=x.shape[-1])


def _axial_angles(length):
    rows = length // GRID_W
    pos = jnp.arange(rows * GRID_W)
    row_id = (pos // GRID_W).astype(jnp.float32)
    col_id = (pos % GRID_W).astype(jnp.float32)
    half = MLA_ROPE // 2
    inv = ROPE_BASE ** (-jnp.arange(0, half, 2, dtype=jnp.float32) / half)
    return row_id[:, None] * inv, col_id[:, None] * inv


def _rotate(x, ang):
    m = x.shape[-1] // 2
    x1, x2 = x[..., :m], x[..., m:]
    cos, sin = jnp.cos(ang), jnp.sin(ang)
    return jnp.concatenate([x1 * cos - x2 * sin, x1 * sin + x2 * cos], axis=-1)


def _axial_rope(x, ang_row, ang_col):
    half = MLA_ROPE // 2
    xf = x.astype(jnp.float32)
    y = jnp.concatenate([_rotate(xf[..., :half], ang_row), _rotate(xf[..., half:], ang_col)], axis=-1)
    return y.astype(x.dtype)


def _gated_delta_chunked(q, k, v, g, beta, s0):
    out_dtype = v.dtype
    bsz, nh, length, _ = q.shape
    dv = v.shape[-1]
    n = length // DELTA_CHUNK
    f32 = jnp.float32
    cs = lambda t: t.astype(f32).reshape(bsz, nh, n, DELTA_CHUNK, t.shape[-1])
    q, k, v = cs(q), cs(k), cs(v)
    g = g.astype(f32).reshape(bsz, nh, n, DELTA_CHUNK)
    beta = beta.astype(f32).reshape(bsz, nh, n, DELTA_CHUNK)
    gam = jnp.cumsum(g, axis=-1)
    idx = jnp.arange(DELTA_CHUNK)
    incl = idx[:, None] >= idx[None, :]
    strict = idx[:, None] > idx[None, :]
    decay = jnp.exp(jnp.where(incl, gam[..., :, None] - gam[..., None, :], -jnp.inf))
    kb = k * beta[..., None]
    m = jnp.where(strict, jnp.einsum('bhnid,bhnjd->bhnij', kb, k) * decay, 0.0)
    rhs = jnp.concatenate([v * beta[..., None], kb * jnp.exp(gam)[..., None]], axis=-1)
    sol = lax.linalg.triangular_solve(m + jnp.eye(DELTA_CHUNK, dtype=f32), rhs,
                                      left_side=True, lower=True, unit_diagonal=True)
    u_base, w = sol[..., :dv], sol[..., dv:]
    a_qk = jnp.einsum('bhnid,bhnjd->bhnij', q, k) * decay
    q_dec = q * jnp.exp(gam)[..., None]
    k_dec = k * jnp.exp(gam[..., -1:] - gam)[..., None]
    c_dec = jnp.exp(gam[..., -1])

    def step(s, xs):
        u_c, w_c, a_c, q_c, k_c, d_c = xs
        u = u_c - jnp.einsum('bhcd,bhdv->bhcv', w_c, s)
        o = jnp.einsum('bhcd,bhdv->bhcv', q_c, s) + jnp.einsum('bhij,bhjv->bhiv', a_c, u)
        s = d_c[..., None, None] * s + jnp.einsum('bhcd,bhcv->bhdv', k_c, u)
        return s, o

    xs = tuple(jnp.moveaxis(t, 2, 0) for t in (u_base, w, a_qk, q_dec, k_dec, c_dec))
    s_fin, o = lax.scan(step, s0.astype(f32), xs)
    o = jnp.moveaxis(o, 0, 2).reshape(bsz, nh, length, dv)
    return o.astype(out_dtype), s_fin


def _gla_chunked(q, k, v, log_a, s0):
    out_dtype = v.dtype
    bsz, nh, length, _ = q.shape
    dv = v.shape[-1]
    n = length // GLA_CHUNK
    f32 = jnp.float32
    cs = lambda t: t.astype(f32).reshape(bsz, nh, n, GLA_CHUNK, t.shape[-1])
    q, k, v, log_a = cs(q), cs(k), cs(v), cs(log_a)
    b = jnp.cumsum(log_a, axis=-2)
    idx = jnp.arange(GLA_CHUNK)
    incl = (idx[:, None] >= idx[None, :])[:, :, None]
    pair = jnp.exp(jnp.where(incl, b[..., :, None, :] - b[..., None, :, :], -jnp.inf))
    a_qk = jnp.einsum('bhnid,bhnjd,bhnijd->bhnij', q, k, pair)
    o_intra = jnp.einsum('bhnij,bhnjv->bhniv', a_qk, v)
    q_dec = q * jnp.exp(b)
    k_dec = k * jnp.exp(b[..., -1:, :] - b)
    c_dec = jnp.exp(b[..., -1, :])

    def step(s, xs):
        q_c, k_c, v_c, d_c = xs
        o = jnp.einsum('bhcd,bhdv->bhcv', q_c, s)
        s = d_c[..., None] * s + jnp.einsum('bhcd,bhcv->bhdv', k_c, v_c)
        return s, o

    xs = tuple(jnp.moveaxis(t, 2, 0) for t in (q_dec, k_dec, v, c_dec))
    s_fin, o_inter = lax.scan(step, s0.astype(f32), xs)
    o = o_intra + jnp.moveaxis(o_inter, 0, 2)
    return o.reshape(bsz, nh, length, dv).astype(out_dtype), s_fin


def _gdn_branch(pr, conv_w, a_log, dt_bias, norm_w, s0):
    bsz, length, _ = pr['gdn_q'].shape
    qkv = jnp.concatenate([pr['gdn_q'], pr['gdn_k'], pr['gdn_v']], axis=-1)
    qkv = jax.nn.silu(_centred_conv(qkv, conv_w))
    q, k, v = jnp.split(qkv, [GDN_H * GDN_DK, 2 * GDN_H * GDN_DK], axis=-1)
    q = _l2norm(_heads(q, GDN_H)) * (GDN_DK ** -0.5)
    k = _l2norm(_heads(k, GDN_H))
    v = _heads(v, GDN_H)
    f32 = jnp.float32
    beta = jax.nn.sigmoid(pr['gdn_b'].astype(f32)).reshape(bsz, length, 2, GDN_H).transpose(2, 0, 3, 1)
    a = pr['gdn_a'].astype(f32).reshape(bsz, length, 2, GDN_H).transpose(2, 0, 3, 1)
    g = -jnp.exp(a_log.astype(f32))[:, None, :, None] * jax.nn.softplus(
        a + dt_bias.astype(f32)[:, None, :, None])
    o_f, s_f = _gated_delta_chunked(q, k, v, g[0], beta[0], s0[:, 0])
    o_b, s_b = _gated_delta_chunked(_flip(q), _flip(k), _flip(v), _flip(g[1]), _flip(beta[1]), s0[:, 1])
    o = _rmsnorm(o_f + _flip(o_b), norm_w) * jax.nn.silu(_heads(pr['gdn_z'], GDN_H))
    return _merge_heads(o), jnp.stack([s_f, s_b], axis=1)


def _gla_branch(pr, gate_w2, gate_b, norm_w, s0):
    bsz, length, _ = pr['gla_q'].shape
    q = _heads(pr['gla_q'], GLA_H) * (GLA_DK ** -0.5)
    k = _heads(pr['gla_k'], GLA_H)
    v = _heads(pr['gla_v'], GLA_H)
    glr = pr['gla_g'].reshape(bsz, length, 2, GLA_RANK)
    logits = jnp.einsum('bltr,trk->tblk', glr, gate_w2) + gate_b[:, None, None, :]
    log_a = jax.nn.log_sigmoid(logits.astype(jnp.float32)) / GLA_TAU
    log_a = log_a.reshape(2, bsz, length, GLA_H, GLA_DK).transpose(0, 1, 3, 2, 4)
    o_f, s_f = _gla_chunked(q, k, v, log_a[0], s0[:, 0])
    o_b, s_b = _gla_chunked(_flip(q), _flip(k), _flip(v), _flip(log_a[1]), s0[:, 1])
    o = _rmsnorm(o_f + _flip(o_b), norm_w) * jax.nn.silu(_heads(pr['gla_r'], GLA_H))
    return _merge_heads(o), jnp.stack([s_f, s_b], axis=1)


def _hgrn_branch(pr, lb, norm_w, s0):
    bsz, length, _ = pr['hg_q'].shape
    q = jax.nn.silu(_heads(pr['hg_q'], HG_H))
    v = _heads(pr['hg_i'], HG_H)
    zf = pr['hg_f'].astype(jnp.float32).reshape(bsz, length, 2, HG_H * HG_DK).transpose(2, 0, 1, 3)
    lbb = lb[:, None, None, :]
    log_f = jnp.logaddexp(jnp.log(lbb), jnp.log1p(-lbb) + jax.nn.log_sigmoid(zf))
    one_minus_f = (1.0 - lbb) * jax.nn.sigmoid(-zf)
    to_h = lambda t: t.reshape(2, bsz, length, HG_H, HG_DK).transpose(0, 1, 3, 2, 4)
    log_f, one_minus_f = to_h(log_f), to_h(one_minus_f)
    o_f, s_f = _gla_chunked(q, one_minus_f[0], v, log_f[0], s0[:, 0])
    o_b, s_b = _gla_chunked(_flip(q), _flip(one_minus_f[1]), _flip(v), _flip(log_f[1]), s0[:, 1])
    o = _rmsnorm(o_f + _flip(o_b), norm_w) * jax.nn.sigmoid(_heads(pr['hg_g'], HG_H))
    return _merge_heads(o), jnp.stack([s_f, s_b], axis=1)


def _mla_project(pr, q_norm, wq_b, kv_norm, ang):
    q = _heads(_rmsnorm(pr['mla_qa'], q_norm) @ wq_b, MLA_H)
    q_nope, q_pe = q[..., :MLA_NOPE], q[..., MLA_NOPE:]
    ckv = _rmsnorm(pr['mla_kva'], kv_norm)
    kpe = pr['mla_kpe']
    if ang is not None:
        q_pe = _axial_rope(q_pe, ang[0], ang[1])
        kpe = _axial_rope(kpe, ang[0], ang[1])
    return q_nope, q_pe, ckv, kpe


def _mla_attend(q_nope, q_pe, k_nope, k_pe, v):
    bsz, nh, lq, _ = q_nope.shape
    nb = lq // ATTN_QBLOCK
    scale = (MLA_NOPE + MLA_ROPE) ** -0.5

    def blocks(t):
        return t.reshape(bsz, nh, nb, ATTN_QBLOCK, t.shape[-1]).transpose(2, 0, 1, 3, 4)

    def attend(qs):
        qn, qp = qs
        s = jnp.einsum('bhqd,bhkd->bhqk', qn, k_nope) + jnp.einsum('bhqd,bkd->bhqk', qp, k_pe)
        p = jax.nn.softmax(s.astype(jnp.float32) * scale, axis=-1)
        return jnp.einsum('bhqk,bhkd->bhqd', p.astype(v.dtype), v)

    o = lax.map(attend, (blocks(q_nope), blocks(q_pe)))
    return o.transpose(1, 2, 0, 3, 4).reshape(bsz, nh, lq, MLA_V)


def _layer(x, mod, P, l, lb, s_gdn0, s_gla0, s_hg0, ctx_ckv, ctx_kpe, ang, alpha):
    bsz, length, _ = x.shape
    shift1, scale1, gate1, shift2, scale2, gate2 = jnp.split(mod[:, None, :], 6, axis=-1)
    h = x * (1.0 + scale1) + shift1
    pr = _split_in(h @ P['w_in'][l])
    o_gdn, s_gdn = _gdn_branch(pr, P['gdn_conv'][l], P['gdn_a_log'][l], P['gdn_dt_bias'][l],
                               P['gdn_norm'][l], s_gdn0)
    o_gla, s_gla = _gla_branch(pr, P['gla_gate_w2'][l], P['gla_gate_b'][l], P['gla_norm'][l], s_gla0)
    o_hg, s_hg = _hgrn_branch(pr, lb, P['hgrn_norm'][l], s_hg0)
    q_nope, q_pe, ckv, kpe = _mla_project(pr, P['mla_q_norm'][l], P['mla_wq_b'][l],
                                          P['mla_kv_norm'][l], ang)
    if ctx_ckv is None:
        keys_ckv, keys_kpe = ckv, kpe
    else:
        keys_ckv = jnp.concatenate([ckv, ctx_ckv.astype(ckv.dtype)], axis=1)
        keys_kpe = jnp.concatenate([kpe, ctx_kpe.astype(kpe.dtype)], axis=1)
    kv = _heads(keys_ckv @ P['mla_wkv_b'][l], MLA_H)
    o_mla = _merge_heads(_mla_attend(q_nope, q_pe, kv[..., :MLA_NOPE], keys_kpe, kv[..., MLA_NOPE:]))
    branches = jnp.stack([o_gdn, o_gla, o_hg, o_mla], axis=0)
    proj = jnp.einsum('kbln,knd->kbld', branches, P['w_branch'][l])
    gates = jax.nn.sigmoid(pr['gates'].reshape(bsz, length, N_BRANCH, D_MODEL) + P['b_gates'][l])
    merged = jnp.einsum('blkd,kbld->bld', gates, proj)
    x = _layernorm(alpha * x + gate1 * (merged @ P['w_out'][l]), P['ln1_g'][l], P['ln1_b'][l])
    h2 = x * (1.0 + scale2) + shift2
    ff = (jax.nn.silu(h2 @ P['ffn_w1'][l]) * (h2 @ P['ffn_w3'][l])) @ P['ffn_w2'][l]
    x = _layernorm(alpha * x + gate2 * ff, P['ln2_g'][l], P['ln2_b'][l])
    return x, s_gdn, s_gla, s_hg, ckv, kpe


def setup_inputs(seed: int = 0) -> dict:
    key = jax.random.key(seed)
    ks = iter(jax.random.split(key, 40))

    def nrm(shape, scale):
        return jax.random.normal(next(ks), shape, jnp.float32) * scale

    def gain(shape):
        return 1.0 + nrm(shape, 0.1)

    res_scale = (8.0 * DEPTH) ** -0.25
    x_prompt = nrm((BATCH, SEQ, D_MODEL), 1.0)
    x_sample = nrm((DEC_BATCH, DEC_SEQ, D_MODEL), 1.0)
    c = nrm((DEC_BATCH, D_MODEL), 1.0)
    state_gdn = nrm((DEC_BATCH, DEPTH, 2, GDN_H, GDN_DK, GDN_DV), 0.1)
    state_gla = nrm((DEC_BATCH, DEPTH, 2, GLA_H, GLA_DK, GLA_DV), 0.5)
    state_hgrn = nrm((DEC_BATCH, DEPTH, 2, HG_H, HG_DK, HG_DV), 0.5)
    cache_mla_ckv = nrm((DEC_BATCH, DEPTH, PAST_LEN, KV_LORA), 1.0)
    cache_mla_kpe = nrm((DEC_BATCH, DEPTH, PAST_LEN, MLA_ROPE), 1.0)
    c_ctx = nrm((D_MODEL,), 1.0)
    w_ada = nrm((DEPTH, D_MODEL, 6 * D_MODEL), 0.5 * D_MODEL ** -0.5)
    b_ada = nrm((DEPTH, 6 * D_MODEL), 0.02)
    w_in = nrm((DEPTH, D_MODEL, IN_WIDTH), D_MODEL ** -0.5)
    gdn_conv = nrm((DEPTH, GDN_CONV, GDN_QKV), GDN_CONV ** -0.5)
    gdn_a_log = jnp.log(jax.random.uniform(next(ks), (DEPTH, 2, GDN_H), jnp.float32, 1.0, 16.0))
    dt = jnp.exp(jax.random.uniform(next(ks), (DEPTH, 2, GDN_H), jnp.float32,
                                    math.log(1e-3), math.log(1e-1)))
    gdn_dt_bias = dt + jnp.log(-jnp.expm1(-dt))
    gdn_norm = gain((DEPTH, GDN_DV))
    gla_gate_w2 = nrm((DEPTH, 2, GLA_RANK, GLA_H * GLA_DK), GLA_RANK ** -0.5)
    gla_gate_b = nrm((DEPTH, 2, GLA_H * GLA_DK), 0.1)
    gla_norm = gain((DEPTH, GLA_DV))
    hgrn_lb = nrm((DEPTH, 2, HG_H * HG_DK), 0.5)
    hgrn_norm = gain((DEPTH, HG_DV))
    mla_q_norm = gain((DEPTH, Q_LORA))
    mla_wq_b = nrm((DEPTH, Q_LORA, MLA_H * (MLA_NOPE + MLA_ROPE)), Q_LORA ** -0.5)
    mla_kv_norm = gain((DEPTH, KV_LORA))
    mla_wkv_b = nrm((DEPTH, KV_LORA, MLA_H * (MLA_NOPE + MLA_V)), KV_LORA ** -0.5)
    w_branch = nrm((DEPTH, N_BRANCH, BRANCH_W, D_MODEL), BRANCH_W ** -0.5 * res_scale)
    b_gates = nrm((DEPTH, N_BRANCH, D_MODEL), 0.02)
    w_out = nrm((DEPTH, D_MODEL, D_MODEL), D_MODEL ** -0.5 * res_scale)
    ln1_g = gain((DEPTH, D_MODEL))
    ln1_b = nrm((DEPTH, D_MODEL), 0.02)
    ln2_g = gain((DEPTH, D_MODEL))
    ln2_b = nrm((DEPTH, D_MODEL), 0.02)
    ffn_w1 = nrm((DEPTH, D_MODEL, D_FF), D_MODEL ** -0.5)
    ffn_w3 = nrm((DEPTH, D_MODEL, D_FF), D_MODEL ** -0.5)
    ffn_w2 = nrm((DEPTH, D_FF, D_MODEL), D_FF ** -0.5 * res_scale)
    return {
        'x_prompt': x_prompt, 'x_sample': x_sample, 'c': c,
        'state_gdn': state_gdn, 'state_gla': state_gla, 'state_hgrn': state_hgrn,
        'cache_mla_ckv': cache_mla_ckv, 'cache_mla_kpe': cache_mla_kpe,
        'c_ctx': c_ctx, 'w_ada': w_ada, 'b_ada': b_ada, 'w_in': w_in,
        'gdn_conv': gdn_conv, 'gdn_a_log': gdn_a_log, 'gdn_dt_bias': gdn_dt_bias, 'gdn_norm': gdn_norm,
        'gla_gate_w2': gla_gate_w2, 'gla_gate_b': gla_gate_b, 'gla_norm': gla_norm,
        'hgrn_lb': hgrn_lb, 'hgrn_norm': hgrn_norm,
        'mla_q_norm': mla_q_norm, 'mla_wq_b': mla_wq_b, 'mla_kv_norm': mla_kv_norm, 'mla_wkv_b': mla_wkv_b,
        'w_branch': w_branch, 'b_gates': b_gates, 'w_out': w_out,
        'ln1_g': ln1_g, 'ln1_b': ln1_b, 'ln2_g': ln2_g, 'ln2_b': ln2_b,
        'ffn_w1': ffn_w1, 'ffn_w3': ffn_w3, 'ffn_w2': ffn_w2,
    }


def reference(x_prompt, x_sample, c, state_gdn, state_gla, state_hgrn, cache_mla_ckv, cache_mla_kpe,
              c_ctx, w_ada, b_ada, w_in, gdn_conv, gdn_a_log, gdn_dt_bias, gdn_norm,
              gla_gate_w2, gla_gate_b, gla_norm, hgrn_lb, hgrn_norm,
              mla_q_norm, mla_wq_b, mla_kv_norm, mla_wkv_b, w_branch, b_gates, w_out,
              ln1_g, ln1_b, ln2_g, ln2_b, ffn_w1, ffn_w3, ffn_w2):
    P = {
        'w_in': w_in, 'gdn_conv': gdn_conv, 'gdn_a_log': gdn_a_log, 'gdn_dt_bias': gdn_dt_bias,
        'gdn_norm': gdn_norm, 'gla_gate_w2': gla_gate_w2, 'gla_gate_b': gla_gate_b, 'gla_norm': gla_norm,
        'hgrn_norm': hgrn_norm, 'mla_q_norm': mla_q_norm, 'mla_wq_b': mla_wq_b,
        'mla_kv_norm': mla_kv_norm, 'mla_wkv_b': mla_wkv_b, 'w_branch': w_branch, 'b_gates': b_gates,
        'w_out': w_out, 'ln1_g': ln1_g, 'ln1_b': ln1_b, 'ln2_g': ln2_g, 'ln2_b': ln2_b,
        'ffn_w1': ffn_w1, 'ffn_w3': ffn_w3, 'ffn_w2': ffn_w2,
    }
    alpha = (2.0 * DEPTH) ** 0.25
    cum = jnp.cumsum(jax.nn.softmax(hgrn_lb.astype(jnp.float32), axis=0), axis=0)
    lower_bounds = cum - cum[:1]

    bp = x_prompt.shape[0]
    f32 = jnp.float32
    zero_gdn = jnp.zeros((bp, 2, GDN_H, GDN_DK, GDN_DV), f32)
    zero_gla = jnp.zeros((bp, 2, GLA_H, GLA_DK, GLA_DV), f32)
    zero_hg = jnp.zeros((bp, 2, HG_H, HG_DK, HG_DV), f32)
    new_gdn, new_gla, new_hg, new_ckv, new_kpe = [], [], [], [], []
    y = x_prompt
    for l in range(DEPTH):
        mod = jax.nn.silu(c_ctx)[None, :] @ w_ada[l] + b_ada[l]
        y, s_g, s_l, s_h, ckv, kpe = _layer(y, mod, P, l, lower_bounds[l], zero_gdn, zero_gla, zero_hg,
                                            None, None, None, alpha)
        new_gdn.append(s_g)
        new_gla.append(s_l)
        new_hg.append(s_h)
        new_ckv.append(ckv)
        new_kpe.append(kpe)
    y_prompt = y
    sdt = x_prompt.dtype
    state_gdn_new = jnp.stack(new_gdn, axis=1).astype(sdt)
    state_gla_new = jnp.stack(new_gla, axis=1).astype(sdt)
    state_hgrn_new = jnp.stack(new_hg, axis=1).astype(sdt)
    cache_mla_ckv_new = jnp.stack(new_ckv, axis=1)
    cache_mla_kpe_new = jnp.stack(new_kpe, axis=1)

    ang = _axial_angles(x_sample.shape[1])
    z = x_sample
    for l in range(DEPTH):
        mod = jax.nn.silu(c) @ w_ada[l] + b_ada[l]
        z = _layer(z, mod, P, l, lower_bounds[l], state_gdn[:, l], state_gla[:, l], state_hgrn[:, l],
                   cache_mla_ckv[:, l], cache_mla_kpe[:, l], ang, alpha)[0]
    y_sample = z
    return (y_prompt, y_sample, state_gdn_new, state_gla_new, state_hgrn_new, cache_mla_ckv_new, cache_mla_kpe_new)
```

```python
import math
from contextlib import ExitStack
import numpy as np
import concourse.bass as bass
import concourse.mybir as mybir
from concourse.bass_utils import run_bass_kernel_spmd

F32 = mybir.dt.float32
BF16 = mybir.dt.bfloat16
AF = mybir.ActivationFunctionType
ALU = mybir.AluOpType
AX = mybir.AxisListType

D = 2048
DEPTH = 2
LP, LS = 256, 1024
T = 2 * LP + LS
NBLK = 3
EPS = 1e-6
D_FF = 5632
NFF = D_FF // 128
ALPHA = (2.0 * DEPTH) ** 0.25
PAST = 512

IN_SPLITS = (
    ('gdn_q', 512), ('gdn_k', 512), ('gdn_v', 512), ('gdn_z', 512), ('gdn_b', 8), ('gdn_a', 8),
    ('gla_q', 256), ('gla_k', 256), ('gla_v', 512), ('gla_r', 512), ('gla_g', 32),
    ('hg_q', 512), ('hg_f', 1024), ('hg_i', 512), ('hg_g', 512),
    ('mla_qa', 512), ('mla_kva', 512), ('mla_kpe', 64), ('gates', 8192),
)
IN_WIDTH = sum(n for _, n in IN_SPLITS)


def _pieces():
    out, off = {}, 0
    for name, n in IN_SPLITS:
        if name in ('gdn_b',):
            out['gdn_ba'] = [(off, 16)]
        elif name == 'gdn_a':
            pass
        elif name in ('gla_q', 'gla_k'):
            out[name] = [(off + i * 64, 64) for i in range(4)]
        elif name == 'gla_g':
            out[name] = [(off, 16), (off + 16, 16)]
        elif name == 'mla_kpe':
            out[name] = [(off, 64)]
        else:
            out[name] = [(off + i * 128, 128) for i in range(n // 128)]
        off += n
    return out


PIECES = _pieces()

PM_COLS = {}
_o = 0
for _name, _n in (('conv', 60), ('gdn_norm', 1), ('gla_norm', 1), ('hg_norm', 1), ('q_norm', 4), ('kv_norm', 4),
                  ('b_gates', 64), ('hg_lb', 16), ('gla_gb', 8), ('b_ada', 96), ('ln1_g', 16), ('ln1_b', 16),
                  ('ln2_g', 16), ('ln2_b', 16), ('a_log', 8), ('dt_bias', 8)):
    PM_COLS[_name] = _o
    _o += _n
PM_N = _o

CS = {}
_o = 0
for _name, _n in (('ident', 128), ('le', 64), ('ge', 64), ('lt', 64), ('gt', 64), ('ones', 128), ('mean128', 128),
                  ('meanD', 128), ('mean512', 128), ('perm', 64), ('le32', 32), ('ge32', 32), ('cvT', 32), ('eps', 1), ('one', 1)):
    CS[_name] = _o
    _o += _n
CS_N = _o


ALL_RES = []
ALL_TL = []


class Res:
    __slots__ = ('w', 'r', 'name', 'nr', 'nw', 'excl')

    def __init__(self, name='?'):
        self.excl = False
        self.w = None
        self.r = {}
        self.name = name
        self.nr = 0
        self.nw = 0
        ALL_RES.append(self)


class _Rec:
    def __getattr__(self, name):
        return lambda *a, **k: (name, a, k)


_REC = _Rec()


class Prog:
    ENGS = ('pe', 'act', 'dve', 'pool', 'sp')

    def __init__(self, nc, stack):
        self.nc = nc
        self.streams = {e: [] for e in self.ENGS}
        self.cnt = {e: 0 for e in self.ENGS}
        self.esem = {}
        self.semobj = {}
        for e in ('pe', 'act', 'dve', 'pool'):
            s = stack.enter_context(nc.semaphore('tl_' + e))
            self.esem[e] = 'tl_' + e
            self.semobj['tl_' + e] = s
        self.ring = {}
        for q, k in (('sp', 12), ('pool', 12)):
            names = []
            for i in range(k):
                nm = 'rg_%s_%d' % (q, i)
                self.semobj[nm] = stack.enter_context(nc.semaphore(nm))
                names.append(nm)
            self.ring[q] = [names, 0]
        self.waited = {e: {} for e in self.ENGS}
        self.pending = {e: {} for e in self.ENGS}

    def op(self, eng, fn, R=(), W=(), dma=False):
        deps = {}

        def add(tok):
            if tok is None:
                return
            s, v = tok
            if deps.get(s, 0) < v:
                deps[s] = v
        for r in R:
            r.nr += 1
            add(r.w)
            if r.excl:
                for s_, v_ in r.r.items():
                    add((s_, v_))
        for w in W:
            w.nw += 1
            add(w.w)
            for s, v in w.r.items():
                add((s, v))
        if dma:
            names, n = self.ring[eng]
            k = len(names)
            sem = names[n % k]
            val = 16 * (n // k + 1)
            if n >= k:
                add((sem, val - 16))
            self.ring[eng][1] = n + 1
            inc = (sem, 16)
        else:
            self.cnt[eng] += 1
            sem = self.esem[eng]
            val = self.cnt[eng]
            inc = (sem, 1)
        tok = (sem, val)
        for s_, v_ in self.pending[eng].items():
            add((s_, v_))
        self.pending[eng] = {}
        waits = []
        wd = self.waited[eng]
        for s, v in deps.items():
            if eng == 'pe' and s == self.esem.get('pe'):
                continue
            if wd.get(s, 0) >= v:
                continue
            wd[s] = v
            waits.append((s, v))
        self.streams[eng].append((waits, fn(_REC), inc))
        for r in R:
            if r.excl:
                r.w = tok
                r.r = {}
            elif r.r.get(sem, 0) < val:
                r.r[sem] = val
        for w in W:
            w.w = tok
            w.r = {}
        return tok

    def barrier(self):
        cur = {}
        for e in ('pe', 'act', 'dve', 'pool'):
            if self.cnt[e] > 0:
                cur[self.esem[e]] = self.cnt[e]
        for q in self.ring:
            names, n = self.ring[q]
            k = len(names)
            for i, nm in enumerate(names):
                c = (n - i + k - 1) // k if n > i else 0
                if c > 0:
                    cur[nm] = 16 * c
        for e in self.ENGS:
            for s_, v_ in cur.items():
                if self.pending[e].get(s_, 0) < v_:
                    self.pending[e][s_] = v_

    def emit(self, block):
        nc = self.nc
        semobj = self.semobj

        def run(e, name):
            stream = self.streams[name]
            for waits, fn, inc in stream:
                for s, v in waits:
                    e.wait_ge(semobj[s], v)
                ins = getattr(e, fn[0])(*fn[1], **fn[2])
                ins.then_inc(semobj[inc[0]], inc[1])
            if True:
                for q in self.ring:
                    names, n = self.ring[q]
                    k = len(names)
                    for i, nm in enumerate(names):
                        cntq = (n - i + k - 1) // k if n > i else 0
                        if cntq > 0:
                            e.wait_ge(semobj[nm], 16 * cntq)
                for en in ('pe', 'act', 'dve', 'pool'):
                    if self.cnt[en] > 0:
                        e.wait_ge(semobj[self.esem[en]], self.cnt[en])

        @block.sync
        def _(e):
            run(e, 'sp')

        @block.tensor
        def _(e):
            run(e, 'pe')

        @block.scalar
        def _(e):
            run(e, 'act')

        @block.vector
        def _(e):
            run(e, 'dve')

        @block.gpsimd
        def _(e):
            run(e, 'pool')


class Tl:
    def __init__(self, t, name='?'):
        self.t = t
        self.name = name
        self._res = {}
        ALL_TL.append(self)

    def r(self, key=None):
        if key not in self._res:
            self._res[key] = Res(self.name)
            self._res[key].excl = self.name.startswith('ps')
        return self._res[key]

    def __getitem__(self, k):
        return self.t[k]


class _Stop(Exception):
    pass


CUT = [99]
DIRS = [0, 1]


PASSNO = [0]
CUTPASS = [0]


def cut(k):
    if CUT[0] <= k and PASSNO[0] >= CUTPASS[0]:
        raise _Stop()


def build(dbg=None, stage=99):
    nc = bass.Bass("TRN2", target_bir_lowering=False)
    stack = ExitStack()
    ein = lambda name, shape: nc.dram_tensor(name, list(shape), F32, kind="ExternalInput").ap()
    eout = lambda name, shape: nc.dram_tensor(name, list(shape), F32, kind="ExternalOutput").ap()
    xp = ein("xp", [2 * LP, D])
    xs = ein("xs", [LS, D])
    st_gdn = ein("st_gdn", [DEPTH, 2, 4, 128, 128])
    st_gla = ein("st_gla", [DEPTH, 2, 4, 64, 128])
    st_hg = ein("st_hg", [DEPTH, 2, 4, 128, 128])
    cx_ckv = ein("cx_ckv", [DEPTH, PAST, 512])
    cx_kpe = ein("cx_kpe", [DEPTH, PAST, 64])
    pm_d = ein("pm", [DEPTH, 128, PM_N])
    cs_d = ein("cs", [128, CS_N])
    rope_d = ein("rope", [64, 2 * LS])
    gw2_d = ein("gw2", [DEPTH, 16, 2 * 256])
    wada_d = ein("wada", [DEPTH, 128, 16 * 6 * D])
    win_d = ein("win", [DEPTH, 128, 16 * IN_WIDTH])
    wqb_d = ein("wqb", [DEPTH, 128, 4 * 768])
    wkvb_d = ein("wkvb", [DEPTH, 128, 4 * 1024])
    wbr_d = ein("wbr", [DEPTH, 128, 4 * 4 * D])
    wout_d = ein("wout", [DEPTH, 128, 16 * D])
    w1_d = ein("w1", [DEPTH, 128, 16 * D_FF])
    w3_d = ein("w3", [DEPTH, 128, 16 * D_FF])
    w2_d = ein("w2", [DEPTH, 128, NFF * D])
    y_p = eout("y_p", [2 * LP, D])
    y_s = eout("y_s", [LS, D])
    o_gdn = eout("o_gdn", [2, DEPTH, 2, 4, 128, 128])
    o_gla = eout("o_gla", [2, DEPTH, 2, 4, 64, 128])
    o_hg = eout("o_hg", [2, DEPTH, 2, 4, 128, 128])
    o_ckv = eout("o_ckv", [2, DEPTH, LP, 512])
    o_kpe = eout("o_kpe", [2, DEPTH, LP, 64])
    dbg_aps = {}
    if dbg:
        for name, shape in dbg.items():
            dbg_aps[name] = eout("dbg_" + name, shape)
    elif dbg is None and stage < 99:
        dbg_aps['modT0'] = nc.dram_tensor("sink_modT0", [128, 192], F32, kind="Internal").ap()
        dbg_aps['hT'] = nc.dram_tensor("sink_hT", [128, 16 * T], F32, kind="Internal").ap()
    xTd = nc.dram_tensor("xTd", [16, 128, T], F32, kind="Internal").ap()
    mgd = nc.dram_tensor("mgd", [16, 128, T], BF16, kind="Internal").ap()

    P = Prog(nc, stack)
    _uid = [0]

    def _mk(st_, name, shape, dt=F32):
        _uid[0] += 1
        return Tl(st_.enter_context(nc.sbuf_tensor("s%d_%s" % (_uid[0], name), list(shape), dt)), "s%d_%s" % (_uid[0], name))
    sb = lambda name, shape, dt=F32: _mk(stack, name, shape, dt)

    class Phase:
        def __enter__(self):
            self.st = ExitStack()
            return lambda name, shape, dt=F32: _mk(self.st, name, shape, dt)

        def __exit__(self, *a):
            self.st.close()
            P.barrier()
            return False

    cs = sb("cs", [128, CS_N])
    pm = [sb("pm%d" % l, [128, PM_N]) for l in range(DEPTH)]
    modT = [sb("modT%d" % l, [128, 2, 96]) for l in range(DEPTH)]
    hT = sb("hT", [128, 16, T], BF16)
    NWB = 6
    wbs = [sb("wb%d" % i, [128, 16 * 128], BF16) for i in range(NWB)]
    psb = [Tl(stack.enter_context(nc.psum_tensor("ps%d" % i, [128, 512], F32)), "ps%d" % i) for i in range(8)]
    st = {'wb': 0, 'ps': 0}
    obd = nc.dram_tensor("obd", [16, 128, T], BF16, kind="Internal").ap()
    obd_res = [Res() for _ in range(16)]
    mgd_res = [[Res() for _ in range(NBLK)] for _ in range(16)]
    xTd_res = [Res() for _ in range(NBLK)]

    ident = cs[:, CS['ident']:CS['ident'] + 128]
    ones = cs[:, CS['ones']:CS['ones'] + 128]
    cmat = lambda name, n=64: cs[0:n, CS[name]:CS[name] + n]
    EPSC = cs[:, CS['eps']:CS['eps'] + 1]
    ONEC = cs[:, CS['one']:CS['one'] + 1]

    def psum():
        b = psb[st['ps'] % 8]
        st['ps'] += 1
        return b

    def dma(q, out, in_, R=(), W=(), **kw):
        P.op(q, lambda e: e.dma_start(out=out, in_=in_, **kw), R, W, dma=True)

    def load_w(src2d, kc, n, pool=None):
        t = wbs[st['wb'] % NWB]
        st['wb'] += 1
        dma('pool', t[:, 0:kc * n], src2d, W=[t.r()])
        return t, t[:, 0:kc * n].rearrange("p (k n) -> p k n", k=kc)

    def win_piece(l, name, idx):
        c0, n = PIECES[name][idx]
        wt, wv = load_w(win_d[l][:, 16 * c0:16 * c0 + 16 * n], 16, n)
        return wt, wv, n

    def proj_h(wt, wv, n, b, out_ap, psr):
        for kc in range(16):
            P.op('pe', lambda e, kc=kc: e.matmul(out_ap, wv[:, kc, :], hT[:, kc, b * 512:(b + 1) * 512],
                                                 start=(kc == 0), stop=(kc == 15)),
                 R=[wt.r(), hT.r(b)], W=[psr])

    def A(fn, R, W):
        P.op('act', fn, R, W)

    def V(fn, R, W):
        P.op('dve', fn, R, W)

    def G(fn, R, W):
        P.op('pool', fn, R, W)

    def PE(fn, R, W):
        P.op('pe', fn, R, W)

    def dbg_dump(name, src_ap, R):
        if name in dbg_aps:
            dma('pool', dbg_aps[name], src_ap, R=R)

    def rstd_from(ps_ap, out_ap, psr, outr):
        A(lambda e: e.activation(out_ap, ps_ap, AF.Ln, bias=EPSC[0:out_ap.shape[0], :]), [psr, cs.r()], [outr])
        A(lambda e: e.activation(out_ap, out_ap, AF.Exp, scale=-0.5), [outr], [outr])

    dma('sp', cs[:, :], cs_d[:, :], W=[cs.r()])
    for l in range(DEPTH):
        dma('sp', pm[l][:, :], pm_d[l], W=[pm[l].r()])

    with Phase() as ph:
      if stage < 99:
          zt = ph("zt", [128, D])
          V(lambda e: e.memset(zt[:, :], 0.0), [], [zt.r()])
          for i in range(4):
              dma('sp', y_p[i * 128:(i + 1) * 128, :], zt[:, :], R=[zt.r()])
          for i in range(8):
              dma('sp', y_s[i * 128:(i + 1) * 128, :], zt[:, :], R=[zt.r()])
          for o_, dk_ in ((o_gdn, 128), (o_gla, 64), (o_hg, 128)):
              for a_ in range(2):
                  for b_ in range(DEPTH):
                      dma('sp', o_[a_, b_].rearrange("t h k v -> k (t h) v"),
                          zt[0:dk_, 0:1024].rearrange("k (th v) -> k th v", v=128), R=[zt.r()])
          for a_ in range(2):
              for b_ in range(DEPTH):
                  for i in range(2):
                      dma('sp', o_ckv[a_, b_, i * 128:(i + 1) * 128, :], zt[:, 0:512], R=[zt.r()])
                      dma('sp', o_kpe[a_, b_, i * 128:(i + 1) * 128, :], zt[:, 0:64], R=[zt.r()])

    with Phase() as ph:
        scT = ph("scT", [128, 16, 2], BF16)
        A(lambda e: e.activation(scT[:, :, :], cs[:, CS['cvT']:CS['cvT'] + 32].rearrange("p (k t) -> p k t", t=2),
                                 AF.Silu), [cs.r()], [scT.r()])
        for l in range(DEPTH):
            for g in range(24):
                ps = psum()
                for cc in range(4):
                    c = g * 4 + cc
                    wt, wv = load_w(wada_d[l][:, c * 2048:(c + 1) * 2048], 16, 128)
                    for kc in range(16):
                        PE(lambda e, wv=wv, kc=kc, ps=ps, cc=cc: e.matmul(
                            ps[:, cc * 2:cc * 2 + 2], wv[:, kc, :], scT[:, kc, :], start=(kc == 0), stop=(kc == 15)),
                            [wt.r(), scT.r()], [ps.r()])
                for kind in range(2):
                    V(lambda e, ps=ps, g=g, kind=kind, l=l: e.tensor_tensor(
                        modT[l][:, kind, g * 4:g * 4 + 4],
                        ps[:, 0:8].rearrange("p (c t) -> p c t", t=2)[:, :, kind],
                        pm[l][:, PM_COLS['b_ada'] + g * 4:PM_COLS['b_ada'] + g * 4 + 4], ALU.add),
                        [ps.r(), pm[l].r()], [modT[l].r()])
            for j in (1, 4):
                V(lambda e, l=l, j=j: e.tensor_scalar_add(
                    modT[l][:, :, j * 16:(j + 1) * 16], modT[l][:, :, j * 16:(j + 1) * 16], 1.0),
                    [modT[l].r()], [modT[l].r()])
            dbg_dump('modT%d' % l, modT[l][:, :, :].rearrange("p a b -> p (a b)"), [modT[l].r()])

    with Phase() as ph:
        xtok = [ph("xtok%d" % i, [128, D]) for i in range(2)]
        xTb = ph("xTb", [128, 16, 512])
        for tt in range(T // 128):
            xt = xtok[tt % 2]
            src = xp[tt * 128:(tt + 1) * 128, :] if tt < 4 else xs[(tt - 4) * 128:(tt - 3) * 128, :]
            dma('sp', xt[:, :], src, W=[xt.r()])
            for g in range(4):
                ps = psum()
                for cc in range(4):
                    c = g * 4 + cc
                    PE(lambda e, ps=ps, cc=cc, c=c, xt=xt: e.transpose(
                        ps[:, cc * 128:(cc + 1) * 128], xt[:, c * 128:(c + 1) * 128], ident),
                        [xt.r(), cs.r()], [ps.r()])
                dst = xTb[:, g * 4:(g + 1) * 4, (tt % 4) * 128:(tt % 4 + 1) * 128]
                srcp = ps[:, :].rearrange("p (c t) -> p c t", c=4)
                if g % 2 == 0:
                    A(lambda e, dst=dst, srcp=srcp: e.copy(dst, srcp), [ps.r()], [xTb.r(tt % 4)])
                else:
                    V(lambda e, dst=dst, srcp=srcp: e.tensor_copy(dst, srcp), [ps.r()], [xTb.r(tt % 4)])
            if tt % 4 == 3:
                b = tt // 4
                dma('sp', xTd[:, :, b * 512:(b + 1) * 512].rearrange("c p t -> p c t"), xTb[:, :, :],
                    R=[xTb.r(i) for i in range(4)], W=[xTd_res[b]])

    SEQS = [(0, LP, 0, 0), (LP, LP, 0, 1), (2 * LP, LS, 1, 2)]
    TP = T + 12
    SEGS = [(0, 0, 256, 0), (0, 256, 256, 256 + 4), (1, 0, 512, 512 + 8), (2, 0, 512, 1024 + 8)]

    def bc3(ap2d, n):
        p, f = ap2d.shape
        return ap2d.unsqueeze(1).to_broadcast([p, n, f])

    def gdn_unit(l, h, ph0, bet, gg):
      cut(0)
      with Phase() as ph:
        raw = ph("raw", [128, TP])
        cv = ph("cv", [128, 3, TP])
        sq = ph("sq", [128, TP])
        rst = ph("rst", [128, 512])
        zs = ph("zs", [128, T])
        oacc = ph("oacc", [128, T])
        V(lambda e: e.memset(raw[:, :], 0.0), [], [raw.r()])
        for i_ in range(3):
            G(lambda e, i_=i_: e.memset(cv[:, i_, :], 0.0), [], [cv.r(i_)])
        for i, nm in enumerate(('gdn_q', 'gdn_k', 'gdn_v')):
            wt, wv, n_ = win_piece(l, nm, h)
            for b in range(NBLK):
                ps = psum()
                proj_h(wt, wv, 128, b, ps[:, :], ps.r())
                for (sb_, so, sn, cd) in SEGS:
                    if sb_ != b:
                        continue
                    A(lambda e, ps=ps, so=so, sn=sn, cd=cd: e.copy(raw[:, cd + 2:cd + 2 + sn], ps[:, so:so + sn]),
                      [ps.r()], [raw.r()])
            acc = cv[:, i, 2:TP - 2]
            ccol = lambda tap: pm[l][:, PM_COLS['conv'] + (i * 4 + h) * 5 + tap:PM_COLS['conv'] + (i * 4 + h) * 5 + tap + 1]
            V(lambda e, acc=acc, ccol=ccol: e.tensor_scalar(acc, raw[:, 0:TP - 4], ccol(0), None, ALU.mult),
              [raw.r(), pm[l].r()], [cv.r(i)])
            for tap in range(1, 5):
                V(lambda e, acc=acc, ccol=ccol, tap=tap: e.scalar_tensor_tensor(
                    acc, raw[:, tap:TP - 4 + tap], ccol(tap), acc, ALU.mult, ALU.add),
                    [raw.r(), pm[l].r(), cv.r(i)], [cv.r(i)])
            A(lambda e, acc=acc: e.activation(acc, acc, AF.Silu), [cv.r(i)], [cv.r(i)])
            if i < 2:
                A(lambda e, acc=acc: e.activation(sq[:, 2:TP - 2], acc, AF.Square), [cv.r(i)], [sq.r()])
                for (sb_, so, sn, cd) in SEGS:
                    ps = psum()
                    PE(lambda e, ps=ps, sn=sn, cd=cd: e.matmul(ps[:, 0:sn], ones, sq[:, cd + 2:cd + 2 + sn],
                                                               start=True, stop=True), [sq.r(), cs.r()], [ps.r()])
                    rstd_from(ps[:, 0:sn], rst[:, 0:sn], ps.r(), rst.r())
                    sc_ = (128.0 ** -0.5) if i == 0 else 1.0
                    V(lambda e, i=i, sn=sn, cd=cd, sc_=sc_: e.scalar_tensor_tensor(
                        cv[:, i, cd + 2:cd + 2 + sn], cv[:, i, cd + 2:cd + 2 + sn], sc_, rst[:, 0:sn], ALU.mult, ALU.mult),
                        [cv.r(i), rst.r()], [cv.r(i)])
        wt, wv, n_ = win_piece(l, 'gdn_z', h)
        for b in range(NBLK):
            ps = psum()
            proj_h(wt, wv, 128, b, ps[:, :], ps.r())
            A(lambda e, ps=ps, b=b: e.activation(zs[:, b * 512:(b + 1) * 512], ps[:, :], AF.Silu), [ps.r()], [zs.r()])
        if h == 0 and l == 0:
            dbg_dump('cv', cv[:, :, :].rearrange("p a b -> p (a b)"), [cv.r(i) for i in range(3)])
        cut(1)
        for (tok0, L, kind, idx) in SEQS:
          with Phase() as ps_:
            nch = min(8, L // 64)
            nbatch = (L // 64) // nch
            geo = {}
            QT = lambda n: cv[:, 0, geo['cb'] + n * 64:geo['cb'] + (n + 1) * 64]
            KT = lambda n: cv[:, 1, geo['cb'] + n * 64:geo['cb'] + (n + 1) * 64]
            VT = lambda n: cv[:, 2, geo['cb'] + n * 64:geo['cb'] + (n + 1) * 64]
            ktok = ps_("ktok", [64, nch, 128])
            vtok = ps_("vtok", [64, nch, 128])
            R2 = ps_("R2", [64, nch, 64])
            eg = ps_("eg", [128, nch, 64])
            rb = ps_("rb", [64, nch, 64])
            gc = ps_("gc", [64, nch])
            gl = ps_("gl", [128, nch])
            cdec = ps_("cdec", [128, nch])
            kdsc = ps_("kdsc", [64, nch])
            bw = ps_("bw", [64, nch])
            X = ps_("X", [64, nch, 64])
            dT = ps_("dT", [64, nch, 64])
            dd = ps_("dd", [64, nch, 64])
            M = ps_("M", [64, nch, 64])
            MT = ps_("MT", [64, nch, 64])
            Pa = ps_("Pa", [64, nch, 64])
            PTa = ps_("PTa", [64, nch, 64])
            RT = ps_("RT", [64, nch, 64])
            AT = ps_("AT", [128, nch, 64])
            vb = ps_("vb", [64, nch, 128])
            kw = ps_("kw", [64, nch, 128])
            ub = ps_("ub", [64, nch, 128])
            wT = ps_("wT", [128, nch, 64])
            kdec = ps_("kdec", [64, nch, 128])
            qdT = ps_("qdT", [128, nch, 64])
            S = ps_("S", [128, 128])
            u = ps_("u", [128, 128])
            V(lambda e: e.memset(AT[:, :, :], 0.0), [], [AT.r()])
            V(lambda e: e.memset(u[:, :], 0.0), [], [u.r()])
            for dr in DIRS:
                PASSNO[0] += 1
                U = cmat('le') if dr == 0 else cmat('ge')
                inclT = U
                strict = cmat('gt') if dr == 0 else cmat('lt')
                col = dr * 4 + h
                last = 63 if dr == 0 else 0
                gcolv = lambda n: gg[:, geo['n0'] + n, col:col + 1]
                bcolv = lambda n: bet[:, geo['n0'] + n, col:col + 1]
                if kind == 0:
                    V(lambda e: e.memset(S[:, :], 0.0), [], [S.r()])
                else:
                    dma('sp', S[:, :], st_gdn[l, dr, h], W=[S.r()])
                border = range(nbatch) if dr == 0 else range(nbatch - 1, -1, -1)
                for bi in border:
                    tokb = tok0 + bi * nch * 64
                    n0 = tokb // 64
                    cb = tokb + 4 * idx + 2
                    geo['cb'] = cb
                    geo['n0'] = n0
                    for (src, dstt) in ((KT, ktok), (VT, vtok)):
                        for n4 in range(0, nch, 4):
                            ps = psum()
                            for n in range(n4, n4 + 4):
                                PE(lambda e, ps=ps, n=n, n4=n4, src=src: e.transpose(
                                    ps[0:64, (n - n4) * 128:(n - n4 + 1) * 128], src(n), ident),
                                    [cv.r(1), cv.r(2), cs.r()], [ps.r()])
                            A(lambda e, ps=ps, n4=n4, dstt=dstt: e.copy(
                                dstt[:, n4:n4 + 4, :], ps[0:64, :].rearrange("p (n d) -> p n d", n=4)), [ps.r()], [dstt.r()])
                    cut(2)
                    for n in range(nch):
                        V(lambda e, n=n, U=U: e.tensor_scalar(R2[:, n, :], U, gcolv(n), None, ALU.mult),
                          [gg.r(), cs.r()], [R2.r()])
                    for n8 in range(0, nch, 8):
                        w8 = min(8, nch - n8)
                        ps = psum()
                        PE(lambda e, ps=ps, n8=n8, w8=w8: e.matmul(
                            ps[:, 0:w8 * 64], ones[0:64, :], R2[:, n8:n8 + w8, :].rearrange("p n j -> p (n j)"),
                            start=True, stop=True), [R2.r(), cs.r()], [ps.r()])
                        A(lambda e, ps=ps, n8=n8, w8=w8: e.activation(
                            eg[:, n8:n8 + w8, :].rearrange("p n j -> p (n j)"), ps[:, 0:w8 * 64], AF.Exp), [ps.r()], [eg.r()])
                        V(lambda e, ps=ps, n8=n8, w8=w8: e.tensor_copy(
                            rb[:, n8:n8 + w8, :].rearrange("p n j -> p (n j)"), ps[0:64, 0:w8 * 64]), [ps.r()], [rb.r()])
                        V(lambda e, ps=ps, n8=n8, w8=w8: e.tensor_copy(
                            gl[:, n8:n8 + w8], ps[:, 0:w8 * 64].rearrange("p (n j) -> p n j", j=64)[:, :, last]), [ps.r()], [gl.r()])
                    cut(2.2)
                    ps = psum()
                    PE(lambda e, ps=ps, U=U: e.matmul(ps[0:64, 0:nch], U, gg[:, n0:n0 + nch, col], start=True, stop=True),
                       [gg.r(), cs.r()], [ps.r()])
                    A(lambda e, ps=ps: e.copy(gc[:, :], ps[0:64, 0:nch]), [ps.r()], [gc.r()])
                    cut(2.5)
                    A(lambda e: e.activation(cdec[:, :], gl[:, :], AF.Exp), [gl.r()], [cdec.r()])
                    V(lambda e: e.tensor_tensor(kdsc[:, :], gl[0:64, :], gc[:, :], ALU.subtract), [gl.r(), gc.r()], [kdsc.r()])
                    A(lambda e: e.activation(kdsc[:, :], kdsc[:, :], AF.Exp), [kdsc.r()], [kdsc.r()])
                    A(lambda e: e.activation(bw[:, :], gc[:, :], AF.Exp), [gc.r()], [bw.r()])
                    V(lambda e: e.tensor_tensor(bw[:, :], bw[:, :], bet[:, n0:n0 + nch, col], ALU.mult), [bw.r(), bet.r()], [bw.r()])
                    cut(2.7)
                    for n in range(nch):
                        V(lambda e, n=n: e.tensor_scalar(X[:, n, :], rb[:, n, :], gc[:, n:n + 1], 0.0, ALU.subtract, ALU.min),
                          [rb.r(), gc.r()], [X.r()])
                        V(lambda e, n=n: e.tensor_scalar(dd[:, n, :], rb[:, n, :], gc[:, n:n + 1], 0.0, ALU.subtract, ALU.max),
                          [rb.r(), gc.r()], [dd.r()])
                    cut(3)
                    A(lambda e: e.activation(dT[:, :, :], X[:, :, :], AF.Exp), [X.r()], [dT.r()])
                    A(lambda e: e.activation(dd[:, :, :], dd[:, :, :], AF.Exp, scale=-1.0), [dd.r()], [dd.r()])
                    V(lambda e, inclT=inclT: e.tensor_tensor(dT[:, :, :], dT[:, :, :], bc3(inclT, nch), ALU.mult),
                      [dT.r(), cs.r()], [dT.r()])
                    V(lambda e, strict=strict: e.tensor_tensor(dd[:, :, :], dd[:, :, :], bc3(strict, nch), ALU.mult),
                      [dd.r(), cs.r()], [dd.r()])
                    cut(4)
                    for n8 in range(0, nch, 8):
                        w8 = min(8, nch - n8)
                        psA = psum()
                        psQ = psum()
                        for n in range(n8, n8 + w8):
                            PE(lambda e, n=n, n8=n8, psA=psA: e.matmul(psA[0:64, (n - n8) * 64:(n - n8 + 1) * 64], KT(n), KT(n),
                                                                        start=True, stop=True), [cv.r(1)], [psA.r()])
                            PE(lambda e, n=n, n8=n8, psQ=psQ: e.matmul(psQ[0:64, (n - n8) * 64:(n - n8 + 1) * 64], KT(n), QT(n),
                                                                        start=True, stop=True), [cv.r(1), cv.r(0)], [psQ.r()])
                        for n in range(n8, n8 + w8):
                            V(lambda e, n=n, n8=n8, psA=psA: e.scalar_tensor_tensor(
                                M[:, n, :], psA[0:64, (n - n8) * 64:(n - n8 + 1) * 64], bcolv(n), dd[:, n, :], ALU.mult, ALU.mult),
                                [psA.r(), bet.r(), dd.r()], [M.r()])
                        V(lambda e, n8=n8, w8=w8, psQ=psQ: e.tensor_tensor(
                            AT[0:64, n8:n8 + w8, :].rearrange("p n j -> p (n j)"), psQ[0:64, 0:w8 * 64],
                            dT[:, n8:n8 + w8, :].rearrange("p n j -> p (n j)"), ALU.mult), [psQ.r(), dT.r()], [AT.r()])
                    cut(5)
                    for n8 in range(0, nch, 8):
                        w8 = min(8, nch - n8)
                        ps = psum()
                        for n in range(n8, n8 + w8):
                            PE(lambda e, n=n, n8=n8, ps=ps: e.transpose(ps[0:64, (n - n8) * 64:(n - n8 + 1) * 64], M[:, n, :], ident[0:64, 0:64]),
                               [M.r(), cs.r()], [ps.r()])
                        A(lambda e, n8=n8, w8=w8, ps=ps: e.copy(MT[:, n8:n8 + w8, :].rearrange("p n j -> p (n j)"), ps[0:64, 0:w8 * 64]),
                          [ps.r()], [MT.r()])
                    V(lambda e: e.tensor_tensor(RT[:, :, :], bc3(ident[0:64, 0:64], nch), MT[:, :, :], ALU.subtract),
                      [MT.r(), cs.r()], [RT.r()])
                    Pc, PTc = M, MT
                    Pn, PTn = Pa, PTa
                    for it in range(5):
                        for n8 in range(0, nch, 8):
                            w8 = min(8, nch - n8)
                            p1 = psum()
                            p2 = psum()
                            for n in range(n8, n8 + w8):
                                sl = slice((n - n8) * 64, (n - n8 + 1) * 64)
                                PE(lambda e, n=n, sl=sl, p1=p1, Pc=Pc, PTc=PTc: e.matmul(p1[0:64, sl], PTc[:, n, :], Pc[:, n, :], start=True, stop=True),
                                   [Pc.r(), PTc.r()], [p1.r()])
                                PE(lambda e, n=n, sl=sl, p2=p2, Pc=Pc, PTc=PTc: e.matmul(p2[0:64, sl], Pc[:, n, :], PTc[:, n, :], start=True, stop=True),
                                   [Pc.r(), PTc.r()], [p2.r()])
                            A(lambda e, n8=n8, w8=w8, p1=p1, Pn=Pn: e.copy(Pn[:, n8:n8 + w8, :].rearrange("p n j -> p (n j)"), p1[0:64, 0:w8 * 64]),
                              [p1.r()], [Pn.r()])
                            V(lambda e, n8=n8, w8=w8, p2=p2, PTn=PTn: e.tensor_copy(PTn[:, n8:n8 + w8, :].rearrange("p n j -> p (n j)"), p2[0:64, 0:w8 * 64]),
                              [p2.r()], [PTn.r()])
                        for n8 in range(0, nch, 8):
                            w8 = min(8, nch - n8)
                            p3 = psum()
                            for n in range(n8, n8 + w8):
                                sl = slice((n - n8) * 64, (n - n8 + 1) * 64)
                                PE(lambda e, n=n, sl=sl, p3=p3, Pn=Pn: e.matmul(p3[0:64, sl], Pn[:, n, :], RT[:, n, :], start=True, stop=True),
                                   [Pn.r(), RT.r()], [p3.r()])
                            V(lambda e, n8=n8, w8=w8, p3=p3: e.tensor_tensor(
                                RT[:, n8:n8 + w8, :].rearrange("p n j -> p (n j)"), RT[:, n8:n8 + w8, :].rearrange("p n j -> p (n j)"),
                                p3[0:64, 0:w8 * 64], ALU.add), [p3.r(), RT.r()], [RT.r()])
                        Pc, PTc, Pn, PTn = Pn, PTn, Pc, PTc
                    cut(6)
                    for n in range(nch):
                        V(lambda e, n=n: e.tensor_scalar(vb[:, n, :], vtok[:, n, :], bcolv(n), None, ALU.mult), [vtok.r(), bet.r()], [vb.r()])
                        G(lambda e, n=n: e.tensor_scalar(kw[:, n, :], ktok[:, n, :], bw[:, n:n + 1], None, ALU.mult), [ktok.r(), bw.r()], [kw.r()])
                        G(lambda e, n=n: e.tensor_scalar(kdec[:, n, :], ktok[:, n, :], kdsc[:, n:n + 1], None, ALU.mult), [ktok.r(), kdsc.r()], [kdec.r()])
                    for n4 in range(0, nch, 4):
                        ps = psum()
                        for n in range(n4, n4 + 4):
                            PE(lambda e, n=n, n4=n4, ps=ps: e.matmul(ps[0:64, (n - n4) * 128:(n - n4 + 1) * 128], RT[:, n, :], vb[:, n, :],
                                                                       start=True, stop=True), [RT.r(), vb.r()], [ps.r()])
                        A(lambda e, n4=n4, ps=ps: e.copy(ub[:, n4:n4 + 4, :].rearrange("p n d -> p (n d)"), ps[0:64, :]), [ps.r()], [ub.r()])
                    for n8 in range(0, nch, 8):
                        w8 = min(8, nch - n8)
                        ps = psum()
                        for n in range(n8, n8 + w8):
                            PE(lambda e, n=n, n8=n8, ps=ps: e.matmul(ps[:, (n - n8) * 64:(n - n8 + 1) * 64], kw[:, n, :], RT[:, n, :],
                                                                       start=True, stop=True), [RT.r(), kw.r()], [ps.r()])
                        V(lambda e, n8=n8, w8=w8, ps=ps: e.tensor_copy(wT[:, n8:n8 + w8, :].rearrange("p n j -> p (n j)"), ps[:, 0:w8 * 64]),
                          [ps.r()], [wT.r()])
                    V(lambda e: e.tensor_tensor(qdT[:, :, :].rearrange("p n j -> p (n j)"), cv[:, 0, cb:cb + nch * 64],
                                                eg[:, :, :].rearrange("p n j -> p (n j)"), ALU.mult), [cv.r(0), eg.r()], [qdT.r()])
                    cut(7)
                    order = range(nch) if dr == 0 else range(nch - 1, -1, -1)
                    for n in order:
                        pu = psum()
                        PE(lambda e, n=n, pu=pu: e.matmul(pu[0:64, 0:128], wT[:, n, :], S[:, :], start=True, stop=True), [wT.r(), S.r()], [pu.r()])
                        V(lambda e, n=n, pu=pu: e.tensor_tensor(u[0:64, :], ub[:, n, :], pu[0:64, 0:128], ALU.subtract), [ub.r(), pu.r()], [u.r()])
                        cut(8)
                        po = psum()
                        PE(lambda e, n=n, po=po: e.matmul(po[:, 0:64], S[:, :], qdT[:, n, :], start=True, stop=False), [S.r(), qdT.r()], [po.r()])
                        PE(lambda e, n=n, po=po: e.matmul(po[:, 0:64], u[:, :], AT[:, n, :], start=False, stop=True), [u.r(), AT.r()], [po.r()])
                        osl = oacc[:, tokb + n * 64:tokb + (n + 1) * 64]
                        if dr == 0:
                            A(lambda e, po=po, osl=osl: e.copy(osl, po[:, 0:64]), [po.r()], [oacc.r(idx)])
                        else:
                            V(lambda e, po=po, osl=osl: e.tensor_tensor(osl, osl, po[:, 0:64], ALU.add), [po.r(), oacc.r(idx)], [oacc.r(idx)])
                        cut(9)
                        pS = psum()
                        PE(lambda e, n=n, pS=pS: e.matmul(pS[:, 0:128], kdec[:, n, :], u[0:64, :], start=True, stop=True), [kdec.r(), u.r()], [pS.r()])
                        V(lambda e, n=n, pS=pS: e.scalar_tensor_tensor(S[:, :], S[:, :], cdec[:, n:n + 1], pS[:, 0:128], ALU.mult, ALU.add),
                          [S.r(), cdec.r(), pS.r()], [S.r()])
                cut(10)
                if kind == 0:
                    dma('sp', o_gdn[idx, l, dr, h], S[:, :], R=[S.r()])
                cut(11 + idx * 2 + dr)
        if h == 0 and l == 0:
            dbg_dump('oacc', oacc[:, :], [oacc.r(i) for i in range(3)])
        ob = ph("ob", [128, T], BF16)
        A(lambda e: e.activation(sq[:, 0:T], oacc[:, :], AF.Square), [oacc.r(i) for i in range(3)], [sq.r()])
        for b in range(NBLK):
            ps = psum()
            PE(lambda e, ps=ps, b=b: e.matmul(ps[:, :], cs[:, CS['mean128']:CS['mean128'] + 128], sq[:, b * 512:(b + 1) * 512],
                                              start=True, stop=True), [sq.r(), cs.r()], [ps.r()])
            rstd_from(ps[:, :], rst[:, :], ps.r(), rst.r())
            V(lambda e, b=b: e.tensor_tensor(oacc[:, b * 512:(b + 1) * 512], oacc[:, b * 512:(b + 1) * 512], rst[:, :], ALU.mult),
              [rst.r()] + [oacc.r(i) for i in range(3)], [oacc.r(i) for i in range(3)])
            V(lambda e, b=b: e.scalar_tensor_tensor(ob[:, b * 512:(b + 1) * 512], oacc[:, b * 512:(b + 1) * 512],
                                                    pm[l][:, PM_COLS['gdn_norm']:PM_COLS['gdn_norm'] + 1], zs[:, b * 512:(b + 1) * 512],
                                                    ALU.mult, ALU.mult), [zs.r(), pm[l].r()] + [oacc.r(i) for i in range(3)], [ob.r()])
        if stage > 4:
            dma('sp', obd[0 * 4 + h], ob[:, :], R=[ob.r()], W=[obd_res[0 * 4 + h]])
        if h == 0 and l == 0:
            dbg_dump('ob', ob[:, :], [ob.r()])

    def bcl(ap3, c):
        p, n, _ = ap3.shape
        return ap3.to_broadcast([p, n, c])

    CH = 32

    def gla_unit(l, h, mixer, shared):
      PK = 64 if mixer == 'gla' else 128
      bi_ = 1 if mixer == 'gla' else 2
      rmask = shared['rmask']
      with Phase() as ph:
        qT = ph("qT", [PK, T])
        kTs = [ph("kT0", [PK, T])] if mixer == 'gla' else [ph("kT0", [PK, T]), ph("kT1", [PK, T])]
        vT = ph("vT", [128, T])
        las = [ph("la0", [PK, T]), ph("la1", [PK, T])]
        gate = ph("gate", [128, T], BF16)
        oacc = ph("oacc", [128, T])
        bsc = ph("bsc", [PK, T])
        qd = ph("qd", [PK, T])
        kt = ph("kt", [PK, T])
        kd = ph("kd", [PK, T])
        tE = ph("tE", [PK, T])
        xs = ph("xs", [128, 512])
        t1 = ph("t1", [128, 512])

        def inproj(name, idx, fn):
            wt, wv, n_ = win_piece(l, name, idx)
            for b in range(NBLK):
                ps = psum()
                proj_h(wt, wv, n_, b, ps[0:n_, :], ps.r())
                fn(b, ps, n_)
        sl = lambda b: slice(b * 512, (b + 1) * 512)
        if mixer == 'gla':
            inproj('gla_q', h, lambda b, ps, n_: A(lambda e: e.copy(qT[:, sl(b)], ps[0:64, :]), [ps.r()], [qT.r()]))
            inproj('gla_k', h, lambda b, ps, n_: V(lambda e: e.tensor_copy(kTs[0][:, sl(b)], ps[0:64, :]), [ps.r()], [kTs[0].r()]))
            inproj('gla_v', h, lambda b, ps, n_: A(lambda e: e.copy(vT[:, sl(b)], ps[:, :]), [ps.r()], [vT.r()]))
            inproj('gla_r', h, lambda b, ps, n_: A(lambda e: e.activation(gate[:, sl(b)], ps[:, :], AF.Silu), [ps.r()], [gate.r()]))
            glr, gw2 = shared['glr'], shared['gw2']
            for dr in range(2):
                for b in range(NBLK):
                    ps = psum()
                    PE(lambda e: e.matmul(ps[0:64, :], gw2[:, dr * 256 + h * 64:dr * 256 + (h + 1) * 64], glr[dr][:, sl(b)],
                                          start=True, stop=True), [gw2.r(), glr[dr].r()], [ps.r()])
                    gb = pm[l][0:64, PM_COLS['gla_gb'] + dr * 4 + h:PM_COLS['gla_gb'] + dr * 4 + h + 1]
                    A(lambda e: e.activation(xs[0:64, :], ps[0:64, :], AF.Identity, bias=gb), [ps.r(), pm[l].r()], [xs.r()])
                    A(lambda e: e.activation(t1[0:64, :], xs[0:64, :], AF.Abs), [xs.r()], [t1.r()])
                    A(lambda e: e.activation(t1[0:64, :], t1[0:64, :], AF.Exp, scale=-1.0), [t1.r()], [t1.r()])
                    A(lambda e: e.activation(t1[0:64, :], t1[0:64, :], AF.Ln, bias=ONEC[0:64, :]), [t1.r(), cs.r()], [t1.r()])
                    V(lambda e: e.scalar_tensor_tensor(xs[0:64, :], xs[0:64, :], 0.0, t1[0:64, :], ALU.min, ALU.subtract),
                      [xs.r(), t1.r()], [xs.r()])
                    V(lambda e: e.tensor_scalar(las[dr][:, sl(b)], xs[0:64, :], 1.0 / 16.0, None, ALU.mult), [xs.r()], [las[dr].r()])
        else:
            inproj('hg_q', h, lambda b, ps, n_: A(lambda e: e.activation(qT[:, sl(b)], ps[:, :], AF.Silu), [ps.r()], [qT.r()]))
            inproj('hg_i', h, lambda b, ps, n_: A(lambda e: e.copy(vT[:, sl(b)], ps[:, :]), [ps.r()], [vT.r()]))
            inproj('hg_g', h, lambda b, ps, n_: A(lambda e: e.activation(gate[:, sl(b)], ps[:, :], AF.Sigmoid), [ps.r()], [gate.r()]))
            lbt, oml = shared['lbt'], shared['oml']
            for dr in range(2):
                j = dr * 4 + h

                def fz(b, ps, n_, dr=dr, j=j):
                    A(lambda e: e.activation(xs[:, :], ps[:, :], AF.Sigmoid), [ps.r()], [xs.r()])
                    V(lambda e: e.tensor_scalar(xs[:, :], xs[:, :], oml[:, j:j + 1], lbt[:, j:j + 1], ALU.mult, ALU.add),
                      [xs.r(), oml.r(), lbt.r()], [xs.r()])
                    A(lambda e: e.activation(las[dr][:, sl(b)], xs[:, :], AF.Ln), [xs.r()], [las[dr].r()])
                    V(lambda e: e.tensor_scalar(kTs[dr][:, sl(b)], xs[:, :], -1.0, 1.0, ALU.mult, ALU.add), [xs.r()], [kTs[dr].r()])
                inproj('hg_f', j, fz)
        NB = 8
        for dr in range(2):
            kT = kTs[min(dr, len(kTs) - 1)]
            la = las[dr]
            V(lambda e: e.tensor_tensor_scan(bsc[:, :], rmask[0:PK, :], la[:, :], 0.0, ALU.mult, ALU.add),
              [rmask.r(), la.r()], [bsc.r()])
            b3 = bsc[:, :].rearrange("p (n c) -> p n c", c=CH)
            tot3 = b3[:, :, CH - 1:CH]
            if dr == 1:
                V(lambda e: e.tensor_tensor(tE[:, :], la[:, :], bsc[:, :], ALU.subtract), [la.r(), bsc.r()], [tE.r()])
                V(lambda e: e.tensor_tensor(tE[:, :].rearrange("p (n c) -> p n c", c=CH),
                                            tE[:, :].rearrange("p (n c) -> p n c", c=CH), bcl(tot3, CH), ALU.add),
                  [tE.r(), bsc.r()], [tE.r()])
                bcur = tE
            else:
                bcur = bsc
            V(lambda e: e.tensor_tensor(kd[:, :].rearrange("p (n c) -> p n c", c=CH), bcl(tot3, CH),
                                        bcur[:, :].rearrange("p (n c) -> p n c", c=CH), ALU.subtract),
              [bsc.r(), bcur.r()], [kd.r()])
            A(lambda e: e.activation(kd[:, :], kd[:, :], AF.Exp), [kd.r()], [kd.r()])
            V(lambda e: e.tensor_tensor(kd[:, :], kd[:, :], kT[:, :], ALU.mult), [kd.r(), kT.r()], [kd.r()])
            cdec = ph("cdec%d" % dr, [PK, T // CH])
            A(lambda e: e.activation(cdec[:, :], tot3.rearrange("p n c -> p (n c)"), AF.Exp), [bsc.r()], [cdec.r()])
            A(lambda e: e.activation(qd[:, :], bcur[:, :], AF.Exp), [bcur.r()], [qd.r()])
            A(lambda e: e.activation(kt[:, :], bcur[:, :], AF.Exp, scale=-1.0), [bcur.r()], [kt.r()])
            qs_ = (64.0 ** -0.5) if mixer == 'gla' else 1.0
            V(lambda e: e.scalar_tensor_tensor(qd[:, :], qT[:, :], qs_, qd[:, :], ALU.mult, ALU.mult), [qT.r(), qd.r()], [qd.r()])
            G(lambda e: e.tensor_tensor(kt[:, :], kt[:, :], kT[:, :], ALU.mult), [kt.r(), kT.r()], [kt.r()])
            maskT = cmat('le32', 32) if dr == 0 else cmat('ge32', 32)
            for (tok0, L, kind, idx) in SEQS:
              with Phase() as ps_:
                nb = min(NB, L // CH)
                nbatch = (L // CH) // nb
                vtok = ps_("vtok", [CH, nb, 128])
                kdtok = ps_("kdtok", [CH, nb, PK])
                ATm = ps_("ATm", [CH, nb, CH])
                KV = ps_("KV", [PK, nb, 128])
                Sall = ps_("Sall", [PK, nb + 1, 128])
                otmp = ps_("otmp", [128, nb * CH])
                if kind == 0:
                    V(lambda e: e.memset(Sall[:, 0, :], 0.0), [], [Sall.r()])
                else:
                    st_in = st_gla if mixer == 'gla' else st_hg
                    dma('sp', Sall[:, 0, :], st_in[l, dr, h], W=[Sall.r()])
                border = range(nbatch) if dr == 0 else range(nbatch - 1, -1, -1)
                for bix, bi in enumerate(border):
                    tokb = tok0 + bi * nb * CH
                    c0 = tokb // CH
                    csl = lambda n: slice(tokb + n * CH, tokb + (n + 1) * CH)
                    if bix > 0:
                        V(lambda e: e.tensor_copy(Sall[:, 0, :], Sall[:, nb, :]), [Sall.r()], [Sall.r()])
                    for (srcT, dstt, pw) in ((vT, vtok, 128), (kd, kdtok, PK)):
                        per = 512 // pw
                        for n4 in range(0, nb, per):
                            ps = psum()
                            for n in range(n4, min(nb, n4 + per)):
                                PE(lambda e, n=n: e.transpose(ps[0:CH, (n - n4) * pw:(n - n4 + 1) * pw], srcT[:, csl(n)], ident[0:pw, 0:pw]),
                                   [srcT.r(), cs.r()], [ps.r()])
                            w_ = min(nb, n4 + per) - n4
                            A(lambda e: e.copy(dstt[:, n4:n4 + w_, :].rearrange("p n d -> p (n d)"), ps[0:CH, 0:w_ * pw]),
                              [ps.r()], [dstt.r()])
                    ps = psum()
                    for n in range(nb):
                        PE(lambda e, n=n: e.matmul(ps[0:CH, n * CH:(n + 1) * CH], kt[:, csl(n)], qd[:, csl(n)], start=True, stop=True),
                           [kt.r(), qd.r()], [ps.r()])
                    V(lambda e: e.tensor_tensor(ATm[:, :, :], ps[0:CH, 0:nb * CH].rearrange("p (n c) -> p n c", c=CH),
                                                bc3(maskT, nb), ALU.mult), [ps.r(), cs.r()], [ATm.r()])
                    for n4 in range(0, nb, 4):
                        ps = psum()
                        for n in range(n4, n4 + 4):
                            PE(lambda e, n=n: e.matmul(ps[0:PK, (n - n4) * 128:(n - n4 + 1) * 128], kdtok[:, n, :], vtok[:, n, :],
                                                       start=True, stop=True), [kdtok.r(), vtok.r()], [ps.r()])
                        A(lambda e: e.copy(KV[:, n4:n4 + 4, :].rearrange("p n d -> p (n d)"), ps[0:PK, :]), [ps.r()], [KV.r()])
                    order = list(range(nb)) if dr == 0 else list(range(nb - 1, -1, -1))
                    for s_i, n in enumerate(order):
                        V(lambda e, s_i=s_i, n=n: e.scalar_tensor_tensor(Sall[:, s_i + 1, :], Sall[:, s_i, :], cdec[:, c0 + n:c0 + n + 1],
                                                                         KV[:, n, :], ALU.mult, ALU.add), [Sall.r(), cdec.r(), KV.r()], [Sall.r()])
                    pA = psum()
                    pB = psum()
                    for s_i, n in enumerate(order):
                        PE(lambda e, s_i=s_i, n=n: e.matmul(pA[:, n * CH:(n + 1) * CH], Sall[:, s_i, :], qd[:, csl(n)], start=True, stop=True),
                           [Sall.r(), qd.r()], [pA.r()])
                        PE(lambda e, n=n: e.matmul(pB[:, n * CH:(n + 1) * CH], vtok[:, n, :], ATm[:, n, :], start=True, stop=True),
                           [vtok.r(), ATm.r()], [pB.r()])
                    A(lambda e: e.copy(otmp[:, :], pA[:, 0:nb * CH]), [pA.r()], [otmp.r()])
                    osl = oacc[:, tokb:tokb + nb * CH]
                    if dr == 0:
                        V(lambda e: e.tensor_tensor(osl, otmp[:, :], pB[:, 0:nb * CH], ALU.add), [otmp.r(), pB.r()], [oacc.r(idx)])
                    else:
                        V(lambda e: e.tensor_tensor(otmp[:, :], otmp[:, :], pB[:, 0:nb * CH], ALU.add), [otmp.r(), pB.r()], [otmp.r()])
                        V(lambda e: e.tensor_tensor(osl, osl, otmp[:, :], ALU.add), [otmp.r(), oacc.r(idx)], [oacc.r(idx)])
                if kind == 0:
                    o_st = o_gla if mixer == 'gla' else o_hg
                    dma('sp', o_st[idx, l, dr, h], Sall[:, nb, :], R=[Sall.r()])
        ob = ph("ob", [128, T], BF16)
        normc = pm[l][:, PM_COLS['gla_norm' if mixer == 'gla' else 'hg_norm']:PM_COLS['gla_norm' if mixer == 'gla' else 'hg_norm'] + 1]
        for b in range(NBLK):
            A(lambda e: e.activation(xs[:, :], oacc[:, sl(b)], AF.Square), [oacc.r(i) for i in range(3)], [xs.r()])
            ps = psum()
            PE(lambda e: e.matmul(ps[:, :], cs[:, CS['mean128']:CS['mean128'] + 128], xs[:, :], start=True, stop=True),
               [xs.r(), cs.r()], [ps.r()])
            rstd_from(ps[:, :], t1[:, :], ps.r(), t1.r())
            V(lambda e: e.tensor_tensor(xs[:, :], oacc[:, sl(b)], t1[:, :], ALU.mult), [t1.r()] + [oacc.r(i) for i in range(3)], [xs.r()])
            V(lambda e: e.scalar_tensor_tensor(ob[:, sl(b)], xs[:, :], normc, gate[:, sl(b)], ALU.mult, ALU.mult),
              [xs.r(), gate.r(), pm[l].r()], [ob.r()])
        k_ = bi_ * 4 + h
        if stage > 4:
            dma('sp', obd[k_], ob[:, :], R=[ob.r()], W=[obd_res[k_]])
        if h == 0 and l == 0:
            dbg_dump('ob_' + mixer, ob[:, :], [ob.r()])

    NKEY = T + PAST
    KEYR = [(0, LP), (LP, LP), (2 * LP, LS + PAST)]
    SCALE = (128 + 64) ** -0.5

    def mla_layer(l):
      with Phase() as ph:
        qn = ph("qn", [128, 4, T], BF16)
        ckvb = ph("ckvb", [128, 4, NKEY], BF16)
        kpT = ph("kpT", [128, NKEY], BF16)
        ropet = ph("ropet", [64, 2, LS])
        dma('sp', ropet[:, :, :], rope_d[:, :].rearrange("p (a t) -> p a t", a=2), W=[ropet.r()])
        G(lambda e: e.memset(kpT[:, :], 0.0), [], [kpT.r()])

        def rope(dst_bf, src, ncol0, tmpa, tmpb):
            for hb in range(2):
                c = slice(hb * 512, (hb + 1) * 512)
                ps = psum()
                PE(lambda e: e.matmul(ps[0:64, :], cmat('perm'), src[0:64, c], start=True, stop=True), [src.r(), cs.r()], [ps.r()])
                V(lambda e: e.tensor_tensor(tmpa[0:64, :], src[0:64, c], ropet[:, 0, c], ALU.mult), [src.r(), ropet.r()], [tmpa.r()])
                V(lambda e: e.tensor_tensor(tmpb[0:64, :], ps[0:64, :], ropet[:, 1, c], ALU.mult), [ps.r(), ropet.r()], [tmpb.r()])
                V(lambda e: e.tensor_tensor(dst_bf[0:64, ncol0 + hb * 512:ncol0 + (hb + 1) * 512], tmpa[0:64, :], tmpb[0:64, :], ALU.add),
                  [tmpa.r(), tmpb.r()], [dst_bf.r()])
        sl = lambda b: slice(b * 512, (b + 1) * 512)
        with Phase() as pa:
            qa = pa("qa", [128, 4, T])
            kva = pa("kva", [128, 4, T])
            kpe = pa("kpe", [64, T])
            sq = pa("sq", [128, 512])
            rst = pa("rst", [128, 512])
            tmpa = pa("tmpa", [128, 512])
            tmpb = pa("tmpb", [128, 512])
            tokt = pa("tokt", [128, 512])
            for nm, dst in (('mla_qa', qa), ('mla_kva', kva)):
                for c in range(4):
                    wt, wv, n_ = win_piece(l, nm, c)
                    for b in range(NBLK):
                        ps = psum()
                        proj_h(wt, wv, 128, b, ps[:, :], ps.r())
                        if c % 2 == 0:
                            A(lambda e: e.copy(dst[:, c, sl(b)], ps[:, :]), [ps.r()], [dst.r()])
                        else:
                            V(lambda e: e.tensor_copy(dst[:, c, sl(b)], ps[:, :]), [ps.r()], [dst.r()])
            wt, wv, n_ = win_piece(l, 'mla_kpe', 0)
            for b in range(NBLK):
                ps = psum()
                proj_h(wt, wv, 64, b, ps[0:64, :], ps.r())
                A(lambda e: e.copy(kpe[:, sl(b)], ps[0:64, :]), [ps.r()], [kpe.r()])
            for src, ncol, kind_ in ((qa, 'q_norm', 'q'), (kva, 'kv_norm', 'kv')):
                for b in range(NBLK):
                    psm = psum()
                    for c in range(4):
                        A(lambda e: e.activation(sq[:, :], src[:, c, sl(b)], AF.Square), [src.r()], [sq.r()])
                        PE(lambda e: e.matmul(psm[:, :], cs[:, CS['mean512']:CS['mean512'] + 128], sq[:, :], start=(c == 0), stop=(c == 3)),
                           [sq.r(), cs.r()], [psm.r()])
                    rstd_from(psm[:, :], rst[:, :], psm.r(), rst.r())
                    for c in range(4):
                        ncl = pm[l][:, PM_COLS[ncol] + c:PM_COLS[ncol] + c + 1]
                        if kind_ == 'q':
                            V(lambda e: e.scalar_tensor_tensor(qn[:, c, sl(b)], src[:, c, sl(b)], ncl, rst[:, :], ALU.mult, ALU.mult),
                              [src.r(), rst.r(), pm[l].r()], [qn.r()])
                        else:
                            V(lambda e: e.scalar_tensor_tensor(src[:, c, sl(b)], src[:, c, sl(b)], ncl, rst[:, :], ALU.mult, ALU.mult),
                              [src.r(), rst.r(), pm[l].r()], [src.r()])
                            G(lambda e: e.tensor_copy(ckvb[:, c, sl(b)], src[:, c, sl(b)]), [src.r()], [ckvb.r()])
            for tt in range(4):
                idx, t_in = tt // 2, (tt % 2) * 128
                ps = psum()
                for c in range(4):
                    PE(lambda e: e.transpose(ps[:, c * 128:(c + 1) * 128], kva[:, c, tt * 128:(tt + 1) * 128], ident), [kva.r(), cs.r()], [ps.r()])
                A(lambda e: e.copy(tokt[:, :], ps[:, :]), [ps.r()], [tokt.r()])
                dma('sp', o_ckv[idx, l, t_in:t_in + 128, :], tokt[:, :], R=[tokt.r()])
                ps = psum()
                PE(lambda e: e.transpose(ps[:, 0:64], kpe[:, tt * 128:(tt + 1) * 128], ident[0:64, 0:64]), [kpe.r(), cs.r()], [ps.r()])
                A(lambda e: e.copy(tmpa[:, 0:64], ps[:, 0:64]), [ps.r()], [tmpa.r()])
                dma('sp', o_kpe[idx, l, t_in:t_in + 128, :], tmpa[:, 0:64], R=[tmpa.r()])
            V(lambda e: e.tensor_copy(kpT[0:64, 0:2 * LP], kpe[:, 0:2 * LP]), [kpe.r()], [kpT.r()])
            kps = pa("kps", [64, LS])
            V(lambda e: e.tensor_copy(kps[:, :], kpe[:, 2 * LP:T]), [kpe.r()], [kps.r()])
            rope(kpT, kps, 2 * LP, tmpa, tmpb)
            for tt in range(PAST // 128):
                dma('sp', tokt[:, :], cx_ckv[l, tt * 128:(tt + 1) * 128, :], W=[tokt.r()])
                ps = psum()
                for c in range(4):
                    PE(lambda e: e.transpose(ps[:, c * 128:(c + 1) * 128], tokt[:, c * 128:(c + 1) * 128], ident), [tokt.r(), cs.r()], [ps.r()])
                A(lambda e: e.copy(ckvb[:, :, T + tt * 128:T + (tt + 1) * 128], ps[:, :].rearrange("p (c t) -> p c t", c=4)), [ps.r()], [ckvb.r()])
                dma('sp', tmpb[:, 0:64], cx_kpe[l, tt * 128:(tt + 1) * 128, :], W=[tmpb.r()])
                ps = psum()
                PE(lambda e: e.transpose(ps[0:64, 0:128], tmpb[:, 0:64], ident), [tmpb.r(), cs.r()], [ps.r()])
                A(lambda e: e.copy(kpT[0:64, T + tt * 128:T + (tt + 1) * 128], ps[0:64, 0:128]), [ps.r()], [kpT.r()])
        for h in range(4):
          if stage == 4 and h > 0:
              break
          with Phase() as hh:
            qnT = hh("qnT", [128, T], BF16)
            qpT = hh("qpT", [128, T], BF16)
            qpf = hh("qpf", [64, T])
            knT = hh("knT", [128, NKEY], BF16)
            vtk = hh("vtk", [128, NKEY // 128, 128], BF16)
            Pm = hh("Pm", [128, LS + PAST])
            PT = hh("PT", [128, (LS + PAST) // 128, 128], BF16)
            Otok = hh("Otok", [128, 128])
            ob = hh("ob", [128, T], BF16)
            mx = hh("mx", [128, 4])
            sm = hh("sm", [128, 4])
            tmpa = hh("tmpa", [128, 512])
            tmpb = hh("tmpb", [128, 512])
            G(lambda e: e.memset(qpT[:, :], 0.0), [], [qpT.r()])
            wqo = (h * 192) * 4
            wt, wv = load_w(wqb_d[l][:, wqo:wqo + 512], 4, 128)
            for b in range(NBLK):
                ps = psum()
                for kc in range(4):
                    PE(lambda e: e.matmul(ps[:, :], wv[:, kc, :], qn[:, kc, sl(b)], start=(kc == 0), stop=(kc == 3)), [wt.r(), qn.r()], [ps.r()])
                A(lambda e: e.copy(qnT[:, sl(b)], ps[:, :]), [ps.r()], [qnT.r()])
            wt, wv = load_w(wqb_d[l][:, wqo + 512:wqo + 768], 4, 64)
            for b in range(NBLK):
                ps = psum()
                for kc in range(4):
                    PE(lambda e: e.matmul(ps[0:64, :], wv[:, kc, :], qn[:, kc, sl(b)], start=(kc == 0), stop=(kc == 3)), [wt.r(), qn.r()], [ps.r()])
                A(lambda e: e.copy(qpf[:, sl(b)], ps[0:64, :]), [ps.r()], [qpf.r()])
            V(lambda e: e.tensor_copy(qpT[0:64, 0:2 * LP], qpf[:, 0:2 * LP]), [qpf.r()], [qpT.r()])
            qps = hh("qps", [64, LS])
            V(lambda e: e.tensor_copy(qps[:, :], qpf[:, 2 * LP:T]), [qpf.r()], [qps.r()])
            rope(qpT, qps, 2 * LP, tmpa, tmpb)
            wt, wv = load_w(wkvb_d[l][:, (h * 2) * 512:(h * 2) * 512 + 512], 4, 128)
            for kb in range(NKEY // 512):
                ps = psum()
                for kc in range(4):
                    PE(lambda e: e.matmul(ps[:, :], wv[:, kc, :], ckvb[:, kc, kb * 512:(kb + 1) * 512], start=(kc == 0), stop=(kc == 3)),
                       [wt.r(), ckvb.r()], [ps.r()])
                A(lambda e: e.copy(knT[:, kb * 512:(kb + 1) * 512], ps[:, :]), [ps.r()], [knT.r()])
            wt, wv = load_w(wkvb_d[l][:, (h * 2 + 1) * 512:(h * 2 + 1) * 512 + 512], 4, 128)
            for k4 in range(0, NKEY // 128, 4):
                ps = psum()
                for kt_ in range(k4, k4 + 4):
                    for kc in range(4):
                        PE(lambda e: e.matmul(ps[:, (kt_ - k4) * 128:(kt_ - k4 + 1) * 128], ckvb[:, kc, kt_ * 128:(kt_ + 1) * 128], wv[:, kc, :],
                                              start=(kc == 0), stop=(kc == 3)), [wt.r(), ckvb.r()], [ps.r()])
                V(lambda e: e.tensor_copy(vtk[:, k4:k4 + 4, :].rearrange("p k d -> p (k d)"), ps[:, :]), [ps.r()], [vtk.r()])
            for (tok0, L, kind, idx) in SEQS:
                k0, Lk = KEYR[idx]
                nkb = (Lk + 511) // 512
                for qt in range(L // 128):
                    qs = slice(tok0 + qt * 128, tok0 + (qt + 1) * 128)
                    pss = [psum() for _ in range(nkb)]
                    for kb in range(nkb):
                        kw_ = min(512, Lk - kb * 512)
                        ks = slice(k0 + kb * 512, k0 + kb * 512 + kw_)
                        PE(lambda e: e.matmul(pss[kb][:, 0:kw_], qnT[:, qs], knT[:, ks], start=True, stop=False), [qnT.r(), knT.r()], [pss[kb].r()])
                        PE(lambda e: e.matmul(pss[kb][:, 0:kw_], qpT[:, qs], kpT[:, ks], start=False, stop=True), [qpT.r(), kpT.r()], [pss[kb].r()])
                        V(lambda e: e.reduce_max(mx[:, kb:kb + 1], pss[kb][:, 0:kw_], AX.X), [pss[kb].r()], [mx.r()])
                    if nkb > 1:
                        V(lambda e: e.reduce_max(mx[:, 3:4], mx[:, 0:nkb], AX.X), [mx.r()], [mx.r()])
                        mcol = mx[:, 3:4]
                    else:
                        mcol = mx[:, 0:1]
                    V(lambda e: e.tensor_scalar(mx[:, 3:4], mcol, -SCALE, None, ALU.mult), [mx.r()], [mx.r()])
                    for kb in range(nkb):
                        kw_ = min(512, Lk - kb * 512)
                        A(lambda e: e.activation(Pm[:, kb * 512:kb * 512 + kw_], pss[kb][:, 0:kw_], AF.Exp, bias=mx[:, 3:4], scale=SCALE,
                                                 accum_out=sm[:, kb:kb + 1]), [pss[kb].r(), mx.r()], [Pm.r(), sm.r()])
                    if nkb > 1:
                        V(lambda e: e.reduce_sum(sm[:, 3:4], sm[:, 0:nkb], AX.X), [sm.r()], [sm.r()])
                        scol = sm[:, 3:4]
                    else:
                        scol = sm[:, 0:1]
                    V(lambda e: e.reciprocal(sm[:, 3:4], scol), [sm.r()], [sm.r()])
                    nkt = Lk // 128
                    for k4 in range(0, nkt, 4):
                        ps = psum()
                        w4 = min(4, nkt - k4)
                        for kt_ in range(k4, k4 + w4):
                            PE(lambda e: e.transpose(ps[:, (kt_ - k4) * 128:(kt_ - k4 + 1) * 128], Pm[:, kt_ * 128:(kt_ + 1) * 128], ident),
                               [Pm.r(), cs.r()], [ps.r()])
                        A(lambda e: e.copy(PT[:, k4:k4 + w4, :].rearrange("p k q -> p (k q)"), ps[:, 0:w4 * 128]), [ps.r()], [PT.r()])
                    po = psum()
                    for kt_ in range(nkt):
                        PE(lambda e: e.matmul(po[:, 0:128], PT[:, kt_, :], vtk[:, k0 // 128 + kt_, :], start=(kt_ == 0), stop=(kt_ == nkt - 1)),
                           [PT.r(), vtk.r()], [po.r()])
                    A(lambda e: e.activation(Otok[:, :], po[:, 0:128], AF.Copy, scale=sm[:, 3:4]), [po.r(), sm.r()], [Otok.r()])
                    pt2 = psum()
                    PE(lambda e: e.transpose(pt2[:, 0:128], Otok[:, :], ident), [Otok.r(), cs.r()], [pt2.r()])
                    V(lambda e: e.tensor_copy(ob[:, qs], pt2[:, 0:128]), [pt2.r()], [ob.r()])
            if stage > 4:
                dma('sp', obd[12 + h], ob[:, :], R=[ob.r()], W=[obd_res[12 + h]])
            if h == 0 and l == 0:
                dbg_dump('ob_mla', ob[:, :], [ob.r()])

    for l in range(DEPTH):
      try:
        if stage < 1:
            break
        with Phase() as ph:
            xTb = ph("xTb", [128, 16, 512])
            for b in range(NBLK):
                kind = 0 if b == 0 else 1
                dma('sp', xTb[:, :, :], xTd[:, :, b * 512:(b + 1) * 512].rearrange("c p t -> p c t"),
                    R=[xTd_res[b]], W=[xTb.r()])
                for c in range(16):
                    A(lambda e, c=c, b=b, kind=kind, l=l: e.activation(
                        hT[:, c, b * 512:(b + 1) * 512], xTb[:, c, :], AF.Identity,
                        bias=modT[l][:, kind, c:c + 1], scale=modT[l][:, kind, 16 + c:16 + c + 1]),
                        [xTb.r(), modT[l].r()], [hT.r(b)])
            if l == 0:
                dbg_dump('hT', hT[:, :, :].rearrange("p c t -> p (c t)"), [hT.r(b) for b in range(NBLK)])
        if stage < 2:
            break
        with Phase() as ph:
            ba = ph("ba", [64, 24, 16])
            bet = ph("bet", [64, 24, 8])
            gg = ph("gg", [64, 24, 8])
            tt_ = ph("tt_", [64, 24, 8])
            t2_ = ph("t2_", [64, 24, 8])
            nea = ph("nea", [64, 8])
            if CUT[0] == -6:
                win_piece(l, 'gdn_q', 0)
                cut(-6)
            wt, wv, n_ = win_piece(l, 'gdn_ba', 0)
            cut(-5)
            ps = psum()
            for n in range(24):
                for kc in range(16):
                    PE(lambda e, n=n, kc=kc, ps=ps, wv=wv: e.matmul(
                        ps[0:64, n * 16:(n + 1) * 16], hT[:, kc, n * 64:(n + 1) * 64], wv[:, kc, :],
                        start=(kc == 0), stop=(kc == 15)), [wt.r()] + [hT.r(b) for b in range(NBLK)], [ps.r()])
            cut(-4)
            A(lambda e, ps=ps: e.copy(ba[:, :, :], ps[0:64, 0:384].rearrange("p (n c) -> p n c", c=16)), [ps.r()], [ba.r()])
            cut(-3)
            A(lambda e: e.activation(bet[:, :, :], ba[:, :, 0:8], AF.Sigmoid), [ba.r()], [bet.r()])
            for j in range(8):
                V(lambda e, j=j: e.tensor_scalar(tt_[:, :, j], ba[:, :, 8 + j], pm[l][0:64, PM_COLS['dt_bias'] + j:PM_COLS['dt_bias'] + j + 1],
                                               None, ALU.add), [ba.r(), pm[l].r()], [tt_.r()])
            cut(-2)
            A(lambda e: e.activation(t2_[:, :, :], tt_[:, :, :], AF.Abs), [tt_.r()], [t2_.r()])
            A(lambda e: e.activation(t2_[:, :, :], t2_[:, :, :], AF.Exp, scale=-1.0), [t2_.r()], [t2_.r()])
            A(lambda e: e.activation(t2_[:, :, :], t2_[:, :, :], AF.Ln, bias=ONEC[0:64, :]), [t2_.r(), cs.r()], [t2_.r()])
            cut(-1)
            V(lambda e: e.scalar_tensor_tensor(tt_[:, :, :], tt_[:, :, :], 0.0, t2_[:, :, :], ALU.max, ALU.add),
              [tt_.r(), t2_.r()], [tt_.r()])
            A(lambda e: e.activation(nea[:, :], pm[l][0:64, PM_COLS['a_log']:PM_COLS['a_log'] + 8], AF.Exp), [pm[l].r()], [nea.r()])
            V(lambda e: e.tensor_scalar(nea[:, :], nea[:, :], -1.0, None, ALU.mult), [nea.r()], [nea.r()])
            for j in range(8):
                V(lambda e, j=j: e.tensor_scalar(gg[:, :, j], tt_[:, :, j], nea[:, j:j + 1], None, ALU.mult),
                  [tt_.r(), nea.r()], [gg.r()])
            dbg_dump('gg', gg[:, :, :].rearrange("p a b -> p (a b)"), [gg.r()])
            for h in range(4):
                if stage in (2, 3, 4) and h > 0:
                    break
                gdn_unit(l, h, ph, bet, gg)
        if stage < 3:
            break
        with Phase() as ph:
            rmask = ph("rmask", [128, T])
            V(lambda e: e.memset(rmask[:, :], 1.0), [], [rmask.r()])
            V(lambda e: e.memset(rmask[:, :].rearrange("p (n c) -> p n c", c=CH)[:, :, 0], 0.0), [rmask.r()], [rmask.r()])
            glr = [ph("glr0", [16, T]), ph("glr1", [16, T])]
            gw2 = ph("gw2", [16, 512])
            dma('sp', gw2[:, :], gw2_d[l], W=[gw2.r()])
            for dr in range(2):
                wt, wv, n_ = win_piece(l, 'gla_g', dr)
                for b in range(NBLK):
                    ps = psum()
                    proj_h(wt, wv, 16, b, ps[0:16, :], ps.r())
                    A(lambda e: e.copy(glr[dr][:, b * 512:(b + 1) * 512], ps[0:16, :]), [ps.r()], [glr[dr].r()])
            lbt = ph("lbt", [128, 8])
            oml = ph("oml", [128, 8])
            if l == 0:
                V(lambda e: e.memset(lbt[:, :], 0.0), [], [lbt.r()])
                V(lambda e: e.memset(oml[:, :], 1.0), [], [oml.r()])
            else:
                c_ = PM_COLS['hg_lb']
                V(lambda e: e.tensor_tensor(lbt[:, :], pm[l][:, c_ + 8:c_ + 16], pm[l][:, c_:c_ + 8], ALU.subtract), [pm[l].r()], [lbt.r()])
                A(lambda e: e.activation(lbt[:, :], lbt[:, :], AF.Sigmoid), [lbt.r()], [lbt.r()])
                V(lambda e: e.tensor_scalar(oml[:, :], lbt[:, :], -1.0, 1.0, ALU.mult, ALU.add), [lbt.r()], [oml.r()])
            shared = {'rmask': rmask, 'glr': glr, 'gw2': gw2, 'lbt': lbt, 'oml': oml}
            for mixer in ('gla', 'hg'):
                for h in range(4):
                    if stage in (3, 4) and h > 0:
                        break
                    gla_unit(l, h, mixer, shared)
        if stage < 4:
            break
        mla_layer(l)
        if stage < 5:
            break
        with Phase() as ph:
            obr = ph("obr", [128, 16, T], BF16)
            for k_ in range(16):
                dma('sp', obr[:, k_, :], obd[k_], R=[obd_res[k_]], W=[obr.r(k_)])
            sig = ph("sig", [128, 512])
            macc = ph("macc", [128, 512])
            mo = [ph("mo0", [128, 512], BF16), ph("mo1", [128, 512], BF16)]
            obr_all = [obr.r(k_) for k_ in range(16)]
            for c in range(16):
                wgs = []
                for k in range(4):
                    wgs.append(win_piece(l, 'gates', k * 16 + c))
                for b in range(NBLK):
                    for k in range(4):
                        wt, wv, n_ = wgs[k]
                        pg = psum()
                        proj_h(wt, wv, 128, b, pg[:, :], pg.r())
                        bg = pm[l][:, PM_COLS['b_gates'] + k * 16 + c:PM_COLS['b_gates'] + k * 16 + c + 1]
                        A(lambda e: e.activation(sig[:, :], pg[:, :], AF.Sigmoid, bias=bg), [pg.r(), pm[l].r()], [sig.r()])
                        off = (k * 16 + c) * 512
                        wtb, wvb = load_w(wbr_d[l][:, off:off + 512], 4, 128)
                        pp = psum()
                        for kc in range(4):
                            PE(lambda e: e.matmul(pp[:, :], wvb[:, kc, :], obr[:, k * 4 + kc, b * 512:(b + 1) * 512], start=(kc == 0), stop=(kc == 3)),
                               [wtb.r()] + obr_all, [pp.r()])
                        if k == 0:
                            V(lambda e: e.tensor_tensor(macc[:, :], sig[:, :], pp[:, :], ALU.mult), [sig.r(), pp.r()], [macc.r()])
                        else:
                            V(lambda e: e.tensor_tensor(sig[:, :], sig[:, :], pp[:, :], ALU.mult), [sig.r(), pp.r()], [sig.r()])
                            if k < 3:
                                V(lambda e: e.tensor_tensor(macc[:, :], macc[:, :], sig[:, :], ALU.add), [macc.r(), sig.r()], [macc.r()])
                            else:
                                mo_ = mo[(c * NBLK + b) % 2]
                                V(lambda e: e.tensor_tensor(mo_[:, :], macc[:, :], sig[:, :], ALU.add), [macc.r(), sig.r()], [mo_.r()])
                                dma('sp', mgd[c, :, b * 512:(b + 1) * 512], mo_[:, :], R=[mo_.r()], W=[mgd_res[c][b]])
        if stage < 6:
            break
        with Phase() as ph:
            rr = ph("rr", [128, 16, 512])
            aT = ph("aT", [128, NFF, 512], BF16)
            xin = ph("xin", [128, 512])
            sq = ph("sq", [128, 512])
            mean = ph("mean", [128, 512])
            rst = ph("rst", [128, 512])
            tg = ph("tg", [128, 512])
            wbig = [ph("wbig0", [128, NFF * 128], BF16), ph("wbig1", [128, NFF * 128], BF16)]
            otok = ph("otok", [128, 512])
            wbi = [0]

            def layer_norm(gname, bname, post):
                pm1 = psum()
                pm2 = psum()
                for c in range(16):
                    A(lambda e: e.activation(sq[:, :], rr[:, c, :], AF.Square), [rr.r(c)], [sq.r()])
                    PE(lambda e: e.matmul(pm1[:, :], cs[:, CS['meanD']:CS['meanD'] + 128], rr[:, c, :], start=(c == 0), stop=(c == 15)),
                       [rr.r(c), cs.r()], [pm1.r()])
                    PE(lambda e: e.matmul(pm2[:, :], cs[:, CS['meanD']:CS['meanD'] + 128], sq[:, :], start=(c == 0), stop=(c == 15)),
                       [sq.r(), cs.r()], [pm2.r()])
                A(lambda e: e.copy(mean[:, :], pm1[:, :]), [pm1.r()], [mean.r()])
                V(lambda e: e.tensor_tensor(tg[:, :], mean[:, :], mean[:, :], ALU.mult), [mean.r()], [tg.r()])
                V(lambda e: e.tensor_tensor(tg[:, :], pm2[:, :], tg[:, :], ALU.subtract), [pm2.r(), tg.r()], [tg.r()])
                A(lambda e: e.activation(rst[:, :], tg[:, :], AF.Ln, bias=EPSC), [tg.r(), cs.r()], [rst.r()])
                A(lambda e: e.activation(rst[:, :], rst[:, :], AF.Exp, scale=-0.5), [rst.r()], [rst.r()])
                for c in range(16):
                    V(lambda e: e.tensor_tensor(rr[:, c, :], rr[:, c, :], mean[:, :], ALU.subtract), [rr.r(c), mean.r()], [rr.r(c)])
                    G(lambda e: e.tensor_tensor(rr[:, c, :], rr[:, c, :], rst[:, :], ALU.mult), [rr.r(c), rst.r()], [rr.r(c)])
                    A(lambda e: e.activation(rr[:, c, :], rr[:, c, :], AF.Identity,
                                             bias=pm[l][:, PM_COLS[bname] + c:PM_COLS[bname] + c + 1],
                                             scale=pm[l][:, PM_COLS[gname] + c:PM_COLS[gname] + c + 1]), [rr.r(c), pm[l].r()], [rr.r(c)])
                    post(c)
            for b in range(NBLK):
                kind = 0 if b == 0 else 1
                for c in range(16):
                    dma('sp', hT[:, c, 512:1024], mgd[c, :, b * 512:(b + 1) * 512], R=[mgd_res[c][b]], W=[hT.r(1)])
                mg_all = [hT.r(1)]
                for c in range(16):
                    wt, wv = load_w(wout_d[l][:, c * 2048:(c + 1) * 2048], 16, 128)
                    ps = psum()
                    for kc in range(16):
                        PE(lambda e: e.matmul(ps[:, :], wv[:, kc, :], hT[:, kc, 512:1024], start=(kc == 0), stop=(kc == 15)), [wt.r()] + mg_all, [ps.r()])
                    dma('sp', xin[:, :], xTd[c, :, b * 512:(b + 1) * 512], R=[xTd_res[b]], W=[xin.r()])
                    A(lambda e: e.activation(tg[:, :], ps[:, :], AF.Copy, scale=modT[l][:, kind, 32 + c:32 + c + 1]), [ps.r(), modT[l].r()], [tg.r()])
                    V(lambda e: e.scalar_tensor_tensor(rr[:, c, :], xin[:, :], ALPHA, tg[:, :], ALU.mult, ALU.add), [xin.r(), tg.r()], [rr.r(c)])

                def post1(c):
                    A(lambda e: e.activation(hT[:, c, 0:512], rr[:, c, :], AF.Identity, bias=modT[l][:, kind, 48 + c:48 + c + 1],
                                             scale=modT[l][:, kind, 64 + c:64 + c + 1]), [rr.r(c), modT[l].r()], [hT.r(0)])
                layer_norm('ln1_g', 'ln1_b', post1)
                h2_all = [hT.r(0)]
                for f in range(NFF):
                    wt1, wv1 = load_w(w1_d[l][:, f * 2048:(f + 1) * 2048], 16, 128)
                    wt3, wv3 = load_w(w3_d[l][:, f * 2048:(f + 1) * 2048], 16, 128)
                    p1 = psum()
                    p3 = psum()
                    for kc in range(16):
                        PE(lambda e: e.matmul(p1[:, :], wv1[:, kc, :], hT[:, kc, 0:512], start=(kc == 0), stop=(kc == 15)), [wt1.r()] + h2_all, [p1.r()])
                    for kc in range(16):
                        PE(lambda e: e.matmul(p3[:, :], wv3[:, kc, :], hT[:, kc, 0:512], start=(kc == 0), stop=(kc == 15)), [wt3.r()] + h2_all, [p3.r()])
                    A(lambda e: e.activation(tg[:, :], p1[:, :], AF.Silu), [p1.r()], [tg.r()])
                    V(lambda e: e.tensor_tensor(aT[:, f, :], tg[:, :], p3[:, :], ALU.mult), [tg.r(), p3.r()], [aT.r(f)])
                aT_all = [aT.r(f) for f in range(NFF)]
                for c in range(16):
                    wtile = wbig[wbi[0] % 2]
                    wbi[0] += 1
                    dma('pool', wtile[:, :], w2_d[l][:, c * NFF * 128:(c + 1) * NFF * 128], W=[wtile.r()])
                    wv2 = wtile[:, :].rearrange("p (k n) -> p k n", k=NFF)
                    ps = psum()
                    for kc in range(NFF):
                        PE(lambda e: e.matmul(ps[:, :], wv2[:, kc, :], aT[:, kc, :], start=(kc == 0), stop=(kc == NFF - 1)), [wtile.r()] + aT_all, [ps.r()])
                    A(lambda e: e.activation(tg[:, :], ps[:, :], AF.Copy, scale=modT[l][:, kind, 80 + c:80 + c + 1]), [ps.r(), modT[l].r()], [tg.r()])
                    V(lambda e: e.scalar_tensor_tensor(rr[:, c, :], rr[:, c, :], ALPHA, tg[:, :], ALU.mult, ALU.add), [rr.r(c), tg.r()], [rr.r(c)])

                def post2(c):
                    if l < DEPTH - 1:
                        dma('sp', xTd[c, :, b * 512:(b + 1) * 512], rr[:, c, :], R=[rr.r(c)], W=[xTd_res[b]])
                layer_norm('ln2_g', 'ln2_b', post2)
                if l == DEPTH - 1:
                    for tt in range(4):
                        for g4 in range(4):
                            ps = psum()
                            for cc in range(4):
                                c = g4 * 4 + cc
                                PE(lambda e: e.transpose(ps[:, cc * 128:(cc + 1) * 128], rr[:, c, tt * 128:(tt + 1) * 128], ident), [rr.r(c), cs.r()], [ps.r()])
                            A(lambda e: e.copy(otok[:, :], ps[:, :]), [ps.r()], [otok.r()])
                            gt = b * 4 + tt
                            dst = y_p[gt * 128:(gt + 1) * 128, g4 * 512:(g4 + 1) * 512] if gt < 4 else \
                                y_s[(gt - 4) * 128:(gt - 3) * 128, g4 * 512:(g4 + 1) * 512]
                            dma('sp', dst, otok[:, :], R=[otok.r()])
      except _Stop:
        break

    for tl_ in list(ALL_TL):
        rs_ = list(tl_._res.values())
        if tl_.name.startswith('s') and rs_ and sum(r.nw for r in rs_) > 0 and sum(r.nr for r in rs_) == 0:
            try:
                shp_ = list(tl_.t.shape)
                sink_ = nc.dram_tensor("sink_" + tl_.name, shp_, tl_.t.dtype, kind="Internal").ap()
                full_ = tl_.t[tuple(slice(None) for _ in shp_)]
                dma('sp', sink_[tuple(slice(None) for _ in shp_)], full_, R=rs_)
            except Exception as ex_:
                print('autosink failed', tl_.name, ex_)
    agg = {}
    for r_ in ALL_RES:
        a_ = agg.setdefault(r_.name, [0, 0])
        a_[0] += r_.nr
        a_[1] += r_.nw
    dead = [k for k, v in agg.items() if v[1] > 0 and v[0] == 0 and k != '?']
    if dead:
        print('WARNING unread tiles:', dead)
    print('op counts', P.cnt, {q: P.ring[q][1] for q in P.ring})
    stack_close = stack
    with nc.Block() as block:
        P.emit(block)
    stack_close.close()
    return nc


def _pack(W, n=128):
    K, N = W.shape
    kc = K // 128
    return np.ascontiguousarray(
        W.reshape(kc, 128, N // n, n).transpose(1, 2, 0, 3).reshape(128, -1))


def _pack_in(W):
    K = W.shape[0]
    outs = []
    cols = []
    for name in PIECES:
        cols.extend(PIECES[name])
    cols.sort()
    assert sum(n for _, n in cols) == IN_WIDTH
    for c0, n in cols:
        outs.append(W[:, c0:c0 + n].reshape(K // 128, 128, n).transpose(1, 0, 2).reshape(128, -1))
    return np.ascontiguousarray(np.concatenate(outs, axis=1))


def _consts(cvec2):
    cs = np.zeros((128, CS_N), np.float32)
    cs[:, CS['ident']:CS['ident'] + 128] = np.eye(128, dtype=np.float32)
    k = np.arange(64)[:, None]
    i = np.arange(64)[None, :]
    for name, m in (('le', k <= i), ('ge', k >= i), ('lt', k < i), ('gt', k > i)):
        cs[:64, CS[name]:CS[name] + 64] = m.astype(np.float32)
    cs[:, CS['ones']:CS['ones'] + 128] = 1.0
    cs[:, CS['mean128']:CS['mean128'] + 128] = 1.0 / 128
    cs[:, CS['meanD']:CS['meanD'] + 128] = 1.0 / D
    cs[:, CS['mean512']:CS['mean512'] + 128] = 1.0 / 512
    perm = np.zeros((64, 64), np.float32)
    cosT = np.zeros((64, LS), np.float32)
    sinT = np.zeros((64, LS), np.float32)
    pos = np.arange(LS)
    row_id = (pos // 64).astype(np.float32)
    col_id = (pos % 64).astype(np.float32)
    half = 32
    inv = (10000.0 ** (-np.arange(0, half, 2, dtype=np.float32) / half)).astype(np.float32)
    for blk, ids in ((0, row_id), (1, col_id)):
        ang = ids[None, :] * inv[:, None]
        b0 = blk * 32
        for j in range(16):
            cosT[b0 + j] = np.cos(ang[j]); cosT[b0 + 16 + j] = np.cos(ang[j])
            sinT[b0 + j] = -np.sin(ang[j]); sinT[b0 + 16 + j] = np.sin(ang[j])
            perm[b0 + 16 + j, b0 + j] = 1.0
            perm[b0 + j, b0 + 16 + j] = 1.0
    cs[:64, CS['perm']:CS['perm'] + 64] = perm
    k = np.arange(32)[:, None]
    i = np.arange(32)[None, :]
    cs[:32, CS['le32']:CS['le32'] + 32] = (k <= i)
    cs[:32, CS['ge32']:CS['ge32'] + 32] = (k >= i)
    cs[:, CS['eps']] = EPS
    cs[:, CS['one']] = 1.0
    cs[:, CS['cvT']:CS['cvT'] + 32] = cvec2.reshape(2, 16, 128).transpose(2, 1, 0).reshape(128, 32)
    rope = np.zeros((64, 2 * LS), np.float32)
    rope[:, :LS] = cosT
    rope[:, LS:] = sinT
    return cs, rope


def _pm(inp, l):
    pm = np.zeros((128, PM_N), np.float32)

    def put(name, arr2d):
        c = PM_COLS[name]
        pm[:arr2d.shape[1], c:c + arr2d.shape[0]] = arr2d.T
    conv = inp['gdn_conv'][l]
    put('conv', conv.reshape(5, 12, 128).transpose(1, 0, 2).reshape(60, 128))
    put('gdn_norm', inp['gdn_norm'][l][None])
    put('gla_norm', inp['gla_norm'][l][None])
    put('hg_norm', inp['hgrn_norm'][l][None])
    put('q_norm', inp['mla_q_norm'][l].reshape(4, 128))
    put('kv_norm', inp['mla_kv_norm'][l].reshape(4, 128))
    put('b_gates', inp['b_gates'][l].reshape(64, 128))
    put('hg_lb', inp['hgrn_lb'].reshape(16, 128))
    put('gla_gb', inp['gla_gate_b'][l].reshape(8, 64))
    put('b_ada', inp['b_ada'][l].reshape(96, 128))
    for nm in ('ln1_g', 'ln1_b', 'ln2_g', 'ln2_b'):
        put(nm, inp[nm][l].reshape(16, 128))
    pm[:, PM_COLS['a_log']:PM_COLS['a_log'] + 8] = inp['gdn_a_log'][l].reshape(1, 8)
    pm[:, PM_COLS['dt_bias']:PM_COLS['dt_bias'] + 8] = inp['gdn_dt_bias'][l].reshape(1, 8)
    return pm


def make_inputs(inp, ncores=8):
    f = lambda a: np.ascontiguousarray(np.asarray(a, dtype=np.float32))
    inp = {k: f(v) for k, v in inp.items()}
    sh = {}
    sh['pm'] = np.stack([_pm(inp, l) for l in range(DEPTH)])
    sh['gw2'] = np.ascontiguousarray(inp['gla_gate_w2'].transpose(0, 2, 1, 3).reshape(DEPTH, 16, 512))
    sh['wada'] = np.stack([_pack(inp['w_ada'][l]) for l in range(DEPTH)])
    sh['win'] = np.stack([_pack_in(inp['w_in'][l]) for l in range(DEPTH)])
    def pk_cols(W, cols):
        K = W.shape[0]
        return np.concatenate([W[:, c0:c0 + n].reshape(K // 128, 128, n).transpose(1, 0, 2).reshape(128, -1)
                               for c0, n in cols], axis=1)
    qcols = []
    for h in range(4):
        qcols += [(h * 192, 128), (h * 192 + 128, 64)]
    sh['wqb'] = np.stack([pk_cols(inp['mla_wq_b'][l], qcols) for l in range(DEPTH)])
    sh['wkvb'] = np.stack([_pack(inp['mla_wkv_b'][l]) for l in range(DEPTH)])
    sh['wbr'] = np.stack([np.concatenate([_pack(inp['w_branch'][l, k]) for k in range(4)], axis=1)
                          for l in range(DEPTH)])
    sh['wout'] = np.stack([_pack(inp['w_out'][l]) for l in range(DEPTH)])
    sh['w1'] = np.stack([_pack(inp['ffn_w1'][l]) for l in range(DEPTH)])
    sh['w3'] = np.stack([_pack(inp['ffn_w3'][l]) for l in range(DEPTH)])
    sh['w2'] = np.stack([_pack(inp['ffn_w2'][l]) for l in range(DEPTH)])
    maps = []
    for core in range(ncores):
        b = core % 2
        m = dict(sh)
        m['xp'] = inp['x_prompt'][2 * core:2 * core + 2].reshape(2 * LP, D)
        m['xs'] = inp['x_sample'][b]
        m['st_gdn'] = inp['state_gdn'][b]
        m['st_gla'] = inp['state_gla'][b]
        m['st_hg'] = inp['state_hgrn'][b]
        m['cx_ckv'] = inp['cache_mla_ckv'][b]
        m['cx_kpe'] = inp['cache_mla_kpe'][b]
        m['cs'], m['rope'] = _consts(np.stack([inp['c_ctx'], inp['c'][b]]))
        maps.append(m)
    return maps


_NC = None


def kernel(**inputs):
    global _NC
    if _NC is None:
        _NC = build()
    maps = make_inputs(inputs, 8)
    res = run_bass_kernel_spmd(_NC, maps, core_ids=list(range(8)))
    r = res.results
    y_prompt = np.concatenate([r[c]['y_p'].reshape(2, LP, D) for c in range(8)], axis=0)
    y_sample = np.stack([r[0]['y_s'], r[1]['y_s']], axis=0)
    cat = lambda k: np.concatenate([r[c][k] for c in range(8)], axis=0)
    return (y_prompt.astype(np.float32), y_sample.astype(np.float32), cat('o_gdn'), cat('o_gla'), cat('o_hg'),
            cat('o_ckv'), cat('o_kpe'))
```

```python
import math
from contextlib import ExitStack
import numpy as np
import concourse.bass as bass
import concourse.mybir as mybir
from concourse.bass_utils import run_bass_kernel_spmd

F32 = mybir.dt.float32
BF16 = mybir.dt.bfloat16
AF = mybir.ActivationFunctionType
ALU = mybir.AluOpType
AX = mybir.AxisListType

D = 2048
DEPTH = 2
LP, LS = 256, 1024
T = 2 * LP + LS
NBLK = 3
EPS = 1e-6
D_FF = 5632
NFF = D_FF // 128
ALPHA = (2.0 * DEPTH) ** 0.25
PAST = 512

IN_SPLITS = (
    ('gdn_q', 512), ('gdn_k', 512), ('gdn_v', 512), ('gdn_z', 512), ('gdn_b', 8), ('gdn_a', 8),
    ('gla_q', 256), ('gla_k', 256), ('gla_v', 512), ('gla_r', 512), ('gla_g', 32),
    ('hg_q', 512), ('hg_f', 1024), ('hg_i', 512), ('hg_g', 512),
    ('mla_qa', 512), ('mla_kva', 512), ('mla_kpe', 64), ('gates', 8192),
)
IN_WIDTH = sum(n for _, n in IN_SPLITS)


def _pieces():
    out, off = {}, 0
    for name, n in IN_SPLITS:
        if name in ('gdn_b',):
            out['gdn_ba'] = [(off, 16)]
        elif name == 'gdn_a':
            pass
        elif name in ('gla_q', 'gla_k'):
            out[name] = [(off + i * 64, 64) for i in range(4)]
        elif name == 'gla_g':
            out[name] = [(off, 16), (off + 16, 16)]
        elif name == 'mla_kpe':
            out[name] = [(off, 64)]
        else:
            out[name] = [(off + i * 128, 128) for i in range(n // 128)]
        off += n
    return out


PIECES = _pieces()

PM_COLS = {}
_o = 0
for _name, _n in (('conv', 60), ('gdn_norm', 1), ('gla_norm', 1), ('hg_norm', 1), ('q_norm', 4), ('kv_norm', 4),
                  ('b_gates', 64), ('hg_lb', 16), ('gla_gb', 8), ('b_ada', 96), ('ln1_g', 16), ('ln1_b', 16),
                  ('ln2_g', 16), ('ln2_b', 16), ('a_log', 8), ('dt_bias', 8)):
    PM_COLS[_name] = _o
    _o += _n
PM_N = _o

CS = {}
_o = 0
for _name, _n in (('ident', 128), ('le', 64), ('ge', 64), ('lt', 64), ('gt', 64), ('ones', 128), ('mean128', 128),
                  ('meanD', 128), ('mean512', 128), ('perm', 64), ('le32', 32), ('ge32', 32), ('cvT', 32), ('eps', 1), ('one', 1)):
    CS[_name] = _o
    _o += _n
CS_N = _o


ALL_RES = []
ALL_TL = []


class Res:
    __slots__ = ('w', 'r', 'name', 'nr', 'nw', 'excl')

    def __init__(self, name='?'):
        self.excl = False
        self.w = None
        self.r = {}
        self.name = name
        self.nr = 0
        self.nw = 0
        ALL_RES.append(self)


class _Rec:
    def __getattr__(self, name):
        return lambda *a, **k: (name, a, k)


_REC = _Rec()


class Prog:
    ENGS = ('pe', 'act', 'dve', 'pool', 'sp')

    def __init__(self, nc, stack):
        self.nc = nc
        self.streams = {e: [] for e in self.ENGS}
        self.cnt = {e: 0 for e in self.ENGS}
        self.esem = {}
        self.semobj = {}
        for e in ('pe', 'act', 'dve', 'pool'):
            s = stack.enter_context(nc.semaphore('tl_' + e))
            self.esem[e] = 'tl_' + e
            self.semobj['tl_' + e] = s
        self.ring = {}
        for q, k in (('sp', 12), ('pool', 12)):
            names = []
            for i in range(k):
                nm = 'rg_%s_%d' % (q, i)
                self.semobj[nm] = stack.enter_context(nc.semaphore(nm))
                names.append(nm)
            self.ring[q] = [names, 0]
        self.waited = {e: {} for e in self.ENGS}
        self.pending = {e: {} for e in self.ENGS}

    def op(self, eng, fn, R=(), W=(), dma=False, nobarrier=False):
        deps = {}

        def add(tok):
            if tok is None:
                return
            s, v = tok
            if deps.get(s, 0) < v:
                deps[s] = v
        for r in R:
            r.nr += 1
            add(r.w)
            if r.excl:
                for s_, v_ in r.r.items():
                    add((s_, v_))
        for w in W:
            w.nw += 1
            add(w.w)
            for s, v in w.r.items():
                add((s, v))
        if dma:
            names, n = self.ring[eng]
            k = len(names)
            sem = names[n % k]
            val = 16 * (n // k + 1)
            if n >= k:
                add((sem, val - 16))
            self.ring[eng][1] = n + 1
            inc = (sem, 16)
        else:
            self.cnt[eng] += 1
            sem = self.esem[eng]
            val = self.cnt[eng]
            inc = (sem, 1)
        tok = (sem, val)
        if not nobarrier:
            for s_, v_ in self.pending[eng].items():
                add((s_, v_))
            self.pending[eng] = {}
        waits = []
        wd = self.waited[eng]
        for s, v in deps.items():
            if eng == 'pe' and s == self.esem.get('pe'):
                continue
            if wd.get(s, 0) >= v:
                continue
            wd[s] = v
            waits.append((s, v))
        self.streams[eng].append((waits, fn(_REC), inc))
        for r in R:
            if r.excl:
                r.w = tok
                r.r = {}
            elif r.r.get(sem, 0) < val:
                r.r[sem] = val
        for w in W:
            w.w = tok
            w.r = {}
        return tok

    def barrier(self):
        cur = {}
        for e in ('pe', 'act', 'dve', 'pool'):
            if self.cnt[e] > 0:
                cur[self.esem[e]] = self.cnt[e]
        for q in self.ring:
            names, n = self.ring[q]
            k = len(names)
            for i, nm in enumerate(names):
                c = (n - i + k - 1) // k if n > i else 0
                if c > 0:
                    cur[nm] = 16 * c
        for e in self.ENGS:
            for s_, v_ in cur.items():
                if self.pending[e].get(s_, 0) < v_:
                    self.pending[e][s_] = v_

    def emit(self, block):
        nc = self.nc
        semobj = self.semobj

        def run(e, name):
            stream = self.streams[name]
            for waits, fn, inc in stream:
                for s, v in waits:
                    e.wait_ge(semobj[s], v)
                ins = getattr(e, fn[0])(*fn[1], **fn[2])
                ins.then_inc(semobj[inc[0]], inc[1])
            if True:
                for q in self.ring:
                    names, n = self.ring[q]
                    k = len(names)
                    for i, nm in enumerate(names):
                        cntq = (n - i + k - 1) // k if n > i else 0
                        if cntq > 0:
                            e.wait_ge(semobj[nm], 16 * cntq)
                for en in ('pe', 'act', 'dve', 'pool'):
                    if self.cnt[en] > 0:
                        e.wait_ge(semobj[self.esem[en]], self.cnt[en])

        @block.sync
        def _(e):
            run(e, 'sp')

        @block.tensor
        def _(e):
            run(e, 'pe')

        @block.scalar
        def _(e):
            run(e, 'act')

        @block.vector
        def _(e):
            run(e, 'dve')

        @block.gpsimd
        def _(e):
            run(e, 'pool')


class Tl:
    def __init__(self, t, name='?'):
        self.t = t
        self.name = name
        self._res = {}
        ALL_TL.append(self)

    def r(self, key=None):
        if key not in self._res:
            self._res[key] = Res(self.name)
            self._res[key].excl = self.name.startswith('ps')
        return self._res[key]

    def __getitem__(self, k):
        return self.t[k]


class _Stop(Exception):
    pass


CUT = [99]
DIRS = [0, 1]


PASSNO = [0]
CUTPASS = [0]


def cut(k):
    if CUT[0] <= k and PASSNO[0] >= CUTPASS[0]:
        raise _Stop()


def build(dbg=None, stage=99):
    nc = bass.Bass("TRN2", target_bir_lowering=False)
    stack = ExitStack()
    ein = lambda name, shape: nc.dram_tensor(name, list(shape), F32, kind="ExternalInput").ap()
    eout = lambda name, shape: nc.dram_tensor(name, list(shape), F32, kind="ExternalOutput").ap()
    xp = ein("xp", [2 * LP, D])
    xs = ein("xs", [LS, D])
    st_gdn = ein("st_gdn", [DEPTH, 2, 4, 128, 128])
    st_gla = ein("st_gla", [DEPTH, 2, 4, 64, 128])
    st_hg = ein("st_hg", [DEPTH, 2, 4, 128, 128])
    cx_ckv = ein("cx_ckv", [DEPTH, PAST, 512])
    cx_kpe = ein("cx_kpe", [DEPTH, PAST, 64])
    pm_d = ein("pm", [DEPTH, 128, PM_N])
    cs_d = ein("cs", [128, CS_N])
    rope_d = ein("rope", [64, 2 * LS])
    gw2_d = ein("gw2", [DEPTH, 16, 2 * 256])
    wada_d = ein("wada", [DEPTH, 128, 16 * 6 * D])
    win_d = ein("win", [DEPTH, 128, 16 * IN_WIDTH])
    wqb_d = ein("wqb", [DEPTH, 128, 4 * 768])
    wkvb_d = ein("wkvb", [DEPTH, 128, 4 * 1024])
    wbr_d = ein("wbr", [DEPTH, 128, 4 * 4 * D])
    wout_d = ein("wout", [DEPTH, 128, 16 * D])
    w1_d = ein("w1", [DEPTH, 128, 16 * D_FF])
    w3_d = ein("w3", [DEPTH, 128, 16 * D_FF])
    w2_d = ein("w2", [DEPTH, 128, NFF * D])
    y_p = eout("y_p", [2 * LP, D])
    y_s = eout("y_s", [LS, D])
    o_gdn = eout("o_gdn", [2, DEPTH, 2, 4, 128, 128])
    o_gla = eout("o_gla", [2, DEPTH, 2, 4, 64, 128])
    o_hg = eout("o_hg", [2, DEPTH, 2, 4, 128, 128])
    o_ckv = eout("o_ckv", [2, DEPTH, LP, 512])
    o_kpe = eout("o_kpe", [2, DEPTH, LP, 64])
    dbg_aps = {}
    if dbg:
        for name, shape in dbg.items():
            dbg_aps[name] = eout("dbg_" + name, shape)
    elif dbg is None and stage < 99:
        dbg_aps['modT0'] = nc.dram_tensor("sink_modT0", [128, 192], F32, kind="Internal").ap()
        dbg_aps['hT'] = nc.dram_tensor("sink_hT", [128, 16 * T], F32, kind="Internal").ap()
    xTd = nc.dram_tensor("xTd", [16, 128, T], F32, kind="Internal").ap()
    mgd = nc.dram_tensor("mgd", [16, 128, T], BF16, kind="Internal").ap()

    P = Prog(nc, stack)
    _uid = [0]

    def _mk(st_, name, shape, dt=F32):
        _uid[0] += 1
        return Tl(st_.enter_context(nc.sbuf_tensor("s%d_%s" % (_uid[0], name), list(shape), dt)), "s%d_%s" % (_uid[0], name))
    sb = lambda name, shape, dt=F32: _mk(stack, name, shape, dt)

    class Phase:
        def __enter__(self):
            self.st = ExitStack()
            return lambda name, shape, dt=F32: _mk(self.st, name, shape, dt)

        def __exit__(self, *a):
            self.st.close()
            P.barrier()
            return False

    cs = sb("cs", [128, CS_N])
    pm = [sb("pm%d" % l, [128, PM_N]) for l in range(DEPTH)]
    modT = [sb("modT%d" % l, [128, 2, 96]) for l in range(DEPTH)]
    hT = sb("hT", [128, 16, T], BF16)
    NWB = 4
    wbs = [sb("wb%d" % i, [128, 16 * 128], BF16) for i in range(NWB)]
    psb = [Tl(stack.enter_context(nc.psum_tensor("ps%d" % i, [128, 512], F32)), "ps%d" % i) for i in range(8)]
    NPS = 6
    wfill = [sb("wfill%d" % i, [128, 16 * 128], BF16) for i in range(2)]
    gst = [sb("gst%d" % i, [128, 512], BF16) for i in range(2)]
    gsd = nc.dram_tensor("gsd", [64, 128, T], BF16, kind="Internal").ap()
    gsd_res = [[Res() for _ in range(NBLK)] for _ in range(64)]
    st = {'wb': 0, 'ps': 0}
    obd = nc.dram_tensor("obd", [16, 128, T], BF16, kind="Internal").ap()
    obd_res = [Res() for _ in range(16)]
    mgd_res = [[Res() for _ in range(NBLK)] for _ in range(16)]
    xTd_res = [Res() for _ in range(NBLK)]

    ident = cs[:, CS['ident']:CS['ident'] + 128]
    ones = cs[:, CS['ones']:CS['ones'] + 128]
    cmat = lambda name, n=64: cs[0:n, CS[name]:CS[name] + n]
    EPSC = cs[:, CS['eps']:CS['eps'] + 1]
    ONEC = cs[:, CS['one']:CS['one'] + 1]

    def psum():
        b = psb[st['ps'] % NPS]
        st['ps'] += 1
        return b

    def dma(q, out, in_, R=(), W=(), nobarrier=False, **kw):
        P.op(q, lambda e: e.dma_start(out=out, in_=in_, **kw), R, W, dma=True, nobarrier=nobarrier)

    def load_w(src2d, kc, n, pool=None):
        t = wbs[st['wb'] % NWB]
        st['wb'] += 1
        dma('pool', t[:, 0:kc * n], src2d, W=[t.r()], nobarrier=True)
        return t, t[:, 0:kc * n].rearrange("p (k n) -> p k n", k=kc)

    def win_piece(l, name, idx):
        c0, n = PIECES[name][idx]
        wt, wv = load_w(win_d[l][:, 16 * c0:16 * c0 + 16 * n], 16, n)
        return wt, wv, n

    def proj_h(wt, wv, n, b, out_ap, psr):
        for kc in range(16):
            P.op('pe', lambda e, kc=kc: e.matmul(out_ap, wv[:, kc, :], hT[:, kc, b * 512:(b + 1) * 512],
                                                 start=(kc == 0), stop=(kc == 15)),
                 R=[wt.r(), hT.r(b)], W=[psr])


    class Filler:
        def __init__(self, l):
            self.l = l
            self.items = [(c * 4 + k_, b) for c in range(16) for k_ in range(4) for b in range(NBLK)]
            self.pos = 0
            self.n = 0
            self.wv = None
            self.wt = None

        def step(self, n=1):
            for _ in range(n):
                if self.pos >= 4 * len(self.items):
                    return
                it, sub = divmod(self.pos, 4)
                (pc, b) = self.items[it]
                c, k_ = divmod(pc, 4)
                p_ = k_ * 16 + c
                if sub == 0 and b == 0:
                    c0, n_ = PIECES['gates'][p_]
                    self.wt = wfill[(it // NBLK) % 2]
                    dma('pool', self.wt[:, :], win_d[self.l][:, 16 * c0:16 * c0 + 16 * 128], W=[self.wt.r()], nobarrier=True)
                    self.wv = self.wt[:, :].rearrange("p (k n) -> p k n", k=16)
                ps = psb[6 + it % 2]
                wv, wt = self.wv, self.wt
                for kc in range(sub * 4, sub * 4 + 4):
                    P.op('pe', lambda e: e.matmul(ps[:, :], wv[:, kc, :], hT[:, kc, b * 512:(b + 1) * 512], start=(kc == 0), stop=(kc == 15)),
                         [wt.r(), hT.r(b)], [ps.r()], nobarrier=True)
                if sub == 3:
                    g_ = gst[it % 2]
                    if it % 2 == 0:
                        P.op('act', lambda e: e.copy(g_[:, :], ps[:, :]), [ps.r()], [g_.r()], nobarrier=True)
                    else:
                        P.op('dve', lambda e: e.tensor_copy(g_[:, :], ps[:, :]), [ps.r()], [g_.r()], nobarrier=True)
                    dma('sp', gsd[p_, :, b * 512:(b + 1) * 512], g_[:, :], R=[g_.r()], W=[gsd_res[p_][b]], nobarrier=True)
                self.pos += 1

        def flush(self):
            self.step(4 * len(self.items))

    FIL = [None]

    def FILL(n=1):
        if FIL[0] is not None and stage >= 5:
            FIL[0].step(n)

    def A(fn, R, W):
        P.op('act', fn, R, W)

    def V(fn, R, W):
        P.op('dve', fn, R, W)

    def G(fn, R, W):
        P.op('pool', fn, R, W)

    def PE(fn, R, W):
        P.op('pe', fn, R, W)

    def dbg_dump(name, src_ap, R):
        if name in dbg_aps:
            dma('pool', dbg_aps[name], src_ap, R=R)

    def rstd_from(ps_ap, out_ap, psr, outr):
        A(lambda e: e.activation(out_ap, ps_ap, AF.Ln, bias=EPSC[0:out_ap.shape[0], :]), [psr, cs.r()], [outr])
        A(lambda e: e.activation(out_ap, out_ap, AF.Exp, scale=-0.5), [outr], [outr])

    dma('sp', cs[:, :], cs_d[:, :], W=[cs.r()])
    for l in range(DEPTH):
        dma('sp', pm[l][:, :], pm_d[l], W=[pm[l].r()])

    with Phase() as ph:
      if stage < 99:
          zt = ph("zt", [128, D])
          V(lambda e: e.memset(zt[:, :], 0.0), [], [zt.r()])
          for i in range(4):
              dma('sp', y_p[i * 128:(i + 1) * 128, :], zt[:, :], R=[zt.r()])
          for i in range(8):
              dma('sp', y_s[i * 128:(i + 1) * 128, :], zt[:, :], R=[zt.r()])
          for o_, dk_ in ((o_gdn, 128), (o_gla, 64), (o_hg, 128)):
              for a_ in range(2):
                  for b_ in range(DEPTH):
                      dma('sp', o_[a_, b_].rearrange("t h k v -> k (t h) v"),
                          zt[0:dk_, 0:1024].rearrange("k (th v) -> k th v", v=128), R=[zt.r()])
          for a_ in range(2):
              for b_ in range(DEPTH):
                  for i in range(2):
                      dma('sp', o_ckv[a_, b_, i * 128:(i + 1) * 128, :], zt[:, 0:512], R=[zt.r()])
                      dma('sp', o_kpe[a_, b_, i * 128:(i + 1) * 128, :], zt[:, 0:64], R=[zt.r()])

    with Phase() as ph:
        scT = ph("scT", [128, 16, 2], BF16)
        A(lambda e: e.activation(scT[:, :, :], cs[:, CS['cvT']:CS['cvT'] + 32].rearrange("p (k t) -> p k t", t=2),
                                 AF.Silu), [cs.r()], [scT.r()])
        for l in range(DEPTH):
            for g in range(24):
                ps = psum()
                for cc in range(4):
                    c = g * 4 + cc
                    wt, wv = load_w(wada_d[l][:, c * 2048:(c + 1) * 2048], 16, 128)
                    for kc in range(16):
                        PE(lambda e, wv=wv, kc=kc, ps=ps, cc=cc: e.matmul(
                            ps[:, cc * 2:cc * 2 + 2], wv[:, kc, :], scT[:, kc, :], start=(kc == 0), stop=(kc == 15)),
                            [wt.r(), scT.r()], [ps.r()])
                for kind in range(2):
                    V(lambda e, ps=ps, g=g, kind=kind, l=l: e.tensor_tensor(
                        modT[l][:, kind, g * 4:g * 4 + 4],
                        ps[:, 0:8].rearrange("p (c t) -> p c t", t=2)[:, :, kind],
                        pm[l][:, PM_COLS['b_ada'] + g * 4:PM_COLS['b_ada'] + g * 4 + 4], ALU.add),
                        [ps.r(), pm[l].r()], [modT[l].r()])
            for j in (1, 4):
                V(lambda e, l=l, j=j: e.tensor_scalar_add(
                    modT[l][:, :, j * 16:(j + 1) * 16], modT[l][:, :, j * 16:(j + 1) * 16], 1.0),
                    [modT[l].r()], [modT[l].r()])
            dbg_dump('modT%d' % l, modT[l][:, :, :].rearrange("p a b -> p (a b)"), [modT[l].r()])

    with Phase() as ph:
        xtok = [ph("xtok%d" % i, [128, D]) for i in range(2)]
        xTb = ph("xTb", [128, 16, 512])
        for tt in range(T // 128):
            xt = xtok[tt % 2]
            src = xp[tt * 128:(tt + 1) * 128, :] if tt < 4 else xs[(tt - 4) * 128:(tt - 3) * 128, :]
            dma('sp', xt[:, :], src, W=[xt.r()])
            for g in range(4):
                ps = psum()
                for cc in range(4):
                    c = g * 4 + cc
                    PE(lambda e, ps=ps, cc=cc, c=c, xt=xt: e.transpose(
                        ps[:, cc * 128:(cc + 1) * 128], xt[:, c * 128:(c + 1) * 128], ident),
                        [xt.r(), cs.r()], [ps.r()])
                dst = xTb[:, g * 4:(g + 1) * 4, (tt % 4) * 128:(tt % 4 + 1) * 128]
                srcp = ps[:, :].rearrange("p (c t) -> p c t", c=4)
                if g % 2 == 0:
                    A(lambda e, dst=dst, srcp=srcp: e.copy(dst, srcp), [ps.r()], [xTb.r(tt % 4)])
                else:
                    V(lambda e, dst=dst, srcp=srcp: e.tensor_copy(dst, srcp), [ps.r()], [xTb.r(tt % 4)])
            if tt % 4 == 3:
                b = tt // 4
                dma('sp', xTd[:, :, b * 512:(b + 1) * 512].rearrange("c p t -> p c t"), xTb[:, :, :],
                    R=[xTb.r(i) for i in range(4)], W=[xTd_res[b]])

    SEQS = [(0, LP, 0, 0), (LP, LP, 0, 1), (2 * LP, LS, 1, 2)]
    TP = T + 12
    SEGS = [(0, 0, 256, 0), (0, 256, 256, 256 + 4), (1, 0, 512, 512 + 8), (2, 0, 512, 1024 + 8)]

    def bc3(ap2d, n):
        p, f = ap2d.shape
        return ap2d.unsqueeze(1).to_broadcast([p, n, f])

    def gdn_unit(l, h, ph0, bet, gg):
      cut(0)
      with Phase() as ph:
        raw = ph("raw", [128, TP])
        cv = ph("cv", [128, 3, TP])
        sq = ph("sq", [128, TP])
        rst = ph("rst", [128, 512])
        zs = ph("zs", [128, T])
        oacc = ph("oacc", [128, T])
        V(lambda e: e.memset(raw[:, :], 0.0), [], [raw.r()])
        for i_ in range(3):
            G(lambda e, i_=i_: e.memset(cv[:, i_, :], 0.0), [], [cv.r(i_)])
        for i, nm in enumerate(('gdn_q', 'gdn_k', 'gdn_v')):
            wt, wv, n_ = win_piece(l, nm, h)
            for b in range(NBLK):
                ps = psum()
                proj_h(wt, wv, 128, b, ps[:, :], ps.r())
                for (sb_, so, sn, cd) in SEGS:
                    if sb_ != b:
                        continue
                    A(lambda e, ps=ps, so=so, sn=sn, cd=cd: e.copy(raw[:, cd + 2:cd + 2 + sn], ps[:, so:so + sn]),
                      [ps.r()], [raw.r()])
            acc = cv[:, i, 2:TP - 2]
            ccol = lambda tap: pm[l][:, PM_COLS['conv'] + (i * 4 + h) * 5 + tap:PM_COLS['conv'] + (i * 4 + h) * 5 + tap + 1]
            V(lambda e, acc=acc, ccol=ccol: e.tensor_scalar(acc, raw[:, 0:TP - 4], ccol(0), None, ALU.mult),
              [raw.r(), pm[l].r()], [cv.r(i)])
            for tap in range(1, 5):
                V(lambda e, acc=acc, ccol=ccol, tap=tap: e.scalar_tensor_tensor(
                    acc, raw[:, tap:TP - 4 + tap], ccol(tap), acc, ALU.mult, ALU.add),
                    [raw.r(), pm[l].r(), cv.r(i)], [cv.r(i)])
            A(lambda e, acc=acc: e.activation(acc, acc, AF.Silu), [cv.r(i)], [cv.r(i)])
            if i < 2:
                A(lambda e, acc=acc: e.activation(sq[:, 2:TP - 2], acc, AF.Square), [cv.r(i)], [sq.r()])
                for (sb_, so, sn, cd) in SEGS:
                    ps = psum()
                    PE(lambda e, ps=ps, sn=sn, cd=cd: e.matmul(ps[:, 0:sn], ones, sq[:, cd + 2:cd + 2 + sn],
                                                               start=True, stop=True), [sq.r(), cs.r()], [ps.r()])
                    rstd_from(ps[:, 0:sn], rst[:, 0:sn], ps.r(), rst.r())
                    sc_ = (128.0 ** -0.5) if i == 0 else 1.0
                    V(lambda e, i=i, sn=sn, cd=cd, sc_=sc_: e.scalar_tensor_tensor(
                        cv[:, i, cd + 2:cd + 2 + sn], cv[:, i, cd + 2:cd + 2 + sn], sc_, rst[:, 0:sn], ALU.mult, ALU.mult),
                        [cv.r(i), rst.r()], [cv.r(i)])
        wt, wv, n_ = win_piece(l, 'gdn_z', h)
        for b in range(NBLK):
            ps = psum()
            proj_h(wt, wv, 128, b, ps[:, :], ps.r())
            A(lambda e, ps=ps, b=b: e.activation(zs[:, b * 512:(b + 1) * 512], ps[:, :], AF.Silu), [ps.r()], [zs.r()])
        if h == 0 and l == 0:
            dbg_dump('cv', cv[:, :, :].rearrange("p a b -> p (a b)"), [cv.r(i) for i in range(3)])
        cut(1)
        for (tok0, L, kind, idx) in SEQS:
          with Phase() as ps_:
            nch = min(8, L // 64)
            nbatch = (L // 64) // nch
            geo = {}
            QT = lambda n: cv[:, 0, geo['cb'] + n * 64:geo['cb'] + (n + 1) * 64]
            KT = lambda n: cv[:, 1, geo['cb'] + n * 64:geo['cb'] + (n + 1) * 64]
            VT = lambda n: cv[:, 2, geo['cb'] + n * 64:geo['cb'] + (n + 1) * 64]
            ktok = ps_("ktok", [64, nch, 128])
            vtok = ps_("vtok", [64, nch, 128])
            R2 = ps_("R2", [64, nch, 64])
            eg = ps_("eg", [128, nch, 64])
            rb = ps_("rb", [64, nch, 64])
            gc = ps_("gc", [64, nch])
            gl = ps_("gl", [128, nch])
            cdec = ps_("cdec", [128, nch])
            kdsc = ps_("kdsc", [64, nch])
            bw = ps_("bw", [64, nch])
            X = ps_("X", [64, nch, 64])
            dT = ps_("dT", [64, nch, 64])
            dd = ps_("dd", [64, nch, 64])
            M = ps_("M", [64, nch, 64])
            MT = ps_("MT", [64, nch, 64])
            Pa = ps_("Pa", [64, nch, 64])
            PTa = ps_("PTa", [64, nch, 64])
            RT = ps_("RT", [64, nch, 64])
            AT = ps_("AT", [128, nch, 64])
            vb = ps_("vb", [64, nch, 128])
            kw = ps_("kw", [64, nch, 128])
            ub = ps_("ub", [64, nch, 128])
            wT = ps_("wT", [128, nch, 64])
            kdec = ps_("kdec", [64, nch, 128])
            qdT = ps_("qdT", [128, nch, 64])
            S = ps_("S", [128, 128])
            u = ps_("u", [128, 128])
            V(lambda e: e.memset(AT[:, :, :], 0.0), [], [AT.r()])
            V(lambda e: e.memset(u[:, :], 0.0), [], [u.r()])
            for dr in DIRS:
                PASSNO[0] += 1
                U = cmat('le') if dr == 0 else cmat('ge')
                inclT = U
                strict = cmat('gt') if dr == 0 else cmat('lt')
                col = dr * 4 + h
                last = 63 if dr == 0 else 0
                gcolv = lambda n: gg[:, geo['n0'] + n, col:col + 1]
                bcolv = lambda n: bet[:, geo['n0'] + n, col:col + 1]
                if kind == 0:
                    V(lambda e: e.memset(S[:, :], 0.0), [], [S.r()])
                else:
                    dma('sp', S[:, :], st_gdn[l, dr, h], W=[S.r()])
                border = range(nbatch) if dr == 0 else range(nbatch - 1, -1, -1)
                for bi in border:
                    tokb = tok0 + bi * nch * 64
                    n0 = tokb // 64
                    cb = tokb + 4 * idx + 2
                    geo['cb'] = cb
                    geo['n0'] = n0
                    for (src, dstt) in ((KT, ktok), (VT, vtok)):
                        for n4 in range(0, nch, 4):
                            ps = psum()
                            for n in range(n4, n4 + 4):
                                PE(lambda e, ps=ps, n=n, n4=n4, src=src: e.transpose(
                                    ps[0:64, (n - n4) * 128:(n - n4 + 1) * 128], src(n), ident),
                                    [cv.r(1), cv.r(2), cs.r()], [ps.r()])
                            A(lambda e, ps=ps, n4=n4, dstt=dstt: e.copy(
                                dstt[:, n4:n4 + 4, :], ps[0:64, :].rearrange("p (n d) -> p n d", n=4)), [ps.r()], [dstt.r()])
                    cut(2)
                    for n in range(nch):
                        V(lambda e, n=n, U=U: e.tensor_scalar(R2[:, n, :], U, gcolv(n), None, ALU.mult),
                          [gg.r(), cs.r()], [R2.r()])
                    for n8 in range(0, nch, 8):
                        w8 = min(8, nch - n8)
                        ps = psum()
                        PE(lambda e, ps=ps, n8=n8, w8=w8: e.matmul(
                            ps[:, 0:w8 * 64], ones[0:64, :], R2[:, n8:n8 + w8, :].rearrange("p n j -> p (n j)"),
                            start=True, stop=True), [R2.r(), cs.r()], [ps.r()])
                        A(lambda e, ps=ps, n8=n8, w8=w8: e.activation(
                            eg[:, n8:n8 + w8, :].rearrange("p n j -> p (n j)"), ps[:, 0:w8 * 64], AF.Exp), [ps.r()], [eg.r()])
                        V(lambda e, ps=ps, n8=n8, w8=w8: e.tensor_copy(
                            rb[:, n8:n8 + w8, :].rearrange("p n j -> p (n j)"), ps[0:64, 0:w8 * 64]), [ps.r()], [rb.r()])
                        V(lambda e, ps=ps, n8=n8, w8=w8: e.tensor_copy(
                            gl[:, n8:n8 + w8], ps[:, 0:w8 * 64].rearrange("p (n j) -> p n j", j=64)[:, :, last]), [ps.r()], [gl.r()])
                    cut(2.2)
                    ps = psum()
                    PE(lambda e, ps=ps, U=U: e.matmul(ps[0:64, 0:nch], U, gg[:, n0:n0 + nch, col], start=True, stop=True),
                       [gg.r(), cs.r()], [ps.r()])
                    A(lambda e, ps=ps: e.copy(gc[:, :], ps[0:64, 0:nch]), [ps.r()], [gc.r()])
                    cut(2.5)
                    A(lambda e: e.activation(cdec[:, :], gl[:, :], AF.Exp), [gl.r()], [cdec.r()])
                    V(lambda e: e.tensor_tensor(kdsc[:, :], gl[0:64, :], gc[:, :], ALU.subtract), [gl.r(), gc.r()], [kdsc.r()])
                    A(lambda e: e.activation(kdsc[:, :], kdsc[:, :], AF.Exp), [kdsc.r()], [kdsc.r()])
                    A(lambda e: e.activation(bw[:, :], gc[:, :], AF.Exp), [gc.r()], [bw.r()])
                    V(lambda e: e.tensor_tensor(bw[:, :], bw[:, :], bet[:, n0:n0 + nch, col], ALU.mult), [bw.r(), bet.r()], [bw.r()])
                    cut(2.7)
                    for n in range(nch):
                        V(lambda e, n=n: e.tensor_scalar(X[:, n, :], rb[:, n, :], gc[:, n:n + 1], 0.0, ALU.subtract, ALU.min),
                          [rb.r(), gc.r()], [X.r()])
                        V(lambda e, n=n: e.tensor_scalar(dd[:, n, :], rb[:, n, :], gc[:, n:n + 1], 0.0, ALU.subtract, ALU.max),
                          [rb.r(), gc.r()], [dd.r()])
                    cut(3)
                    A(lambda e: e.activation(dT[:, :, :], X[:, :, :], AF.Exp), [X.r()], [dT.r()])
                    A(lambda e: e.activation(dd[:, :, :], dd[:, :, :], AF.Exp, scale=-1.0), [dd.r()], [dd.r()])
                    V(lambda e, inclT=inclT: e.tensor_tensor(dT[:, :, :], dT[:, :, :], bc3(inclT, nch), ALU.mult),
                      [dT.r(), cs.r()], [dT.r()])
                    V(lambda e, strict=strict: e.tensor_tensor(dd[:, :, :], dd[:, :, :], bc3(strict, nch), ALU.mult),
                      [dd.r(), cs.r()], [dd.r()])
                    cut(4)
                    for n8 in range(0, nch, 8):
                        w8 = min(8, nch - n8)
                        psA = psum()
                        psQ = psum()
                        for n in range(n8, n8 + w8):
                            PE(lambda e, n=n, n8=n8, psA=psA: e.matmul(psA[0:64, (n - n8) * 64:(n - n8 + 1) * 64], KT(n), KT(n),
                                                                        start=True, stop=True), [cv.r(1)], [psA.r()])
                            PE(lambda e, n=n, n8=n8, psQ=psQ: e.matmul(psQ[0:64, (n - n8) * 64:(n - n8 + 1) * 64], KT(n), QT(n),
                                                                        start=True, stop=True), [cv.r(1), cv.r(0)], [psQ.r()])
                        for n in range(n8, n8 + w8):
                            V(lambda e, n=n, n8=n8, psA=psA: e.scalar_tensor_tensor(
                                M[:, n, :], psA[0:64, (n - n8) * 64:(n - n8 + 1) * 64], bcolv(n), dd[:, n, :], ALU.mult, ALU.mult),
                                [psA.r(), bet.r(), dd.r()], [M.r()])
                        V(lambda e, n8=n8, w8=w8, psQ=psQ: e.tensor_tensor(
                            AT[0:64, n8:n8 + w8, :].rearrange("p n j -> p (n j)"), psQ[0:64, 0:w8 * 64],
                            dT[:, n8:n8 + w8, :].rearrange("p n j -> p (n j)"), ALU.mult), [psQ.r(), dT.r()], [AT.r()])
                    cut(5)
                    for n8 in range(0, nch, 8):
                        w8 = min(8, nch - n8)
                        ps = psum()
                        for n in range(n8, n8 + w8):
                            PE(lambda e, n=n, n8=n8, ps=ps: e.transpose(ps[0:64, (n - n8) * 64:(n - n8 + 1) * 64], M[:, n, :], ident[0:64, 0:64]),
                               [M.r(), cs.r()], [ps.r()])
                        A(lambda e, n8=n8, w8=w8, ps=ps: e.copy(MT[:, n8:n8 + w8, :].rearrange("p n j -> p (n j)"), ps[0:64, 0:w8 * 64]),
                          [ps.r()], [MT.r()])
                    V(lambda e: e.tensor_tensor(RT[:, :, :], bc3(ident[0:64, 0:64], nch), MT[:, :, :], ALU.subtract),
                      [MT.r(), cs.r()], [RT.r()])
                    Pc, PTc = M, MT
                    Pn, PTn = Pa, PTa
                    for it in range(5):
                        FILL(2)
                        for n8 in range(0, nch, 8):
                            w8 = min(8, nch - n8)
                            p1 = psum()
                            p2 = psum()
                            for n in range(n8, n8 + w8):
                                sl = slice((n - n8) * 64, (n - n8 + 1) * 64)
                                PE(lambda e, n=n, sl=sl, p1=p1, Pc=Pc, PTc=PTc: e.matmul(p1[0:64, sl], PTc[:, n, :], Pc[:, n, :], start=True, stop=True),
                                   [Pc.r(), PTc.r()], [p1.r()])
                                PE(lambda e, n=n, sl=sl, p2=p2, Pc=Pc, PTc=PTc: e.matmul(p2[0:64, sl], Pc[:, n, :], PTc[:, n, :], start=True, stop=True),
                                   [Pc.r(), PTc.r()], [p2.r()])
                            A(lambda e, n8=n8, w8=w8, p1=p1, Pn=Pn: e.copy(Pn[:, n8:n8 + w8, :].rearrange("p n j -> p (n j)"), p1[0:64, 0:w8 * 64]),
                              [p1.r()], [Pn.r()])
                            V(lambda e, n8=n8, w8=w8, p2=p2, PTn=PTn: e.tensor_copy(PTn[:, n8:n8 + w8, :].rearrange("p n j -> p (n j)"), p2[0:64, 0:w8 * 64]),
                              [p2.r()], [PTn.r()])
                        for n8 in range(0, nch, 8):
                            w8 = min(8, nch - n8)
                            p3 = psum()
                            for n in range(n8, n8 + w8):
                                sl = slice((n - n8) * 64, (n - n8 + 1) * 64)
                                PE(lambda e, n=n, sl=sl, p3=p3, Pn=Pn: e.matmul(p3[0:64, sl], Pn[:, n, :], RT[:, n, :], start=True, stop=True),
                                   [Pn.r(), RT.r()], [p3.r()])
                            V(lambda e, n8=n8, w8=w8, p3=p3: e.tensor_tensor(
                                RT[:, n8:n8 + w8, :].rearrange("p n j -> p (n j)"), RT[:, n8:n8 + w8, :].rearrange("p n j -> p (n j)"),
                                p3[0:64, 0:w8 * 64], ALU.add), [p3.r(), RT.r()], [RT.r()])
                        Pc, PTc, Pn, PTn = Pn, PTn, Pc, PTc
                    cut(6)
                    for n in range(nch):
                        V(lambda e, n=n: e.tensor_scalar(vb[:, n, :], vtok[:, n, :], bcolv(n), None, ALU.mult), [vtok.r(), bet.r()], [vb.r()])
                        G(lambda e, n=n: e.tensor_scalar(kw[:, n, :], ktok[:, n, :], bw[:, n:n + 1], None, ALU.mult), [ktok.r(), bw.r()], [kw.r()])
                        G(lambda e, n=n: e.tensor_scalar(kdec[:, n, :], ktok[:, n, :], kdsc[:, n:n + 1], None, ALU.mult), [ktok.r(), kdsc.r()], [kdec.r()])
                    for n4 in range(0, nch, 4):
                        ps = psum()
                        for n in range(n4, n4 + 4):
                            PE(lambda e, n=n, n4=n4, ps=ps: e.matmul(ps[0:64, (n - n4) * 128:(n - n4 + 1) * 128], RT[:, n, :], vb[:, n, :],
                                                                       start=True, stop=True), [RT.r(), vb.r()], [ps.r()])
                        A(lambda e, n4=n4, ps=ps: e.copy(ub[:, n4:n4 + 4, :].rearrange("p n d -> p (n d)"), ps[0:64, :]), [ps.r()], [ub.r()])
                    for n8 in range(0, nch, 8):
                        w8 = min(8, nch - n8)
                        ps = psum()
                        for n in range(n8, n8 + w8):
                            PE(lambda e, n=n, n8=n8, ps=ps: e.matmul(ps[:, (n - n8) * 64:(n - n8 + 1) * 64], kw[:, n, :], RT[:, n, :],
                                                                       start=True, stop=True), [RT.r(), kw.r()], [ps.r()])
                        V(lambda e, n8=n8, w8=w8, ps=ps: e.tensor_copy(wT[:, n8:n8 + w8, :].rearrange("p n j -> p (n j)"), ps[:, 0:w8 * 64]),
                          [ps.r()], [wT.r()])
                    V(lambda e: e.tensor_tensor(qdT[:, :, :].rearrange("p n j -> p (n j)"), cv[:, 0, cb:cb + nch * 64],
                                                eg[:, :, :].rearrange("p n j -> p (n j)"), ALU.mult), [cv.r(0), eg.r()], [qdT.r()])
                    cut(7)
                    order = range(nch) if dr == 0 else range(nch - 1, -1, -1)
                    for n in order:
                        FILL(1)
                        pu = psum()
                        PE(lambda e, n=n, pu=pu: e.matmul(pu[0:64, 0:128], wT[:, n, :], S[:, :], start=True, stop=True), [wT.r(), S.r()], [pu.r()])
                        V(lambda e, n=n, pu=pu: e.tensor_tensor(u[0:64, :], ub[:, n, :], pu[0:64, 0:128], ALU.subtract), [ub.r(), pu.r()], [u.r()])
                        cut(8)
                        po = psum()
                        PE(lambda e, n=n, po=po: e.matmul(po[:, 0:64], S[:, :], qdT[:, n, :], start=True, stop=False), [S.r(), qdT.r()], [po.r()])
                        PE(lambda e, n=n, po=po: e.matmul(po[:, 0:64], u[:, :], AT[:, n, :], start=False, stop=True), [u.r(), AT.r()], [po.r()])
                        osl = oacc[:, tokb + n * 64:tokb + (n + 1) * 64]
                        if dr == 0:
                            A(lambda e, po=po, osl=osl: e.copy(osl, po[:, 0:64]), [po.r()], [oacc.r(idx)])
                        else:
                            V(lambda e, po=po, osl=osl: e.tensor_tensor(osl, osl, po[:, 0:64], ALU.add), [po.r(), oacc.r(idx)], [oacc.r(idx)])
                        cut(9)
                        pS = psum()
                        PE(lambda e, n=n, pS=pS: e.matmul(pS[:, 0:128], kdec[:, n, :], u[0:64, :], start=True, stop=True), [kdec.r(), u.r()], [pS.r()])
                        V(lambda e, n=n, pS=pS: e.scalar_tensor_tensor(S[:, :], S[:, :], cdec[:, n:n + 1], pS[:, 0:128], ALU.mult, ALU.add),
                          [S.r(), cdec.r(), pS.r()], [S.r()])
                cut(10)
                if kind == 0:
                    dma('sp', o_gdn[idx, l, dr, h], S[:, :], R=[S.r()])
                cut(11 + idx * 2 + dr)
        if h == 0 and l == 0:
            dbg_dump('oacc', oacc[:, :], [oacc.r(i) for i in range(3)])
        ob = ph("ob", [128, T], BF16)
        A(lambda e: e.activation(sq[:, 0:T], oacc[:, :], AF.Square), [oacc.r(i) for i in range(3)], [sq.r()])
        for b in range(NBLK):
            ps = psum()
            PE(lambda e, ps=ps, b=b: e.matmul(ps[:, :], cs[:, CS['mean128']:CS['mean128'] + 128], sq[:, b * 512:(b + 1) * 512],
                                              start=True, stop=True), [sq.r(), cs.r()], [ps.r()])
            rstd_from(ps[:, :], rst[:, :], ps.r(), rst.r())
            V(lambda e, b=b: e.tensor_tensor(oacc[:, b * 512:(b + 1) * 512], oacc[:, b * 512:(b + 1) * 512], rst[:, :], ALU.mult),
              [rst.r()] + [oacc.r(i) for i in range(3)], [oacc.r(i) for i in range(3)])
            V(lambda e, b=b: e.scalar_tensor_tensor(ob[:, b * 512:(b + 1) * 512], oacc[:, b * 512:(b + 1) * 512],
                                                    pm[l][:, PM_COLS['gdn_norm']:PM_COLS['gdn_norm'] + 1], zs[:, b * 512:(b + 1) * 512],
                                                    ALU.mult, ALU.mult), [zs.r(), pm[l].r()] + [oacc.r(i) for i in range(3)], [ob.r()])
        if stage > 4:
            dma('sp', obd[0 * 4 + h], ob[:, :], R=[ob.r()], W=[obd_res[0 * 4 + h]])
        if h == 0 and l == 0:
            dbg_dump('ob', ob[:, :], [ob.r()])

    def bcl(ap3, c):
        p, n, _ = ap3.shape
        return ap3.to_broadcast([p, n, c])

    CH = 32

    def gla_unit(l, h, mixer, shared):
      PK = 64 if mixer == 'gla' else 128
      bi_ = 1 if mixer == 'gla' else 2
      rmask = shared['rmask']
      with Phase() as ph:
        qT = ph("qT", [PK, T])
        kTs = [ph("kT0", [PK, T])] if mixer == 'gla' else [ph("kT0", [PK, T]), ph("kT1", [PK, T])]
        vT = ph("vT", [128, T])
        las = [ph("la0", [PK, T]), ph("la1", [PK, T])]
        gate = ph("gate", [128, T], BF16)
        oacc = ph("oacc", [128, T])
        bsc = ph("bsc", [PK, T])
        qd = ph("qd", [PK, T])
        kt = ph("kt", [PK, T])
        kd = ph("kd", [PK, T])
        tE = ph("tE", [PK, T])
        xs = ph("xs", [128, 512])
        t1 = ph("t1", [128, 512])

        def inproj(name, idx, fn):
            wt, wv, n_ = win_piece(l, name, idx)
            for b in range(NBLK):
                ps = psum()
                proj_h(wt, wv, n_, b, ps[0:n_, :], ps.r())
                fn(b, ps, n_)
        sl = lambda b: slice(b * 512, (b + 1) * 512)
        if mixer == 'gla':
            inproj('gla_q', h, lambda b, ps, n_: A(lambda e: e.copy(qT[:, sl(b)], ps[0:64, :]), [ps.r()], [qT.r()]))
            inproj('gla_k', h, lambda b, ps, n_: V(lambda e: e.tensor_copy(kTs[0][:, sl(b)], ps[0:64, :]), [ps.r()], [kTs[0].r()]))
            inproj('gla_v', h, lambda b, ps, n_: A(lambda e: e.copy(vT[:, sl(b)], ps[:, :]), [ps.r()], [vT.r()]))
            inproj('gla_r', h, lambda b, ps, n_: A(lambda e: e.activation(gate[:, sl(b)], ps[:, :], AF.Silu), [ps.r()], [gate.r()]))
            glr, gw2 = shared['glr'], shared['gw2']
            for dr in range(2):
                for b in range(NBLK):
                    ps = psum()
                    PE(lambda e: e.matmul(ps[0:64, :], gw2[:, dr * 256 + h * 64:dr * 256 + (h + 1) * 64], glr[dr][:, sl(b)],
                                          start=True, stop=True), [gw2.r(), glr[dr].r()], [ps.r()])
                    gb = pm[l][0:64, PM_COLS['gla_gb'] + dr * 4 + h:PM_COLS['gla_gb'] + dr * 4 + h + 1]
                    A(lambda e: e.activation(xs[0:64, :], ps[0:64, :], AF.Identity, bias=gb), [ps.r(), pm[l].r()], [xs.r()])
                    A(lambda e: e.activation(t1[0:64, :], xs[0:64, :], AF.Abs), [xs.r()], [t1.r()])
                    A(lambda e: e.activation(t1[0:64, :], t1[0:64, :], AF.Exp, scale=-1.0), [t1.r()], [t1.r()])
                    A(lambda e: e.activation(t1[0:64, :], t1[0:64, :], AF.Ln, bias=ONEC[0:64, :]), [t1.r(), cs.r()], [t1.r()])
                    V(lambda e: e.scalar_tensor_tensor(xs[0:64, :], xs[0:64, :], 0.0, t1[0:64, :], ALU.min, ALU.subtract),
                      [xs.r(), t1.r()], [xs.r()])
                    V(lambda e: e.tensor_scalar(las[dr][:, sl(b)], xs[0:64, :], 1.0 / 16.0, None, ALU.mult), [xs.r()], [las[dr].r()])
        else:
            inproj('hg_q', h, lambda b, ps, n_: A(lambda e: e.activation(qT[:, sl(b)], ps[:, :], AF.Silu), [ps.r()], [qT.r()]))
            inproj('hg_i', h, lambda b, ps, n_: A(lambda e: e.copy(vT[:, sl(b)], ps[:, :]), [ps.r()], [vT.r()]))
            inproj('hg_g', h, lambda b, ps, n_: A(lambda e: e.activation(gate[:, sl(b)], ps[:, :], AF.Sigmoid), [ps.r()], [gate.r()]))
            lbt, oml = shared['lbt'], shared['oml']
            for dr in range(2):
                j = dr * 4 + h

                def fz(b, ps, n_, dr=dr, j=j):
                    A(lambda e: e.activation(xs[:, :], ps[:, :], AF.Sigmoid), [ps.r()], [xs.r()])
                    V(lambda e: e.tensor_scalar(xs[:, :], xs[:, :], oml[:, j:j + 1], lbt[:, j:j + 1], ALU.mult, ALU.add),
                      [xs.r(), oml.r(), lbt.r()], [xs.r()])
                    A(lambda e: e.activation(las[dr][:, sl(b)], xs[:, :], AF.Ln), [xs.r()], [las[dr].r()])
                    V(lambda e: e.tensor_scalar(kTs[dr][:, sl(b)], xs[:, :], -1.0, 1.0, ALU.mult, ALU.add), [xs.r()], [kTs[dr].r()])
                inproj('hg_f', j, fz)
        NB = 8
        for dr in range(2):
            kT = kTs[min(dr, len(kTs) - 1)]
            la = las[dr]
            V(lambda e: e.tensor_tensor_scan(bsc[:, :], rmask[0:PK, :], la[:, :], 0.0, ALU.mult, ALU.add),
              [rmask.r(), la.r()], [bsc.r()])
            b3 = bsc[:, :].rearrange("p (n c) -> p n c", c=CH)
            tot3 = b3[:, :, CH - 1:CH]
            if dr == 1:
                V(lambda e: e.tensor_tensor(tE[:, :], la[:, :], bsc[:, :], ALU.subtract), [la.r(), bsc.r()], [tE.r()])
                V(lambda e: e.tensor_tensor(tE[:, :].rearrange("p (n c) -> p n c", c=CH),
                                            tE[:, :].rearrange("p (n c) -> p n c", c=CH), bcl(tot3, CH), ALU.add),
                  [tE.r(), bsc.r()], [tE.r()])
                bcur = tE
            else:
                bcur = bsc
            V(lambda e: e.tensor_tensor(kd[:, :].rearrange("p (n c) -> p n c", c=CH), bcl(tot3, CH),
                                        bcur[:, :].rearrange("p (n c) -> p n c", c=CH), ALU.subtract),
              [bsc.r(), bcur.r()], [kd.r()])
            A(lambda e: e.activation(kd[:, :], kd[:, :], AF.Exp), [kd.r()], [kd.r()])
            V(lambda e: e.tensor_tensor(kd[:, :], kd[:, :], kT[:, :], ALU.mult), [kd.r(), kT.r()], [kd.r()])
            cdec = ph("cdec%d" % dr, [PK, T // CH])
            A(lambda e: e.activation(cdec[:, :], tot3.rearrange("p n c -> p (n c)"), AF.Exp), [bsc.r()], [cdec.r()])
            A(lambda e: e.activation(qd[:, :], bcur[:, :], AF.Exp), [bcur.r()], [qd.r()])
            A(lambda e: e.activation(kt[:, :], bcur[:, :], AF.Exp, scale=-1.0), [bcur.r()], [kt.r()])
            qs_ = (64.0 ** -0.5) if mixer == 'gla' else 1.0
            V(lambda e: e.scalar_tensor_tensor(qd[:, :], qT[:, :], qs_, qd[:, :], ALU.mult, ALU.mult), [qT.r(), qd.r()], [qd.r()])
            G(lambda e: e.tensor_tensor(kt[:, :], kt[:, :], kT[:, :], ALU.mult), [kt.r(), kT.r()], [kt.r()])
            maskT = cmat('le32', 32) if dr == 0 else cmat('ge32', 32)
            if dr == 0:
                nb = NB
                vtok = ph("vtok", [CH, nb, 128])
                kdtok = ph("kdtok", [CH, nb, PK])
                ATm = ph("ATm", [CH, nb, CH])
                KV = ph("KV", [PK, nb, 128])
                Sall = ph("Sall", [PK, nb + 1, 128])
                otmp = ph("otmp", [128, nb * CH])
            for (tok0, L, kind, idx) in SEQS:
              if True:
                nbatch = (L // CH) // nb
                if kind == 0:
                    V(lambda e: e.memset(Sall[:, 0, :], 0.0), [], [Sall.r()])
                else:
                    st_in = st_gla if mixer == 'gla' else st_hg
                    dma('sp', Sall[:, 0, :], st_in[l, dr, h], W=[Sall.r()])
                border = range(nbatch) if dr == 0 else range(nbatch - 1, -1, -1)
                for bix, bi in enumerate(border):
                    tokb = tok0 + bi * nb * CH
                    c0 = tokb // CH
                    csl = lambda n: slice(tokb + n * CH, tokb + (n + 1) * CH)
                    if bix > 0:
                        V(lambda e: e.tensor_copy(Sall[:, 0, :], Sall[:, nb, :]), [Sall.r()], [Sall.r()])
                    for (srcT, dstt, pw) in ((vT, vtok, 128), (kd, kdtok, PK)):
                        per = 512 // pw
                        for n4 in range(0, nb, per):
                            ps = psum()
                            for n in range(n4, min(nb, n4 + per)):
                                PE(lambda e, n=n: e.transpose(ps[0:CH, (n - n4) * pw:(n - n4 + 1) * pw], srcT[:, csl(n)], ident[0:pw, 0:pw]),
                                   [srcT.r(), cs.r()], [ps.r()])
                            w_ = min(nb, n4 + per) - n4
                            A(lambda e: e.copy(dstt[:, n4:n4 + w_, :].rearrange("p n d -> p (n d)"), ps[0:CH, 0:w_ * pw]),
                              [ps.r()], [dstt.r()])
                    FILL(2)
                    ps = psum()
                    for n in range(nb):
                        PE(lambda e, n=n: e.matmul(ps[0:CH, n * CH:(n + 1) * CH], kt[:, csl(n)], qd[:, csl(n)], start=True, stop=True),
                           [kt.r(), qd.r()], [ps.r()])
                    V(lambda e: e.tensor_tensor(ATm[:, :, :], ps[0:CH, 0:nb * CH].rearrange("p (n c) -> p n c", c=CH),
                                                bc3(maskT, nb), ALU.mult), [ps.r(), cs.r()], [ATm.r()])
                    FILL(2)
                    for n4 in range(0, nb, 4):
                        ps = psum()
                        for n in range(n4, n4 + 4):
                            PE(lambda e, n=n: e.matmul(ps[0:PK, (n - n4) * 128:(n - n4 + 1) * 128], kdtok[:, n, :], vtok[:, n, :],
                                                       start=True, stop=True), [kdtok.r(), vtok.r()], [ps.r()])
                        A(lambda e: e.copy(KV[:, n4:n4 + 4, :].rearrange("p n d -> p (n d)"), ps[0:PK, :]), [ps.r()], [KV.r()])
                    order = list(range(nb)) if dr == 0 else list(range(nb - 1, -1, -1))
                    for s_i, n in enumerate(order):
                        V(lambda e, s_i=s_i, n=n: e.scalar_tensor_tensor(Sall[:, s_i + 1, :], Sall[:, s_i, :], cdec[:, c0 + n:c0 + n + 1],
                                                                         KV[:, n, :], ALU.mult, ALU.add), [Sall.r(), cdec.r(), KV.r()], [Sall.r()])
                    FILL(3)
                    pA = psum()
                    pB = psum()
                    for s_i, n in enumerate(order):
                        PE(lambda e, s_i=s_i, n=n: e.matmul(pA[:, n * CH:(n + 1) * CH], Sall[:, s_i, :], qd[:, csl(n)], start=True, stop=True),
                           [Sall.r(), qd.r()], [pA.r()])
                        PE(lambda e, n=n: e.matmul(pB[:, n * CH:(n + 1) * CH], vtok[:, n, :], ATm[:, n, :], start=True, stop=True),
                           [vtok.r(), ATm.r()], [pB.r()])
                    A(lambda e: e.copy(otmp[:, :], pA[:, 0:nb * CH]), [pA.r()], [otmp.r()])
                    osl = oacc[:, tokb:tokb + nb * CH]
                    if dr == 0:
                        V(lambda e: e.tensor_tensor(osl, otmp[:, :], pB[:, 0:nb * CH], ALU.add), [otmp.r(), pB.r()], [oacc.r(idx)])
                    else:
                        V(lambda e: e.tensor_tensor(otmp[:, :], otmp[:, :], pB[:, 0:nb * CH], ALU.add), [otmp.r(), pB.r()], [otmp.r()])
                        V(lambda e: e.tensor_tensor(osl, osl, otmp[:, :], ALU.add), [otmp.r(), oacc.r(idx)], [oacc.r(idx)])
                if kind == 0:
                    o_st = o_gla if mixer == 'gla' else o_hg
                    dma('sp', o_st[idx, l, dr, h], Sall[:, nb, :], R=[Sall.r()])
        ob = ph("ob", [128, T], BF16)
        normc = pm[l][:, PM_COLS['gla_norm' if mixer == 'gla' else 'hg_norm']:PM_COLS['gla_norm' if mixer == 'gla' else 'hg_norm'] + 1]
        for b in range(NBLK):
            A(lambda e: e.activation(xs[:, :], oacc[:, sl(b)], AF.Square), [oacc.r(i) for i in range(3)], [xs.r()])
            ps = psum()
            PE(lambda e: e.matmul(ps[:, :], cs[:, CS['mean128']:CS['mean128'] + 128], xs[:, :], start=True, stop=True),
               [xs.r(), cs.r()], [ps.r()])
            rstd_from(ps[:, :], t1[:, :], ps.r(), t1.r())
            V(lambda e: e.tensor_tensor(xs[:, :], oacc[:, sl(b)], t1[:, :], ALU.mult), [t1.r()] + [oacc.r(i) for i in range(3)], [xs.r()])
            V(lambda e: e.scalar_tensor_tensor(ob[:, sl(b)], xs[:, :], normc, gate[:, sl(b)], ALU.mult, ALU.mult),
              [xs.r(), gate.r(), pm[l].r()], [ob.r()])
        k_ = bi_ * 4 + h
        if stage > 4:
            dma('sp', obd[k_], ob[:, :], R=[ob.r()], W=[obd_res[k_]])
        if h == 0 and l == 0:
            dbg_dump('ob_' + mixer, ob[:, :], [ob.r()])

    NKEY = T + PAST
    KEYR = [(0, LP), (LP, LP), (2 * LP, LS + PAST)]
    SCALE = (128 + 64) ** -0.5

    def mla_layer(l):
      with Phase() as ph:
        qn = ph("qn", [128, 4, T], BF16)
        ckvb = ph("ckvb", [128, 4, NKEY], BF16)
        kpT = ph("kpT", [128, NKEY], BF16)
        ropet = ph("ropet", [64, 2, LS])
        dma('sp', ropet[:, :, :], rope_d[:, :].rearrange("p (a t) -> p a t", a=2), W=[ropet.r()])
        G(lambda e: e.memset(kpT[:, :], 0.0), [], [kpT.r()])

        def rope(dst_bf, src, ncol0, tmpa, tmpb):
            for hb in range(2):
                c = slice(hb * 512, (hb + 1) * 512)
                ps = psum()
                PE(lambda e: e.matmul(ps[0:64, :], cmat('perm'), src[0:64, c], start=True, stop=True), [src.r(), cs.r()], [ps.r()])
                V(lambda e: e.tensor_tensor(tmpa[0:64, :], src[0:64, c], ropet[:, 0, c], ALU.mult), [src.r(), ropet.r()], [tmpa.r()])
                V(lambda e: e.tensor_tensor(tmpb[0:64, :], ps[0:64, :], ropet[:, 1, c], ALU.mult), [ps.r(), ropet.r()], [tmpb.r()])
                V(lambda e: e.tensor_tensor(dst_bf[0:64, ncol0 + hb * 512:ncol0 + (hb + 1) * 512], tmpa[0:64, :], tmpb[0:64, :], ALU.add),
                  [tmpa.r(), tmpb.r()], [dst_bf.r()])
        sl = lambda b: slice(b * 512, (b + 1) * 512)
        with Phase() as pa:
            qa = pa("qa", [128, 4, T])
            kva = pa("kva", [128, 4, T])
            kpe = pa("kpe", [64, T])
            sq = pa("sq", [128, 512])
            rst = pa("rst", [128, 512])
            tmpa = pa("tmpa", [128, 512])
            tmpb = pa("tmpb", [128, 512])
            tokt = pa("tokt", [128, 512])
            for nm, dst in (('mla_qa', qa), ('mla_kva', kva)):
                for c in range(4):
                    wt, wv, n_ = win_piece(l, nm, c)
                    for b in range(NBLK):
                        ps = psum()
                        proj_h(wt, wv, 128, b, ps[:, :], ps.r())
                        if c % 2 == 0:
                            A(lambda e: e.copy(dst[:, c, sl(b)], ps[:, :]), [ps.r()], [dst.r()])
                        else:
                            V(lambda e: e.tensor_copy(dst[:, c, sl(b)], ps[:, :]), [ps.r()], [dst.r()])
            wt, wv, n_ = win_piece(l, 'mla_kpe', 0)
            for b in range(NBLK):
                ps = psum()
                proj_h(wt, wv, 64, b, ps[0:64, :], ps.r())
                A(lambda e: e.copy(kpe[:, sl(b)], ps[0:64, :]), [ps.r()], [kpe.r()])
            for src, ncol, kind_ in ((qa, 'q_norm', 'q'), (kva, 'kv_norm', 'kv')):
                for b in range(NBLK):
                    psm = psum()
                    for c in range(4):
                        A(lambda e: e.activation(sq[:, :], src[:, c, sl(b)], AF.Square), [src.r()], [sq.r()])
                        PE(lambda e: e.matmul(psm[:, :], cs[:, CS['mean512']:CS['mean512'] + 128], sq[:, :], start=(c == 0), stop=(c == 3)),
                           [sq.r(), cs.r()], [psm.r()])
                    rstd_from(psm[:, :], rst[:, :], psm.r(), rst.r())
                    for c in range(4):
                        ncl = pm[l][:, PM_COLS[ncol] + c:PM_COLS[ncol] + c + 1]
                        if kind_ == 'q':
                            V(lambda e: e.scalar_tensor_tensor(qn[:, c, sl(b)], src[:, c, sl(b)], ncl, rst[:, :], ALU.mult, ALU.mult),
                              [src.r(), rst.r(), pm[l].r()], [qn.r()])
                        else:
                            V(lambda e: e.scalar_tensor_tensor(src[:, c, sl(b)], src[:, c, sl(b)], ncl, rst[:, :], ALU.mult, ALU.mult),
                              [src.r(), rst.r(), pm[l].r()], [src.r()])
                            G(lambda e: e.tensor_copy(ckvb[:, c, sl(b)], src[:, c, sl(b)]), [src.r()], [ckvb.r()])
            for tt in range(4):
                idx, t_in = tt // 2, (tt % 2) * 128
                ps = psum()
                for c in range(4):
                    PE(lambda e: e.transpose(ps[:, c * 128:(c + 1) * 128], kva[:, c, tt * 128:(tt + 1) * 128], ident), [kva.r(), cs.r()], [ps.r()])
                A(lambda e: e.copy(tokt[:, :], ps[:, :]), [ps.r()], [tokt.r()])
                dma('sp', o_ckv[idx, l, t_in:t_in + 128, :], tokt[:, :], R=[tokt.r()])
                ps = psum()
                PE(lambda e: e.transpose(ps[:, 0:64], kpe[:, tt * 128:(tt + 1) * 128], ident[0:64, 0:64]), [kpe.r(), cs.r()], [ps.r()])
                A(lambda e: e.copy(tmpa[:, 0:64], ps[:, 0:64]), [ps.r()], [tmpa.r()])
                dma('sp', o_kpe[idx, l, t_in:t_in + 128, :], tmpa[:, 0:64], R=[tmpa.r()])
            V(lambda e: e.tensor_copy(kpT[0:64, 0:2 * LP], kpe[:, 0:2 * LP]), [kpe.r()], [kpT.r()])
            kps = pa("kps", [64, LS])
            V(lambda e: e.tensor_copy(kps[:, :], kpe[:, 2 * LP:T]), [kpe.r()], [kps.r()])
            rope(kpT, kps, 2 * LP, tmpa, tmpb)
            for tt in range(PAST // 128):
                dma('sp', tokt[:, :], cx_ckv[l, tt * 128:(tt + 1) * 128, :], W=[tokt.r()])
                ps = psum()
                for c in range(4):
                    PE(lambda e: e.transpose(ps[:, c * 128:(c + 1) * 128], tokt[:, c * 128:(c + 1) * 128], ident), [tokt.r(), cs.r()], [ps.r()])
                A(lambda e: e.copy(ckvb[:, :, T + tt * 128:T + (tt + 1) * 128], ps[:, :].rearrange("p (c t) -> p c t", c=4)), [ps.r()], [ckvb.r()])
                dma('sp', tmpb[:, 0:64], cx_kpe[l, tt * 128:(tt + 1) * 128, :], W=[tmpb.r()])
                ps = psum()
                PE(lambda e: e.transpose(ps[0:64, 0:128], tmpb[:, 0:64], ident), [tmpb.r(), cs.r()], [ps.r()])
                A(lambda e: e.copy(kpT[0:64, T + tt * 128:T + (tt + 1) * 128], ps[0:64, 0:128]), [ps.r()], [kpT.r()])
        for h in range(4):
          if stage == 4 and h > 0:
              break
          with Phase() as hh:
            qnT = hh("qnT", [128, T], BF16)
            qpT = hh("qpT", [128, T], BF16)
            qpf = hh("qpf", [64, T])
            knT = hh("knT", [128, NKEY], BF16)
            vtk = hh("vtk", [128, NKEY // 128, 128], BF16)
            Pm = hh("Pm", [128, LS + PAST])
            PT = hh("PT", [128, (LS + PAST) // 128, 128], BF16)
            Otok = hh("Otok", [128, 128])
            ob = hh("ob", [128, T], BF16)
            mx = hh("mx", [128, 4])
            sm = hh("sm", [128, 4])
            tmpa = hh("tmpa", [128, 512])
            tmpb = hh("tmpb", [128, 512])
            G(lambda e: e.memset(qpT[:, :], 0.0), [], [qpT.r()])
            wqo = (h * 192) * 4
            wt, wv = load_w(wqb_d[l][:, wqo:wqo + 512], 4, 128)
            for b in range(NBLK):
                ps = psum()
                for kc in range(4):
                    PE(lambda e: e.matmul(ps[:, :], wv[:, kc, :], qn[:, kc, sl(b)], start=(kc == 0), stop=(kc == 3)), [wt.r(), qn.r()], [ps.r()])
                A(lambda e: e.copy(qnT[:, sl(b)], ps[:, :]), [ps.r()], [qnT.r()])
            wt, wv = load_w(wqb_d[l][:, wqo + 512:wqo + 768], 4, 64)
            for b in range(NBLK):
                ps = psum()
                for kc in range(4):
                    PE(lambda e: e.matmul(ps[0:64, :], wv[:, kc, :], qn[:, kc, sl(b)], start=(kc == 0), stop=(kc == 3)), [wt.r(), qn.r()], [ps.r()])
                A(lambda e: e.copy(qpf[:, sl(b)], ps[0:64, :]), [ps.r()], [qpf.r()])
            V(lambda e: e.tensor_copy(qpT[0:64, 0:2 * LP], qpf[:, 0:2 * LP]), [qpf.r()], [qpT.r()])
            qps = hh("qps", [64, LS])
            V(lambda e: e.tensor_copy(qps[:, :], qpf[:, 2 * LP:T]), [qpf.r()], [qps.r()])
            rope(qpT, qps, 2 * LP, tmpa, tmpb)
            wt, wv = load_w(wkvb_d[l][:, (h * 2) * 512:(h * 2) * 512 + 512], 4, 128)
            for kb in range(NKEY // 512):
                ps = psum()
                for kc in range(4):
                    PE(lambda e: e.matmul(ps[:, :], wv[:, kc, :], ckvb[:, kc, kb * 512:(kb + 1) * 512], start=(kc == 0), stop=(kc == 3)),
                       [wt.r(), ckvb.r()], [ps.r()])
                A(lambda e: e.copy(knT[:, kb * 512:(kb + 1) * 512], ps[:, :]), [ps.r()], [knT.r()])
            wt, wv = load_w(wkvb_d[l][:, (h * 2 + 1) * 512:(h * 2 + 1) * 512 + 512], 4, 128)
            for k4 in range(0, NKEY // 128, 4):
                ps = psum()
                for kt_ in range(k4, k4 + 4):
                    for kc in range(4):
                        PE(lambda e: e.matmul(ps[:, (kt_ - k4) * 128:(kt_ - k4 + 1) * 128], ckvb[:, kc, kt_ * 128:(kt_ + 1) * 128], wv[:, kc, :],
                                              start=(kc == 0), stop=(kc == 3)), [wt.r(), ckvb.r()], [ps.r()])
                V(lambda e: e.tensor_copy(vtk[:, k4:k4 + 4, :].rearrange("p k d -> p (k d)"), ps[:, :]), [ps.r()], [vtk.r()])
            for (tok0, L, kind, idx) in SEQS:
                k0, Lk = KEYR[idx]
                nkb = (Lk + 511) // 512
                for qt in range(L // 128):
                    qs = slice(tok0 + qt * 128, tok0 + (qt + 1) * 128)
                    pss = [psum() for _ in range(nkb)]
                    for kb in range(nkb):
                        kw_ = min(512, Lk - kb * 512)
                        ks = slice(k0 + kb * 512, k0 + kb * 512 + kw_)
                        PE(lambda e: e.matmul(pss[kb][:, 0:kw_], qnT[:, qs], knT[:, ks], start=True, stop=False), [qnT.r(), knT.r()], [pss[kb].r()])
                        PE(lambda e: e.matmul(pss[kb][:, 0:kw_], qpT[:, qs], kpT[:, ks], start=False, stop=True), [qpT.r(), kpT.r()], [pss[kb].r()])
                        V(lambda e: e.reduce_max(mx[:, kb:kb + 1], pss[kb][:, 0:kw_], AX.X), [pss[kb].r()], [mx.r()])
                    if nkb > 1:
                        V(lambda e: e.reduce_max(mx[:, 3:4], mx[:, 0:nkb], AX.X), [mx.r()], [mx.r()])
                        mcol = mx[:, 3:4]
                    else:
                        mcol = mx[:, 0:1]
                    V(lambda e: e.tensor_scalar(mx[:, 3:4], mcol, -SCALE, None, ALU.mult), [mx.r()], [mx.r()])
                    for kb in range(nkb):
                        kw_ = min(512, Lk - kb * 512)
                        A(lambda e: e.activation(Pm[:, kb * 512:kb * 512 + kw_], pss[kb][:, 0:kw_], AF.Exp, bias=mx[:, 3:4], scale=SCALE,
                                                 accum_out=sm[:, kb:kb + 1]), [pss[kb].r(), mx.r()], [Pm.r(), sm.r()])
                    if nkb > 1:
                        V(lambda e: e.reduce_sum(sm[:, 3:4], sm[:, 0:nkb], AX.X), [sm.r()], [sm.r()])
                        scol = sm[:, 3:4]
                    else:
                        scol = sm[:, 0:1]
                    V(lambda e: e.reciprocal(sm[:, 3:4], scol), [sm.r()], [sm.r()])
                    FILL(2)
                    nkt = Lk // 128
                    for k4 in range(0, nkt, 4):
                        ps = psum()
                        w4 = min(4, nkt - k4)
                        for kt_ in range(k4, k4 + w4):
                            PE(lambda e: e.transpose(ps[:, (kt_ - k4) * 128:(kt_ - k4 + 1) * 128], Pm[:, kt_ * 128:(kt_ + 1) * 128], ident),
                               [Pm.r(), cs.r()], [ps.r()])
                        A(lambda e: e.copy(PT[:, k4:k4 + w4, :].rearrange("p k q -> p (k q)"), ps[:, 0:w4 * 128]), [ps.r()], [PT.r()])
                    po = psum()
                    for kt_ in range(nkt):
                        PE(lambda e: e.matmul(po[:, 0:128], PT[:, kt_, :], vtk[:, k0 // 128 + kt_, :], start=(kt_ == 0), stop=(kt_ == nkt - 1)),
                           [PT.r(), vtk.r()], [po.r()])
                    A(lambda e: e.activation(Otok[:, :], po[:, 0:128], AF.Copy, scale=sm[:, 3:4]), [po.r(), sm.r()], [Otok.r()])
                    pt2 = psum()
                    PE(lambda e: e.transpose(pt2[:, 0:128], Otok[:, :], ident), [Otok.r(), cs.r()], [pt2.r()])
                    V(lambda e: e.tensor_copy(ob[:, qs], pt2[:, 0:128]), [pt2.r()], [ob.r()])
            if stage > 4:
                dma('sp', obd[12 + h], ob[:, :], R=[ob.r()], W=[obd_res[12 + h]])
            if h == 0 and l == 0:
                dbg_dump('ob_mla', ob[:, :], [ob.r()])

    for l in range(DEPTH):
      try:
        if stage < 1:
            break
        with Phase() as ph:
            xTb = ph("xTb", [128, 16, 512])
            for b in range(NBLK):
                kind = 0 if b == 0 else 1
                dma('sp', xTb[:, :, :], xTd[:, :, b * 512:(b + 1) * 512].rearrange("c p t -> p c t"),
                    R=[xTd_res[b]], W=[xTb.r()])
                for c in range(16):
                    A(lambda e, c=c, b=b, kind=kind, l=l: e.activation(
                        hT[:, c, b * 512:(b + 1) * 512], xTb[:, c, :], AF.Identity,
                        bias=modT[l][:, kind, c:c + 1], scale=modT[l][:, kind, 16 + c:16 + c + 1]),
                        [xTb.r(), modT[l].r()], [hT.r(b)])
            if l == 0:
                dbg_dump('hT', hT[:, :, :].rearrange("p c t -> p (c t)"), [hT.r(b) for b in range(NBLK)])
        if stage < 2:
            break
        FIL[0] = Filler(l)
        with Phase() as ph:
            ba = ph("ba", [64, 24, 16])
            bet = ph("bet", [64, 24, 8])
            gg = ph("gg", [64, 24, 8])
            tt_ = ph("tt_", [64, 24, 8])
            t2_ = ph("t2_", [64, 24, 8])
            nea = ph("nea", [64, 8])
            if CUT[0] == -6:
                win_piece(l, 'gdn_q', 0)
                cut(-6)
            wt, wv, n_ = win_piece(l, 'gdn_ba', 0)
            cut(-5)
            ps = psum()
            for n in range(24):
                for kc in range(16):
                    PE(lambda e, n=n, kc=kc, ps=ps, wv=wv: e.matmul(
                        ps[0:64, n * 16:(n + 1) * 16], hT[:, kc, n * 64:(n + 1) * 64], wv[:, kc, :],
                        start=(kc == 0), stop=(kc == 15)), [wt.r()] + [hT.r(b) for b in range(NBLK)], [ps.r()])
            cut(-4)
            A(lambda e, ps=ps: e.copy(ba[:, :, :], ps[0:64, 0:384].rearrange("p (n c) -> p n c", c=16)), [ps.r()], [ba.r()])
            cut(-3)
            A(lambda e: e.activation(bet[:, :, :], ba[:, :, 0:8], AF.Sigmoid), [ba.r()], [bet.r()])
            for j in range(8):
                V(lambda e, j=j: e.tensor_scalar(tt_[:, :, j], ba[:, :, 8 + j], pm[l][0:64, PM_COLS['dt_bias'] + j:PM_COLS['dt_bias'] + j + 1],
                                               None, ALU.add), [ba.r(), pm[l].r()], [tt_.r()])
            cut(-2)
            A(lambda e: e.activation(t2_[:, :, :], tt_[:, :, :], AF.Abs), [tt_.r()], [t2_.r()])
            A(lambda e: e.activation(t2_[:, :, :], t2_[:, :, :], AF.Exp, scale=-1.0), [t2_.r()], [t2_.r()])
            A(lambda e: e.activation(t2_[:, :, :], t2_[:, :, :], AF.Ln, bias=ONEC[0:64, :]), [t2_.r(), cs.r()], [t2_.r()])
            cut(-1)
            V(lambda e: e.scalar_tensor_tensor(tt_[:, :, :], tt_[:, :, :], 0.0, t2_[:, :, :], ALU.max, ALU.add),
              [tt_.r(), t2_.r()], [tt_.r()])
            A(lambda e: e.activation(nea[:, :], pm[l][0:64, PM_COLS['a_log']:PM_COLS['a_log'] + 8], AF.Exp), [pm[l].r()], [nea.r()])
            V(lambda e: e.tensor_scalar(nea[:, :], nea[:, :], -1.0, None, ALU.mult), [nea.r()], [nea.r()])
            for j in range(8):
                V(lambda e, j=j: e.tensor_scalar(gg[:, :, j], tt_[:, :, j], nea[:, j:j + 1], None, ALU.mult),
                  [tt_.r(), nea.r()], [gg.r()])
            dbg_dump('gg', gg[:, :, :].rearrange("p a b -> p (a b)"), [gg.r()])
            for h in range(4):
                if stage in (2, 3, 4) and h > 0:
                    break
                gdn_unit(l, h, ph, bet, gg)
        if stage < 3:
            break
        with Phase() as ph:
            rmask = ph("rmask", [128, T])
            V(lambda e: e.memset(rmask[:, :], 1.0), [], [rmask.r()])
            V(lambda e: e.memset(rmask[:, :].rearrange("p (n c) -> p n c", c=CH)[:, :, 0], 0.0), [rmask.r()], [rmask.r()])
            glr = [ph("glr0", [16, T]), ph("glr1", [16, T])]
            gw2 = ph("gw2", [16, 512])
            dma('sp', gw2[:, :], gw2_d[l], W=[gw2.r()])
            for dr in range(2):
                wt, wv, n_ = win_piece(l, 'gla_g', dr)
                for b in range(NBLK):
                    ps = psum()
                    proj_h(wt, wv, 16, b, ps[0:16, :], ps.r())
                    A(lambda e: e.copy(glr[dr][:, b * 512:(b + 1) * 512], ps[0:16, :]), [ps.r()], [glr[dr].r()])
            lbt = ph("lbt", [128, 8])
            oml = ph("oml", [128, 8])
            if l == 0:
                V(lambda e: e.memset(lbt[:, :], 0.0), [], [lbt.r()])
                V(lambda e: e.memset(oml[:, :], 1.0), [], [oml.r()])
            else:
                c_ = PM_COLS['hg_lb']
                V(lambda e: e.tensor_tensor(lbt[:, :], pm[l][:, c_ + 8:c_ + 16], pm[l][:, c_:c_ + 8], ALU.subtract), [pm[l].r()], [lbt.r()])
                A(lambda e: e.activation(lbt[:, :], lbt[:, :], AF.Sigmoid), [lbt.r()], [lbt.r()])
                V(lambda e: e.tensor_scalar(oml[:, :], lbt[:, :], -1.0, 1.0, ALU.mult, ALU.add), [lbt.r()], [oml.r()])
            shared = {'rmask': rmask, 'glr': glr, 'gw2': gw2, 'lbt': lbt, 'oml': oml}
            for mixer in ('gla', 'hg'):
                for h in range(4):
                    if stage in (3, 4) and h > 0:
                        break
                    gla_unit(l, h, mixer, shared)
        if stage < 4:
            break
        mla_layer(l)
        if stage < 5:
            break
        FIL[0].flush()
        with Phase() as ph:
            obr = ph("obr", [128, 16, T], BF16)
            for k_ in range(16):
                dma('sp', obr[:, k_, :], obd[k_], R=[obd_res[k_]], W=[obr.r(k_)])
            sig = ph("sig", [128, 512])
            macc = ph("macc", [128, 512])
            gin = [ph("gin0", [128, 512], BF16), ph("gin1", [128, 512], BF16)]
            mo = [ph("mo0", [128, 512], BF16), ph("mo1", [128, 512], BF16)]
            obr_all = [obr.r(k_) for k_ in range(16)]
            for c in range(16):
                for b in range(NBLK):
                    for k in range(4):
                        gi_ = gin[(c * 12 + b * 4 + k) % 2]
                        dma('sp', gi_[:, :], gsd[k * 16 + c, :, b * 512:(b + 1) * 512], R=[gsd_res[k * 16 + c][b]], W=[gi_.r()])
                        bg = pm[l][:, PM_COLS['b_gates'] + k * 16 + c:PM_COLS['b_gates'] + k * 16 + c + 1]
                        A(lambda e: e.activation(sig[:, :], gi_[:, :], AF.Sigmoid, bias=bg), [gi_.r(), pm[l].r()], [sig.r()])
                        off = (k * 16 + c) * 512
                        wtb, wvb = load_w(wbr_d[l][:, off:off + 512], 4, 128)
                        pp = psum()
                        for kc in range(4):
                            PE(lambda e: e.matmul(pp[:, :], wvb[:, kc, :], obr[:, k * 4 + kc, b * 512:(b + 1) * 512], start=(kc == 0), stop=(kc == 3)),
                               [wtb.r()] + obr_all, [pp.r()])
                        if k == 0:
                            V(lambda e: e.tensor_tensor(macc[:, :], sig[:, :], pp[:, :], ALU.mult), [sig.r(), pp.r()], [macc.r()])
                        else:
                            V(lambda e: e.tensor_tensor(sig[:, :], sig[:, :], pp[:, :], ALU.mult), [sig.r(), pp.r()], [sig.r()])
                            if k < 3:
                                V(lambda e: e.tensor_tensor(macc[:, :], macc[:, :], sig[:, :], ALU.add), [macc.r(), sig.r()], [macc.r()])
                            else:
                                mo_ = mo[(c * NBLK + b) % 2]
                                V(lambda e: e.tensor_tensor(mo_[:, :], macc[:, :], sig[:, :], ALU.add), [macc.r(), sig.r()], [mo_.r()])
                                dma('sp', mgd[c, :, b * 512:(b + 1) * 512], mo_[:, :], R=[mo_.r()], W=[mgd_res[c][b]])
        if stage < 6:
            break
        with Phase() as ph:
            rr = ph("rr", [128, 16, 512])
            aT = ph("aT", [128, NFF, 512], BF16)
            xin = ph("xin", [128, 512])
            sq = ph("sq", [128, 512])
            mean = ph("mean", [128, 512])
            rst = ph("rst", [128, 512])
            tg = ph("tg", [128, 512])
            wbig = [ph("wbig0", [128, NFF * 128], BF16), ph("wbig1", [128, NFF * 128], BF16)]
            otok = ph("otok", [128, 512])
            wbi = [0]

            def layer_norm(gname, bname, post):
                pm1 = psum()
                pm2 = psum()
                for c in range(16):
                    A(lambda e: e.activation(sq[:, :], rr[:, c, :], AF.Square), [rr.r(c)], [sq.r()])
                    PE(lambda e: e.matmul(pm1[:, :], cs[:, CS['meanD']:CS['meanD'] + 128], rr[:, c, :], start=(c == 0), stop=(c == 15)),
                       [rr.r(c), cs.r()], [pm1.r()])
                    PE(lambda e: e.matmul(pm2[:, :], cs[:, CS['meanD']:CS['meanD'] + 128], sq[:, :], start=(c == 0), stop=(c == 15)),
                       [sq.r(), cs.r()], [pm2.r()])
                A(lambda e: e.copy(mean[:, :], pm1[:, :]), [pm1.r()], [mean.r()])
                V(lambda e: e.tensor_tensor(tg[:, :], mean[:, :], mean[:, :], ALU.mult), [mean.r()], [tg.r()])
                V(lambda e: e.tensor_tensor(tg[:, :], pm2[:, :], tg[:, :], ALU.subtract), [pm2.r(), tg.r()], [tg.r()])
                A(lambda e: e.activation(rst[:, :], tg[:, :], AF.Ln, bias=EPSC), [tg.r(), cs.r()], [rst.r()])
                A(lambda e: e.activation(rst[:, :], rst[:, :], AF.Exp, scale=-0.5), [rst.r()], [rst.r()])
                for c in range(16):
                    V(lambda e: e.tensor_tensor(rr[:, c, :], rr[:, c, :], mean[:, :], ALU.subtract), [rr.r(c), mean.r()], [rr.r(c)])
                    G(lambda e: e.tensor_tensor(rr[:, c, :], rr[:, c, :], rst[:, :], ALU.mult), [rr.r(c), rst.r()], [rr.r(c)])
                    A(lambda e: e.activation(rr[:, c, :], rr[:, c, :], AF.Identity,
                                             bias=pm[l][:, PM_COLS[bname] + c:PM_COLS[bname] + c + 1],
                                             scale=pm[l][:, PM_COLS[gname] + c:PM_COLS[gname] + c + 1]), [rr.r(c), pm[l].r()], [rr.r(c)])
                    post(c)
            for b in range(NBLK):
                kind = 0 if b == 0 else 1
                for c in range(16):
                    dma('sp', hT[:, c, 512:1024], mgd[c, :, b * 512:(b + 1) * 512], R=[mgd_res[c][b]], W=[hT.r(1)])
                mg_all = [hT.r(1)]
                for c in range(16):
                    wt, wv = load_w(wout_d[l][:, c * 2048:(c + 1) * 2048], 16, 128)
                    ps = psum()
                    for kc in range(16):
                        PE(lambda e: e.matmul(ps[:, :], wv[:, kc, :], hT[:, kc, 512:1024], start=(kc == 0), stop=(kc == 15)), [wt.r()] + mg_all, [ps.r()])
                    dma('sp', xin[:, :], xTd[c, :, b * 512:(b + 1) * 512], R=[xTd_res[b]], W=[xin.r()])
                    A(lambda e: e.activation(tg[:, :], ps[:, :], AF.Copy, scale=modT[l][:, kind, 32 + c:32 + c + 1]), [ps.r(), modT[l].r()], [tg.r()])
                    V(lambda e: e.scalar_tensor_tensor(rr[:, c, :], xin[:, :], ALPHA, tg[:, :], ALU.mult, ALU.add), [xin.r(), tg.r()], [rr.r(c)])

                def post1(c):
                    A(lambda e: e.activation(hT[:, c, 0:512], rr[:, c, :], AF.Identity, bias=modT[l][:, kind, 48 + c:48 + c + 1],
                                             scale=modT[l][:, kind, 64 + c:64 + c + 1]), [rr.r(c), modT[l].r()], [hT.r(0)])
                layer_norm('ln1_g', 'ln1_b', post1)
                h2_all = [hT.r(0)]
                for f in range(NFF):
                    wt1, wv1 = load_w(w1_d[l][:, f * 2048:(f + 1) * 2048], 16, 128)
                    wt3, wv3 = load_w(w3_d[l][:, f * 2048:(f + 1) * 2048], 16, 128)
                    p1 = psum()
                    p3 = psum()
                    for kc in range(16):
                        PE(lambda e: e.matmul(p1[:, :], wv1[:, kc, :], hT[:, kc, 0:512], start=(kc == 0), stop=(kc == 15)), [wt1.r()] + h2_all, [p1.r()])
                    for kc in range(16):
                        PE(lambda e: e.matmul(p3[:, :], wv3[:, kc, :], hT[:, kc, 0:512], start=(kc == 0), stop=(kc == 15)), [wt3.r()] + h2_all, [p3.r()])
                    A(lambda e: e.activation(tg[:, :], p1[:, :], AF.Silu), [p1.r()], [tg.r()])
                    V(lambda e: e.tensor_tensor(aT[:, f, :], tg[:, :], p3[:, :], ALU.mult), [tg.r(), p3.r()], [aT.r(f)])
                aT_all = [aT.r(f) for f in range(NFF)]
                for c in range(16):
                    wtile = wbig[wbi[0] % 2]
                    wbi[0] += 1
                    dma('pool', wtile[:, :], w2_d[l][:, c * NFF * 128:(c + 1) * NFF * 128], W=[wtile.r()])
                    wv2 = wtile[:, :].rearrange("p (k n) -> p k n", k=NFF)
                    ps = psum()
                    for kc in range(NFF):
                        PE(lambda e: e.matmul(ps[:, :], wv2[:, kc, :], aT[:, kc, :], start=(kc == 0), stop=(kc == NFF - 1)), [wtile.r()] + aT_all, [ps.r()])
                    A(lambda e: e.activation(tg[:, :], ps[:, :], AF.Copy, scale=modT[l][:, kind, 80 + c:80 + c + 1]), [ps.r(), modT[l].r()], [tg.r()])
                    V(lambda e: e.scalar_tensor_tensor(rr[:, c, :], rr[:, c, :], ALPHA, tg[:, :], ALU.mult, ALU.add), [rr.r(c), tg.r()], [rr.r(c)])

                def post2(c):
                    if l < DEPTH - 1:
                        dma('sp', xTd[c, :, b * 512:(b + 1) * 512], rr[:, c, :], R=[rr.r(c)], W=[xTd_res[b]])
                layer_norm('ln2_g', 'ln2_b', post2)
                if l == DEPTH - 1:
                    for tt in range(4):
                        for g4 in range(4):
                            ps = psum()
                            for cc in range(4):
                                c = g4 * 4 + cc
                                PE(lambda e: e.transpose(ps[:, cc * 128:(cc + 1) * 128], rr[:, c, tt * 128:(tt + 1) * 128], ident), [rr.r(c), cs.r()], [ps.r()])
                            A(lambda e: e.copy(otok[:, :], ps[:, :]), [ps.r()], [otok.r()])
                            gt = b * 4 + tt
                            dst = y_p[gt * 128:(gt + 1) * 128, g4 * 512:(g4 + 1) * 512] if gt < 4 else \
                                y_s[(gt - 4) * 128:(gt - 3) * 128, g4 * 512:(g4 + 1) * 512]
                            dma('sp', dst, otok[:, :], R=[otok.r()])
      except _Stop:
        break

    for tl_ in list(ALL_TL):
        rs_ = list(tl_._res.values())
        if tl_.name.startswith('s') and rs_ and sum(r.nw for r in rs_) > 0 and sum(r.nr for r in rs_) == 0:
            try:
                shp_ = list(tl_.t.shape)
                sink_ = nc.dram_tensor("sink_" + tl_.name, shp_, tl_.t.dtype, kind="Internal").ap()
                full_ = tl_.t[tuple(slice(None) for _ in shp_)]
                dma('sp', sink_[tuple(slice(None) for _ in shp_)], full_, R=rs_)
            except Exception as ex_:
                print('autosink failed', tl_.name, ex_)
    agg = {}
    for r_ in ALL_RES:
        a_ = agg.setdefault(r_.name, [0, 0])
        a_[0] += r_.nr
        a_[1] += r_.nw
    dead = [k for k, v in agg.items() if v[1] > 0 and v[0] == 0 and k != '?']
    if dead:
        print('WARNING unread tiles:', dead)
    print('op counts', P.cnt, {q: P.ring[q][1] for q in P.ring})
    stack_close = stack
    with nc.Block() as block:
        P.emit(block)
    stack_close.close()
    return nc


def _pack(W, n=128):
    K, N = W.shape
    kc = K // 128
    return np.ascontiguousarray(
        W.reshape(kc, 128, N // n, n).transpose(1, 2, 0, 3).reshape(128, -1))


def _pack_in(W):
    K = W.shape[0]
    outs = []
    cols = []
    for name in PIECES:
        cols.extend(PIECES[name])
    cols.sort()
    assert sum(n for _, n in cols) == IN_WIDTH
    for c0, n in cols:
        outs.append(W[:, c0:c0 + n].reshape(K // 128, 128, n).transpose(1, 0, 2).reshape(128, -1))
    return np.ascontiguousarray(np.concatenate(outs, axis=1))


def _consts(cvec2):
    cs = np.zeros((128, CS_N), np.float32)
    cs[:, CS['ident']:CS['ident'] + 128] = np.eye(128, dtype=np.float32)
    k = np.arange(64)[:, None]
    i = np.arange(64)[None, :]
    for name, m in (('le', k <= i), ('ge', k >= i), ('lt', k < i), ('gt', k > i)):
        cs[:64, CS[name]:CS[name] + 64] = m.astype(np.float32)
    cs[:, CS['ones']:CS['ones'] + 128] = 1.0
    cs[:, CS['mean128']:CS['mean128'] + 128] = 1.0 / 128
    cs[:, CS['meanD']:CS['meanD'] + 128] = 1.0 / D
    cs[:, CS['mean512']:CS['mean512'] + 128] = 1.0 / 512
    perm = np.zeros((64, 64), np.float32)
    cosT = np.zeros((64, LS), np.float32)
    sinT = np.zeros((64, LS), np.float32)
    pos = np.arange(LS)
    row_id = (pos // 64).astype(np.float32)
    col_id = (pos % 64).astype(np.float32)
    half = 32
    inv = (10000.0 ** (-np.arange(0, half, 2, dtype=np.float32) / half)).astype(np.float32)
    for blk, ids in ((0, row_id), (1, col_id)):
        ang = ids[None, :] * inv[:, None]
        b0 = blk * 32
        for j in range(16):
            cosT[b0 + j] = np.cos(ang[j]); cosT[b0 + 16 + j] = np.cos(ang[j])
            sinT[b0 + j] = -np.sin(ang[j]); sinT[b0 + 16 + j] = np.sin(ang[j])
            perm[b0 + 16 + j, b0 + j] = 1.0
            perm[b0 + j, b0 + 16 + j] = 1.0
    cs[:64, CS['perm']:CS['perm'] + 64] = perm
    k = np.arange(32)[:, None]
    i = np.arange(32)[None, :]
    cs[:32, CS['le32']:CS['le32'] + 32] = (k <= i)
    cs[:32, CS['ge32']:CS['ge32'] + 32] = (k >= i)
    cs[:, CS['eps']] = EPS
    cs[:, CS['one']] = 1.0
    cs[:, CS['cvT']:CS['cvT'] + 32] = cvec2.reshape(2, 16, 128).transpose(2, 1, 0).reshape(128, 32)
    rope = np.zeros((64, 2 * LS), np.float32)
    rope[:, :LS] = cosT
    rope[:, LS:] = sinT
    return cs, rope


def _pm(inp, l):
    pm = np.zeros((128, PM_N), np.float32)

    def put(name, arr2d):
        c = PM_COLS[name]
        pm[:arr2d.shape[1], c:c + arr2d.shape[0]] = arr2d.T
    conv = inp['gdn_conv'][l]
    put('conv', conv.reshape(5, 12, 128).transpose(1, 0, 2).reshape(60, 128))
    put('gdn_norm', inp['gdn_norm'][l][None])
    put('gla_norm', inp['gla_norm'][l][None])
    put('hg_norm', inp['hgrn_norm'][l][None])
    put('q_norm', inp['mla_q_norm'][l].reshape(4, 128))
    put('kv_norm', inp['mla_kv_norm'][l].reshape(4, 128))
    put('b_gates', inp['b_gates'][l].reshape(64, 128))
    put('hg_lb', inp['hgrn_lb'].reshape(16, 128))
    put('gla_gb', inp['gla_gate_b'][l].reshape(8, 64))
    put('b_ada', inp['b_ada'][l].reshape(96, 128))
    for nm in ('ln1_g', 'ln1_b', 'ln2_g', 'ln2_b'):
        put(nm, inp[nm][l].reshape(16, 128))
    pm[:, PM_COLS['a_log']:PM_COLS['a_log'] + 8] = inp['gdn_a_log'][l].reshape(1, 8)
    pm[:, PM_COLS['dt_bias']:PM_COLS['dt_bias'] + 8] = inp['gdn_dt_bias'][l].reshape(1, 8)
    return pm


def make_inputs(inp, ncores=8):
    f = lambda a: np.ascontiguousarray(np.asarray(a, dtype=np.float32))
    inp = {k: f(v) for k, v in inp.items()}
    sh = {}
    sh['pm'] = np.stack([_pm(inp, l) for l in range(DEPTH)])
    sh['gw2'] = np.ascontiguousarray(inp['gla_gate_w2'].transpose(0, 2, 1, 3).reshape(DEPTH, 16, 512))
    sh['wada'] = np.stack([_pack(inp['w_ada'][l]) for l in range(DEPTH)])
    sh['win'] = np.stack([_pack_in(inp['w_in'][l]) for l in range(DEPTH)])
    def pk_cols(W, cols):
        K = W.shape[0]
        return np.concatenate([W[:, c0:c0 + n].reshape(K // 128, 128, n).transpose(1, 0, 2).reshape(128, -1)
                               for c0, n in cols], axis=1)
    qcols = []
    for h in range(4):
        qcols += [(h * 192, 128), (h * 192 + 128, 64)]
    sh['wqb'] = np.stack([pk_cols(inp['mla_wq_b'][l], qcols) for l in range(DEPTH)])
    sh['wkvb'] = np.stack([_pack(inp['mla_wkv_b'][l]) for l in range(DEPTH)])
    sh['wbr'] = np.stack([np.concatenate([_pack(inp['w_branch'][l, k]) for k in range(4)], axis=1)
                          for l in range(DEPTH)])
    sh['wout'] = np.stack([_pack(inp['w_out'][l]) for l in range(DEPTH)])
    sh['w1'] = np.stack([_pack(inp['ffn_w1'][l]) for l in range(DEPTH)])
    sh['w3'] = np.stack([_pack(inp['ffn_w3'][l]) for l in range(DEPTH)])
    sh['w2'] = np.stack([_pack(inp['ffn_w2'][l]) for l in range(DEPTH)])
    maps = []
    for core in range(ncores):
        b = core % 2
        m = dict(sh)
        m['xp'] = inp['x_prompt'][2 * core:2 * core + 2].reshape(2 * LP, D)
        m['xs'] = inp['x_sample'][b]
        m['st_gdn'] = inp['state_gdn'][b]
        m['st_gla'] = inp['state_gla'][b]
        m['st_hg'] = inp['state_hgrn'][b]
        m['cx_ckv'] = inp['cache_mla_ckv'][b]
        m['cx_kpe'] = inp['cache_mla_kpe'][b]
        m['cs'], m['rope'] = _consts(np.stack([inp['c_ctx'], inp['c'][b]]))
        maps.append(m)
    return maps


_NC = None


def kernel(**inputs):
    global _NC
    if _NC is None:
        _NC = build()
    maps = make_inputs(inputs, 8)
    res = run_bass_kernel_spmd(_NC, maps, core_ids=list(range(8)))
    r = res.results
    y_prompt = np.concatenate([r[c]['y_p'].reshape(2, LP, D) for c in range(8)], axis=0)
    y_sample = np.stack([r[0]['y_s'], r[1]['y_s']], axis=0)
    cat = lambda k: np.concatenate([r[c][k] for c in range(8)], axis=0)
    return (y_prompt.astype(np.float32), y_sample.astype(np.float32), cat('o_gdn'), cat('o_gla'), cat('o_hg'),
            cat('o_ckv'), cat('o_kpe'))
```

```python
import math
from contextlib import ExitStack
import numpy as np
import concourse.bass as bass
import concourse.mybir as mybir
from concourse.bass_utils import run_bass_kernel_spmd

F32 = mybir.dt.float32
BF16 = mybir.dt.bfloat16
AF = mybir.ActivationFunctionType
ALU = mybir.AluOpType
AX = mybir.AxisListType

D = 2048
DEPTH = 2
LP, LS = 256, 1024
T = 2 * LP + LS
NBLK = 3
EPS = 1e-6
D_FF = 5632
NFF = D_FF // 128
ALPHA = (2.0 * DEPTH) ** 0.25
PAST = 512

IN_SPLITS = (
    ('gdn_q', 512), ('gdn_k', 512), ('gdn_v', 512), ('gdn_z', 512), ('gdn_b', 8), ('gdn_a', 8),
    ('gla_q', 256), ('gla_k', 256), ('gla_v', 512), ('gla_r', 512), ('gla_g', 32),
    ('hg_q', 512), ('hg_f', 1024), ('hg_i', 512), ('hg_g', 512),
    ('mla_qa', 512), ('mla_kva', 512), ('mla_kpe', 64), ('gates', 8192),
)
IN_WIDTH = sum(n for _, n in IN_SPLITS)


def _pieces():
    out, off = {}, 0
    for name, n in IN_SPLITS:
        if name in ('gdn_b',):
            out['gdn_ba'] = [(off, 16)]
        elif name == 'gdn_a':
            pass
        elif name in ('gla_q', 'gla_k'):
            out[name] = [(off + i * 64, 64) for i in range(4)]
        elif name == 'gla_g':
            out[name] = [(off, 16), (off + 16, 16)]
        elif name == 'mla_kpe':
            out[name] = [(off, 64)]
        else:
            out[name] = [(off + i * 128, 128) for i in range(n // 128)]
        off += n
    return out


PIECES = _pieces()

PM_COLS = {}
_o = 0
for _name, _n in (('conv', 60), ('gdn_norm', 1), ('gla_norm', 1), ('hg_norm', 1), ('q_norm', 4), ('kv_norm', 4),
                  ('b_gates', 64), ('hg_lb', 16), ('gla_gb', 8), ('b_ada', 96), ('ln1_g', 16), ('ln1_b', 16),
                  ('ln2_g', 16), ('ln2_b', 16), ('a_log', 8), ('dt_bias', 8)):
    PM_COLS[_name] = _o
    _o += _n
PM_N = _o

CS = {}
_o = 0
for _name, _n in (('ident', 128), ('le', 64), ('ge', 64), ('lt', 64), ('gt', 64), ('ones', 128), ('mean128', 128),
                  ('meanD', 128), ('mean512', 128), ('perm', 64), ('le32', 32), ('ge32', 32), ('cvT', 32), ('eps', 1), ('one', 1)):
    CS[_name] = _o
    _o += _n
CS_N = _o


ALL_RES = []
ALL_TL = []


class Res:
    __slots__ = ('w', 'r', 'name', 'nr', 'nw', 'excl')

    def __init__(self, name='?'):
        self.excl = False
        self.w = None
        self.r = {}
        self.name = name
        self.nr = 0
        self.nw = 0
        ALL_RES.append(self)


class _Rec:
    def __getattr__(self, name):
        return lambda *a, **k: (name, a, k)


_REC = _Rec()


class Prog:
    ENGS = ('pe', 'act', 'dve', 'pool', 'sp')

    def __init__(self, nc, stack):
        self.nc = nc
        self.streams = {e: [] for e in self.ENGS}
        self.cnt = {e: 0 for e in self.ENGS}
        self.esem = {}
        self.semobj = {}
        for e in ('pe', 'act', 'dve', 'pool'):
            s = stack.enter_context(nc.semaphore('tl_' + e))
            self.esem[e] = 'tl_' + e
            self.semobj['tl_' + e] = s
        self.ring = {}
        for q, k in (('sp', 12), ('pool', 12)):
            names = []
            for i in range(k):
                nm = 'rg_%s_%d' % (q, i)
                self.semobj[nm] = stack.enter_context(nc.semaphore(nm))
                names.append(nm)
            self.ring[q] = [names, 0]
        self.waited = {e: {} for e in self.ENGS}
        self.pending = {e: {} for e in self.ENGS}

    def op(self, eng, fn, R=(), W=(), dma=False, nobarrier=False):
        deps = {}

        def add(tok):
            if tok is None:
                return
            s, v = tok
            if deps.get(s, 0) < v:
                deps[s] = v
        for r in R:
            r.nr += 1
            add(r.w)
            if r.excl:
                for s_, v_ in r.r.items():
                    add((s_, v_))
        for w in W:
            w.nw += 1
            add(w.w)
            for s, v in w.r.items():
                add((s, v))
        if dma:
            names, n = self.ring[eng]
            k = len(names)
            sem = names[n % k]
            val = 16 * (n // k + 1)
            if n >= k:
                add((sem, val - 16))
            self.ring[eng][1] = n + 1
            inc = (sem, 16)
        else:
            self.cnt[eng] += 1
            sem = self.esem[eng]
            val = self.cnt[eng]
            inc = (sem, 1)
        tok = (sem, val)
        if not nobarrier:
            for s_, v_ in self.pending[eng].items():
                add((s_, v_))
            self.pending[eng] = {}
        waits = []
        wd = self.waited[eng]
        for s, v in deps.items():
            if eng == 'pe' and s == self.esem.get('pe'):
                continue
            if wd.get(s, 0) >= v:
                continue
            wd[s] = v
            waits.append((s, v))
        self.streams[eng].append((waits, fn(_REC), inc))
        for r in R:
            if r.excl:
                r.w = tok
                r.r = {}
            elif r.r.get(sem, 0) < val:
                r.r[sem] = val
        for w in W:
            w.w = tok
            w.r = {}
        return tok

    def barrier(self):
        cur = {}
        for e in ('pe', 'act', 'dve', 'pool'):
            if self.cnt[e] > 0:
                cur[self.esem[e]] = self.cnt[e]
        for q in self.ring:
            names, n = self.ring[q]
            k = len(names)
            for i, nm in enumerate(names):
                c = (n - i + k - 1) // k if n > i else 0
                if c > 0:
                    cur[nm] = 16 * c
        for e in self.ENGS:
            for s_, v_ in cur.items():
                if self.pending[e].get(s_, 0) < v_:
                    self.pending[e][s_] = v_

    def emit(self, block):
        nc = self.nc
        semobj = self.semobj

        def run(e, name):
            stream = self.streams[name]
            for waits, fn, inc in stream:
                for s, v in waits:
                    e.wait_ge(semobj[s], v)
                ins = getattr(e, fn[0])(*fn[1], **fn[2])
                ins.then_inc(semobj[inc[0]], inc[1])
            if True:
                for q in self.ring:
                    names, n = self.ring[q]
                    k = len(names)
                    for i, nm in enumerate(names):
                        cntq = (n - i + k - 1) // k if n > i else 0
                        if cntq > 0:
                            e.wait_ge(semobj[nm], 16 * cntq)
                for en in ('pe', 'act', 'dve', 'pool'):
                    if self.cnt[en] > 0:
                        e.wait_ge(semobj[self.esem[en]], self.cnt[en])

        @block.sync
        def _(e):
            run(e, 'sp')

        @block.tensor
        def _(e):
            run(e, 'pe')

        @block.scalar
        def _(e):
            run(e, 'act')

        @block.vector
        def _(e):
            run(e, 'dve')

        @block.gpsimd
        def _(e):
            run(e, 'pool')


class Tl:
    def __init__(self, t, name='?'):
        self.t = t
        self.name = name
        self._res = {}
        ALL_TL.append(self)

    def r(self, key=None):
        if key not in self._res:
            self._res[key] = Res(self.name)
            self._res[key].excl = self.name.startswith('ps')
        return self._res[key]

    def __getitem__(self, k):
        return self.t[k]


class _Stop(Exception):
    pass


CUT = [99]
NLAY = [DEPTH]
DIRS = [0, 1]


PASSNO = [0]
CUTPASS = [0]


def cut(k):
    if CUT[0] <= k and PASSNO[0] >= CUTPASS[0]:
        raise _Stop()


def build(dbg=None, stage=99):
    nc = bass.Bass("TRN2", target_bir_lowering=False)
    stack = ExitStack()
    ein = lambda name, shape: nc.dram_tensor(name, list(shape), F32, kind="ExternalInput").ap()
    eout = lambda name, shape: nc.dram_tensor(name, list(shape), F32, kind="ExternalOutput").ap()
    xp = ein("xp", [2 * LP, D])
    xs = ein("xs", [LS, D])
    st_gdn = ein("st_gdn", [DEPTH, 2, 4, 128, 128])
    st_gla = ein("st_gla", [DEPTH, 2, 4, 64, 128])
    st_hg = ein("st_hg", [DEPTH, 2, 4, 128, 128])
    cx_ckv = ein("cx_ckv", [DEPTH, PAST, 512])
    cx_kpe = ein("cx_kpe", [DEPTH, PAST, 64])
    pm_d = ein("pm", [DEPTH, 128, PM_N])
    cs_d = ein("cs", [128, CS_N])
    rope_d = ein("rope", [64, 2 * LS])
    gw2_d = ein("gw2", [DEPTH, 16, 2 * 256])
    wada_d = ein("wada", [DEPTH, 128, 16 * 6 * D])
    win_d = ein("win", [DEPTH, 128, 16 * IN_WIDTH])
    wqb_d = ein("wqb", [DEPTH, 128, 4 * 768])
    wkvb_d = ein("wkvb", [DEPTH, 128, 4 * 1024])
    wbr_d = ein("wbr", [DEPTH, 128, 4 * 4 * D])
    wout_d = ein("wout", [DEPTH, 128, 16 * D])
    w1_d = ein("w1", [DEPTH, 128, 16 * D_FF])
    w3_d = ein("w3", [DEPTH, 128, 16 * D_FF])
    w2_d = ein("w2", [DEPTH, 128, NFF * D])
    y_p = eout("y_p", [2 * LP, D])
    y_s = eout("y_s", [LS, D])
    o_gdn = eout("o_gdn", [2, DEPTH, 2, 4, 128, 128])
    o_gla = eout("o_gla", [2, DEPTH, 2, 4, 64, 128])
    o_hg = eout("o_hg", [2, DEPTH, 2, 4, 128, 128])
    o_ckv = eout("o_ckv", [2, DEPTH, LP, 512])
    o_kpe = eout("o_kpe", [2, DEPTH, LP, 64])
    dbg_aps = {}
    if dbg:
        for name, shape in dbg.items():
            dbg_aps[name] = eout("dbg_" + name, shape)
    elif dbg is None and stage < 99:
        dbg_aps['modT0'] = nc.dram_tensor("sink_modT0", [128, 192], F32, kind="Internal").ap()
        dbg_aps['hT'] = nc.dram_tensor("sink_hT", [128, 16 * T], F32, kind="Internal").ap()
    xTd = nc.dram_tensor("xTd", [16, 128, T], F32, kind="Internal").ap()
    mgd = nc.dram_tensor("mgd", [16, 128, T], BF16, kind="Internal").ap()

    P = Prog(nc, stack)
    _uid = [0]

    def _mk(st_, name, shape, dt=F32):
        _uid[0] += 1
        return Tl(st_.enter_context(nc.sbuf_tensor("s%d_%s" % (_uid[0], name), list(shape), dt)), "s%d_%s" % (_uid[0], name))
    sb = lambda name, shape, dt=F32: _mk(stack, name, shape, dt)

    class Phase:
        def __enter__(self):
            self.st = ExitStack()
            return lambda name, shape, dt=F32: _mk(self.st, name, shape, dt)

        def __exit__(self, *a):
            self.st.close()
            P.barrier()
            return False

    cs = sb("cs", [128, CS_N])
    pm = [sb("pm%d" % l, [128, PM_N]) for l in range(DEPTH)]
    modT = [sb("modT%d" % l, [128, 2, 96]) for l in range(DEPTH)]
    hT = sb("hT", [128, 16, T], BF16)
    NWB = 4
    wbs = [sb("wb%d" % i, [128, 16 * 128], BF16) for i in range(NWB)]
    psb = [Tl(stack.enter_context(nc.psum_tensor("ps%d" % i, [128, 512], F32)), "ps%d" % i) for i in range(8)]
    NPS = 6
    wfill = [sb("wfill%d" % i, [128, 16 * 128], BF16) for i in range(2)]
    gst = [sb("gst%d" % i, [128, 512], BF16) for i in range(2)]
    gsd = nc.dram_tensor("gsd", [64, 128, T], BF16, kind="Internal").ap()
    gsd_res = [[Res() for _ in range(NBLK)] for _ in range(64)]
    st = {'wb': 0, 'ps': 0}
    obd = nc.dram_tensor("obd", [16, 128, T], BF16, kind="Internal").ap()
    obd_res = [Res() for _ in range(16)]
    mgd_res = [[Res() for _ in range(NBLK)] for _ in range(16)]
    xTd_res = [Res() for _ in range(NBLK)]

    ident = cs[:, CS['ident']:CS['ident'] + 128]
    ones = cs[:, CS['ones']:CS['ones'] + 128]
    cmat = lambda name, n=64: cs[0:n, CS[name]:CS[name] + n]
    EPSC = cs[:, CS['eps']:CS['eps'] + 1]
    ONEC = cs[:, CS['one']:CS['one'] + 1]

    def psum():
        b = psb[st['ps'] % NPS]
        st['ps'] += 1
        return b

    def dma(q, out, in_, R=(), W=(), nobarrier=False, **kw):
        P.op(q, lambda e: e.dma_start(out=out, in_=in_, **kw), R, W, dma=True, nobarrier=nobarrier)

    def load_w(src2d, kc, n, pool=None):
        t = wbs[st['wb'] % NWB]
        st['wb'] += 1
        dma('pool', t[:, 0:kc * n], src2d, W=[t.r()], nobarrier=True)
        return t, t[:, 0:kc * n].rearrange("p (k n) -> p k n", k=kc)

    def win_piece(l, name, idx):
        c0, n = PIECES[name][idx]
        wt, wv = load_w(win_d[l][:, 16 * c0:16 * c0 + 16 * n], 16, n)
        return wt, wv, n

    def proj_h(wt, wv, n, b, out_ap, psr):
        for kc in range(16):
            P.op('pe', lambda e, kc=kc: e.matmul(out_ap, wv[:, kc, :], hT[:, kc, b * 512:(b + 1) * 512],
                                                 start=(kc == 0), stop=(kc == 15)),
                 R=[wt.r(), hT.r(b)], W=[psr])


    class Filler:
        def __init__(self, l):
            self.l = l
            self.items = [(c * 4 + k_, b) for c in range(16) for k_ in range(4) for b in range(NBLK)]
            self.pos = 0
            self.n = 0
            self.wv = None
            self.wt = None

        def step(self, n=1):
            for _ in range(n):
                if self.pos >= 4 * len(self.items):
                    return
                it, sub = divmod(self.pos, 4)
                (pc, b) = self.items[it]
                c, k_ = divmod(pc, 4)
                p_ = k_ * 16 + c
                if sub == 0 and b == 0:
                    c0, n_ = PIECES['gates'][p_]
                    self.wt = wfill[(it // NBLK) % 2]
                    dma('pool', self.wt[:, :], win_d[self.l][:, 16 * c0:16 * c0 + 16 * 128], W=[self.wt.r()], nobarrier=True)
                    self.wv = self.wt[:, :].rearrange("p (k n) -> p k n", k=16)
                ps = psb[6 + it % 2]
                wv, wt = self.wv, self.wt
                for kc in range(sub * 4, sub * 4 + 4):
                    P.op('pe', lambda e: e.matmul(ps[:, :], wv[:, kc, :], hT[:, kc, b * 512:(b + 1) * 512], start=(kc == 0), stop=(kc == 15)),
                         [wt.r(), hT.r(b)], [ps.r()], nobarrier=True)
                if sub == 3:
                    g_ = gst[it % 2]
                    if it % 2 == 0:
                        P.op('act', lambda e: e.copy(g_[:, :], ps[:, :]), [ps.r()], [g_.r()], nobarrier=True)
                    else:
                        P.op('dve', lambda e: e.tensor_copy(g_[:, :], ps[:, :]), [ps.r()], [g_.r()], nobarrier=True)
                    dma('sp', gsd[p_, :, b * 512:(b + 1) * 512], g_[:, :], R=[g_.r()], W=[gsd_res[p_][b]], nobarrier=True)
                self.pos += 1

        def flush(self):
            self.step(4 * len(self.items))

    FIL = [None]

    def FILL(n=1):
        if FIL[0] is not None and stage >= 5:
            FIL[0].step(n)

    def A(fn, R, W):
        P.op('act', fn, R, W)

    def V(fn, R, W):
        P.op('dve', fn, R, W)

    def G(fn, R, W):
        P.op('pool', fn, R, W)

    def PE(fn, R, W):
        P.op('pe', fn, R, W)

    def dbg_dump(name, src_ap, R):
        if name in dbg_aps:
            dma('pool', dbg_aps[name], src_ap, R=R)

    def rstd_from(ps_ap, out_ap, psr, outr):
        A(lambda e: e.activation(out_ap, ps_ap, AF.Ln, bias=EPSC[0:out_ap.shape[0], :]), [psr, cs.r()], [outr])
        A(lambda e: e.activation(out_ap, out_ap, AF.Exp, scale=-0.5), [outr], [outr])

    dma('sp', cs[:, :], cs_d[:, :], W=[cs.r()])
    for l in range(DEPTH):
        dma('sp', pm[l][:, :], pm_d[l], W=[pm[l].r()])

    with Phase() as ph:
      if stage < 99:
          zt = ph("zt", [128, D])
          V(lambda e: e.memset(zt[:, :], 0.0), [], [zt.r()])
          for i in range(4):
              dma('sp', y_p[i * 128:(i + 1) * 128, :], zt[:, :], R=[zt.r()])
          for i in range(8):
              dma('sp', y_s[i * 128:(i + 1) * 128, :], zt[:, :], R=[zt.r()])
          for o_, dk_ in ((o_gdn, 128), (o_gla, 64), (o_hg, 128)):
              for a_ in range(2):
                  for b_ in range(DEPTH):
                      dma('sp', o_[a_, b_].rearrange("t h k v -> k (t h) v"),
                          zt[0:dk_, 0:1024].rearrange("k (th v) -> k th v", v=128), R=[zt.r()])
          for a_ in range(2):
              for b_ in range(DEPTH):
                  for i in range(2):
                      dma('sp', o_ckv[a_, b_, i * 128:(i + 1) * 128, :], zt[:, 0:512], R=[zt.r()])
                      dma('sp', o_kpe[a_, b_, i * 128:(i + 1) * 128, :], zt[:, 0:64], R=[zt.r()])

    with Phase() as ph:
        scT = ph("scT", [128, 16, 2], BF16)
        A(lambda e: e.activation(scT[:, :, :], cs[:, CS['cvT']:CS['cvT'] + 32].rearrange("p (k t) -> p k t", t=2),
                                 AF.Silu), [cs.r()], [scT.r()])
        for l in range(DEPTH):
            for g in range(24):
                ps = psum()
                for cc in range(4):
                    c = g * 4 + cc
                    wt, wv = load_w(wada_d[l][:, c * 2048:(c + 1) * 2048], 16, 128)
                    for kc in range(16):
                        PE(lambda e, wv=wv, kc=kc, ps=ps, cc=cc: e.matmul(
                            ps[:, cc * 2:cc * 2 + 2], wv[:, kc, :], scT[:, kc, :], start=(kc == 0), stop=(kc == 15)),
                            [wt.r(), scT.r()], [ps.r()])
                for kind in range(2):
                    V(lambda e, ps=ps, g=g, kind=kind, l=l: e.tensor_tensor(
                        modT[l][:, kind, g * 4:g * 4 + 4],
                        ps[:, 0:8].rearrange("p (c t) -> p c t", t=2)[:, :, kind],
                        pm[l][:, PM_COLS['b_ada'] + g * 4:PM_COLS['b_ada'] + g * 4 + 4], ALU.add),
                        [ps.r(), pm[l].r()], [modT[l].r()])
            for j in (1, 4):
                V(lambda e, l=l, j=j: e.tensor_scalar_add(
                    modT[l][:, :, j * 16:(j + 1) * 16], modT[l][:, :, j * 16:(j + 1) * 16], 1.0),
                    [modT[l].r()], [modT[l].r()])
            dbg_dump('modT%d' % l, modT[l][:, :, :].rearrange("p a b -> p (a b)"), [modT[l].r()])

    with Phase() as ph:
        xtok = [ph("xtok%d" % i, [128, D]) for i in range(2)]
        xTb = ph("xTb", [128, 16, 512])
        for tt in range(T // 128):
            xt = xtok[tt % 2]
            src = xp[tt * 128:(tt + 1) * 128, :] if tt < 4 else xs[(tt - 4) * 128:(tt - 3) * 128, :]
            dma('sp', xt[:, :], src, W=[xt.r()])
            for g in range(4):
                ps = psum()
                for cc in range(4):
                    c = g * 4 + cc
                    PE(lambda e, ps=ps, cc=cc, c=c, xt=xt: e.transpose(
                        ps[:, cc * 128:(cc + 1) * 128], xt[:, c * 128:(c + 1) * 128], ident),
                        [xt.r(), cs.r()], [ps.r()])
                dst = xTb[:, g * 4:(g + 1) * 4, (tt % 4) * 128:(tt % 4 + 1) * 128]
                srcp = ps[:, :].rearrange("p (c t) -> p c t", c=4)
                if g % 2 == 0:
                    A(lambda e, dst=dst, srcp=srcp: e.copy(dst, srcp), [ps.r()], [xTb.r(tt % 4)])
                else:
                    V(lambda e, dst=dst, srcp=srcp: e.tensor_copy(dst, srcp), [ps.r()], [xTb.r(tt % 4)])
            if tt % 4 == 3:
                b = tt // 4
                dma('sp', xTd[:, :, b * 512:(b + 1) * 512].rearrange("c p t -> p c t"), xTb[:, :, :],
                    R=[xTb.r(i) for i in range(4)], W=[xTd_res[b]])

    SEQS = [(0, LP, 0, 0), (LP, LP, 0, 1), (2 * LP, LS, 1, 2)]
    TP = T + 12
    SEGS = [(0, 0, 256, 0), (0, 256, 256, 256 + 4), (1, 0, 512, 512 + 8), (2, 0, 512, 1024 + 8)]

    def bc3(ap2d, n):
        p, f = ap2d.shape
        return ap2d.unsqueeze(1).to_broadcast([p, n, f])

    def gdn_unit(l, h, ph0, bet, gg):
      cut(0)
      with Phase() as ph:
        raw = ph("raw", [128, TP])
        cv = ph("cv", [128, 3, TP])
        sq = ph("sq", [128, TP])
        rst = ph("rst", [128, 512])
        zs = ph("zs", [128, T])
        oacc = ph("oacc", [128, T])
        V(lambda e: e.memset(raw[:, :], 0.0), [], [raw.r()])
        for i_ in range(3):
            G(lambda e, i_=i_: e.memset(cv[:, i_, :], 0.0), [], [cv.r(i_)])
        for i, nm in enumerate(('gdn_q', 'gdn_k', 'gdn_v')):
            wt, wv, n_ = win_piece(l, nm, h)
            for b in range(NBLK):
                ps = psum()
                proj_h(wt, wv, 128, b, ps[:, :], ps.r())
                for (sb_, so, sn, cd) in SEGS:
                    if sb_ != b:
                        continue
                    A(lambda e, ps=ps, so=so, sn=sn, cd=cd: e.copy(raw[:, cd + 2:cd + 2 + sn], ps[:, so:so + sn]),
                      [ps.r()], [raw.r()])
            acc = cv[:, i, 2:TP - 2]
            ccol = lambda tap: pm[l][:, PM_COLS['conv'] + (i * 4 + h) * 5 + tap:PM_COLS['conv'] + (i * 4 + h) * 5 + tap + 1]
            V(lambda e, acc=acc, ccol=ccol: e.tensor_scalar(acc, raw[:, 0:TP - 4], ccol(0), None, ALU.mult),
              [raw.r(), pm[l].r()], [cv.r(i)])
            for tap in range(1, 5):
                V(lambda e, acc=acc, ccol=ccol, tap=tap: e.scalar_tensor_tensor(
                    acc, raw[:, tap:TP - 4 + tap], ccol(tap), acc, ALU.mult, ALU.add),
                    [raw.r(), pm[l].r(), cv.r(i)], [cv.r(i)])
            A(lambda e, acc=acc: e.activation(acc, acc, AF.Silu), [cv.r(i)], [cv.r(i)])
            if i < 2:
                A(lambda e, acc=acc: e.activation(sq[:, 2:TP - 2], acc, AF.Square), [cv.r(i)], [sq.r()])
                for (sb_, so, sn, cd) in SEGS:
                    ps = psum()
                    PE(lambda e, ps=ps, sn=sn, cd=cd: e.matmul(ps[:, 0:sn], ones, sq[:, cd + 2:cd + 2 + sn],
                                                               start=True, stop=True), [sq.r(), cs.r()], [ps.r()])
                    rstd_from(ps[:, 0:sn], rst[:, 0:sn], ps.r(), rst.r())
                    sc_ = (128.0 ** -0.5) if i == 0 else 1.0
                    V(lambda e, i=i, sn=sn, cd=cd, sc_=sc_: e.scalar_tensor_tensor(
                        cv[:, i, cd + 2:cd + 2 + sn], cv[:, i, cd + 2:cd + 2 + sn], sc_, rst[:, 0:sn], ALU.mult, ALU.mult),
                        [cv.r(i), rst.r()], [cv.r(i)])
        wt, wv, n_ = win_piece(l, 'gdn_z', h)
        for b in range(NBLK):
            ps = psum()
            proj_h(wt, wv, 128, b, ps[:, :], ps.r())
            A(lambda e, ps=ps, b=b: e.activation(zs[:, b * 512:(b + 1) * 512], ps[:, :], AF.Silu), [ps.r()], [zs.r()])
        if h == 0 and l == 0:
            dbg_dump('cv', cv[:, :, :].rearrange("p a b -> p (a b)"), [cv.r(i) for i in range(3)])
        cut(1)
        for (tok0, L, kind, idx) in SEQS:
          with Phase() as ps_:
            nch = min(8, L // 64)
            nbatch = (L // 64) // nch
            geo = {}
            QT = lambda n: cv[:, 0, geo['cb'] + n * 64:geo['cb'] + (n + 1) * 64]
            KT = lambda n: cv[:, 1, geo['cb'] + n * 64:geo['cb'] + (n + 1) * 64]
            VT = lambda n: cv[:, 2, geo['cb'] + n * 64:geo['cb'] + (n + 1) * 64]
            ktok = ps_("ktok", [64, nch, 128])
            vtok = ps_("vtok", [64, nch, 128])
            R2 = ps_("R2", [64, nch, 64])
            eg = ps_("eg", [128, nch, 64])
            rb = ps_("rb", [64, nch, 64])
            gc = ps_("gc", [64, nch])
            gl = ps_("gl", [128, nch])
            cdec = ps_("cdec", [128, nch])
            kdsc = ps_("kdsc", [64, nch])
            bw = ps_("bw", [64, nch])
            X = ps_("X", [64, nch, 64])
            dT = ps_("dT", [64, nch, 64])
            dd = ps_("dd", [64, nch, 64])
            M = ps_("M", [64, nch, 64])
            MT = ps_("MT", [64, nch, 64])
            Pa = ps_("Pa", [64, nch, 64])
            PTa = ps_("PTa", [64, nch, 64])
            RT = ps_("RT", [64, nch, 64])
            AT = ps_("AT", [128, nch, 64])
            vb = ps_("vb", [64, nch, 128])
            kw = ps_("kw", [64, nch, 128])
            ub = ps_("ub", [64, nch, 128])
            wT = ps_("wT", [128, nch, 64])
            kdec = ps_("kdec", [64, nch, 128])
            qdT = ps_("qdT", [128, nch, 64])
            S = ps_("S", [128, 128])
            u = ps_("u", [128, 128])
            V(lambda e: e.memset(AT[:, :, :], 0.0), [], [AT.r()])
            V(lambda e: e.memset(u[:, :], 0.0), [], [u.r()])
            for dr in DIRS:
                PASSNO[0] += 1
                U = cmat('le') if dr == 0 else cmat('ge')
                inclT = U
                strict = cmat('gt') if dr == 0 else cmat('lt')
                col = dr * 4 + h
                last = 63 if dr == 0 else 0
                gcolv = lambda n: gg[:, geo['n0'] + n, col:col + 1]
                bcolv = lambda n: bet[:, geo['n0'] + n, col:col + 1]
                if kind == 0:
                    V(lambda e: e.memset(S[:, :], 0.0), [], [S.r()])
                else:
                    dma('sp', S[:, :], st_gdn[l, dr, h], W=[S.r()])
                border = range(nbatch) if dr == 0 else range(nbatch - 1, -1, -1)
                for bi in border:
                    tokb = tok0 + bi * nch * 64
                    n0 = tokb // 64
                    cb = tokb + 4 * idx + 2
                    geo['cb'] = cb
                    geo['n0'] = n0
                    for (src, dstt) in ((KT, ktok), (VT, vtok)):
                        for n4 in range(0, nch, 4):
                            ps = psum()
                            for n in range(n4, n4 + 4):
                                PE(lambda e, ps=ps, n=n, n4=n4, src=src: e.transpose(
                                    ps[0:64, (n - n4) * 128:(n - n4 + 1) * 128], src(n), ident),
                                    [cv.r(1), cv.r(2), cs.r()], [ps.r()])
                            A(lambda e, ps=ps, n4=n4, dstt=dstt: e.copy(
                                dstt[:, n4:n4 + 4, :], ps[0:64, :].rearrange("p (n d) -> p n d", n=4)), [ps.r()], [dstt.r()])
                    cut(2)
                    V(lambda e: e.tensor_tensor(R2[:, :, :], bc3(U, nch), bcl(gg[:, n0:n0 + nch, col:col + 1], 64), ALU.mult),
                      [gg.r(), cs.r()], [R2.r()])
                    for n8 in range(0, nch, 8):
                        w8 = min(8, nch - n8)
                        ps = psum()
                        PE(lambda e, ps=ps, n8=n8, w8=w8: e.matmul(
                            ps[:, 0:w8 * 64], ones[0:64, :], R2[:, n8:n8 + w8, :].rearrange("p n j -> p (n j)"),
                            start=True, stop=True), [R2.r(), cs.r()], [ps.r()])
                        A(lambda e, ps=ps, n8=n8, w8=w8: e.activation(
                            eg[:, n8:n8 + w8, :].rearrange("p n j -> p (n j)"), ps[:, 0:w8 * 64], AF.Exp), [ps.r()], [eg.r()])
                        V(lambda e, ps=ps, n8=n8, w8=w8: e.tensor_copy(
                            rb[:, n8:n8 + w8, :].rearrange("p n j -> p (n j)"), ps[0:64, 0:w8 * 64]), [ps.r()], [rb.r()])
                        V(lambda e, ps=ps, n8=n8, w8=w8: e.tensor_copy(
                            gl[:, n8:n8 + w8], ps[:, 0:w8 * 64].rearrange("p (n j) -> p n j", j=64)[:, :, last]), [ps.r()], [gl.r()])
                    cut(2.2)
                    ps = psum()
                    PE(lambda e, ps=ps, U=U: e.matmul(ps[0:64, 0:nch], U, gg[:, n0:n0 + nch, col], start=True, stop=True),
                       [gg.r(), cs.r()], [ps.r()])
                    A(lambda e, ps=ps: e.copy(gc[:, :], ps[0:64, 0:nch]), [ps.r()], [gc.r()])
                    cut(2.5)
                    A(lambda e: e.activation(cdec[:, :], gl[:, :], AF.Exp), [gl.r()], [cdec.r()])
                    V(lambda e: e.tensor_tensor(kdsc[:, :], gl[0:64, :], gc[:, :], ALU.subtract), [gl.r(), gc.r()], [kdsc.r()])
                    A(lambda e: e.activation(kdsc[:, :], kdsc[:, :], AF.Exp), [kdsc.r()], [kdsc.r()])
                    A(lambda e: e.activation(bw[:, :], gc[:, :], AF.Exp), [gc.r()], [bw.r()])
                    V(lambda e: e.tensor_tensor(bw[:, :], bw[:, :], bet[:, n0:n0 + nch, col], ALU.mult), [bw.r(), bet.r()], [bw.r()])
                    cut(2.7)
                    V(lambda e: e.tensor_tensor(X[:, :, :], rb[:, :, :], bcl(gc[:, :].unsqueeze(2), 64), ALU.subtract), [rb.r(), gc.r()], [X.r()])
                    V(lambda e: e.tensor_scalar(dd[:, :, :], X[:, :, :], 0.0, None, ALU.max), [X.r()], [dd.r()])
                    V(lambda e: e.tensor_scalar(X[:, :, :], X[:, :, :], 0.0, None, ALU.min), [X.r()], [X.r()])
                    cut(3)
                    A(lambda e: e.activation(dT[:, :, :], X[:, :, :], AF.Exp), [X.r()], [dT.r()])
                    A(lambda e: e.activation(dd[:, :, :], dd[:, :, :], AF.Exp, scale=-1.0), [dd.r()], [dd.r()])
                    V(lambda e, inclT=inclT: e.tensor_tensor(dT[:, :, :], dT[:, :, :], bc3(inclT, nch), ALU.mult),
                      [dT.r(), cs.r()], [dT.r()])
                    V(lambda e, strict=strict: e.tensor_tensor(dd[:, :, :], dd[:, :, :], bc3(strict, nch), ALU.mult),
                      [dd.r(), cs.r()], [dd.r()])
                    cut(4)
                    for n8 in range(0, nch, 8):
                        w8 = min(8, nch - n8)
                        psA = psum()
                        psQ = psum()
                        for n in range(n8, n8 + w8):
                            PE(lambda e, n=n, n8=n8, psA=psA: e.matmul(psA[0:64, (n - n8) * 64:(n - n8 + 1) * 64], KT(n), KT(n),
                                                                        start=True, stop=True), [cv.r(1)], [psA.r()])
                            PE(lambda e, n=n, n8=n8, psQ=psQ: e.matmul(psQ[0:64, (n - n8) * 64:(n - n8 + 1) * 64], KT(n), QT(n),
                                                                        start=True, stop=True), [cv.r(1), cv.r(0)], [psQ.r()])
                        V(lambda e: e.tensor_tensor(M[:, n8:n8 + w8, :].rearrange("p n j -> p (n j)"), psA[0:64, 0:w8 * 64],
                                                    dd[:, n8:n8 + w8, :].rearrange("p n j -> p (n j)"), ALU.mult), [psA.r(), dd.r()], [M.r()])
                        V(lambda e: e.tensor_tensor(M[:, n8:n8 + w8, :], M[:, n8:n8 + w8, :],
                                                    bcl(bet[:, n0 + n8:n0 + n8 + w8, col:col + 1], 64), ALU.mult), [M.r(), bet.r()], [M.r()])
                        V(lambda e, n8=n8, w8=w8, psQ=psQ: e.tensor_tensor(
                            AT[0:64, n8:n8 + w8, :].rearrange("p n j -> p (n j)"), psQ[0:64, 0:w8 * 64],
                            dT[:, n8:n8 + w8, :].rearrange("p n j -> p (n j)"), ALU.mult), [psQ.r(), dT.r()], [AT.r()])
                    cut(5)
                    for n8 in range(0, nch, 8):
                        w8 = min(8, nch - n8)
                        ps = psum()
                        for n in range(n8, n8 + w8):
                            PE(lambda e, n=n, n8=n8, ps=ps: e.transpose(ps[0:64, (n - n8) * 64:(n - n8 + 1) * 64], M[:, n, :], ident[0:64, 0:64]),
                               [M.r(), cs.r()], [ps.r()])
                        A(lambda e, n8=n8, w8=w8, ps=ps: e.copy(MT[:, n8:n8 + w8, :].rearrange("p n j -> p (n j)"), ps[0:64, 0:w8 * 64]),
                          [ps.r()], [MT.r()])
                    V(lambda e: e.tensor_tensor(RT[:, :, :], bc3(ident[0:64, 0:64], nch), MT[:, :, :], ALU.subtract),
                      [MT.r(), cs.r()], [RT.r()])
                    Pc, PTc = M, MT
                    Pn, PTn = Pa, PTa
                    for it in range(5):
                        FILL(2)
                        for n8 in range(0, nch, 8):
                            w8 = min(8, nch - n8)
                            p1 = psum()
                            p2 = psum()
                            for n in range(n8, n8 + w8):
                                sl = slice((n - n8) * 64, (n - n8 + 1) * 64)
                                PE(lambda e, n=n, sl=sl, p1=p1, Pc=Pc, PTc=PTc: e.matmul(p1[0:64, sl], PTc[:, n, :], Pc[:, n, :], start=True, stop=True),
                                   [Pc.r(), PTc.r()], [p1.r()])
                                PE(lambda e, n=n, sl=sl, p2=p2, Pc=Pc, PTc=PTc: e.matmul(p2[0:64, sl], Pc[:, n, :], PTc[:, n, :], start=True, stop=True),
                                   [Pc.r(), PTc.r()], [p2.r()])
                            A(lambda e, n8=n8, w8=w8, p1=p1, Pn=Pn: e.copy(Pn[:, n8:n8 + w8, :].rearrange("p n j -> p (n j)"), p1[0:64, 0:w8 * 64]),
                              [p1.r()], [Pn.r()])
                            V(lambda e, n8=n8, w8=w8, p2=p2, PTn=PTn: e.tensor_copy(PTn[:, n8:n8 + w8, :].rearrange("p n j -> p (n j)"), p2[0:64, 0:w8 * 64]),
                              [p2.r()], [PTn.r()])
                        for n8 in range(0, nch, 8):
                            w8 = min(8, nch - n8)
                            p3 = psum()
                            for n in range(n8, n8 + w8):
                                sl = slice((n - n8) * 64, (n - n8 + 1) * 64)
                                PE(lambda e, n=n, sl=sl, p3=p3, Pn=Pn: e.matmul(p3[0:64, sl], Pn[:, n, :], RT[:, n, :], start=True, stop=True),
                                   [Pn.r(), RT.r()], [p3.r()])
                            V(lambda e, n8=n8, w8=w8, p3=p3: e.tensor_tensor(
                                RT[:, n8:n8 + w8, :].rearrange("p n j -> p (n j)"), RT[:, n8:n8 + w8, :].rearrange("p n j -> p (n j)"),
                                p3[0:64, 0:w8 * 64], ALU.add), [p3.r(), RT.r()], [RT.r()])
                        Pc, PTc, Pn, PTn = Pn, PTn, Pc, PTc
                    cut(6)
                    V(lambda e: e.tensor_tensor(vb[:, :, :], vtok[:, :, :], bcl(bet[:, n0:n0 + nch, col:col + 1], 128), ALU.mult),
                      [vtok.r(), bet.r()], [vb.r()])
                    G(lambda e: e.tensor_tensor(kw[:, :, :], ktok[:, :, :], bcl(bw[:, :].unsqueeze(2), 128), ALU.mult), [ktok.r(), bw.r()], [kw.r()])
                    G(lambda e: e.tensor_tensor(kdec[:, :, :], ktok[:, :, :], bcl(kdsc[:, :].unsqueeze(2), 128), ALU.mult), [ktok.r(), kdsc.r()], [kdec.r()])
                    for n4 in range(0, nch, 4):
                        ps = psum()
                        for n in range(n4, n4 + 4):
                            PE(lambda e, n=n, n4=n4, ps=ps: e.matmul(ps[0:64, (n - n4) * 128:(n - n4 + 1) * 128], RT[:, n, :], vb[:, n, :],
                                                                       start=True, stop=True), [RT.r(), vb.r()], [ps.r()])
                        A(lambda e, n4=n4, ps=ps: e.copy(ub[:, n4:n4 + 4, :].rearrange("p n d -> p (n d)"), ps[0:64, :]), [ps.r()], [ub.r()])
                    for n8 in range(0, nch, 8):
                        w8 = min(8, nch - n8)
                        ps = psum()
                        for n in range(n8, n8 + w8):
                            PE(lambda e, n=n, n8=n8, ps=ps: e.matmul(ps[:, (n - n8) * 64:(n - n8 + 1) * 64], kw[:, n, :], RT[:, n, :],
                                                                       start=True, stop=True), [RT.r(), kw.r()], [ps.r()])
                        V(lambda e, n8=n8, w8=w8, ps=ps: e.tensor_copy(wT[:, n8:n8 + w8, :].rearrange("p n j -> p (n j)"), ps[:, 0:w8 * 64]),
                          [ps.r()], [wT.r()])
                    V(lambda e: e.tensor_tensor(qdT[:, :, :].rearrange("p n j -> p (n j)"), cv[:, 0, cb:cb + nch * 64],
                                                eg[:, :, :].rearrange("p n j -> p (n j)"), ALU.mult), [cv.r(0), eg.r()], [qdT.r()])
                    cut(7)
                    order = range(nch) if dr == 0 else range(nch - 1, -1, -1)
                    for n in order:
                        FILL(1)
                        pu = psum()
                        PE(lambda e, n=n, pu=pu: e.matmul(pu[0:64, 0:128], wT[:, n, :], S[:, :], start=True, stop=True), [wT.r(), S.r()], [pu.r()])
                        V(lambda e, n=n, pu=pu: e.tensor_tensor(u[0:64, :], ub[:, n, :], pu[0:64, 0:128], ALU.subtract), [ub.r(), pu.r()], [u.r()])
                        cut(8)
                        po = psum()
                        PE(lambda e, n=n, po=po: e.matmul(po[:, 0:64], S[:, :], qdT[:, n, :], start=True, stop=False), [S.r(), qdT.r()], [po.r()])
                        PE(lambda e, n=n, po=po: e.matmul(po[:, 0:64], u[:, :], AT[:, n, :], start=False, stop=True), [u.r(), AT.r()], [po.r()])
                        osl = oacc[:, tokb + n * 64:tokb + (n + 1) * 64]
                        if dr == 0:
                            A(lambda e, po=po, osl=osl: e.copy(osl, po[:, 0:64]), [po.r()], [oacc.r(idx)])
                        else:
                            V(lambda e, po=po, osl=osl: e.tensor_tensor(osl, osl, po[:, 0:64], ALU.add), [po.r(), oacc.r(idx)], [oacc.r(idx)])
                        cut(9)
                        pS = psum()
                        PE(lambda e, n=n, pS=pS: e.matmul(pS[:, 0:128], kdec[:, n, :], u[0:64, :], start=True, stop=True), [kdec.r(), u.r()], [pS.r()])
                        V(lambda e, n=n, pS=pS: e.scalar_tensor_tensor(S[:, :], S[:, :], cdec[:, n:n + 1], pS[:, 0:128], ALU.mult, ALU.add),
                          [S.r(), cdec.r(), pS.r()], [S.r()])
                cut(10)
                if kind == 0:
                    dma('sp', o_gdn[idx, l, dr, h], S[:, :], R=[S.r()])
                cut(11 + idx * 2 + dr)
        if h == 0 and l == 0:
            dbg_dump('oacc', oacc[:, :], [oacc.r(i) for i in range(3)])
        ob = ph("ob", [128, T], BF16)
        A(lambda e: e.activation(sq[:, 0:T], oacc[:, :], AF.Square), [oacc.r(i) for i in range(3)], [sq.r()])
        for b in range(NBLK):
            ps = psum()
            PE(lambda e, ps=ps, b=b: e.matmul(ps[:, :], cs[:, CS['mean128']:CS['mean128'] + 128], sq[:, b * 512:(b + 1) * 512],
                                              start=True, stop=True), [sq.r(), cs.r()], [ps.r()])
            rstd_from(ps[:, :], rst[:, :], ps.r(), rst.r())
            V(lambda e, b=b: e.tensor_tensor(oacc[:, b * 512:(b + 1) * 512], oacc[:, b * 512:(b + 1) * 512], rst[:, :], ALU.mult),
              [rst.r()] + [oacc.r(i) for i in range(3)], [oacc.r(i) for i in range(3)])
            V(lambda e, b=b: e.scalar_tensor_tensor(ob[:, b * 512:(b + 1) * 512], oacc[:, b * 512:(b + 1) * 512],
                                                    pm[l][:, PM_COLS['gdn_norm']:PM_COLS['gdn_norm'] + 1], zs[:, b * 512:(b + 1) * 512],
                                                    ALU.mult, ALU.mult), [zs.r(), pm[l].r()] + [oacc.r(i) for i in range(3)], [ob.r()])
        if stage > 4:
            dma('sp', obd[0 * 4 + h], ob[:, :], R=[ob.r()], W=[obd_res[0 * 4 + h]])
        if h == 0 and l == 0:
            dbg_dump('ob', ob[:, :], [ob.r()])

    def bcl(ap3, c):
        p, n, _ = ap3.shape
        return ap3.to_broadcast([p, n, c])

    CH = 32

    def gla_unit(l, h, mixer, shared):
      PK = 64 if mixer == 'gla' else 128
      bi_ = 1 if mixer == 'gla' else 2
      rmask = shared['rmask']
      with Phase() as ph:
        qT = ph("qT", [PK, T])
        kTs = [ph("kT0", [PK, T])] if mixer == 'gla' else [ph("kT0", [PK, T]), ph("kT1", [PK, T])]
        vT = ph("vT", [128, T])
        las = [ph("la0", [PK, T]), ph("la1", [PK, T])]
        gate = ph("gate", [128, T], BF16)
        oacc = ph("oacc", [128, T])
        bsc = ph("bsc", [PK, T])
        qd = ph("qd", [PK, T])
        kt = ph("kt", [PK, T])
        kd = ph("kd", [PK, T])
        tE = ph("tE", [PK, T])
        xs = ph("xs", [128, 512])
        t1 = ph("t1", [128, 512])

        def inproj(name, idx, fn):
            wt, wv, n_ = win_piece(l, name, idx)
            for b in range(NBLK):
                ps = psum()
                proj_h(wt, wv, n_, b, ps[0:n_, :], ps.r())
                fn(b, ps, n_)
        sl = lambda b: slice(b * 512, (b + 1) * 512)
        if mixer == 'gla':
            inproj('gla_q', h, lambda b, ps, n_: A(lambda e: e.copy(qT[:, sl(b)], ps[0:64, :]), [ps.r()], [qT.r()]))
            inproj('gla_k', h, lambda b, ps, n_: V(lambda e: e.tensor_copy(kTs[0][:, sl(b)], ps[0:64, :]), [ps.r()], [kTs[0].r()]))
            inproj('gla_v', h, lambda b, ps, n_: A(lambda e: e.copy(vT[:, sl(b)], ps[:, :]), [ps.r()], [vT.r()]))
            inproj('gla_r', h, lambda b, ps, n_: A(lambda e: e.activation(gate[:, sl(b)], ps[:, :], AF.Silu), [ps.r()], [gate.r()]))
            glr, gw2 = shared['glr'], shared['gw2']
            for dr in range(2):
                for b in range(NBLK):
                    ps = psum()
                    PE(lambda e: e.matmul(ps[0:64, :], gw2[:, dr * 256 + h * 64:dr * 256 + (h + 1) * 64], glr[dr][:, sl(b)],
                                          start=True, stop=True), [gw2.r(), glr[dr].r()], [ps.r()])
                    gb = pm[l][0:64, PM_COLS['gla_gb'] + dr * 4 + h:PM_COLS['gla_gb'] + dr * 4 + h + 1]
                    A(lambda e: e.activation(xs[0:64, :], ps[0:64, :], AF.Identity, bias=gb), [ps.r(), pm[l].r()], [xs.r()])
                    A(lambda e: e.activation(t1[0:64, :], xs[0:64, :], AF.Abs), [xs.r()], [t1.r()])
                    A(lambda e: e.activation(t1[0:64, :], t1[0:64, :], AF.Exp, scale=-1.0), [t1.r()], [t1.r()])
                    A(lambda e: e.activation(t1[0:64, :], t1[0:64, :], AF.Ln, bias=ONEC[0:64, :]), [t1.r(), cs.r()], [t1.r()])
                    V(lambda e: e.scalar_tensor_tensor(xs[0:64, :], xs[0:64, :], 0.0, t1[0:64, :], ALU.min, ALU.subtract),
                      [xs.r(), t1.r()], [xs.r()])
                    V(lambda e: e.tensor_scalar(las[dr][:, sl(b)], xs[0:64, :], 1.0 / 16.0, None, ALU.mult), [xs.r()], [las[dr].r()])
        else:
            inproj('hg_q', h, lambda b, ps, n_: A(lambda e: e.activation(qT[:, sl(b)], ps[:, :], AF.Silu), [ps.r()], [qT.r()]))
            inproj('hg_i', h, lambda b, ps, n_: A(lambda e: e.copy(vT[:, sl(b)], ps[:, :]), [ps.r()], [vT.r()]))
            inproj('hg_g', h, lambda b, ps, n_: A(lambda e: e.activation(gate[:, sl(b)], ps[:, :], AF.Sigmoid), [ps.r()], [gate.r()]))
            lbt, oml = shared['lbt'], shared['oml']
            for dr in range(2):
                j = dr * 4 + h

                def fz(b, ps, n_, dr=dr, j=j):
                    A(lambda e: e.activation(xs[:, :], ps[:, :], AF.Sigmoid), [ps.r()], [xs.r()])
                    V(lambda e: e.tensor_scalar(xs[:, :], xs[:, :], oml[:, j:j + 1], lbt[:, j:j + 1], ALU.mult, ALU.add),
                      [xs.r(), oml.r(), lbt.r()], [xs.r()])
                    A(lambda e: e.activation(las[dr][:, sl(b)], xs[:, :], AF.Ln), [xs.r()], [las[dr].r()])
                    V(lambda e: e.tensor_scalar(kTs[dr][:, sl(b)], xs[:, :], -1.0, 1.0, ALU.mult, ALU.add), [xs.r()], [kTs[dr].r()])
                inproj('hg_f', j, fz)
        NB = 8
        for dr in range(2):
            kT = kTs[min(dr, len(kTs) - 1)]
            la = las[dr]
            V(lambda e: e.tensor_tensor_scan(bsc[:, :], rmask[0:PK, :], la[:, :], 0.0, ALU.mult, ALU.add),
              [rmask.r(), la.r()], [bsc.r()])
            b3 = bsc[:, :].rearrange("p (n c) -> p n c", c=CH)
            tot3 = b3[:, :, CH - 1:CH]
            if dr == 1:
                V(lambda e: e.tensor_tensor(tE[:, :], la[:, :], bsc[:, :], ALU.subtract), [la.r(), bsc.r()], [tE.r()])
                V(lambda e: e.tensor_tensor(tE[:, :].rearrange("p (n c) -> p n c", c=CH),
                                            tE[:, :].rearrange("p (n c) -> p n c", c=CH), bcl(tot3, CH), ALU.add),
                  [tE.r(), bsc.r()], [tE.r()])
                bcur = tE
            else:
                bcur = bsc
            V(lambda e: e.tensor_tensor(kd[:, :].rearrange("p (n c) -> p n c", c=CH), bcl(tot3, CH),
                                        bcur[:, :].rearrange("p (n c) -> p n c", c=CH), ALU.subtract),
              [bsc.r(), bcur.r()], [kd.r()])
            A(lambda e: e.activation(kd[:, :], kd[:, :], AF.Exp), [kd.r()], [kd.r()])
            V(lambda e: e.tensor_tensor(kd[:, :], kd[:, :], kT[:, :], ALU.mult), [kd.r(), kT.r()], [kd.r()])
            cdec = ph("cdec%d" % dr, [PK, T // CH])
            A(lambda e: e.activation(cdec[:, :], tot3.rearrange("p n c -> p (n c)"), AF.Exp), [bsc.r()], [cdec.r()])
            A(lambda e: e.activation(qd[:, :], bcur[:, :], AF.Exp), [bcur.r()], [qd.r()])
            A(lambda e: e.activation(kt[:, :], bcur[:, :], AF.Exp, scale=-1.0), [bcur.r()], [kt.r()])
            qs_ = (64.0 ** -0.5) if mixer == 'gla' else 1.0
            V(lambda e: e.scalar_tensor_tensor(qd[:, :], qT[:, :], qs_, qd[:, :], ALU.mult, ALU.mult), [qT.r(), qd.r()], [qd.r()])
            G(lambda e: e.tensor_tensor(kt[:, :], kt[:, :], kT[:, :], ALU.mult), [kt.r(), kT.r()], [kt.r()])
            maskT = cmat('le32', 32) if dr == 0 else cmat('ge32', 32)
            if dr == 0:
                nb = NB
                vtok = ph("vtok", [CH, nb, 128])
                kdtok = ph("kdtok", [CH, nb, PK])
                ATm = ph("ATm", [CH, nb, CH])
                KV = ph("KV", [PK, nb, 128])
                Sall = ph("Sall", [PK, nb + 1, 128])
                otmp = ph("otmp", [128, nb * CH])
            for (tok0, L, kind, idx) in SEQS:
              if True:
                nbatch = (L // CH) // nb
                if kind == 0:
                    V(lambda e: e.memset(Sall[:, 0, :], 0.0), [], [Sall.r()])
                else:
                    st_in = st_gla if mixer == 'gla' else st_hg
                    dma('sp', Sall[:, 0, :], st_in[l, dr, h], W=[Sall.r()])
                border = range(nbatch) if dr == 0 else range(nbatch - 1, -1, -1)
                for bix, bi in enumerate(border):
                    tokb = tok0 + bi * nb * CH
                    c0 = tokb // CH
                    csl = lambda n: slice(tokb + n * CH, tokb + (n + 1) * CH)
                    if bix > 0:
                        V(lambda e: e.tensor_copy(Sall[:, 0, :], Sall[:, nb, :]), [Sall.r()], [Sall.r()])
                    for (srcT, dstt, pw) in ((vT, vtok, 128), (kd, kdtok, PK)):
                        per = 512 // pw
                        for n4 in range(0, nb, per):
                            ps = psum()
                            for n in range(n4, min(nb, n4 + per)):
                                PE(lambda e, n=n: e.transpose(ps[0:CH, (n - n4) * pw:(n - n4 + 1) * pw], srcT[:, csl(n)], ident[0:pw, 0:pw]),
                                   [srcT.r(), cs.r()], [ps.r()])
                            w_ = min(nb, n4 + per) - n4
                            A(lambda e: e.copy(dstt[:, n4:n4 + w_, :].rearrange("p n d -> p (n d)"), ps[0:CH, 0:w_ * pw]),
                              [ps.r()], [dstt.r()])
                    FILL(2)
                    ps = psum()
                    for n in range(nb):
                        PE(lambda e, n=n: e.matmul(ps[0:CH, n * CH:(n + 1) * CH], kt[:, csl(n)], qd[:, csl(n)], start=True, stop=True),
                           [kt.r(), qd.r()], [ps.r()])
                    V(lambda e: e.tensor_tensor(ATm[:, :, :], ps[0:CH, 0:nb * CH].rearrange("p (n c) -> p n c", c=CH),
                                                bc3(maskT, nb), ALU.mult), [ps.r(), cs.r()], [ATm.r()])
                    FILL(2)
                    for n4 in range(0, nb, 4):
                        ps = psum()
                        for n in range(n4, n4 + 4):
                            PE(lambda e, n=n: e.matmul(ps[0:PK, (n - n4) * 128:(n - n4 + 1) * 128], kdtok[:, n, :], vtok[:, n, :],
                                                       start=True, stop=True), [kdtok.r(), vtok.r()], [ps.r()])
                        A(lambda e: e.copy(KV[:, n4:n4 + 4, :].rearrange("p n d -> p (n d)"), ps[0:PK, :]), [ps.r()], [KV.r()])
                    order = list(range(nb)) if dr == 0 else list(range(nb - 1, -1, -1))
                    for s_i, n in enumerate(order):
                        V(lambda e, s_i=s_i, n=n: e.scalar_tensor_tensor(Sall[:, s_i + 1, :], Sall[:, s_i, :], cdec[:, c0 + n:c0 + n + 1],
                                                                         KV[:, n, :], ALU.mult, ALU.add), [Sall.r(), cdec.r(), KV.r()], [Sall.r()])
                    FILL(3)
                    pA = psum()
                    pB = psum()
                    for s_i, n in enumerate(order):
                        PE(lambda e, s_i=s_i, n=n: e.matmul(pA[:, n * CH:(n + 1) * CH], Sall[:, s_i, :], qd[:, csl(n)], start=True, stop=True),
                           [Sall.r(), qd.r()], [pA.r()])
                        PE(lambda e, n=n: e.matmul(pB[:, n * CH:(n + 1) * CH], vtok[:, n, :], ATm[:, n, :], start=True, stop=True),
                           [vtok.r(), ATm.r()], [pB.r()])
                    A(lambda e: e.copy(otmp[:, :], pA[:, 0:nb * CH]), [pA.r()], [otmp.r()])
                    osl = oacc[:, tokb:tokb + nb * CH]
                    if dr == 0:
                        V(lambda e: e.tensor_tensor(osl, otmp[:, :], pB[:, 0:nb * CH], ALU.add), [otmp.r(), pB.r()], [oacc.r(idx)])
                    else:
                        V(lambda e: e.tensor_tensor(otmp[:, :], otmp[:, :], pB[:, 0:nb * CH], ALU.add), [otmp.r(), pB.r()], [otmp.r()])
                        V(lambda e: e.tensor_tensor(osl, osl, otmp[:, :], ALU.add), [otmp.r(), oacc.r(idx)], [oacc.r(idx)])
                if kind == 0:
                    o_st = o_gla if mixer == 'gla' else o_hg
                    dma('sp', o_st[idx, l, dr, h], Sall[:, nb, :], R=[Sall.r()])
        ob = ph("ob", [128, T], BF16)
        normc = pm[l][:, PM_COLS['gla_norm' if mixer == 'gla' else 'hg_norm']:PM_COLS['gla_norm' if mixer == 'gla' else 'hg_norm'] + 1]
        for b in range(NBLK):
            A(lambda e: e.activation(xs[:, :], oacc[:, sl(b)], AF.Square), [oacc.r(i) for i in range(3)], [xs.r()])
            ps = psum()
            PE(lambda e: e.matmul(ps[:, :], cs[:, CS['mean128']:CS['mean128'] + 128], xs[:, :], start=True, stop=True),
               [xs.r(), cs.r()], [ps.r()])
            rstd_from(ps[:, :], t1[:, :], ps.r(), t1.r())
            V(lambda e: e.tensor_tensor(xs[:, :], oacc[:, sl(b)], t1[:, :], ALU.mult), [t1.r()] + [oacc.r(i) for i in range(3)], [xs.r()])
            V(lambda e: e.scalar_tensor_tensor(ob[:, sl(b)], xs[:, :], normc, gate[:, sl(b)], ALU.mult, ALU.mult),
              [xs.r(), gate.r(), pm[l].r()], [ob.r()])
        k_ = bi_ * 4 + h
        if stage > 4:
            dma('sp', obd[k_], ob[:, :], R=[ob.r()], W=[obd_res[k_]])
        if h == 0 and l == 0:
            dbg_dump('ob_' + mixer, ob[:, :], [ob.r()])

    NKEY = T + PAST
    KEYR = [(0, LP), (LP, LP), (2 * LP, LS + PAST)]
    SCALE = (128 + 64) ** -0.5

    def mla_layer(l):
      with Phase() as ph:
        qn = ph("qn", [128, 4, T], BF16)
        ckvb = ph("ckvb", [128, 4, NKEY], BF16)
        kpT = ph("kpT", [128, NKEY], BF16)
        ropet = ph("ropet", [64, 2, LS])
        dma('sp', ropet[:, :, :], rope_d[:, :].rearrange("p (a t) -> p a t", a=2), W=[ropet.r()])
        G(lambda e: e.memset(kpT[:, :], 0.0), [], [kpT.r()])

        def rope(dst_bf, src, ncol0, tmpa, tmpb):
            for hb in range(2):
                c = slice(hb * 512, (hb + 1) * 512)
                ps = psum()
                PE(lambda e: e.matmul(ps[0:64, :], cmat('perm'), src[0:64, c], start=True, stop=True), [src.r(), cs.r()], [ps.r()])
                V(lambda e: e.tensor_tensor(tmpa[0:64, :], src[0:64, c], ropet[:, 0, c], ALU.mult), [src.r(), ropet.r()], [tmpa.r()])
                V(lambda e: e.tensor_tensor(tmpb[0:64, :], ps[0:64, :], ropet[:, 1, c], ALU.mult), [ps.r(), ropet.r()], [tmpb.r()])
                V(lambda e: e.tensor_tensor(dst_bf[0:64, ncol0 + hb * 512:ncol0 + (hb + 1) * 512], tmpa[0:64, :], tmpb[0:64, :], ALU.add),
                  [tmpa.r(), tmpb.r()], [dst_bf.r()])
        sl = lambda b: slice(b * 512, (b + 1) * 512)
        with Phase() as pa:
            qa = pa("qa", [128, 4, T])
            kva = pa("kva", [128, 4, T])
            kpe = pa("kpe", [64, T])
            sq = pa("sq", [128, 512])
            rst = pa("rst", [128, 512])
            tmpa = pa("tmpa", [128, 512])
            tmpb = pa("tmpb", [128, 512])
            tokt = pa("tokt", [128, 512])
            for nm, dst in (('mla_qa', qa), ('mla_kva', kva)):
                for c in range(4):
                    wt, wv, n_ = win_piece(l, nm, c)
                    for b in range(NBLK):
                        ps = psum()
                        proj_h(wt, wv, 128, b, ps[:, :], ps.r())
                        if c % 2 == 0:
                            A(lambda e: e.copy(dst[:, c, sl(b)], ps[:, :]), [ps.r()], [dst.r()])
                        else:
                            V(lambda e: e.tensor_copy(dst[:, c, sl(b)], ps[:, :]), [ps.r()], [dst.r()])
            wt, wv, n_ = win_piece(l, 'mla_kpe', 0)
            for b in range(NBLK):
                ps = psum()
                proj_h(wt, wv, 64, b, ps[0:64, :], ps.r())
                A(lambda e: e.copy(kpe[:, sl(b)], ps[0:64, :]), [ps.r()], [kpe.r()])
            for src, ncol, kind_ in ((qa, 'q_norm', 'q'), (kva, 'kv_norm', 'kv')):
                for b in range(NBLK):
                    psm = psum()
                    for c in range(4):
                        A(lambda e: e.activation(sq[:, :], src[:, c, sl(b)], AF.Square), [src.r()], [sq.r()])
                        PE(lambda e: e.matmul(psm[:, :], cs[:, CS['mean512']:CS['mean512'] + 128], sq[:, :], start=(c == 0), stop=(c == 3)),
                           [sq.r(), cs.r()], [psm.r()])
                    rstd_from(psm[:, :], rst[:, :], psm.r(), rst.r())
                    for c in range(4):
                        ncl = pm[l][:, PM_COLS[ncol] + c:PM_COLS[ncol] + c + 1]
                        if kind_ == 'q':
                            V(lambda e: e.scalar_tensor_tensor(qn[:, c, sl(b)], src[:, c, sl(b)], ncl, rst[:, :], ALU.mult, ALU.mult),
                              [src.r(), rst.r(), pm[l].r()], [qn.r()])
                        else:
                            V(lambda e: e.scalar_tensor_tensor(src[:, c, sl(b)], src[:, c, sl(b)], ncl, rst[:, :], ALU.mult, ALU.mult),
                              [src.r(), rst.r(), pm[l].r()], [src.r()])
                            G(lambda e: e.tensor_copy(ckvb[:, c, sl(b)], src[:, c, sl(b)]), [src.r()], [ckvb.r()])
            for tt in range(4):
                idx, t_in = tt // 2, (tt % 2) * 128
                ps = psum()
                for c in range(4):
                    PE(lambda e: e.transpose(ps[:, c * 128:(c + 1) * 128], kva[:, c, tt * 128:(tt + 1) * 128], ident), [kva.r(), cs.r()], [ps.r()])
                A(lambda e: e.copy(tokt[:, :], ps[:, :]), [ps.r()], [tokt.r()])
                dma('sp', o_ckv[idx, l, t_in:t_in + 128, :], tokt[:, :], R=[tokt.r()])
                ps = psum()
                PE(lambda e: e.transpose(ps[:, 0:64], kpe[:, tt * 128:(tt + 1) * 128], ident[0:64, 0:64]), [kpe.r(), cs.r()], [ps.r()])
                A(lambda e: e.copy(tmpa[:, 0:64], ps[:, 0:64]), [ps.r()], [tmpa.r()])
                dma('sp', o_kpe[idx, l, t_in:t_in + 128, :], tmpa[:, 0:64], R=[tmpa.r()])
            V(lambda e: e.tensor_copy(kpT[0:64, 0:2 * LP], kpe[:, 0:2 * LP]), [kpe.r()], [kpT.r()])
            kps = pa("kps", [64, LS])
            V(lambda e: e.tensor_copy(kps[:, :], kpe[:, 2 * LP:T]), [kpe.r()], [kps.r()])
            rope(kpT, kps, 2 * LP, tmpa, tmpb)
            for tt in range(PAST // 128):
                dma('sp', tokt[:, :], cx_ckv[l, tt * 128:(tt + 1) * 128, :], W=[tokt.r()])
                ps = psum()
                for c in range(4):
                    PE(lambda e: e.transpose(ps[:, c * 128:(c + 1) * 128], tokt[:, c * 128:(c + 1) * 128], ident), [tokt.r(), cs.r()], [ps.r()])
                A(lambda e: e.copy(ckvb[:, :, T + tt * 128:T + (tt + 1) * 128], ps[:, :].rearrange("p (c t) -> p c t", c=4)), [ps.r()], [ckvb.r()])
                dma('sp', tmpb[:, 0:64], cx_kpe[l, tt * 128:(tt + 1) * 128, :], W=[tmpb.r()])
                ps = psum()
                PE(lambda e: e.transpose(ps[0:64, 0:128], tmpb[:, 0:64], ident), [tmpb.r(), cs.r()], [ps.r()])
                A(lambda e: e.copy(kpT[0:64, T + tt * 128:T + (tt + 1) * 128], ps[0:64, 0:128]), [ps.r()], [kpT.r()])
        for h in range(4):
          if stage == 4 and h > 0:
              break
          with Phase() as hh:
            qnT = hh("qnT", [128, T], BF16)
            qpT = hh("qpT", [128, T], BF16)
            qpf = hh("qpf", [64, T])
            knT = hh("knT", [128, NKEY], BF16)
            vtk = hh("vtk", [128, NKEY // 128, 128], BF16)
            Pm = hh("Pm", [128, LS + PAST])
            PT = hh("PT", [128, (LS + PAST) // 128, 128], BF16)
            Otok = hh("Otok", [128, 128])
            ob = hh("ob", [128, T], BF16)
            mx = hh("mx", [128, 4])
            sm = hh("sm", [128, 4])
            tmpa = hh("tmpa", [128, 512])
            tmpb = hh("tmpb", [128, 512])
            G(lambda e: e.memset(qpT[:, :], 0.0), [], [qpT.r()])
            wqo = (h * 192) * 4
            wt, wv = load_w(wqb_d[l][:, wqo:wqo + 512], 4, 128)
            for b in range(NBLK):
                ps = psum()
                for kc in range(4):
                    PE(lambda e: e.matmul(ps[:, :], wv[:, kc, :], qn[:, kc, sl(b)], start=(kc == 0), stop=(kc == 3)), [wt.r(), qn.r()], [ps.r()])
                A(lambda e: e.copy(qnT[:, sl(b)], ps[:, :]), [ps.r()], [qnT.r()])
            wt, wv = load_w(wqb_d[l][:, wqo + 512:wqo + 768], 4, 64)
            for b in range(NBLK):
                ps = psum()
                for kc in range(4):
                    PE(lambda e: e.matmul(ps[0:64, :], wv[:, kc, :], qn[:, kc, sl(b)], start=(kc == 0), stop=(kc == 3)), [wt.r(), qn.r()], [ps.r()])
                A(lambda e: e.copy(qpf[:, sl(b)], ps[0:64, :]), [ps.r()], [qpf.r()])
            V(lambda e: e.tensor_copy(qpT[0:64, 0:2 * LP], qpf[:, 0:2 * LP]), [qpf.r()], [qpT.r()])
            qps = hh("qps", [64, LS])
            V(lambda e: e.tensor_copy(qps[:, :], qpf[:, 2 * LP:T]), [qpf.r()], [qps.r()])
            rope(qpT, qps, 2 * LP, tmpa, tmpb)
            wt, wv = load_w(wkvb_d[l][:, (h * 2) * 512:(h * 2) * 512 + 512], 4, 128)
            for kb in range(NKEY // 512):
                ps = psum()
                for kc in range(4):
                    PE(lambda e: e.matmul(ps[:, :], wv[:, kc, :], ckvb[:, kc, kb * 512:(kb + 1) * 512], start=(kc == 0), stop=(kc == 3)),
                       [wt.r(), ckvb.r()], [ps.r()])
                A(lambda e: e.copy(knT[:, kb * 512:(kb + 1) * 512], ps[:, :]), [ps.r()], [knT.r()])
            wt, wv = load_w(wkvb_d[l][:, (h * 2 + 1) * 512:(h * 2 + 1) * 512 + 512], 4, 128)
            for k4 in range(0, NKEY // 128, 4):
                ps = psum()
                for kt_ in range(k4, k4 + 4):
                    for kc in range(4):
                        PE(lambda e: e.matmul(ps[:, (kt_ - k4) * 128:(kt_ - k4 + 1) * 128], ckvb[:, kc, kt_ * 128:(kt_ + 1) * 128], wv[:, kc, :],
                                              start=(kc == 0), stop=(kc == 3)), [wt.r(), ckvb.r()], [ps.r()])
                V(lambda e: e.tensor_copy(vtk[:, k4:k4 + 4, :].rearrange("p k d -> p (k d)"), ps[:, :]), [ps.r()], [vtk.r()])
            for (tok0, L, kind, idx) in SEQS:
                k0, Lk = KEYR[idx]
                nkb = (Lk + 511) // 512
                for qt in range(L // 128):
                    qs = slice(tok0 + qt * 128, tok0 + (qt + 1) * 128)
                    pss = [psum() for _ in range(nkb)]
                    for kb in range(nkb):
                        kw_ = min(512, Lk - kb * 512)
                        ks = slice(k0 + kb * 512, k0 + kb * 512 + kw_)
                        PE(lambda e: e.matmul(pss[kb][:, 0:kw_], qnT[:, qs], knT[:, ks], start=True, stop=False), [qnT.r(), knT.r()], [pss[kb].r()])
                        PE(lambda e: e.matmul(pss[kb][:, 0:kw_], qpT[:, qs], kpT[:, ks], start=False, stop=True), [qpT.r(), kpT.r()], [pss[kb].r()])
                        V(lambda e: e.reduce_max(mx[:, kb:kb + 1], pss[kb][:, 0:kw_], AX.X), [pss[kb].r()], [mx.r()])
                    if nkb > 1:
                        V(lambda e: e.reduce_max(mx[:, 3:4], mx[:, 0:nkb], AX.X), [mx.r()], [mx.r()])
                        mcol = mx[:, 3:4]
                    else:
                        mcol = mx[:, 0:1]
                    V(lambda e: e.tensor_scalar(mx[:, 3:4], mcol, -SCALE, None, ALU.mult), [mx.r()], [mx.r()])
                    for kb in range(nkb):
                        kw_ = min(512, Lk - kb * 512)
                        A(lambda e: e.activation(Pm[:, kb * 512:kb * 512 + kw_], pss[kb][:, 0:kw_], AF.Exp, bias=mx[:, 3:4], scale=SCALE,
                                                 accum_out=sm[:, kb:kb + 1]), [pss[kb].r(), mx.r()], [Pm.r(), sm.r()])
                    if nkb > 1:
                        V(lambda e: e.reduce_sum(sm[:, 3:4], sm[:, 0:nkb], AX.X), [sm.r()], [sm.r()])
                        scol = sm[:, 3:4]
                    else:
                        scol = sm[:, 0:1]
                    V(lambda e: e.reciprocal(sm[:, 3:4], scol), [sm.r()], [sm.r()])
                    FILL(2)
                    nkt = Lk // 128
                    for k4 in range(0, nkt, 4):
                        ps = psum()
                        w4 = min(4, nkt - k4)
                        for kt_ in range(k4, k4 + w4):
                            PE(lambda e: e.transpose(ps[:, (kt_ - k4) * 128:(kt_ - k4 + 1) * 128], Pm[:, kt_ * 128:(kt_ + 1) * 128], ident),
                               [Pm.r(), cs.r()], [ps.r()])
                        A(lambda e: e.copy(PT[:, k4:k4 + w4, :].rearrange("p k q -> p (k q)"), ps[:, 0:w4 * 128]), [ps.r()], [PT.r()])
                    po = psum()
                    for kt_ in range(nkt):
                        PE(lambda e: e.matmul(po[:, 0:128], PT[:, kt_, :], vtk[:, k0 // 128 + kt_, :], start=(kt_ == 0), stop=(kt_ == nkt - 1)),
                           [PT.r(), vtk.r()], [po.r()])
                    A(lambda e: e.activation(Otok[:, :], po[:, 0:128], AF.Copy, scale=sm[:, 3:4]), [po.r(), sm.r()], [Otok.r()])
                    pt2 = psum()
                    PE(lambda e: e.transpose(pt2[:, 0:128], Otok[:, :], ident), [Otok.r(), cs.r()], [pt2.r()])
                    V(lambda e: e.tensor_copy(ob[:, qs], pt2[:, 0:128]), [pt2.r()], [ob.r()])
            if stage > 4:
                dma('sp', obd[12 + h], ob[:, :], R=[ob.r()], W=[obd_res[12 + h]])
            if h == 0 and l == 0:
                dbg_dump('ob_mla', ob[:, :], [ob.r()])

    for l in range(NLAY[0]):
      try:
        if stage < 1:
            break
        with Phase() as ph:
            xTb = ph("xTb", [128, 16, 512])
            for b in range(NBLK):
                kind = 0 if b == 0 else 1
                dma('sp', xTb[:, :, :], xTd[:, :, b * 512:(b + 1) * 512].rearrange("c p t -> p c t"),
                    R=[xTd_res[b]], W=[xTb.r()])
                for c in range(16):
                    A(lambda e, c=c, b=b, kind=kind, l=l: e.activation(
                        hT[:, c, b * 512:(b + 1) * 512], xTb[:, c, :], AF.Identity,
                        bias=modT[l][:, kind, c:c + 1], scale=modT[l][:, kind, 16 + c:16 + c + 1]),
                        [xTb.r(), modT[l].r()], [hT.r(b)])
            if l == 0:
                dbg_dump('hT', hT[:, :, :].rearrange("p c t -> p (c t)"), [hT.r(b) for b in range(NBLK)])
        if stage < 2:
            break
        FIL[0] = Filler(l)
        with Phase() as ph:
            ba = ph("ba", [64, 24, 16])
            bet = ph("bet", [64, 24, 8])
            gg = ph("gg", [64, 24, 8])
            tt_ = ph("tt_", [64, 24, 8])
            t2_ = ph("t2_", [64, 24, 8])
            nea = ph("nea", [64, 8])
            if CUT[0] == -6:
                win_piece(l, 'gdn_q', 0)
                cut(-6)
            wt, wv, n_ = win_piece(l, 'gdn_ba', 0)
            cut(-5)
            ps = psum()
            for n in range(24):
                for kc in range(16):
                    PE(lambda e, n=n, kc=kc, ps=ps, wv=wv: e.matmul(
                        ps[0:64, n * 16:(n + 1) * 16], hT[:, kc, n * 64:(n + 1) * 64], wv[:, kc, :],
                        start=(kc == 0), stop=(kc == 15)), [wt.r()] + [hT.r(b) for b in range(NBLK)], [ps.r()])
            cut(-4)
            A(lambda e, ps=ps: e.copy(ba[:, :, :], ps[0:64, 0:384].rearrange("p (n c) -> p n c", c=16)), [ps.r()], [ba.r()])
            cut(-3)
            A(lambda e: e.activation(bet[:, :, :], ba[:, :, 0:8], AF.Sigmoid), [ba.r()], [bet.r()])
            for j in range(8):
                V(lambda e, j=j: e.tensor_scalar(tt_[:, :, j], ba[:, :, 8 + j], pm[l][0:64, PM_COLS['dt_bias'] + j:PM_COLS['dt_bias'] + j + 1],
                                               None, ALU.add), [ba.r(), pm[l].r()], [tt_.r()])
            cut(-2)
            A(lambda e: e.activation(t2_[:, :, :], tt_[:, :, :], AF.Abs), [tt_.r()], [t2_.r()])
            A(lambda e: e.activation(t2_[:, :, :], t2_[:, :, :], AF.Exp, scale=-1.0), [t2_.r()], [t2_.r()])
            A(lambda e: e.activation(t2_[:, :, :], t2_[:, :, :], AF.Ln, bias=ONEC[0:64, :]), [t2_.r(), cs.r()], [t2_.r()])
            cut(-1)
            V(lambda e: e.scalar_tensor_tensor(tt_[:, :, :], tt_[:, :, :], 0.0, t2_[:, :, :], ALU.max, ALU.add),
              [tt_.r(), t2_.r()], [tt_.r()])
            A(lambda e: e.activation(nea[:, :], pm[l][0:64, PM_COLS['a_log']:PM_COLS['a_log'] + 8], AF.Exp), [pm[l].r()], [nea.r()])
            V(lambda e: e.tensor_scalar(nea[:, :], nea[:, :], -1.0, None, ALU.mult), [nea.r()], [nea.r()])
            for j in range(8):
                V(lambda e, j=j: e.tensor_scalar(gg[:, :, j], tt_[:, :, j], nea[:, j:j + 1], None, ALU.mult),
                  [tt_.r(), nea.r()], [gg.r()])
            dbg_dump('gg', gg[:, :, :].rearrange("p a b -> p (a b)"), [gg.r()])
            for h in range(4):
                if stage in (2, 3, 4) and h > 0:
                    break
                gdn_unit(l, h, ph, bet, gg)
        if stage < 3:
            break
        with Phase() as ph:
            rmask = ph("rmask", [128, T])
            V(lambda e: e.memset(rmask[:, :], 1.0), [], [rmask.r()])
            V(lambda e: e.memset(rmask[:, :].rearrange("p (n c) -> p n c", c=CH)[:, :, 0], 0.0), [rmask.r()], [rmask.r()])
            glr = [ph("glr0", [16, T]), ph("glr1", [16, T])]
            gw2 = ph("gw2", [16, 512])
            dma('sp', gw2[:, :], gw2_d[l], W=[gw2.r()])
            for dr in range(2):
                wt, wv, n_ = win_piece(l, 'gla_g', dr)
                for b in range(NBLK):
                    ps = psum()
                    proj_h(wt, wv, 16, b, ps[0:16, :], ps.r())
                    A(lambda e: e.copy(glr[dr][:, b * 512:(b + 1) * 512], ps[0:16, :]), [ps.r()], [glr[dr].r()])
            lbt = ph("lbt", [128, 8])
            oml = ph("oml", [128, 8])
            if l == 0:
                V(lambda e: e.memset(lbt[:, :], 0.0), [], [lbt.r()])
                V(lambda e: e.memset(oml[:, :], 1.0), [], [oml.r()])
            else:
                c_ = PM_COLS['hg_lb']
                V(lambda e: e.tensor_tensor(lbt[:, :], pm[l][:, c_ + 8:c_ + 16], pm[l][:, c_:c_ + 8], ALU.subtract), [pm[l].r()], [lbt.r()])
                A(lambda e: e.activation(lbt[:, :], lbt[:, :], AF.Sigmoid), [lbt.r()], [lbt.r()])
                V(lambda e: e.tensor_scalar(oml[:, :], lbt[:, :], -1.0, 1.0, ALU.mult, ALU.add), [lbt.r()], [oml.r()])
            shared = {'rmask': rmask, 'glr': glr, 'gw2': gw2, 'lbt': lbt, 'oml': oml}
            for mixer in ('gla', 'hg'):
                for h in range(4):
                    if stage in (3, 4) and h > 0:
                        break
                    gla_unit(l, h, mixer, shared)
        if stage < 4:
            break
        mla_layer(l)
        if stage < 5:
            break
        FIL[0].flush()
        with Phase() as ph:
            obr = ph("obr", [128, 16, T], BF16)
            for k_ in range(16):
                dma('sp', obr[:, k_, :], obd[k_], R=[obd_res[k_]], W=[obr.r(k_)])
            sig = ph("sig", [128, 512])
            macc = ph("macc", [128, 512])
            gin = [ph("gin0", [128, 512], BF16), ph("gin1", [128, 512], BF16)]
            mo = [ph("mo0", [128, 512], BF16), ph("mo1", [128, 512], BF16)]
            obr_all = [obr.r(k_) for k_ in range(16)]
            for c in range(16):
                for b in range(NBLK):
                    for k in range(4):
                        gi_ = gin[(c * 12 + b * 4 + k) % 2]
                        dma('sp', gi_[:, :], gsd[k * 16 + c, :, b * 512:(b + 1) * 512], R=[gsd_res[k * 16 + c][b]], W=[gi_.r()])
                        bg = pm[l][:, PM_COLS['b_gates'] + k * 16 + c:PM_COLS['b_gates'] + k * 16 + c + 1]
                        A(lambda e: e.activation(sig[:, :], gi_[:, :], AF.Sigmoid, bias=bg), [gi_.r(), pm[l].r()], [sig.r()])
                        off = (k * 16 + c) * 512
                        wtb, wvb = load_w(wbr_d[l][:, off:off + 512], 4, 128)
                        pp = psum()
                        for kc in range(4):
                            PE(lambda e: e.matmul(pp[:, :], wvb[:, kc, :], obr[:, k * 4 + kc, b * 512:(b + 1) * 512], start=(kc == 0), stop=(kc == 3)),
                               [wtb.r()] + obr_all, [pp.r()])
                        if k == 0:
                            V(lambda e: e.tensor_tensor(macc[:, :], sig[:, :], pp[:, :], ALU.mult), [sig.r(), pp.r()], [macc.r()])
                        else:
                            V(lambda e: e.tensor_tensor(sig[:, :], sig[:, :], pp[:, :], ALU.mult), [sig.r(), pp.r()], [sig.r()])
                            if k < 3:
                                V(lambda e: e.tensor_tensor(macc[:, :], macc[:, :], sig[:, :], ALU.add), [macc.r(), sig.r()], [macc.r()])
                            else:
                                mo_ = mo[(c * NBLK + b) % 2]
                                V(lambda e: e.tensor_tensor(mo_[:, :], macc[:, :], sig[:, :], ALU.add), [macc.r(), sig.r()], [mo_.r()])
                                dma('sp', mgd[c, :, b * 512:(b + 1) * 512], mo_[:, :], R=[mo_.r()], W=[mgd_res[c][b]])
        if stage < 6:
            break
        with Phase() as ph:
            rr = ph("rr", [128, 16, 512])
            aT = ph("aT", [128, NFF, 512], BF16)
            xin = ph("xin", [128, 512])
            sq = ph("sq", [128, 512])
            mean = ph("mean", [128, 512])
            rst = ph("rst", [128, 512])
            tg = ph("tg", [128, 512])
            wbig = [ph("wbig0", [128, NFF * 128], BF16), ph("wbig1", [128, NFF * 128], BF16)]
            otok = ph("otok", [128, 512])
            wbi = [0]

            def layer_norm(gname, bname, post):
                pm1 = psum()
                pm2 = psum()
                for c in range(16):
                    A(lambda e: e.activation(sq[:, :], rr[:, c, :], AF.Square), [rr.r(c)], [sq.r()])
                    PE(lambda e: e.matmul(pm1[:, :], cs[:, CS['meanD']:CS['meanD'] + 128], rr[:, c, :], start=(c == 0), stop=(c == 15)),
                       [rr.r(c), cs.r()], [pm1.r()])
                    PE(lambda e: e.matmul(pm2[:, :], cs[:, CS['meanD']:CS['meanD'] + 128], sq[:, :], start=(c == 0), stop=(c == 15)),
                       [sq.r(), cs.r()], [pm2.r()])
                A(lambda e: e.copy(mean[:, :], pm1[:, :]), [pm1.r()], [mean.r()])
                V(lambda e: e.tensor_tensor(tg[:, :], mean[:, :], mean[:, :], ALU.mult), [mean.r()], [tg.r()])
                V(lambda e: e.tensor_tensor(tg[:, :], pm2[:, :], tg[:, :], ALU.subtract), [pm2.r(), tg.r()], [tg.r()])
                A(lambda e: e.activation(rst[:, :], tg[:, :], AF.Ln, bias=EPSC), [tg.r(), cs.r()], [rst.r()])
                A(lambda e: e.activation(rst[:, :], rst[:, :], AF.Exp, scale=-0.5), [rst.r()], [rst.r()])
                for c in range(16):
                    V(lambda e: e.tensor_tensor(rr[:, c, :], rr[:, c, :], mean[:, :], ALU.subtract), [rr.r(c), mean.r()], [rr.r(c)])
                    G(lambda e: e.tensor_tensor(rr[:, c, :], rr[:, c, :], rst[:, :], ALU.mult), [rr.r(c), rst.r()], [rr.r(c)])
                    A(lambda e: e.activation(rr[:, c, :], rr[:, c, :], AF.Identity,
                                             bias=pm[l][:, PM_COLS[bname] + c:PM_COLS[bname] + c + 1],
                                             scale=pm[l][:, PM_COLS[gname] + c:PM_COLS[gname] + c + 1]), [rr.r(c), pm[l].r()], [rr.r(c)])
                    post(c)
            for b in range(NBLK):
                kind = 0 if b == 0 else 1
                for c in range(16):
                    dma('sp', hT[:, c, 512:1024], mgd[c, :, b * 512:(b + 1) * 512], R=[mgd_res[c][b]], W=[hT.r(1)])
                mg_all = [hT.r(1)]
                for c in range(16):
                    wt, wv = load_w(wout_d[l][:, c * 2048:(c + 1) * 2048], 16, 128)
                    ps = psum()
                    for kc in range(16):
                        PE(lambda e: e.matmul(ps[:, :], wv[:, kc, :], hT[:, kc, 512:1024], start=(kc == 0), stop=(kc == 15)), [wt.r()] + mg_all, [ps.r()])
                    dma('sp', xin[:, :], xTd[c, :, b * 512:(b + 1) * 512], R=[xTd_res[b]], W=[xin.r()])
                    A(lambda e: e.activation(tg[:, :], ps[:, :], AF.Copy, scale=modT[l][:, kind, 32 + c:32 + c + 1]), [ps.r(), modT[l].r()], [tg.r()])
                    V(lambda e: e.scalar_tensor_tensor(rr[:, c, :], xin[:, :], ALPHA, tg[:, :], ALU.mult, ALU.add), [xin.r(), tg.r()], [rr.r(c)])

                def post1(c):
                    A(lambda e: e.activation(hT[:, c, 0:512], rr[:, c, :], AF.Identity, bias=modT[l][:, kind, 48 + c:48 + c + 1],
                                             scale=modT[l][:, kind, 64 + c:64 + c + 1]), [rr.r(c), modT[l].r()], [hT.r(0)])
                layer_norm('ln1_g', 'ln1_b', post1)
                h2_all = [hT.r(0)]
                for f in range(NFF):
                    wt1, wv1 = load_w(w1_d[l][:, f * 2048:(f + 1) * 2048], 16, 128)
                    wt3, wv3 = load_w(w3_d[l][:, f * 2048:(f + 1) * 2048], 16, 128)
                    p1 = psum()
                    p3 = psum()
                    for kc in range(16):
                        PE(lambda e: e.matmul(p1[:, :], wv1[:, kc, :], hT[:, kc, 0:512], start=(kc == 0), stop=(kc == 15)), [wt1.r()] + h2_all, [p1.r()])
                    for kc in range(16):
                        PE(lambda e: e.matmul(p3[:, :], wv3[:, kc, :], hT[:, kc, 0:512], start=(kc == 0), stop=(kc == 15)), [wt3.r()] + h2_all, [p3.r()])
                    A(lambda e: e.activation(tg[:, :], p1[:, :], AF.Silu), [p1.r()], [tg.r()])
                    V(lambda e: e.tensor_tensor(aT[:, f, :], tg[:, :], p3[:, :], ALU.mult), [tg.r(), p3.r()], [aT.r(f)])
                aT_all = [aT.r(f) for f in range(NFF)]
                for c in range(16):
                    wtile = wbig[wbi[0] % 2]
                    wbi[0] += 1
                    dma('pool', wtile[:, :], w2_d[l][:, c * NFF * 128:(c + 1) * NFF * 128], W=[wtile.r()])
                    wv2 = wtile[:, :].rearrange("p (k n) -> p k n", k=NFF)
                    ps = psum()
                    for kc in range(NFF):
                        PE(lambda e: e.matmul(ps[:, :], wv2[:, kc, :], aT[:, kc, :], start=(kc == 0), stop=(kc == NFF - 1)), [wtile.r()] + aT_all, [ps.r()])
                    A(lambda e: e.activation(tg[:, :], ps[:, :], AF.Copy, scale=modT[l][:, kind, 80 + c:80 + c + 1]), [ps.r(), modT[l].r()], [tg.r()])
                    V(lambda e: e.scalar_tensor_tensor(rr[:, c, :], rr[:, c, :], ALPHA, tg[:, :], ALU.mult, ALU.add), [rr.r(c), tg.r()], [rr.r(c)])

                def post2(c):
                    if l < DEPTH - 1:
                        dma('sp', xTd[c, :, b * 512:(b + 1) * 512], rr[:, c, :], R=[rr.r(c)], W=[xTd_res[b]])
                layer_norm('ln2_g', 'ln2_b', post2)
                if l == DEPTH - 1:
                    for tt in range(4):
                        for g4 in range(4):
                            ps = psum()
                            for cc in range(4):
                                c = g4 * 4 + cc
                                PE(lambda e: e.transpose(ps[:, cc * 128:(cc + 1) * 128], rr[:, c, tt * 128:(tt + 1) * 128], ident), [rr.r(c), cs.r()], [ps.r()])
                            A(lambda e: e.copy(otok[:, :], ps[:, :]), [ps.r()], [otok.r()])
                            gt = b * 4 + tt
                            dst = y_p[gt * 128:(gt + 1) * 128, g4 * 512:(g4 + 1) * 512] if gt < 4 else \
                                y_s[(gt - 4) * 128:(gt - 3) * 128, g4 * 512:(g4 + 1) * 512]
                            dma('sp', dst, otok[:, :], R=[otok.r()])
      except _Stop:
        break

    for tl_ in list(ALL_TL):
        rs_ = list(tl_._res.values())
        if tl_.name.startswith('s') and rs_ and sum(r.nw for r in rs_) > 0 and sum(r.nr for r in rs_) == 0:
            try:
                shp_ = list(tl_.t.shape)
                sink_ = nc.dram_tensor("sink_" + tl_.name, shp_, tl_.t.dtype, kind="Internal").ap()
                full_ = tl_.t[tuple(slice(None) for _ in shp_)]
                dma('sp', sink_[tuple(slice(None) for _ in shp_)], full_, R=rs_)
            except Exception as ex_:
                print('autosink failed', tl_.name, ex_)
    agg = {}
    for r_ in ALL_RES:
        a_ = agg.setdefault(r_.name, [0, 0])
        a_[0] += r_.nr
        a_[1] += r_.nw
    dead = [k for k, v in agg.items() if v[1] > 0 and v[0] == 0 and k != '?']
    if dead:
        print('WARNING unread tiles:', dead)
    print('op counts', P.cnt, {q: P.ring[q][1] for q in P.ring})
    stack_close = stack
    with nc.Block() as block:
        P.emit(block)
    stack_close.close()
    return nc


def _pack(W, n=128):
    K, N = W.shape
    kc = K // 128
    return np.ascontiguousarray(
        W.reshape(kc, 128, N // n, n).transpose(1, 2, 0, 3).reshape(128, -1))


def _pack_in(W):
    K = W.shape[0]
    outs = []
    cols = []
    for name in PIECES:
        cols.extend(PIECES[name])
    cols.sort()
    assert sum(n for _, n in cols) == IN_WIDTH
    for c0, n in cols:
        outs.append(W[:, c0:c0 + n].reshape(K // 128, 128, n).transpose(1, 0, 2).reshape(128, -1))
    return np.ascontiguousarray(np.concatenate(outs, axis=1))


def _consts(cvec2):
    cs = np.zeros((128, CS_N), np.float32)
    cs[:, CS['ident']:CS['ident'] + 128] = np.eye(128, dtype=np.float32)
    k = np.arange(64)[:, None]
    i = np.arange(64)[None, :]
    for name, m in (('le', k <= i), ('ge', k >= i), ('lt', k < i), ('gt', k > i)):
        cs[:64, CS[name]:CS[name] + 64] = m.astype(np.float32)
    cs[:, CS['ones']:CS['ones'] + 128] = 1.0
    cs[:, CS['mean128']:CS['mean128'] + 128] = 1.0 / 128
    cs[:, CS['meanD']:CS['meanD'] + 128] = 1.0 / D
    cs[:, CS['mean512']:CS['mean512'] + 128] = 1.0 / 512
    perm = np.zeros((64, 64), np.float32)
    cosT = np.zeros((64, LS), np.float32)
    sinT = np.zeros((64, LS), np.float32)
    pos = np.arange(LS)
    row_id = (pos // 64).astype(np.float32)
    col_id = (pos % 64).astype(np.float32)
    half = 32
    inv = (10000.0 ** (-np.arange(0, half, 2, dtype=np.float32) / half)).astype(np.float32)
    for blk, ids in ((0, row_id), (1, col_id)):
        ang = ids[None, :] * inv[:, None]
        b0 = blk * 32
        for j in range(16):
            cosT[b0 + j] = np.cos(ang[j]); cosT[b0 + 16 + j] = np.cos(ang[j])
            sinT[b0 + j] = -np.sin(ang[j]); sinT[b0 + 16 + j] = np.sin(ang[j])
            perm[b0 + 16 + j, b0 + j] = 1.0
            perm[b0 + j, b0 + 16 + j] = 1.0
    cs[:64, CS['perm']:CS['perm'] + 64] = perm
    k = np.arange(32)[:, None]
    i = np.arange(32)[None, :]
    cs[:32, CS['le32']:CS['le32'] + 32] = (k <= i)
    cs[:32, CS['ge32']:CS['ge32'] + 32] = (k >= i)
    cs[:, CS['eps']] = EPS
    cs[:, CS['one']] = 1.0
    cs[:, CS['cvT']:CS['cvT'] + 32] = cvec2.reshape(2, 16, 128).transpose(2, 1, 0).reshape(128, 32)
    rope = np.zeros((64, 2 * LS), np.float32)
    rope[:, :LS] = cosT
    rope[:, LS:] = sinT
    return cs, rope


def _pm(inp, l):
    pm = np.zeros((128, PM_N), np.float32)

    def put(name, arr2d):
        c = PM_COLS[name]
        pm[:arr2d.shape[1], c:c + arr2d.shape[0]] = arr2d.T
    conv = inp['gdn_conv'][l]
    put('conv', conv.reshape(5, 12, 128).transpose(1, 0, 2).reshape(60, 128))
    put('gdn_norm', inp['gdn_norm'][l][None])
    put('gla_norm', inp['gla_norm'][l][None])
    put('hg_norm', inp['hgrn_norm'][l][None])
    put('q_norm', inp['mla_q_norm'][l].reshape(4, 128))
    put('kv_norm', inp['mla_kv_norm'][l].reshape(4, 128))
    put('b_gates', inp['b_gates'][l].reshape(64, 128))
    put('hg_lb', inp['hgrn_lb'].reshape(16, 128))
    put('gla_gb', inp['gla_gate_b'][l].reshape(8, 64))
    put('b_ada', inp['b_ada'][l].reshape(96, 128))
    for nm in ('ln1_g', 'ln1_b', 'ln2_g', 'ln2_b'):
        put(nm, inp[nm][l].reshape(16, 128))
    pm[:, PM_COLS['a_log']:PM_COLS['a_log'] + 8] = inp['gdn_a_log'][l].reshape(1, 8)
    pm[:, PM_COLS['dt_bias']:PM_COLS['dt_bias'] + 8] = inp['gdn_dt_bias'][l].reshape(1, 8)
    return pm


def make_inputs(inp, ncores=8):
    f = lambda a: np.ascontiguousarray(np.asarray(a, dtype=np.float32))
    inp = {k: f(v) for k, v in inp.items()}
    sh = {}
    sh['pm'] = np.stack([_pm(inp, l) for l in range(DEPTH)])
    sh['gw2'] = np.ascontiguousarray(inp['gla_gate_w2'].transpose(0, 2, 1, 3).reshape(DEPTH, 16, 512))
    sh['wada'] = np.stack([_pack(inp['w_ada'][l]) for l in range(DEPTH)])
    sh['win'] = np.stack([_pack_in(inp['w_in'][l]) for l in range(DEPTH)])
    def pk_cols(W, cols):
        K = W.shape[0]
        return np.concatenate([W[:, c0:c0 + n].reshape(K // 128, 128, n).transpose(1, 0, 2).reshape(128, -1)
                               for c0, n in cols], axis=1)
    qcols = []
    for h in range(4):
        qcols += [(h * 192, 128), (h * 192 + 128, 64)]
    sh['wqb'] = np.stack([pk_cols(inp['mla_wq_b'][l], qcols) for l in range(DEPTH)])
    sh['wkvb'] = np.stack([_pack(inp['mla_wkv_b'][l]) for l in range(DEPTH)])
    sh['wbr'] = np.stack([np.concatenate([_pack(inp['w_branch'][l, k]) for k in range(4)], axis=1)
                          for l in range(DEPTH)])
    sh['wout'] = np.stack([_pack(inp['w_out'][l]) for l in range(DEPTH)])
    sh['w1'] = np.stack([_pack(inp['ffn_w1'][l]) for l in range(DEPTH)])
    sh['w3'] = np.stack([_pack(inp['ffn_w3'][l]) for l in range(DEPTH)])
    sh['w2'] = np.stack([_pack(inp['ffn_w2'][l]) for l in range(DEPTH)])
    maps = []
    for core in range(ncores):
        b = core % 2
        m = dict(sh)
        m['xp'] = inp['x_prompt'][2 * core:2 * core + 2].reshape(2 * LP, D)
        m['xs'] = inp['x_sample'][b]
        m['st_gdn'] = inp['state_gdn'][b]
        m['st_gla'] = inp['state_gla'][b]
        m['st_hg'] = inp['state_hgrn'][b]
        m['cx_ckv'] = inp['cache_mla_ckv'][b]
        m['cx_kpe'] = inp['cache_mla_kpe'][b]
        m['cs'], m['rope'] = _consts(np.stack([inp['c_ctx'], inp['c'][b]]))
        maps.append(m)
    return maps


_NC = None


def kernel(**inputs):
    global _NC
    if _NC is None:
        _NC = build()
    maps = make_inputs(inputs, 8)
    res = run_bass_kernel_spmd(_NC, maps, core_ids=list(range(8)))
    r = res.results
    y_prompt = np.concatenate([r[c]['y_p'].reshape(2, LP, D) for c in range(8)], axis=0)
    y_sample = np.stack([r[0]['y_s'], r[1]['y_s']], axis=0)
    cat = lambda k: np.concatenate([r[c][k] for c in range(8)], axis=0)
    return (y_prompt.astype(np.float32), y_sample.astype(np.float32), cat('o_gdn'), cat('o_gla'), cat('o_hg'),
            cat('o_ckv'), cat('o_kpe'))
```

```python
import math
from contextlib import ExitStack
import numpy as np
import concourse.bass as bass
import concourse.mybir as mybir
from concourse.bass_utils import run_bass_kernel_spmd

F32 = mybir.dt.float32
BF16 = mybir.dt.bfloat16
AF = mybir.ActivationFunctionType
ALU = mybir.AluOpType
AX = mybir.AxisListType

D = 2048
DEPTH = 2
LP, LS = 256, 1024
T = 2 * LP + LS
NBLK = 3
EPS = 1e-6
D_FF = 5632
NFF = D_FF // 128
ALPHA = (2.0 * DEPTH) ** 0.25
PAST = 512

IN_SPLITS = (
    ('gdn_q', 512), ('gdn_k', 512), ('gdn_v', 512), ('gdn_z', 512), ('gdn_b', 8), ('gdn_a', 8),
    ('gla_q', 256), ('gla_k', 256), ('gla_v', 512), ('gla_r', 512), ('gla_g', 32),
    ('hg_q', 512), ('hg_f', 1024), ('hg_i', 512), ('hg_g', 512),
    ('mla_qa', 512), ('mla_kva', 512), ('mla_kpe', 64), ('gates', 8192),
)
IN_WIDTH = sum(n for _, n in IN_SPLITS)


def _pieces():
    out, off = {}, 0
    for name, n in IN_SPLITS:
        if name in ('gdn_b',):
            out['gdn_ba'] = [(off, 16)]
        elif name == 'gdn_a':
            pass
        elif name in ('gla_q', 'gla_k'):
            out[name] = [(off + i * 64, 64) for i in range(4)]
        elif name == 'gla_g':
            out[name] = [(off, 16), (off + 16, 16)]
        elif name == 'mla_kpe':
            out[name] = [(off, 64)]
        else:
            out[name] = [(off + i * 128, 128) for i in range(n // 128)]
        off += n
    return out


PIECES = _pieces()

PM_COLS = {}
_o = 0
for _name, _n in (('conv', 60), ('gdn_norm', 1), ('gla_norm', 1), ('hg_norm', 1), ('q_norm', 4), ('kv_norm', 4),
                  ('b_gates', 64), ('hg_lb', 16), ('gla_gb', 8), ('b_ada', 96), ('ln1_g', 16), ('ln1_b', 16),
                  ('ln2_g', 16), ('ln2_b', 16), ('a_log', 8), ('dt_bias', 8)):
    PM_COLS[_name] = _o
    _o += _n
PM_N = _o

CS = {}
_o = 0
for _name, _n in (('ident', 128), ('le', 64), ('ge', 64), ('lt', 64), ('gt', 64), ('ones', 128), ('mean128', 128),
                  ('meanD', 128), ('mean512', 128), ('perm', 64), ('le32', 32), ('ge32', 32), ('cvT', 32), ('eps', 1), ('one', 1)):
    CS[_name] = _o
    _o += _n
CS_N = _o


ALL_RES = []
ALL_TL = []


class Res:
    __slots__ = ('w', 'r', 'name', 'nr', 'nw', 'excl')

    def __init__(self, name='?'):
        self.excl = False
        self.w = None
        self.r = {}
        self.name = name
        self.nr = 0
        self.nw = 0
        ALL_RES.append(self)


class _Rec:
    def __getattr__(self, name):
        return lambda *a, **k: (name, a, k)


_REC = _Rec()


class Prog:
    ENGS = ('pe', 'act', 'dve', 'pool', 'sp')

    def __init__(self, nc, stack):
        self.nc = nc
        self.streams = {e: [] for e in self.ENGS}
        self.cnt = {e: 0 for e in self.ENGS}
        self.esem = {}
        self.semobj = {}
        for e in ('pe', 'act', 'dve', 'pool'):
            s = stack.enter_context(nc.semaphore('tl_' + e))
            self.esem[e] = 'tl_' + e
            self.semobj['tl_' + e] = s
        self.ring = {}
        for q, k in (('sp', 12), ('pool', 12)):
            names = []
            for i in range(k):
                nm = 'rg_%s_%d' % (q, i)
                self.semobj[nm] = stack.enter_context(nc.semaphore(nm))
                names.append(nm)
            self.ring[q] = [names, 0]
        self.waited = {e: {} for e in self.ENGS}
        self.pending = {e: {} for e in self.ENGS}

    def op(self, eng, fn, R=(), W=(), dma=False, nobarrier=False):
        deps = {}

        def add(tok):
            if tok is None:
                return
            s, v = tok
            if deps.get(s, 0) < v:
                deps[s] = v
        for r in R:
            r.nr += 1
            add(r.w)
            if r.excl:
                for s_, v_ in r.r.items():
                    add((s_, v_))
        for w in W:
            w.nw += 1
            add(w.w)
            for s, v in w.r.items():
                add((s, v))
        if dma:
            names, n = self.ring[eng]
            k = len(names)
            sem = names[n % k]
            val = 16 * (n // k + 1)
            if n >= k:
                add((sem, val - 16))
            self.ring[eng][1] = n + 1
            inc = (sem, 16)
        else:
            self.cnt[eng] += 1
            sem = self.esem[eng]
            val = self.cnt[eng]
            inc = (sem, 1)
        tok = (sem, val)
        if not nobarrier:
            for s_, v_ in self.pending[eng].items():
                add((s_, v_))
            self.pending[eng] = {}
        waits = []
        wd = self.waited[eng]
        for s, v in deps.items():
            if eng == 'pe' and s == self.esem.get('pe'):
                continue
            if wd.get(s, 0) >= v:
                continue
            wd[s] = v
            waits.append((s, v))
        self.streams[eng].append((waits, fn(_REC), inc))
        for r in R:
            if r.excl:
                r.w = tok
                r.r = {}
            elif r.r.get(sem, 0) < val:
                r.r[sem] = val
        for w in W:
            w.w = tok
            w.r = {}
        return tok

    def barrier(self):
        cur = {}
        for e in ('pe', 'act', 'dve', 'pool'):
            if self.cnt[e] > 0:
                cur[self.esem[e]] = self.cnt[e]
        for q in self.ring:
            names, n = self.ring[q]
            k = len(names)
            for i, nm in enumerate(names):
                c = (n - i + k - 1) // k if n > i else 0
                if c > 0:
                    cur[nm] = 16 * c
        for e in self.ENGS:
            for s_, v_ in cur.items():
                if self.pending[e].get(s_, 0) < v_:
                    self.pending[e][s_] = v_

    def emit(self, block):
        nc = self.nc
        semobj = self.semobj

        def run(e, name):
            stream = self.streams[name]
            for waits, fn, inc in stream:
                for s, v in waits:
                    e.wait_ge(semobj[s], v)
                ins = getattr(e, fn[0])(*fn[1], **fn[2])
                ins.then_inc(semobj[inc[0]], inc[1])
            if True:
                for q in self.ring:
                    names, n = self.ring[q]
                    k = len(names)
                    for i, nm in enumerate(names):
                        cntq = (n - i + k - 1) // k if n > i else 0
                        if cntq > 0:
                            e.wait_ge(semobj[nm], 16 * cntq)
                for en in ('pe', 'act', 'dve', 'pool'):
                    if self.cnt[en] > 0:
                        e.wait_ge(semobj[self.esem[en]], self.cnt[en])

        @block.sync
        def _(e):
            run(e, 'sp')

        @block.tensor
        def _(e):
            run(e, 'pe')

        @block.scalar
        def _(e):
            run(e, 'act')

        @block.vector
        def _(e):
            run(e, 'dve')

        @block.gpsimd
        def _(e):
            run(e, 'pool')


class Tl:
    def __init__(self, t, name='?'):
        self.t = t
        self.name = name
        self._res = {}
        ALL_TL.append(self)

    def r(self, key=None):
        if key not in self._res:
            self._res[key] = Res(self.name)
            self._res[key].excl = self.name.startswith('ps')
        return self._res[key]

    def __getitem__(self, k):
        return self.t[k]


class _Stop(Exception):
    pass


CUT = [99]
NLAY = [DEPTH]
DIRS = [0, 1]


PASSNO = [0]
CUTPASS = [0]


def cut(k):
    if CUT[0] <= k and PASSNO[0] >= CUTPASS[0]:
        raise _Stop()


def build(dbg=None, stage=99):
    nc = bass.Bass("TRN2", target_bir_lowering=False)
    stack = ExitStack()
    ein = lambda name, shape: nc.dram_tensor(name, list(shape), F32, kind="ExternalInput").ap()
    eout = lambda name, shape: nc.dram_tensor(name, list(shape), F32, kind="ExternalOutput").ap()
    xp = ein("xp", [2 * LP, D])
    xs = ein("xs", [LS, D])
    st_gdn = ein("st_gdn", [DEPTH, 2, 4, 128, 128])
    st_gla = ein("st_gla", [DEPTH, 2, 4, 64, 128])
    st_hg = ein("st_hg", [DEPTH, 2, 4, 128, 128])
    cx_ckv = ein("cx_ckv", [DEPTH, PAST, 512])
    cx_kpe = ein("cx_kpe", [DEPTH, PAST, 64])
    pm_d = ein("pm", [DEPTH, 128, PM_N])
    cs_d = ein("cs", [128, CS_N])
    rope_d = ein("rope", [64, 2 * LS])
    gw2_d = ein("gw2", [DEPTH, 16, 2 * 256])
    wada_d = ein("wada", [DEPTH, 128, 16 * 6 * D])
    win_d = ein("win", [DEPTH, 128, 16 * IN_WIDTH])
    wqb_d = ein("wqb", [DEPTH, 128, 4 * 768])
    wkvb_d = ein("wkvb", [DEPTH, 128, 4 * 1024])
    wbr_d = ein("wbr", [DEPTH, 128, 4 * 4 * D])
    wout_d = ein("wout", [DEPTH, 128, 16 * D])
    w1_d = ein("w1", [DEPTH, 128, 16 * D_FF])
    w3_d = ein("w3", [DEPTH, 128, 16 * D_FF])
    w2_d = ein("w2", [DEPTH, 128, NFF * D])
    y_p = eout("y_p", [2 * LP, D])
    y_s = eout("y_s", [LS, D])
    o_gdn = eout("o_gdn", [2, DEPTH, 2, 4, 128, 128])
    o_gla = eout("o_gla", [2, DEPTH, 2, 4, 64, 128])
    o_hg = eout("o_hg", [2, DEPTH, 2, 4, 128, 128])
    o_ckv = eout("o_ckv", [2, DEPTH, LP, 512])
    o_kpe = eout("o_kpe", [2, DEPTH, LP, 64])
    dbg_aps = {}
    if dbg:
        for name, shape in dbg.items():
            dbg_aps[name] = eout("dbg_" + name, shape)
    elif dbg is None and stage < 99:
        dbg_aps['modT0'] = nc.dram_tensor("sink_modT0", [128, 192], F32, kind="Internal").ap()
        dbg_aps['hT'] = nc.dram_tensor("sink_hT", [128, 16 * T], F32, kind="Internal").ap()
    xTd = nc.dram_tensor("xTd", [16, 128, T], F32, kind="Internal").ap()
    mgd = nc.dram_tensor("mgd", [16, 128, T], BF16, kind="Internal").ap()

    P = Prog(nc, stack)
    _uid = [0]

    def _mk(st_, name, shape, dt=F32):
        _uid[0] += 1
        return Tl(st_.enter_context(nc.sbuf_tensor("s%d_%s" % (_uid[0], name), list(shape), dt)), "s%d_%s" % (_uid[0], name))
    sb = lambda name, shape, dt=F32: _mk(stack, name, shape, dt)

    class Phase:
        def __enter__(self):
            self.st = ExitStack()
            return lambda name, shape, dt=F32: _mk(self.st, name, shape, dt)

        def __exit__(self, *a):
            self.st.close()
            P.barrier()
            return False

    cs = sb("cs", [128, CS_N])
    pm = [sb("pm%d" % l, [128, PM_N]) for l in range(DEPTH)]
    modT = [sb("modT%d" % l, [128, 2, 96]) for l in range(DEPTH)]
    hT = sb("hT", [128, 16, T], BF16)
    NWB = 4
    wbs = [sb("wb%d" % i, [128, 16 * 128], BF16) for i in range(NWB)]
    psb = [Tl(stack.enter_context(nc.psum_tensor("ps%d" % i, [128, 512], F32)), "ps%d" % i) for i in range(8)]
    NPS = 6
    wfill = [sb("wfill%d" % i, [128, 16 * 128], BF16) for i in range(2)]
    gst = [sb("gst%d" % i, [128, 512], BF16) for i in range(2)]
    gsd = nc.dram_tensor("gsd", [64, 128, T], BF16, kind="Internal").ap()
    gsd_res = [[Res() for _ in range(NBLK)] for _ in range(64)]
    st = {'wb': 0, 'ps': 0}
    obd = nc.dram_tensor("obd", [16, 128, T], BF16, kind="Internal").ap()
    obd_res = [Res() for _ in range(16)]
    mgd_res = [[Res() for _ in range(NBLK)] for _ in range(16)]
    xTd_res = [Res() for _ in range(NBLK)]

    ident = cs[:, CS['ident']:CS['ident'] + 128]
    ones = cs[:, CS['ones']:CS['ones'] + 128]
    cmat = lambda name, n=64: cs[0:n, CS[name]:CS[name] + n]
    EPSC = cs[:, CS['eps']:CS['eps'] + 1]
    ONEC = cs[:, CS['one']:CS['one'] + 1]

    def psum():
        b = psb[st['ps'] % NPS]
        st['ps'] += 1
        return b

    def dma(q, out, in_, R=(), W=(), nobarrier=False, **kw):
        P.op(q, lambda e: e.dma_start(out=out, in_=in_, **kw), R, W, dma=True, nobarrier=nobarrier)

    def load_w(src2d, kc, n, pool=None):
        t = wbs[st['wb'] % NWB]
        st['wb'] += 1
        dma('pool', t[:, 0:kc * n], src2d, W=[t.r()], nobarrier=True)
        return t, t[:, 0:kc * n].rearrange("p (k n) -> p k n", k=kc)

    def win_piece(l, name, idx):
        c0, n = PIECES[name][idx]
        wt, wv = load_w(win_d[l][:, 16 * c0:16 * c0 + 16 * n], 16, n)
        return wt, wv, n

    def proj_h(wt, wv, n, b, out_ap, psr):
        for kc in range(16):
            P.op('pe', lambda e, kc=kc: e.matmul(out_ap, wv[:, kc, :], hT[:, kc, b * 512:(b + 1) * 512],
                                                 start=(kc == 0), stop=(kc == 15)),
                 R=[wt.r(), hT.r(b)], W=[psr])


    class Filler:
        def __init__(self, l):
            self.l = l
            self.items = [(c * 4 + k_, b) for c in range(16) for k_ in range(4) for b in range(NBLK)]
            self.pos = 0
            self.n = 0
            self.wv = None
            self.wt = None

        def step(self, n=1):
            for _ in range(n):
                if self.pos >= 4 * len(self.items):
                    return
                it, sub = divmod(self.pos, 4)
                (pc, b) = self.items[it]
                c, k_ = divmod(pc, 4)
                p_ = k_ * 16 + c
                if sub == 0 and b == 0:
                    c0, n_ = PIECES['gates'][p_]
                    self.wt = wfill[(it // NBLK) % 2]
                    dma('pool', self.wt[:, :], win_d[self.l][:, 16 * c0:16 * c0 + 16 * 128], W=[self.wt.r()], nobarrier=True)
                    self.wv = self.wt[:, :].rearrange("p (k n) -> p k n", k=16)
                ps = psb[6 + it % 2]
                wv, wt = self.wv, self.wt
                for kc in range(sub * 4, sub * 4 + 4):
                    P.op('pe', lambda e: e.matmul(ps[:, :], wv[:, kc, :], hT[:, kc, b * 512:(b + 1) * 512], start=(kc == 0), stop=(kc == 15)),
                         [wt.r(), hT.r(b)], [ps.r()], nobarrier=True)
                if sub == 3:
                    g_ = gst[it % 2]
                    if it % 2 == 0:
                        P.op('act', lambda e: e.copy(g_[:, :], ps[:, :]), [ps.r()], [g_.r()], nobarrier=True)
                    else:
                        P.op('dve', lambda e: e.tensor_copy(g_[:, :], ps[:, :]), [ps.r()], [g_.r()], nobarrier=True)
                    dma('sp', gsd[p_, :, b * 512:(b + 1) * 512], g_[:, :], R=[g_.r()], W=[gsd_res[p_][b]], nobarrier=True)
                self.pos += 1

        def flush(self):
            self.step(4 * len(self.items))

    FIL = [None]

    def FILL(n=1):
        if FIL[0] is not None and stage >= 5:
            FIL[0].step(n)

    def A(fn, R, W):
        P.op('act', fn, R, W)

    def V(fn, R, W):
        P.op('dve', fn, R, W)

    def G(fn, R, W):
        P.op('pool', fn, R, W)

    def PE(fn, R, W):
        P.op('pe', fn, R, W)

    def dbg_dump(name, src_ap, R):
        if name in dbg_aps:
            dma('pool', dbg_aps[name], src_ap, R=R)

    def rstd_from(ps_ap, out_ap, psr, outr):
        A(lambda e: e.activation(out_ap, ps_ap, AF.Ln, bias=EPSC[0:out_ap.shape[0], :]), [psr, cs.r()], [outr])
        A(lambda e: e.activation(out_ap, out_ap, AF.Exp, scale=-0.5), [outr], [outr])

    dma('sp', cs[:, :], cs_d[:, :], W=[cs.r()])
    for l in range(DEPTH):
        dma('sp', pm[l][:, :], pm_d[l], W=[pm[l].r()])

    with Phase() as ph:
      if stage < 99:
          zt = ph("zt", [128, D])
          V(lambda e: e.memset(zt[:, :], 0.0), [], [zt.r()])
          for i in range(4):
              dma('sp', y_p[i * 128:(i + 1) * 128, :], zt[:, :], R=[zt.r()])
          for i in range(8):
              dma('sp', y_s[i * 128:(i + 1) * 128, :], zt[:, :], R=[zt.r()])
          for o_, dk_ in ((o_gdn, 128), (o_gla, 64), (o_hg, 128)):
              for a_ in range(2):
                  for b_ in range(DEPTH):
                      dma('sp', o_[a_, b_].rearrange("t h k v -> k (t h) v"),
                          zt[0:dk_, 0:1024].rearrange("k (th v) -> k th v", v=128), R=[zt.r()])
          for a_ in range(2):
              for b_ in range(DEPTH):
                  for i in range(2):
                      dma('sp', o_ckv[a_, b_, i * 128:(i + 1) * 128, :], zt[:, 0:512], R=[zt.r()])
                      dma('sp', o_kpe[a_, b_, i * 128:(i + 1) * 128, :], zt[:, 0:64], R=[zt.r()])

    with Phase() as ph:
        scT = ph("scT", [128, 16, 2], BF16)
        A(lambda e: e.activation(scT[:, :, :], cs[:, CS['cvT']:CS['cvT'] + 32].rearrange("p (k t) -> p k t", t=2),
                                 AF.Silu), [cs.r()], [scT.r()])
        for l in range(DEPTH):
            for g in range(24):
                ps = psum()
                for cc in range(4):
                    c = g * 4 + cc
                    wt, wv = load_w(wada_d[l][:, c * 2048:(c + 1) * 2048], 16, 128)
                    for kc in range(16):
                        PE(lambda e, wv=wv, kc=kc, ps=ps, cc=cc: e.matmul(
                            ps[:, cc * 2:cc * 2 + 2], wv[:, kc, :], scT[:, kc, :], start=(kc == 0), stop=(kc == 15)),
                            [wt.r(), scT.r()], [ps.r()])
                for kind in range(2):
                    V(lambda e, ps=ps, g=g, kind=kind, l=l: e.tensor_tensor(
                        modT[l][:, kind, g * 4:g * 4 + 4],
                        ps[:, 0:8].rearrange("p (c t) -> p c t", t=2)[:, :, kind],
                        pm[l][:, PM_COLS['b_ada'] + g * 4:PM_COLS['b_ada'] + g * 4 + 4], ALU.add),
                        [ps.r(), pm[l].r()], [modT[l].r()])
            for j in (1, 4):
                V(lambda e, l=l, j=j: e.tensor_scalar_add(
                    modT[l][:, :, j * 16:(j + 1) * 16], modT[l][:, :, j * 16:(j + 1) * 16], 1.0),
                    [modT[l].r()], [modT[l].r()])
            dbg_dump('modT%d' % l, modT[l][:, :, :].rearrange("p a b -> p (a b)"), [modT[l].r()])

    with Phase() as ph:
        xtok = [ph("xtok%d" % i, [128, D]) for i in range(2)]
        xTb = ph("xTb", [128, 16, 512])
        for tt in range(T // 128):
            xt = xtok[tt % 2]
            src = xp[tt * 128:(tt + 1) * 128, :] if tt < 4 else xs[(tt - 4) * 128:(tt - 3) * 128, :]
            dma('sp', xt[:, :], src, W=[xt.r()])
            for g in range(4):
                ps = psum()
                for cc in range(4):
                    c = g * 4 + cc
                    PE(lambda e, ps=ps, cc=cc, c=c, xt=xt: e.transpose(
                        ps[:, cc * 128:(cc + 1) * 128], xt[:, c * 128:(c + 1) * 128], ident),
                        [xt.r(), cs.r()], [ps.r()])
                dst = xTb[:, g * 4:(g + 1) * 4, (tt % 4) * 128:(tt % 4 + 1) * 128]
                srcp = ps[:, :].rearrange("p (c t) -> p c t", c=4)
                if g % 2 == 0:
                    A(lambda e, dst=dst, srcp=srcp: e.copy(dst, srcp), [ps.r()], [xTb.r(tt % 4)])
                else:
                    V(lambda e, dst=dst, srcp=srcp: e.tensor_copy(dst, srcp), [ps.r()], [xTb.r(tt % 4)])
            if tt % 4 == 3:
                b = tt // 4
                dma('sp', xTd[:, :, b * 512:(b + 1) * 512].rearrange("c p t -> p c t"), xTb[:, :, :],
                    R=[xTb.r(i) for i in range(4)], W=[xTd_res[b]])

    SEQS = [(0, LP, 0, 0), (LP, LP, 0, 1), (2 * LP, LS, 1, 2)]
    TP = T + 12
    SEGS = [(0, 0, 256, 0), (0, 256, 256, 256 + 4), (1, 0, 512, 512 + 8), (2, 0, 512, 1024 + 8)]

    def bc3(ap2d, n):
        p, f = ap2d.shape
        return ap2d.unsqueeze(1).to_broadcast([p, n, f])

    def gdn_unit(l, h, ph0, bet, gg):
      cut(0)
      with Phase() as ph:
        raw = ph("raw", [128, TP])
        cv = ph("cv", [128, 3, TP])
        sq = ph("sq", [128, TP])
        rst = ph("rst", [128, 512])
        zs = ph("zs", [128, T])
        oacc = ph("oacc", [128, T])
        V(lambda e: e.memset(raw[:, :], 0.0), [], [raw.r()])
        for i_ in range(3):
            G(lambda e, i_=i_: e.memset(cv[:, i_, :], 0.0), [], [cv.r(i_)])
        for i, nm in enumerate(('gdn_q', 'gdn_k', 'gdn_v')):
            wt, wv, n_ = win_piece(l, nm, h)
            for b in range(NBLK):
                ps = psum()
                proj_h(wt, wv, 128, b, ps[:, :], ps.r())
                for (sb_, so, sn, cd) in SEGS:
                    if sb_ != b:
                        continue
                    A(lambda e, ps=ps, so=so, sn=sn, cd=cd: e.copy(raw[:, cd + 2:cd + 2 + sn], ps[:, so:so + sn]),
                      [ps.r()], [raw.r()])
            acc = cv[:, i, 2:TP - 2]
            ccol = lambda tap: pm[l][:, PM_COLS['conv'] + (i * 4 + h) * 5 + tap:PM_COLS['conv'] + (i * 4 + h) * 5 + tap + 1]
            V(lambda e, acc=acc, ccol=ccol: e.tensor_scalar(acc, raw[:, 0:TP - 4], ccol(0), None, ALU.mult),
              [raw.r(), pm[l].r()], [cv.r(i)])
            for tap in range(1, 5):
                V(lambda e, acc=acc, ccol=ccol, tap=tap: e.scalar_tensor_tensor(
                    acc, raw[:, tap:TP - 4 + tap], ccol(tap), acc, ALU.mult, ALU.add),
                    [raw.r(), pm[l].r(), cv.r(i)], [cv.r(i)])
            A(lambda e, acc=acc: e.activation(acc, acc, AF.Silu), [cv.r(i)], [cv.r(i)])
            if i < 2:
                A(lambda e, acc=acc: e.activation(sq[:, 2:TP - 2], acc, AF.Square), [cv.r(i)], [sq.r()])
                for (sb_, so, sn, cd) in SEGS:
                    ps = psum()
                    PE(lambda e, ps=ps, sn=sn, cd=cd: e.matmul(ps[:, 0:sn], ones, sq[:, cd + 2:cd + 2 + sn],
                                                               start=True, stop=True), [sq.r(), cs.r()], [ps.r()])
                    rstd_from(ps[:, 0:sn], rst[:, 0:sn], ps.r(), rst.r())
                    sc_ = (128.0 ** -0.5) if i == 0 else 1.0
                    V(lambda e, i=i, sn=sn, cd=cd, sc_=sc_: e.scalar_tensor_tensor(
                        cv[:, i, cd + 2:cd + 2 + sn], cv[:, i, cd + 2:cd + 2 + sn], sc_, rst[:, 0:sn], ALU.mult, ALU.mult),
                        [cv.r(i), rst.r()], [cv.r(i)])
        wt, wv, n_ = win_piece(l, 'gdn_z', h)
        for b in range(NBLK):
            ps = psum()
            proj_h(wt, wv, 128, b, ps[:, :], ps.r())
            A(lambda e, ps=ps, b=b: e.activation(zs[:, b * 512:(b + 1) * 512], ps[:, :], AF.Silu), [ps.r()], [zs.r()])
        if h == 0 and l == 0:
            dbg_dump('cv', cv[:, :, :].rearrange("p a b -> p (a b)"), [cv.r(i) for i in range(3)])
        cut(1)
        for (tok0, L, kind, idx) in SEQS:
          with Phase() as ps_:
            nch = min(8, L // 64)
            nbatch = (L // 64) // nch
            geo = {}
            QT = lambda n: cv[:, 0, geo['cb'] + n * 64:geo['cb'] + (n + 1) * 64]
            KT = lambda n: cv[:, 1, geo['cb'] + n * 64:geo['cb'] + (n + 1) * 64]
            VT = lambda n: cv[:, 2, geo['cb'] + n * 64:geo['cb'] + (n + 1) * 64]
            ktok = ps_("ktok", [64, nch, 128])
            vtok = ps_("vtok", [64, nch, 128])
            R2 = ps_("R2", [64, nch, 64])
            eg = ps_("eg", [128, nch, 64])
            rb = ps_("rb", [64, nch, 64])
            gc = ps_("gc", [64, nch])
            gl = ps_("gl", [128, nch])
            cdec = ps_("cdec", [128, nch])
            kdsc = ps_("kdsc", [64, nch])
            bw = ps_("bw", [64, nch])
            X = ps_("X", [64, nch, 64])
            dT = ps_("dT", [64, nch, 64])
            dd = ps_("dd", [64, nch, 64])
            M = ps_("M", [64, nch, 64])
            MT = ps_("MT", [64, nch, 64])
            Pa = ps_("Pa", [64, nch, 64])
            PTa = ps_("PTa", [64, nch, 64])
            RT = ps_("RT", [64, nch, 64])
            AT = ps_("AT", [128, nch, 64])
            VH = ps_("VH", [128, nch, 128])
            KG = ps_("KG", [128, nch, 128])
            wtok = ps_("wtok", [64, nch, 128])
            Sall = ps_("Sall", [128, nch + 1, 128])
            ub = ps_("ub", [64, nch, 128])
            wT = ps_("wT", [128, nch, 64])
            kdec = ps_("kdec", [64, nch, 128])
            qdT = ps_("qdT", [128, nch, 64])
            u = ps_("u", [128, nch, 128])
            V(lambda e: e.memset(AT[:, :, :], 0.0), [], [AT.r()])
            V(lambda e: e.memset(u[:, :, :], 0.0), [], [u.r()])
            for dr in DIRS:
                PASSNO[0] += 1
                U = cmat('le') if dr == 0 else cmat('ge')
                inclT = U
                strict = cmat('gt') if dr == 0 else cmat('lt')
                col = dr * 4 + h
                last = 63 if dr == 0 else 0
                gcolv = lambda n: gg[:, geo['n0'] + n, col:col + 1]
                bcolv = lambda n: bet[:, geo['n0'] + n, col:col + 1]
                if kind == 0:
                    V(lambda e: e.memset(Sall[:, 0, :], 0.0), [], [Sall.r()])
                else:
                    dma('sp', Sall[:, 0, :], st_gdn[l, dr, h], W=[Sall.r()])
                border = range(nbatch) if dr == 0 else range(nbatch - 1, -1, -1)
                for bix_, bi in enumerate(border):
                    if bix_ > 0:
                        V(lambda e: e.tensor_copy(Sall[:, 0, :], Sall[:, nch, :]), [Sall.r()], [Sall.r()])
                    tokb = tok0 + bi * nch * 64
                    n0 = tokb // 64
                    cb = tokb + 4 * idx + 2
                    geo['cb'] = cb
                    geo['n0'] = n0
                    for (src, dstt) in ((KT, ktok), (VT, vtok)):
                        for n4 in range(0, nch, 4):
                            ps = psum()
                            for n in range(n4, n4 + 4):
                                PE(lambda e, ps=ps, n=n, n4=n4, src=src: e.transpose(
                                    ps[0:64, (n - n4) * 128:(n - n4 + 1) * 128], src(n), ident),
                                    [cv.r(1), cv.r(2), cs.r()], [ps.r()])
                            A(lambda e, ps=ps, n4=n4, dstt=dstt: e.copy(
                                dstt[:, n4:n4 + 4, :], ps[0:64, :].rearrange("p (n d) -> p n d", n=4)), [ps.r()], [dstt.r()])
                    cut(2)
                    V(lambda e: e.tensor_tensor(R2[:, :, :], bc3(U, nch), bcl(gg[:, n0:n0 + nch, col:col + 1], 64), ALU.mult),
                      [gg.r(), cs.r()], [R2.r()])
                    for n8 in range(0, nch, 8):
                        w8 = min(8, nch - n8)
                        ps = psum()
                        PE(lambda e, ps=ps, n8=n8, w8=w8: e.matmul(
                            ps[:, 0:w8 * 64], ones[0:64, :], R2[:, n8:n8 + w8, :].rearrange("p n j -> p (n j)"),
                            start=True, stop=True), [R2.r(), cs.r()], [ps.r()])
                        A(lambda e, ps=ps, n8=n8, w8=w8: e.activation(
                            eg[:, n8:n8 + w8, :].rearrange("p n j -> p (n j)"), ps[:, 0:w8 * 64], AF.Exp), [ps.r()], [eg.r()])
                        V(lambda e, ps=ps, n8=n8, w8=w8: e.tensor_copy(
                            rb[:, n8:n8 + w8, :].rearrange("p n j -> p (n j)"), ps[0:64, 0:w8 * 64]), [ps.r()], [rb.r()])
                        V(lambda e, ps=ps, n8=n8, w8=w8: e.tensor_copy(
                            gl[:, n8:n8 + w8], ps[:, 0:w8 * 64].rearrange("p (n j) -> p n j", j=64)[:, :, last]), [ps.r()], [gl.r()])
                    cut(2.2)
                    ps = psum()
                    PE(lambda e, ps=ps, U=U: e.matmul(ps[0:64, 0:nch], U, gg[:, n0:n0 + nch, col], start=True, stop=True),
                       [gg.r(), cs.r()], [ps.r()])
                    A(lambda e, ps=ps: e.copy(gc[:, :], ps[0:64, 0:nch]), [ps.r()], [gc.r()])
                    cut(2.5)
                    A(lambda e: e.activation(cdec[:, :], gl[:, :], AF.Exp), [gl.r()], [cdec.r()])
                    V(lambda e: e.tensor_tensor(kdsc[:, :], gl[0:64, :], gc[:, :], ALU.subtract), [gl.r(), gc.r()], [kdsc.r()])
                    A(lambda e: e.activation(kdsc[:, :], kdsc[:, :], AF.Exp), [kdsc.r()], [kdsc.r()])
                    A(lambda e: e.activation(bw[:, :], gc[:, :], AF.Exp), [gc.r()], [bw.r()])
                    V(lambda e: e.tensor_tensor(bw[:, :], bw[:, :], bet[:, n0:n0 + nch, col], ALU.mult), [bw.r(), bet.r()], [bw.r()])
                    cut(2.7)
                    V(lambda e: e.tensor_tensor(X[:, :, :], rb[:, :, :], bcl(gc[:, :].unsqueeze(2), 64), ALU.subtract), [rb.r(), gc.r()], [X.r()])
                    V(lambda e: e.tensor_scalar(dd[:, :, :], X[:, :, :], 0.0, None, ALU.max), [X.r()], [dd.r()])
                    V(lambda e: e.tensor_scalar(X[:, :, :], X[:, :, :], 0.0, None, ALU.min), [X.r()], [X.r()])
                    cut(3)
                    A(lambda e: e.activation(dT[:, :, :], X[:, :, :], AF.Exp), [X.r()], [dT.r()])
                    A(lambda e: e.activation(dd[:, :, :], dd[:, :, :], AF.Exp, scale=-1.0), [dd.r()], [dd.r()])
                    V(lambda e, inclT=inclT: e.tensor_tensor(dT[:, :, :], dT[:, :, :], bc3(inclT, nch), ALU.mult),
                      [dT.r(), cs.r()], [dT.r()])
                    V(lambda e, strict=strict: e.tensor_tensor(dd[:, :, :], dd[:, :, :], bc3(strict, nch), ALU.mult),
                      [dd.r(), cs.r()], [dd.r()])
                    cut(4)
                    for n8 in range(0, nch, 8):
                        w8 = min(8, nch - n8)
                        psA = psum()
                        psQ = psum()
                        for n in range(n8, n8 + w8):
                            PE(lambda e, n=n, n8=n8, psA=psA: e.matmul(psA[0:64, (n - n8) * 64:(n - n8 + 1) * 64], KT(n), KT(n),
                                                                        start=True, stop=True), [cv.r(1)], [psA.r()])
                            PE(lambda e, n=n, n8=n8, psQ=psQ: e.matmul(psQ[0:64, (n - n8) * 64:(n - n8 + 1) * 64], KT(n), QT(n),
                                                                        start=True, stop=True), [cv.r(1), cv.r(0)], [psQ.r()])
                        V(lambda e: e.tensor_tensor(M[:, n8:n8 + w8, :].rearrange("p n j -> p (n j)"), psA[0:64, 0:w8 * 64],
                                                    dd[:, n8:n8 + w8, :].rearrange("p n j -> p (n j)"), ALU.mult), [psA.r(), dd.r()], [M.r()])
                        V(lambda e: e.tensor_tensor(M[:, n8:n8 + w8, :], M[:, n8:n8 + w8, :],
                                                    bcl(bet[:, n0 + n8:n0 + n8 + w8, col:col + 1], 64), ALU.mult), [M.r(), bet.r()], [M.r()])
                        V(lambda e, n8=n8, w8=w8, psQ=psQ: e.tensor_tensor(
                            AT[0:64, n8:n8 + w8, :].rearrange("p n j -> p (n j)"), psQ[0:64, 0:w8 * 64],
                            dT[:, n8:n8 + w8, :].rearrange("p n j -> p (n j)"), ALU.mult), [psQ.r(), dT.r()], [AT.r()])
                    cut(5)
                    for n8 in range(0, nch, 8):
                        w8 = min(8, nch - n8)
                        ps = psum()
                        for n in range(n8, n8 + w8):
                            PE(lambda e, n=n, n8=n8, ps=ps: e.transpose(ps[0:64, (n - n8) * 64:(n - n8 + 1) * 64], M[:, n, :], ident[0:64, 0:64]),
                               [M.r(), cs.r()], [ps.r()])
                        A(lambda e, n8=n8, w8=w8, ps=ps: e.copy(MT[:, n8:n8 + w8, :].rearrange("p n j -> p (n j)"), ps[0:64, 0:w8 * 64]),
                          [ps.r()], [MT.r()])
                    V(lambda e: e.tensor_tensor(RT[:, :, :], bc3(ident[0:64, 0:64], nch), MT[:, :, :], ALU.subtract),
                      [MT.r(), cs.r()], [RT.r()])
                    Pc, PTc = M, MT
                    Pn, PTn = Pa, PTa
                    for it in range(5):
                        FILL(2)
                        for n8 in range(0, nch, 8):
                            w8 = min(8, nch - n8)
                            p1 = psum()
                            p2 = psum()
                            for n in range(n8, n8 + w8):
                                sl = slice((n - n8) * 64, (n - n8 + 1) * 64)
                                PE(lambda e, n=n, sl=sl, p1=p1, Pc=Pc, PTc=PTc: e.matmul(p1[0:64, sl], PTc[:, n, :], Pc[:, n, :], start=True, stop=True),
                                   [Pc.r(), PTc.r()], [p1.r()])
                                PE(lambda e, n=n, sl=sl, p2=p2, Pc=Pc, PTc=PTc: e.matmul(p2[0:64, sl], Pc[:, n, :], PTc[:, n, :], start=True, stop=True),
                                   [Pc.r(), PTc.r()], [p2.r()])
                            A(lambda e, n8=n8, w8=w8, p1=p1, Pn=Pn: e.copy(Pn[:, n8:n8 + w8, :].rearrange("p n j -> p (n j)"), p1[0:64, 0:w8 * 64]),
                              [p1.r()], [Pn.r()])
                            V(lambda e, n8=n8, w8=w8, p2=p2, PTn=PTn: e.tensor_copy(PTn[:, n8:n8 + w8, :].rearrange("p n j -> p (n j)"), p2[0:64, 0:w8 * 64]),
                              [p2.r()], [PTn.r()])
                        for n8 in range(0, nch, 8):
                            w8 = min(8, nch - n8)
                            p3 = psum()
                            for n in range(n8, n8 + w8):
                                sl = slice((n - n8) * 64, (n - n8 + 1) * 64)
                                PE(lambda e, n=n, sl=sl, p3=p3, Pn=Pn: e.matmul(p3[0:64, sl], Pn[:, n, :], RT[:, n, :], start=True, stop=True),
                                   [Pn.r(), RT.r()], [p3.r()])
                            V(lambda e, n8=n8, w8=w8, p3=p3: e.tensor_tensor(
                                RT[:, n8:n8 + w8, :].rearrange("p n j -> p (n j)"), RT[:, n8:n8 + w8, :].rearrange("p n j -> p (n j)"),
                                p3[0:64, 0:w8 * 64], ALU.add), [p3.r(), RT.r()], [RT.r()])
                        Pc, PTc, Pn, PTn = Pn, PTn, Pc, PTc
                    cut(6)
                    V(lambda e: e.tensor_tensor(VH[0:64, :, :], vtok[:, :, :], bcl(bet[:, n0:n0 + nch, col:col + 1], 128), ALU.mult),
                      [vtok.r(), bet.r()], [VH.r()])
                    G(lambda e: e.tensor_tensor(KG[0:64, :, :], ktok[:, :, :], bcl(bw[:, :].unsqueeze(2), 128), ALU.mult), [ktok.r(), bw.r()], [KG.r()])
                    G(lambda e: e.tensor_tensor(kdec[:, :, :], ktok[:, :, :], bcl(kdsc[:, :].unsqueeze(2), 128), ALU.mult), [ktok.r(), kdsc.r()], [kdec.r()])
                    for n4 in range(0, nch, 4):
                        ps = psum()
                        for n in range(n4, n4 + 4):
                            PE(lambda e, n=n, n4=n4, ps=ps: e.matmul(ps[0:64, (n - n4) * 128:(n - n4 + 1) * 128], RT[:, n, :], VH[0:64, n, :],
                                                                       start=True, stop=True), [RT.r(), VH.r()], [ps.r()])
                        A(lambda e, n4=n4, ps=ps: e.copy(ub[:, n4:n4 + 4, :].rearrange("p n d -> p (n d)"), ps[0:64, :]), [ps.r()], [ub.r()])
                    for n8 in range(0, nch, 8):
                        w8 = min(8, nch - n8)
                        ps = psum()
                        for n in range(n8, n8 + w8):
                            PE(lambda e, n=n, n8=n8, ps=ps: e.matmul(ps[:, (n - n8) * 64:(n - n8 + 1) * 64], KG[0:64, n, :], RT[:, n, :],
                                                                       start=True, stop=True), [RT.r(), KG.r()], [ps.r()])
                        V(lambda e, n8=n8, w8=w8, ps=ps: e.tensor_copy(wT[:, n8:n8 + w8, :].rearrange("p n j -> p (n j)"), ps[:, 0:w8 * 64]),
                          [ps.r()], [wT.r()])
                    V(lambda e: e.tensor_tensor(qdT[:, :, :].rearrange("p n j -> p (n j)"), cv[:, 0, cb:cb + nch * 64],
                                                eg[:, :, :].rearrange("p n j -> p (n j)"), ALU.mult), [cv.r(0), eg.r()], [qdT.r()])
                    cut(7)
                    for n4 in range(0, nch, 4):
                        ps = psum()
                        for n in range(n4, n4 + 4):
                            PE(lambda e: e.matmul(ps[0:64, (n - n4) * 128:(n - n4 + 1) * 128], RT[:, n, :], KG[0:64, n, :], start=True, stop=True),
                               [RT.r(), KG.r()], [ps.r()])
                        V(lambda e: e.tensor_copy(wtok[:, n4:n4 + 4, :].rearrange("p n d -> p (n d)"), ps[0:64, :]), [ps.r()], [wtok.r()])
                    FILL(1)
                    for n4 in range(0, nch, 4):
                        ps = psum()
                        for n in range(n4, n4 + 4):
                            PE(lambda e: e.matmul(ps[:, (n - n4) * 128:(n - n4 + 1) * 128], kdec[:, n, :], ub[:, n, :], start=True, stop=True),
                               [kdec.r(), ub.r()], [ps.r()])
                        A(lambda e: e.copy(VH[:, n4:n4 + 4, :].rearrange("p n d -> p (n d)"), ps[:, :]), [ps.r()], [VH.r()])
                    for n4 in range(0, nch, 4):
                        ps = psum()
                        for n in range(n4, n4 + 4):
                            PE(lambda e: e.matmul(ps[:, (n - n4) * 128:(n - n4 + 1) * 128], wtok[:, n, :], kdec[:, n, :], start=True, stop=True),
                               [wtok.r(), kdec.r()], [ps.r()])
                        for n in range(n4, n4 + 4):
                            V(lambda e: e.scalar_tensor_tensor(KG[:, n, :], ident, cdec[:, n:n + 1], ps[:, (n - n4) * 128:(n - n4 + 1) * 128],
                                                               ALU.mult, ALU.subtract), [cs.r(), cdec.r(), ps.r()], [KG.r()])
                    order = list(range(nch)) if dr == 0 else list(range(nch - 1, -1, -1))
                    step_of = {n: s_i for s_i, n in enumerate(order)}
                    for s_i, n in enumerate(order):
                        if s_i % 2 == 0:
                            FILL(1)
                        pS = psum()
                        PE(lambda e: e.matmul(pS[:, 0:128], KG[:, n, :], Sall[:, s_i, :], start=True, stop=True), [KG.r(), Sall.r()], [pS.r()])
                        V(lambda e: e.tensor_tensor(Sall[:, s_i + 1, :], pS[:, 0:128], VH[:, n, :], ALU.add), [pS.r(), VH.r()], [Sall.r()])
                    for n4 in range(0, nch, 4):
                        ps = psum()
                        for n in range(n4, n4 + 4):
                            PE(lambda e: e.matmul(ps[0:64, (n - n4) * 128:(n - n4 + 1) * 128], wT[:, n, :], Sall[:, step_of[n], :], start=True, stop=True),
                               [wT.r(), Sall.r()], [ps.r()])
                        V(lambda e: e.tensor_tensor(u[0:64, n4:n4 + 4, :].rearrange("p n d -> p (n d)"),
                                                    ub[:, n4:n4 + 4, :].rearrange("p n d -> p (n d)"), ps[0:64, :], ALU.subtract), [ub.r(), ps.r()], [u.r()])
                    FILL(1)
                    po = psum()
                    for n in range(nch):
                        PE(lambda e: e.matmul(po[:, n * 64:(n + 1) * 64], Sall[:, step_of[n], :], qdT[:, n, :], start=True, stop=False), [Sall.r(), qdT.r()], [po.r()])
                        PE(lambda e: e.matmul(po[:, n * 64:(n + 1) * 64], u[:, n, :], AT[:, n, :], start=False, stop=True), [u.r(), AT.r()], [po.r()])
                    osl = oacc[:, tokb:tokb + nch * 64]
                    if dr == DIRS[0]:
                        A(lambda e: e.copy(osl, po[:, 0:nch * 64]), [po.r()], [oacc.r(idx)])
                    else:
                        V(lambda e: e.tensor_tensor(osl, osl, po[:, 0:nch * 64], ALU.add), [po.r(), oacc.r(idx)], [oacc.r(idx)])
                cut(10)
                if kind == 0:
                    dma('sp', o_gdn[idx, l, dr, h], Sall[:, nch, :], R=[Sall.r()])
                cut(11 + idx * 2 + dr)
        if h == 0 and l == 0:
            dbg_dump('oacc', oacc[:, :], [oacc.r(i) for i in range(3)])
        ob = ph("ob", [128, T], BF16)
        A(lambda e: e.activation(sq[:, 0:T], oacc[:, :], AF.Square), [oacc.r(i) for i in range(3)], [sq.r()])
        for b in range(NBLK):
            ps = psum()
            PE(lambda e, ps=ps, b=b: e.matmul(ps[:, :], cs[:, CS['mean128']:CS['mean128'] + 128], sq[:, b * 512:(b + 1) * 512],
                                              start=True, stop=True), [sq.r(), cs.r()], [ps.r()])
            rstd_from(ps[:, :], rst[:, :], ps.r(), rst.r())
            V(lambda e, b=b: e.tensor_tensor(oacc[:, b * 512:(b + 1) * 512], oacc[:, b * 512:(b + 1) * 512], rst[:, :], ALU.mult),
              [rst.r()] + [oacc.r(i) for i in range(3)], [oacc.r(i) for i in range(3)])
            V(lambda e, b=b: e.scalar_tensor_tensor(ob[:, b * 512:(b + 1) * 512], oacc[:, b * 512:(b + 1) * 512],
                                                    pm[l][:, PM_COLS['gdn_norm']:PM_COLS['gdn_norm'] + 1], zs[:, b * 512:(b + 1) * 512],
                                                    ALU.mult, ALU.mult), [zs.r(), pm[l].r()] + [oacc.r(i) for i in range(3)], [ob.r()])
        if stage > 4:
            dma('sp', obd[0 * 4 + h], ob[:, :], R=[ob.r()], W=[obd_res[0 * 4 + h]])
        if h == 0 and l == 0:
            dbg_dump('ob', ob[:, :], [ob.r()])

    def bcl(ap3, c):
        p, n, _ = ap3.shape
        return ap3.to_broadcast([p, n, c])

    CH = 32

    def gla_unit(l, h, mixer, shared):
      PK = 64 if mixer == 'gla' else 128
      bi_ = 1 if mixer == 'gla' else 2
      rmask = shared['rmask']
      with Phase() as ph:
        qT = ph("qT", [PK, T])
        kTs = [ph("kT0", [PK, T])] if mixer == 'gla' else [ph("kT0", [PK, T]), ph("kT1", [PK, T])]
        vT = ph("vT", [128, T])
        las = [ph("la0", [PK, T]), ph("la1", [PK, T])]
        gate = ph("gate", [128, T], BF16)
        oacc = ph("oacc", [128, T])
        bsc = ph("bsc", [PK, T])
        qd = ph("qd", [PK, T])
        kt = ph("kt", [PK, T])
        kd = ph("kd", [PK, T])
        tE = ph("tE", [PK, T])
        xs = ph("xs", [128, 512])
        t1 = ph("t1", [128, 512])

        def inproj(name, idx, fn):
            wt, wv, n_ = win_piece(l, name, idx)
            for b in range(NBLK):
                ps = psum()
                proj_h(wt, wv, n_, b, ps[0:n_, :], ps.r())
                fn(b, ps, n_)
        sl = lambda b: slice(b * 512, (b + 1) * 512)
        if mixer == 'gla':
            inproj('gla_q', h, lambda b, ps, n_: A(lambda e: e.copy(qT[:, sl(b)], ps[0:64, :]), [ps.r()], [qT.r()]))
            inproj('gla_k', h, lambda b, ps, n_: V(lambda e: e.tensor_copy(kTs[0][:, sl(b)], ps[0:64, :]), [ps.r()], [kTs[0].r()]))
            inproj('gla_v', h, lambda b, ps, n_: A(lambda e: e.copy(vT[:, sl(b)], ps[:, :]), [ps.r()], [vT.r()]))
            inproj('gla_r', h, lambda b, ps, n_: A(lambda e: e.activation(gate[:, sl(b)], ps[:, :], AF.Silu), [ps.r()], [gate.r()]))
            glr, gw2 = shared['glr'], shared['gw2']
            for dr in range(2):
                for b in range(NBLK):
                    ps = psum()
                    PE(lambda e: e.matmul(ps[0:64, :], gw2[:, dr * 256 + h * 64:dr * 256 + (h + 1) * 64], glr[dr][:, sl(b)],
                                          start=True, stop=True), [gw2.r(), glr[dr].r()], [ps.r()])
                    gb = pm[l][0:64, PM_COLS['gla_gb'] + dr * 4 + h:PM_COLS['gla_gb'] + dr * 4 + h + 1]
                    A(lambda e: e.activation(xs[0:64, :], ps[0:64, :], AF.Identity, bias=gb), [ps.r(), pm[l].r()], [xs.r()])
                    A(lambda e: e.activation(t1[0:64, :], xs[0:64, :], AF.Abs), [xs.r()], [t1.r()])
                    A(lambda e: e.activation(t1[0:64, :], t1[0:64, :], AF.Exp, scale=-1.0), [t1.r()], [t1.r()])
                    A(lambda e: e.activation(t1[0:64, :], t1[0:64, :], AF.Ln, bias=ONEC[0:64, :]), [t1.r(), cs.r()], [t1.r()])
                    V(lambda e: e.scalar_tensor_tensor(xs[0:64, :], xs[0:64, :], 0.0, t1[0:64, :], ALU.min, ALU.subtract),
                      [xs.r(), t1.r()], [xs.r()])
                    V(lambda e: e.tensor_scalar(las[dr][:, sl(b)], xs[0:64, :], 1.0 / 16.0, None, ALU.mult), [xs.r()], [las[dr].r()])
        else:
            inproj('hg_q', h, lambda b, ps, n_: A(lambda e: e.activation(qT[:, sl(b)], ps[:, :], AF.Silu), [ps.r()], [qT.r()]))
            inproj('hg_i', h, lambda b, ps, n_: A(lambda e: e.copy(vT[:, sl(b)], ps[:, :]), [ps.r()], [vT.r()]))
            inproj('hg_g', h, lambda b, ps, n_: A(lambda e: e.activation(gate[:, sl(b)], ps[:, :], AF.Sigmoid), [ps.r()], [gate.r()]))
            lbt, oml = shared['lbt'], shared['oml']
            for dr in range(2):
                j = dr * 4 + h

                def fz(b, ps, n_, dr=dr, j=j):
                    A(lambda e: e.activation(xs[:, :], ps[:, :], AF.Sigmoid), [ps.r()], [xs.r()])
                    V(lambda e: e.tensor_scalar(xs[:, :], xs[:, :], oml[:, j:j + 1], lbt[:, j:j + 1], ALU.mult, ALU.add),
                      [xs.r(), oml.r(), lbt.r()], [xs.r()])
                    A(lambda e: e.activation(las[dr][:, sl(b)], xs[:, :], AF.Ln), [xs.r()], [las[dr].r()])
                    V(lambda e: e.tensor_scalar(kTs[dr][:, sl(b)], xs[:, :], -1.0, 1.0, ALU.mult, ALU.add), [xs.r()], [kTs[dr].r()])
                inproj('hg_f', j, fz)
        NB = 8
        for dr in range(2):
            kT = kTs[min(dr, len(kTs) - 1)]
            la = las[dr]
            V(lambda e: e.tensor_tensor_scan(bsc[:, :], rmask[0:PK, :], la[:, :], 0.0, ALU.mult, ALU.add),
              [rmask.r(), la.r()], [bsc.r()])
            b3 = bsc[:, :].rearrange("p (n c) -> p n c", c=CH)
            tot3 = b3[:, :, CH - 1:CH]
            if dr == 1:
                V(lambda e: e.tensor_tensor(tE[:, :], la[:, :], bsc[:, :], ALU.subtract), [la.r(), bsc.r()], [tE.r()])
                V(lambda e: e.tensor_tensor(tE[:, :].rearrange("p (n c) -> p n c", c=CH),
                                            tE[:, :].rearrange("p (n c) -> p n c", c=CH), bcl(tot3, CH), ALU.add),
                  [tE.r(), bsc.r()], [tE.r()])
                bcur = tE
            else:
                bcur = bsc
            V(lambda e: e.tensor_tensor(kd[:, :].rearrange("p (n c) -> p n c", c=CH), bcl(tot3, CH),
                                        bcur[:, :].rearrange("p (n c) -> p n c", c=CH), ALU.subtract),
              [bsc.r(), bcur.r()], [kd.r()])
            A(lambda e: e.activation(kd[:, :], kd[:, :], AF.Exp), [kd.r()], [kd.r()])
            V(lambda e: e.tensor_tensor(kd[:, :], kd[:, :], kT[:, :], ALU.mult), [kd.r(), kT.r()], [kd.r()])
            cdec = ph("cdec%d" % dr, [PK, T // CH])
            A(lambda e: e.activation(cdec[:, :], tot3.rearrange("p n c -> p (n c)"), AF.Exp), [bsc.r()], [cdec.r()])
            A(lambda e: e.activation(qd[:, :], bcur[:, :], AF.Exp), [bcur.r()], [qd.r()])
            A(lambda e: e.activation(kt[:, :], bcur[:, :], AF.Exp, scale=-1.0), [bcur.r()], [kt.r()])
            qs_ = (64.0 ** -0.5) if mixer == 'gla' else 1.0
            V(lambda e: e.scalar_tensor_tensor(qd[:, :], qT[:, :], qs_, qd[:, :], ALU.mult, ALU.mult), [qT.r(), qd.r()], [qd.r()])
            G(lambda e: e.tensor_tensor(kt[:, :], kt[:, :], kT[:, :], ALU.mult), [kt.r(), kT.r()], [kt.r()])
            maskT = cmat('le32', 32) if dr == 0 else cmat('ge32', 32)
            if dr == 0:
                nb = NB
                vtok = ph("vtok", [CH, nb, 128])
                kdtok = ph("kdtok", [CH, nb, PK])
                ATm = ph("ATm", [CH, nb, CH])
                KV = ph("KV", [PK, nb, 128])
                Sall = ph("Sall", [PK, nb + 1, 128])
                otmp = ph("otmp", [128, nb * CH])
            for (tok0, L, kind, idx) in SEQS:
              if True:
                nbatch = (L // CH) // nb
                if kind == 0:
                    V(lambda e: e.memset(Sall[:, 0, :], 0.0), [], [Sall.r()])
                else:
                    st_in = st_gla if mixer == 'gla' else st_hg
                    dma('sp', Sall[:, 0, :], st_in[l, dr, h], W=[Sall.r()])
                border = range(nbatch) if dr == 0 else range(nbatch - 1, -1, -1)
                for bix, bi in enumerate(border):
                    tokb = tok0 + bi * nb * CH
                    c0 = tokb // CH
                    csl = lambda n: slice(tokb + n * CH, tokb + (n + 1) * CH)
                    if bix > 0:
                        V(lambda e: e.tensor_copy(Sall[:, 0, :], Sall[:, nb, :]), [Sall.r()], [Sall.r()])
                    for (srcT, dstt, pw) in ((vT, vtok, 128), (kd, kdtok, PK)):
                        per = 512 // pw
                        for n4 in range(0, nb, per):
                            ps = psum()
                            for n in range(n4, min(nb, n4 + per)):
                                PE(lambda e, n=n: e.transpose(ps[0:CH, (n - n4) * pw:(n - n4 + 1) * pw], srcT[:, csl(n)], ident[0:pw, 0:pw]),
                                   [srcT.r(), cs.r()], [ps.r()])
                            w_ = min(nb, n4 + per) - n4
                            A(lambda e: e.copy(dstt[:, n4:n4 + w_, :].rearrange("p n d -> p (n d)"), ps[0:CH, 0:w_ * pw]),
                              [ps.r()], [dstt.r()])
                    FILL(2)
                    ps = psum()
                    for n in range(nb):
                        PE(lambda e, n=n: e.matmul(ps[0:CH, n * CH:(n + 1) * CH], kt[:, csl(n)], qd[:, csl(n)], start=True, stop=True),
                           [kt.r(), qd.r()], [ps.r()])
                    V(lambda e: e.tensor_tensor(ATm[:, :, :], ps[0:CH, 0:nb * CH].rearrange("p (n c) -> p n c", c=CH),
                                                bc3(maskT, nb), ALU.mult), [ps.r(), cs.r()], [ATm.r()])
                    FILL(2)
                    for n4 in range(0, nb, 4):
                        ps = psum()
                        for n in range(n4, n4 + 4):
                            PE(lambda e, n=n: e.matmul(ps[0:PK, (n - n4) * 128:(n - n4 + 1) * 128], kdtok[:, n, :], vtok[:, n, :],
                                                       start=True, stop=True), [kdtok.r(), vtok.r()], [ps.r()])
                        A(lambda e: e.copy(KV[:, n4:n4 + 4, :].rearrange("p n d -> p (n d)"), ps[0:PK, :]), [ps.r()], [KV.r()])
                    order = list(range(nb)) if dr == 0 else list(range(nb - 1, -1, -1))
                    for s_i, n in enumerate(order):
                        V(lambda e, s_i=s_i, n=n: e.scalar_tensor_tensor(Sall[:, s_i + 1, :], Sall[:, s_i, :], cdec[:, c0 + n:c0 + n + 1],
                                                                         KV[:, n, :], ALU.mult, ALU.add), [Sall.r(), cdec.r(), KV.r()], [Sall.r()])
                    FILL(3)
                    pA = psum()
                    pB = psum()
                    for s_i, n in enumerate(order):
                        PE(lambda e, s_i=s_i, n=n: e.matmul(pA[:, n * CH:(n + 1) * CH], Sall[:, s_i, :], qd[:, csl(n)], start=True, stop=True),
                           [Sall.r(), qd.r()], [pA.r()])
                        PE(lambda e, n=n: e.matmul(pB[:, n * CH:(n + 1) * CH], vtok[:, n, :], ATm[:, n, :], start=True, stop=True),
                           [vtok.r(), ATm.r()], [pB.r()])
                    A(lambda e: e.copy(otmp[:, :], pA[:, 0:nb * CH]), [pA.r()], [otmp.r()])
                    osl = oacc[:, tokb:tokb + nb * CH]
                    if dr == 0:
                        V(lambda e: e.tensor_tensor(osl, otmp[:, :], pB[:, 0:nb * CH], ALU.add), [otmp.r(), pB.r()], [oacc.r(idx)])
                    else:
                        V(lambda e: e.tensor_tensor(otmp[:, :], otmp[:, :], pB[:, 0:nb * CH], ALU.add), [otmp.r(), pB.r()], [otmp.r()])
                        V(lambda e: e.tensor_tensor(osl, osl, otmp[:, :], ALU.add), [otmp.r(), oacc.r(idx)], [oacc.r(idx)])
                if kind == 0:
                    o_st = o_gla if mixer == 'gla' else o_hg
                    dma('sp', o_st[idx, l, dr, h], Sall[:, nb, :], R=[Sall.r()])
        ob = ph("ob", [128, T], BF16)
        normc = pm[l][:, PM_COLS['gla_norm' if mixer == 'gla' else 'hg_norm']:PM_COLS['gla_norm' if mixer == 'gla' else 'hg_norm'] + 1]
        for b in range(NBLK):
            A(lambda e: e.activation(xs[:, :], oacc[:, sl(b)], AF.Square), [oacc.r(i) for i in range(3)], [xs.r()])
            ps = psum()
            PE(lambda e: e.matmul(ps[:, :], cs[:, CS['mean128']:CS['mean128'] + 128], xs[:, :], start=True, stop=True),
               [xs.r(), cs.r()], [ps.r()])
            rstd_from(ps[:, :], t1[:, :], ps.r(), t1.r())
            V(lambda e: e.tensor_tensor(xs[:, :], oacc[:, sl(b)], t1[:, :], ALU.mult), [t1.r()] + [oacc.r(i) for i in range(3)], [xs.r()])
            V(lambda e: e.scalar_tensor_tensor(ob[:, sl(b)], xs[:, :], normc, gate[:, sl(b)], ALU.mult, ALU.mult),
              [xs.r(), gate.r(), pm[l].r()], [ob.r()])
        k_ = bi_ * 4 + h
        if stage > 4:
            dma('sp', obd[k_], ob[:, :], R=[ob.r()], W=[obd_res[k_]])
        if h == 0 and l == 0:
            dbg_dump('ob_' + mixer, ob[:, :], [ob.r()])

    NKEY = T + PAST
    KEYR = [(0, LP), (LP, LP), (2 * LP, LS + PAST)]
    SCALE = (128 + 64) ** -0.5

    def mla_layer(l):
      with Phase() as ph:
        qn = ph("qn", [128, 4, T], BF16)
        ckvb = ph("ckvb", [128, 4, NKEY], BF16)
        kpT = ph("kpT", [128, NKEY], BF16)
        ropet = ph("ropet", [64, 2, LS])
        dma('sp', ropet[:, :, :], rope_d[:, :].rearrange("p (a t) -> p a t", a=2), W=[ropet.r()])
        G(lambda e: e.memset(kpT[:, :], 0.0), [], [kpT.r()])

        def rope(dst_bf, src, ncol0, tmpa, tmpb):
            for hb in range(2):
                c = slice(hb * 512, (hb + 1) * 512)
                ps = psum()
                PE(lambda e: e.matmul(ps[0:64, :], cmat('perm'), src[0:64, c], start=True, stop=True), [src.r(), cs.r()], [ps.r()])
                V(lambda e: e.tensor_tensor(tmpa[0:64, :], src[0:64, c], ropet[:, 0, c], ALU.mult), [src.r(), ropet.r()], [tmpa.r()])
                V(lambda e: e.tensor_tensor(tmpb[0:64, :], ps[0:64, :], ropet[:, 1, c], ALU.mult), [ps.r(), ropet.r()], [tmpb.r()])
                V(lambda e: e.tensor_tensor(dst_bf[0:64, ncol0 + hb * 512:ncol0 + (hb + 1) * 512], tmpa[0:64, :], tmpb[0:64, :], ALU.add),
                  [tmpa.r(), tmpb.r()], [dst_bf.r()])
        sl = lambda b: slice(b * 512, (b + 1) * 512)
        with Phase() as pa:
            qa = pa("qa", [128, 4, T])
            kva = pa("kva", [128, 4, T])
            kpe = pa("kpe", [64, T])
            sq = pa("sq", [128, 512])
            rst = pa("rst", [128, 512])
            tmpa = pa("tmpa", [128, 512])
            tmpb = pa("tmpb", [128, 512])
            tokt = pa("tokt", [128, 512])
            for nm, dst in (('mla_qa', qa), ('mla_kva', kva)):
                for c in range(4):
                    wt, wv, n_ = win_piece(l, nm, c)
                    for b in range(NBLK):
                        ps = psum()
                        proj_h(wt, wv, 128, b, ps[:, :], ps.r())
                        if c % 2 == 0:
                            A(lambda e: e.copy(dst[:, c, sl(b)], ps[:, :]), [ps.r()], [dst.r()])
                        else:
                            V(lambda e: e.tensor_copy(dst[:, c, sl(b)], ps[:, :]), [ps.r()], [dst.r()])
            wt, wv, n_ = win_piece(l, 'mla_kpe', 0)
            for b in range(NBLK):
                ps = psum()
                proj_h(wt, wv, 64, b, ps[0:64, :], ps.r())
                A(lambda e: e.copy(kpe[:, sl(b)], ps[0:64, :]), [ps.r()], [kpe.r()])
            for src, ncol, kind_ in ((qa, 'q_norm', 'q'), (kva, 'kv_norm', 'kv')):
                for b in range(NBLK):
                    psm = psum()
                    for c in range(4):
                        A(lambda e: e.activation(sq[:, :], src[:, c, sl(b)], AF.Square), [src.r()], [sq.r()])
                        PE(lambda e: e.matmul(psm[:, :], cs[:, CS['mean512']:CS['mean512'] + 128], sq[:, :], start=(c == 0), stop=(c == 3)),
                           [sq.r(), cs.r()], [psm.r()])
                    rstd_from(psm[:, :], rst[:, :], psm.r(), rst.r())
                    for c in range(4):
                        ncl = pm[l][:, PM_COLS[ncol] + c:PM_COLS[ncol] + c + 1]
                        if kind_ == 'q':
                            V(lambda e: e.scalar_tensor_tensor(qn[:, c, sl(b)], src[:, c, sl(b)], ncl, rst[:, :], ALU.mult, ALU.mult),
                              [src.r(), rst.r(), pm[l].r()], [qn.r()])
                        else:
                            V(lambda e: e.scalar_tensor_tensor(src[:, c, sl(b)], src[:, c, sl(b)], ncl, rst[:, :], ALU.mult, ALU.mult),
                              [src.r(), rst.r(), pm[l].r()], [src.r()])
                            G(lambda e: e.tensor_copy(ckvb[:, c, sl(b)], src[:, c, sl(b)]), [src.r()], [ckvb.r()])
            for tt in range(4):
                idx, t_in = tt // 2, (tt % 2) * 128
                ps = psum()
                for c in range(4):
                    PE(lambda e: e.transpose(ps[:, c * 128:(c + 1) * 128], kva[:, c, tt * 128:(tt + 1) * 128], ident), [kva.r(), cs.r()], [ps.r()])
                A(lambda e: e.copy(tokt[:, :], ps[:, :]), [ps.r()], [tokt.r()])
                dma('sp', o_ckv[idx, l, t_in:t_in + 128, :], tokt[:, :], R=[tokt.r()])
                ps = psum()
                PE(lambda e: e.transpose(ps[:, 0:64], kpe[:, tt * 128:(tt + 1) * 128], ident[0:64, 0:64]), [kpe.r(), cs.r()], [ps.r()])
                A(lambda e: e.copy(tmpa[:, 0:64], ps[:, 0:64]), [ps.r()], [tmpa.r()])
                dma('sp', o_kpe[idx, l, t_in:t_in + 128, :], tmpa[:, 0:64], R=[tmpa.r()])
            V(lambda e: e.tensor_copy(kpT[0:64, 0:2 * LP], kpe[:, 0:2 * LP]), [kpe.r()], [kpT.r()])
            kps = pa("kps", [64, LS])
            V(lambda e: e.tensor_copy(kps[:, :], kpe[:, 2 * LP:T]), [kpe.r()], [kps.r()])
            rope(kpT, kps, 2 * LP, tmpa, tmpb)
            for tt in range(PAST // 128):
                dma('sp', tokt[:, :], cx_ckv[l, tt * 128:(tt + 1) * 128, :], W=[tokt.r()])
                ps = psum()
                for c in range(4):
                    PE(lambda e: e.transpose(ps[:, c * 128:(c + 1) * 128], tokt[:, c * 128:(c + 1) * 128], ident), [tokt.r(), cs.r()], [ps.r()])
                A(lambda e: e.copy(ckvb[:, :, T + tt * 128:T + (tt + 1) * 128], ps[:, :].rearrange("p (c t) -> p c t", c=4)), [ps.r()], [ckvb.r()])
                dma('sp', tmpb[:, 0:64], cx_kpe[l, tt * 128:(tt + 1) * 128, :], W=[tmpb.r()])
                ps = psum()
                PE(lambda e: e.transpose(ps[0:64, 0:128], tmpb[:, 0:64], ident), [tmpb.r(), cs.r()], [ps.r()])
                A(lambda e: e.copy(kpT[0:64, T + tt * 128:T + (tt + 1) * 128], ps[0:64, 0:128]), [ps.r()], [kpT.r()])
        for h in range(4):
          if stage == 4 and h > 0:
              break
          with Phase() as hh:
            qnT = hh("qnT", [128, T], BF16)
            qpT = hh("qpT", [128, T], BF16)
            qpf = hh("qpf", [64, T])
            knT = hh("knT", [128, NKEY], BF16)
            vtk = hh("vtk", [128, NKEY // 128, 128], BF16)
            Pm = hh("Pm", [128, LS + PAST])
            PT = hh("PT", [128, (LS + PAST) // 128, 128], BF16)
            Otok = hh("Otok", [128, 128])
            ob = hh("ob", [128, T], BF16)
            mx = hh("mx", [128, 4])
            sm = hh("sm", [128, 4])
            tmpa = hh("tmpa", [128, 512])
            tmpb = hh("tmpb", [128, 512])
            G(lambda e: e.memset(qpT[:, :], 0.0), [], [qpT.r()])
            wqo = (h * 192) * 4
            wt, wv = load_w(wqb_d[l][:, wqo:wqo + 512], 4, 128)
            for b in range(NBLK):
                ps = psum()
                for kc in range(4):
                    PE(lambda e: e.matmul(ps[:, :], wv[:, kc, :], qn[:, kc, sl(b)], start=(kc == 0), stop=(kc == 3)), [wt.r(), qn.r()], [ps.r()])
                A(lambda e: e.copy(qnT[:, sl(b)], ps[:, :]), [ps.r()], [qnT.r()])
            wt, wv = load_w(wqb_d[l][:, wqo + 512:wqo + 768], 4, 64)
            for b in range(NBLK):
                ps = psum()
                for kc in range(4):
                    PE(lambda e: e.matmul(ps[0:64, :], wv[:, kc, :], qn[:, kc, sl(b)], start=(kc == 0), stop=(kc == 3)), [wt.r(), qn.r()], [ps.r()])
                A(lambda e: e.copy(qpf[:, sl(b)], ps[0:64, :]), [ps.r()], [qpf.r()])
            V(lambda e: e.tensor_copy(qpT[0:64, 0:2 * LP], qpf[:, 0:2 * LP]), [qpf.r()], [qpT.r()])
            qps = hh("qps", [64, LS])
            V(lambda e: e.tensor_copy(qps[:, :], qpf[:, 2 * LP:T]), [qpf.r()], [qps.r()])
            rope(qpT, qps, 2 * LP, tmpa, tmpb)
            wt, wv = load_w(wkvb_d[l][:, (h * 2) * 512:(h * 2) * 512 + 512], 4, 128)
            for kb in range(NKEY // 512):
                ps = psum()
                for kc in range(4):
                    PE(lambda e: e.matmul(ps[:, :], wv[:, kc, :], ckvb[:, kc, kb * 512:(kb + 1) * 512], start=(kc == 0), stop=(kc == 3)),
                       [wt.r(), ckvb.r()], [ps.r()])
                A(lambda e: e.copy(knT[:, kb * 512:(kb + 1) * 512], ps[:, :]), [ps.r()], [knT.r()])
            wt, wv = load_w(wkvb_d[l][:, (h * 2 + 1) * 512:(h * 2 + 1) * 512 + 512], 4, 128)
            for k4 in range(0, NKEY // 128, 4):
                ps = psum()
                for kt_ in range(k4, k4 + 4):
                    for kc in range(4):
                        PE(lambda e: e.matmul(ps[:, (kt_ - k4) * 128:(kt_ - k4 + 1) * 128], ckvb[:, kc, kt_ * 128:(kt_ + 1) * 128], wv[:, kc, :],
                                              start=(kc == 0), stop=(kc == 3)), [wt.r(), ckvb.r()], [ps.r()])
                V(lambda e: e.tensor_copy(vtk[:, k4:k4 + 4, :].rearrange("p k d -> p (k d)"), ps[:, :]), [ps.r()], [vtk.r()])
            for (tok0, L, kind, idx) in SEQS:
                k0, Lk = KEYR[idx]
                nkb = (Lk + 511) // 512
                for qt in range(L // 128):
                    qs = slice(tok0 + qt * 128, tok0 + (qt + 1) * 128)
                    pss = [psum() for _ in range(nkb)]
                    for kb in range(nkb):
                        kw_ = min(512, Lk - kb * 512)
                        ks = slice(k0 + kb * 512, k0 + kb * 512 + kw_)
                        PE(lambda e: e.matmul(pss[kb][:, 0:kw_], qnT[:, qs], knT[:, ks], start=True, stop=False), [qnT.r(), knT.r()], [pss[kb].r()])
                        PE(lambda e: e.matmul(pss[kb][:, 0:kw_], qpT[:, qs], kpT[:, ks], start=False, stop=True), [qpT.r(), kpT.r()], [pss[kb].r()])
                        V(lambda e: e.reduce_max(mx[:, kb:kb + 1], pss[kb][:, 0:kw_], AX.X), [pss[kb].r()], [mx.r()])
                    if nkb > 1:
                        V(lambda e: e.reduce_max(mx[:, 3:4], mx[:, 0:nkb], AX.X), [mx.r()], [mx.r()])
                        mcol = mx[:, 3:4]
                    else:
                        mcol = mx[:, 0:1]
                    V(lambda e: e.tensor_scalar(mx[:, 3:4], mcol, -SCALE, None, ALU.mult), [mx.r()], [mx.r()])
                    for kb in range(nkb):
                        kw_ = min(512, Lk - kb * 512)
                        A(lambda e: e.activation(Pm[:, kb * 512:kb * 512 + kw_], pss[kb][:, 0:kw_], AF.Exp, bias=mx[:, 3:4], scale=SCALE,
                                                 accum_out=sm[:, kb:kb + 1]), [pss[kb].r(), mx.r()], [Pm.r(), sm.r()])
                    if nkb > 1:
                        V(lambda e: e.reduce_sum(sm[:, 3:4], sm[:, 0:nkb], AX.X), [sm.r()], [sm.r()])
                        scol = sm[:, 3:4]
                    else:
                        scol = sm[:, 0:1]
                    V(lambda e: e.reciprocal(sm[:, 3:4], scol), [sm.r()], [sm.r()])
                    FILL(2)
                    nkt = Lk // 128
                    for k4 in range(0, nkt, 4):
                        ps = psum()
                        w4 = min(4, nkt - k4)
                        for kt_ in range(k4, k4 + w4):
                            PE(lambda e: e.transpose(ps[:, (kt_ - k4) * 128:(kt_ - k4 + 1) * 128], Pm[:, kt_ * 128:(kt_ + 1) * 128], ident),
                               [Pm.r(), cs.r()], [ps.r()])
                        A(lambda e: e.copy(PT[:, k4:k4 + w4, :].rearrange("p k q -> p (k q)"), ps[:, 0:w4 * 128]), [ps.r()], [PT.r()])
                    po = psum()
                    for kt_ in range(nkt):
                        PE(lambda e: e.matmul(po[:, 0:128], PT[:, kt_, :], vtk[:, k0 // 128 + kt_, :], start=(kt_ == 0), stop=(kt_ == nkt - 1)),
                           [PT.r(), vtk.r()], [po.r()])
                    A(lambda e: e.activation(Otok[:, :], po[:, 0:128], AF.Copy, scale=sm[:, 3:4]), [po.r(), sm.r()], [Otok.r()])
                    pt2 = psum()
                    PE(lambda e: e.transpose(pt2[:, 0:128], Otok[:, :], ident), [Otok.r(), cs.r()], [pt2.r()])
                    V(lambda e: e.tensor_copy(ob[:, qs], pt2[:, 0:128]), [pt2.r()], [ob.r()])
            if stage > 4:
                dma('sp', obd[12 + h], ob[:, :], R=[ob.r()], W=[obd_res[12 + h]])
            if h == 0 and l == 0:
                dbg_dump('ob_mla', ob[:, :], [ob.r()])

    for l in range(NLAY[0]):
      try:
        if stage < 1:
            break
        with Phase() as ph:
            xTb = ph("xTb", [128, 16, 512])
            for b in range(NBLK):
                kind = 0 if b == 0 else 1
                dma('sp', xTb[:, :, :], xTd[:, :, b * 512:(b + 1) * 512].rearrange("c p t -> p c t"),
                    R=[xTd_res[b]], W=[xTb.r()])
                for c in range(16):
                    A(lambda e, c=c, b=b, kind=kind, l=l: e.activation(
                        hT[:, c, b * 512:(b + 1) * 512], xTb[:, c, :], AF.Identity,
                        bias=modT[l][:, kind, c:c + 1], scale=modT[l][:, kind, 16 + c:16 + c + 1]),
                        [xTb.r(), modT[l].r()], [hT.r(b)])
            if l == 0:
                dbg_dump('hT', hT[:, :, :].rearrange("p c t -> p (c t)"), [hT.r(b) for b in range(NBLK)])
        if stage < 2:
            break
        FIL[0] = Filler(l)
        with Phase() as ph:
            ba = ph("ba", [64, 24, 16])
            bet = ph("bet", [64, 24, 8])
            gg = ph("gg", [64, 24, 8])
            tt_ = ph("tt_", [64, 24, 8])
            t2_ = ph("t2_", [64, 24, 8])
            nea = ph("nea", [64, 8])
            if CUT[0] == -6:
                win_piece(l, 'gdn_q', 0)
                cut(-6)
            wt, wv, n_ = win_piece(l, 'gdn_ba', 0)
            cut(-5)
            ps = psum()
            for n in range(24):
                for kc in range(16):
                    PE(lambda e, n=n, kc=kc, ps=ps, wv=wv: e.matmul(
                        ps[0:64, n * 16:(n + 1) * 16], hT[:, kc, n * 64:(n + 1) * 64], wv[:, kc, :],
                        start=(kc == 0), stop=(kc == 15)), [wt.r()] + [hT.r(b) for b in range(NBLK)], [ps.r()])
            cut(-4)
            A(lambda e, ps=ps: e.copy(ba[:, :, :], ps[0:64, 0:384].rearrange("p (n c) -> p n c", c=16)), [ps.r()], [ba.r()])
            cut(-3)
            A(lambda e: e.activation(bet[:, :, :], ba[:, :, 0:8], AF.Sigmoid), [ba.r()], [bet.r()])
            for j in range(8):
                V(lambda e, j=j: e.tensor_scalar(tt_[:, :, j], ba[:, :, 8 + j], pm[l][0:64, PM_COLS['dt_bias'] + j:PM_COLS['dt_bias'] + j + 1],
                                               None, ALU.add), [ba.r(), pm[l].r()], [tt_.r()])
            cut(-2)
            A(lambda e: e.activation(t2_[:, :, :], tt_[:, :, :], AF.Abs), [tt_.r()], [t2_.r()])
            A(lambda e: e.activation(t2_[:, :, :], t2_[:, :, :], AF.Exp, scale=-1.0), [t2_.r()], [t2_.r()])
            A(lambda e: e.activation(t2_[:, :, :], t2_[:, :, :], AF.Ln, bias=ONEC[0:64, :]), [t2_.r(), cs.r()], [t2_.r()])
            cut(-1)
            V(lambda e: e.scalar_tensor_tensor(tt_[:, :, :], tt_[:, :, :], 0.0, t2_[:, :, :], ALU.max, ALU.add),
              [tt_.r(), t2_.r()], [tt_.r()])
            A(lambda e: e.activation(nea[:, :], pm[l][0:64, PM_COLS['a_log']:PM_COLS['a_log'] + 8], AF.Exp), [pm[l].r()], [nea.r()])
            V(lambda e: e.tensor_scalar(nea[:, :], nea[:, :], -1.0, None, ALU.mult), [nea.r()], [nea.r()])
            for j in range(8):
                V(lambda e, j=j: e.tensor_scalar(gg[:, :, j], tt_[:, :, j], nea[:, j:j + 1], None, ALU.mult),
                  [tt_.r(), nea.r()], [gg.r()])
            dbg_dump('gg', gg[:, :, :].rearrange("p a b -> p (a b)"), [gg.r()])
            for h in range(4):
                if stage in (2, 3, 4) and h > 0:
                    break
                gdn_unit(l, h, ph, bet, gg)
        if stage < 3:
            break
        with Phase() as ph:
            rmask = ph("rmask", [128, T])
            V(lambda e: e.memset(rmask[:, :], 1.0), [], [rmask.r()])
            V(lambda e: e.memset(rmask[:, :].rearrange("p (n c) -> p n c", c=CH)[:, :, 0], 0.0), [rmask.r()], [rmask.r()])
            glr = [ph("glr0", [16, T]), ph("glr1", [16, T])]
            gw2 = ph("gw2", [16, 512])
            dma('sp', gw2[:, :], gw2_d[l], W=[gw2.r()])
            for dr in range(2):
                wt, wv, n_ = win_piece(l, 'gla_g', dr)
                for b in range(NBLK):
                    ps = psum()
                    proj_h(wt, wv, 16, b, ps[0:16, :], ps.r())
                    A(lambda e: e.copy(glr[dr][:, b * 512:(b + 1) * 512], ps[0:16, :]), [ps.r()], [glr[dr].r()])
            lbt = ph("lbt", [128, 8])
            oml = ph("oml", [128, 8])
            if l == 0:
                V(lambda e: e.memset(lbt[:, :], 0.0), [], [lbt.r()])
                V(lambda e: e.memset(oml[:, :], 1.0), [], [oml.r()])
            else:
                c_ = PM_COLS['hg_lb']
                V(lambda e: e.tensor_tensor(lbt[:, :], pm[l][:, c_ + 8:c_ + 16], pm[l][:, c_:c_ + 8], ALU.subtract), [pm[l].r()], [lbt.r()])
                A(lambda e: e.activation(lbt[:, :], lbt[:, :], AF.Sigmoid), [lbt.r()], [lbt.r()])
                V(lambda e: e.tensor_scalar(oml[:, :], lbt[:, :], -1.0, 1.0, ALU.mult, ALU.add), [lbt.r()], [oml.r()])
            shared = {'rmask': rmask, 'glr': glr, 'gw2': gw2, 'lbt': lbt, 'oml': oml}
            for mixer in ('gla', 'hg'):
                for h in range(4):
                    if stage in (3, 4) and h > 0:
                        break
                    gla_unit(l, h, mixer, shared)
        if stage < 4:
            break
        mla_layer(l)
        if stage < 5:
            break
        FIL[0].flush()
        with Phase() as ph:
            obr = ph("obr", [128, 16, T], BF16)
            for k_ in range(16):
                dma('sp', obr[:, k_, :], obd[k_], R=[obd_res[k_]], W=[obr.r(k_)])
            sig = ph("sig", [128, 512])
            macc = ph("macc", [128, 512])
            gin = [ph("gin0", [128, 512], BF16), ph("gin1", [128, 512], BF16)]
            mo = [ph("mo0", [128, 512], BF16), ph("mo1", [128, 512], BF16)]
            obr_all = [obr.r(k_) for k_ in range(16)]
            for c in range(16):
                for b in range(NBLK):
                    for k in range(4):
                        gi_ = gin[(c * 12 + b * 4 + k) % 2]
                        dma('sp', gi_[:, :], gsd[k * 16 + c, :, b * 512:(b + 1) * 512], R=[gsd_res[k * 16 + c][b]], W=[gi_.r()])
                        bg = pm[l][:, PM_COLS['b_gates'] + k * 16 + c:PM_COLS['b_gates'] + k * 16 + c + 1]
                        A(lambda e: e.activation(sig[:, :], gi_[:, :], AF.Sigmoid, bias=bg), [gi_.r(), pm[l].r()], [sig.r()])
                        off = (k * 16 + c) * 512
                        wtb, wvb = load_w(wbr_d[l][:, off:off + 512], 4, 128)
                        pp = psum()
                        for kc in range(4):
                            PE(lambda e: e.matmul(pp[:, :], wvb[:, kc, :], obr[:, k * 4 + kc, b * 512:(b + 1) * 512], start=(kc == 0), stop=(kc == 3)),
                               [wtb.r()] + obr_all, [pp.r()])
                        if k == 0:
                            V(lambda e: e.tensor_tensor(macc[:, :], sig[:, :], pp[:, :], ALU.mult), [sig.r(), pp.r()], [macc.r()])
                        else:
                            V(lambda e: e.tensor_tensor(sig[:, :], sig[:, :], pp[:, :], ALU.mult), [sig.r(), pp.r()], [sig.r()])
                            if k < 3:
                                V(lambda e: e.tensor_tensor(macc[:, :], macc[:, :], sig[:, :], ALU.add), [macc.r(), sig.r()], [macc.r()])
                            else:
                                mo_ = mo[(c * NBLK + b) % 2]
                                V(lambda e: e.tensor_tensor(mo_[:, :], macc[:, :], sig[:, :], ALU.add), [macc.r(), sig.r()], [mo_.r()])
                                dma('sp', mgd[c, :, b * 512:(b + 1) * 512], mo_[:, :], R=[mo_.r()], W=[mgd_res[c][b]])
        if stage < 6:
            break
        with Phase() as ph:
            rr = ph("rr", [128, 16, 512])
            aT = ph("aT", [128, NFF, 512], BF16)
            xin = ph("xin", [128, 512])
            sq = ph("sq", [128, 512])
            mean = ph("mean", [128, 512])
            rst = ph("rst", [128, 512])
            tg = ph("tg", [128, 512])
            wbig = [ph("wbig0", [128, NFF * 128], BF16), ph("wbig1", [128, NFF * 128], BF16)]
            otok = ph("otok", [128, 512])
            wbi = [0]

            def layer_norm(gname, bname, post):
                pm1 = psum()
                pm2 = psum()
                for c in range(16):
                    A(lambda e: e.activation(sq[:, :], rr[:, c, :], AF.Square), [rr.r(c)], [sq.r()])
                    PE(lambda e: e.matmul(pm1[:, :], cs[:, CS['meanD']:CS['meanD'] + 128], rr[:, c, :], start=(c == 0), stop=(c == 15)),
                       [rr.r(c), cs.r()], [pm1.r()])
                    PE(lambda e: e.matmul(pm2[:, :], cs[:, CS['meanD']:CS['meanD'] + 128], sq[:, :], start=(c == 0), stop=(c == 15)),
                       [sq.r(), cs.r()], [pm2.r()])
                A(lambda e: e.copy(mean[:, :], pm1[:, :]), [pm1.r()], [mean.r()])
                V(lambda e: e.tensor_tensor(tg[:, :], mean[:, :], mean[:, :], ALU.mult), [mean.r()], [tg.r()])
                V(lambda e: e.tensor_tensor(tg[:, :], pm2[:, :], tg[:, :], ALU.subtract), [pm2.r(), tg.r()], [tg.r()])
                A(lambda e: e.activation(rst[:, :], tg[:, :], AF.Ln, bias=EPSC), [tg.r(), cs.r()], [rst.r()])
                A(lambda e: e.activation(rst[:, :], rst[:, :], AF.Exp, scale=-0.5), [rst.r()], [rst.r()])
                for c in range(16):
                    V(lambda e: e.tensor_tensor(rr[:, c, :], rr[:, c, :], mean[:, :], ALU.subtract), [rr.r(c), mean.r()], [rr.r(c)])
                    G(lambda e: e.tensor_tensor(rr[:, c, :], rr[:, c, :], rst[:, :], ALU.mult), [rr.r(c), rst.r()], [rr.r(c)])
                    A(lambda e: e.activation(rr[:, c, :], rr[:, c, :], AF.Identity,
                                             bias=pm[l][:, PM_COLS[bname] + c:PM_COLS[bname] + c + 1],
                                             scale=pm[l][:, PM_COLS[gname] + c:PM_COLS[gname] + c + 1]), [rr.r(c), pm[l].r()], [rr.r(c)])
                    post(c)
            for b in range(NBLK):
                kind = 0 if b == 0 else 1
                for c in range(16):
                    dma('sp', hT[:, c, 512:1024], mgd[c, :, b * 512:(b + 1) * 512], R=[mgd_res[c][b]], W=[hT.r(1)])
                mg_all = [hT.r(1)]
                for c in range(16):
                    wt, wv = load_w(wout_d[l][:, c * 2048:(c + 1) * 2048], 16, 128)
                    ps = psum()
                    for kc in range(16):
                        PE(lambda e: e.matmul(ps[:, :], wv[:, kc, :], hT[:, kc, 512:1024], start=(kc == 0), stop=(kc == 15)), [wt.r()] + mg_all, [ps.r()])
                    dma('sp', xin[:, :], xTd[c, :, b * 512:(b + 1) * 512], R=[xTd_res[b]], W=[xin.r()])
                    A(lambda e: e.activation(tg[:, :], ps[:, :], AF.Copy, scale=modT[l][:, kind, 32 + c:32 + c + 1]), [ps.r(), modT[l].r()], [tg.r()])
                    V(lambda e: e.scalar_tensor_tensor(rr[:, c, :], xin[:, :], ALPHA, tg[:, :], ALU.mult, ALU.add), [xin.r(), tg.r()], [rr.r(c)])

                def post1(c):
                    A(lambda e: e.activation(hT[:, c, 0:512], rr[:, c, :], AF.Identity, bias=modT[l][:, kind, 48 + c:48 + c + 1],
                                             scale=modT[l][:, kind, 64 + c:64 + c + 1]), [rr.r(c), modT[l].r()], [hT.r(0)])
                layer_norm('ln1_g', 'ln1_b', post1)
                h2_all = [hT.r(0)]
                for f in range(NFF):
                    wt1, wv1 = load_w(w1_d[l][:, f * 2048:(f + 1) * 2048], 16, 128)
                    wt3, wv3 = load_w(w3_d[l][:, f * 2048:(f + 1) * 2048], 16, 128)
                    p1 = psum()
                    p3 = psum()
                    for kc in range(16):
                        PE(lambda e: e.matmul(p1[:, :], wv1[:, kc, :], hT[:, kc, 0:512], start=(kc == 0), stop=(kc == 15)), [wt1.r()] + h2_all, [p1.r()])
                    for kc in range(16):
                        PE(lambda e: e.matmul(p3[:, :], wv3[:, kc, :], hT[:, kc, 0:512], start=(kc == 0), stop=(kc == 15)), [wt3.r()] + h2_all, [p3.r()])
                    A(lambda e: e.activation(tg[:, :], p1[:, :], AF.Silu), [p1.r()], [tg.r()])
                    V(lambda e: e.tensor_tensor(aT[:, f, :], tg[:, :], p3[:, :], ALU.mult), [tg.r(), p3.r()], [aT.r(f)])
                aT_all = [aT.r(f) for f in range(NFF)]
                for c in range(16):
                    wtile = wbig[wbi[0] % 2]
                    wbi[0] += 1
                    dma('pool', wtile[:, :], w2_d[l][:, c * NFF * 128:(c + 1) * NFF * 128], W=[wtile.r()])
                    wv2 = wtile[:, :].rearrange("p (k n) -> p k n", k=NFF)
                    ps = psum()
                    for kc in range(NFF):
                        PE(lambda e: e.matmul(ps[:, :], wv2[:, kc, :], aT[:, kc, :], start=(kc == 0), stop=(kc == NFF - 1)), [wtile.r()] + aT_all, [ps.r()])
                    A(lambda e: e.activation(tg[:, :], ps[:, :], AF.Copy, scale=modT[l][:, kind, 80 + c:80 + c + 1]), [ps.r(), modT[l].r()], [tg.r()])
                    V(lambda e: e.scalar_tensor_tensor(rr[:, c, :], rr[:, c, :], ALPHA, tg[:, :], ALU.mult, ALU.add), [rr.r(c), tg.r()], [rr.r(c)])

                def post2(c):
                    if l < DEPTH - 1:
                        dma('sp', xTd[c, :, b * 512:(b + 1) * 512], rr[:, c, :], R=[rr.r(c)], W=[xTd_res[b]])
                layer_norm('ln2_g', 'ln2_b', post2)
                if l == DEPTH - 1:
                    for tt in range(4):
                        for g4 in range(4):
                            ps = psum()
                            for cc in range(4):
                                c = g4 * 4 + cc
                                PE(lambda e: e.transpose(ps[:, cc * 128:(cc + 1) * 128], rr[:, c, tt * 128:(tt + 1) * 128], ident), [rr.r(c), cs.r()], [ps.r()])
                            A(lambda e: e.copy(otok[:, :], ps[:, :]), [ps.r()], [otok.r()])
                            gt = b * 4 + tt
                            dst = y_p[gt * 128:(gt + 1) * 128, g4 * 512:(g4 + 1) * 512] if gt < 4 else \
                                y_s[(gt - 4) * 128:(gt - 3) * 128, g4 * 512:(g4 + 1) * 512]
                            dma('sp', dst, otok[:, :], R=[otok.r()])
      except _Stop:
        break

    for tl_ in list(ALL_TL):
        rs_ = list(tl_._res.values())
        if tl_.name.startswith('s') and rs_ and sum(r.nw for r in rs_) > 0 and sum(r.nr for r in rs_) == 0:
            try:
                shp_ = list(tl_.t.shape)
                sink_ = nc.dram_tensor("sink_" + tl_.name, shp_, tl_.t.dtype, kind="Internal").ap()
                full_ = tl_.t[tuple(slice(None) for _ in shp_)]
                dma('sp', sink_[tuple(slice(None) for _ in shp_)], full_, R=rs_)
            except Exception as ex_:
                print('autosink failed', tl_.name, ex_)
    agg = {}
    for r_ in ALL_RES:
        a_ = agg.setdefault(r_.name, [0, 0])
        a_[0] += r_.nr
        a_[1] += r_.nw
    dead = [k for k, v in agg.items() if v[1] > 0 and v[0] == 0 and k != '?']
    if dead:
        print('WARNING unread tiles:', dead)
    print('op counts', P.cnt, {q: P.ring[q][1] for q in P.ring})
    stack_close = stack
    with nc.Block() as block:
        P.emit(block)
    stack_close.close()
    return nc


def _pack(W, n=128):
    K, N = W.shape
    kc = K // 128
    return np.ascontiguousarray(
        W.reshape(kc, 128, N // n, n).transpose(1, 2, 0, 3).reshape(128, -1))


def _pack_in(W):
    K = W.shape[0]
    outs = []
    cols = []
    for name in PIECES:
        cols.extend(PIECES[name])
    cols.sort()
    assert sum(n for _, n in cols) == IN_WIDTH
    for c0, n in cols:
        outs.append(W[:, c0:c0 + n].reshape(K // 128, 128, n).transpose(1, 0, 2).reshape(128, -1))
    return np.ascontiguousarray(np.concatenate(outs, axis=1))


def _consts(cvec2):
    cs = np.zeros((128, CS_N), np.float32)
    cs[:, CS['ident']:CS['ident'] + 128] = np.eye(128, dtype=np.float32)
    k = np.arange(64)[:, None]
    i = np.arange(64)[None, :]
    for name, m in (('le', k <= i), ('ge', k >= i), ('lt', k < i), ('gt', k > i)):
        cs[:64, CS[name]:CS[name] + 64] = m.astype(np.float32)
    cs[:, CS['ones']:CS['ones'] + 128] = 1.0
    cs[:, CS['mean128']:CS['mean128'] + 128] = 1.0 / 128
    cs[:, CS['meanD']:CS['meanD'] + 128] = 1.0 / D
    cs[:, CS['mean512']:CS['mean512'] + 128] = 1.0 / 512
    perm = np.zeros((64, 64), np.float32)
    cosT = np.zeros((64, LS), np.float32)
    sinT = np.zeros((64, LS), np.float32)
    pos = np.arange(LS)
    row_id = (pos // 64).astype(np.float32)
    col_id = (pos % 64).astype(np.float32)
    half = 32
    inv = (10000.0 ** (-np.arange(0, half, 2, dtype=np.float32) / half)).astype(np.float32)
    for blk, ids in ((0, row_id), (1, col_id)):
        ang = ids[None, :] * inv[:, None]
        b0 = blk * 32
        for j in range(16):
            cosT[b0 + j] = np.cos(ang[j]); cosT[b0 + 16 + j] = np.cos(ang[j])
            sinT[b0 + j] = -np.sin(ang[j]); sinT[b0 + 16 + j] = np.sin(ang[j])
            perm[b0 + 16 + j, b0 + j] = 1.0
            perm[b0 + j, b0 + 16 + j] = 1.0
    cs[:64, CS['perm']:CS['perm'] + 64] = perm
    k = np.arange(32)[:, None]
    i = np.arange(32)[None, :]
    cs[:32, CS['le32']:CS['le32'] + 32] = (k <= i)
    cs[:32, CS['ge32']:CS['ge32'] + 32] = (k >= i)
    cs[:, CS['eps']] = EPS
    cs[:, CS['one']] = 1.0
    cs[:, CS['cvT']:CS['cvT'] + 32] = cvec2.reshape(2, 16, 128).transpose(2, 1, 0).reshape(128, 32)
    rope = np.zeros((64, 2 * LS), np.float32)
    rope[:, :LS] = cosT
    rope[:, LS:] = sinT
    return cs, rope


def _pm(inp, l):
    pm = np.zeros((128, PM_N), np.float32)

    def put(name, arr2d):
        c = PM_COLS[name]
        pm[:arr2d.shape[1], c:c + arr2d.shape[0]] = arr2d.T
    conv = inp['gdn_conv'][l]
    put('conv', conv.reshape(5, 12, 128).transpose(1, 0, 2).reshape(60, 128))
    put('gdn_norm', inp['gdn_norm'][l][None])
    put('gla_norm', inp['gla_norm'][l][None])
    put('hg_norm', inp['hgrn_norm'][l][None])
    put('q_norm', inp['mla_q_norm'][l].reshape(4, 128))
    put('kv_norm', inp['mla_kv_norm'][l].reshape(4, 128))
    put('b_gates', inp['b_gates'][l].reshape(64, 128))
    put('hg_lb', inp['hgrn_lb'].reshape(16, 128))
    put('gla_gb', inp['gla_gate_b'][l].reshape(8, 64))
    put('b_ada', inp['b_ada'][l].reshape(96, 128))
    for nm in ('ln1_g', 'ln1_b', 'ln2_g', 'ln2_b'):
        put(nm, inp[nm][l].reshape(16, 128))
    pm[:, PM_COLS['a_log']:PM_COLS['a_log'] + 8] = inp['gdn_a_log'][l].reshape(1, 8)
    pm[:, PM_COLS['dt_bias']:PM_COLS['dt_bias'] + 8] = inp['gdn_dt_bias'][l].reshape(1, 8)
    return pm


def make_inputs(inp, ncores=8):
    f = lambda a: np.ascontiguousarray(np.asarray(a, dtype=np.float32))
    inp = {k: f(v) for k, v in inp.items()}
    sh = {}
    sh['pm'] = np.stack([_pm(inp, l) for l in range(DEPTH)])
    sh['gw2'] = np.ascontiguousarray(inp['gla_gate_w2'].transpose(0, 2, 1, 3).reshape(DEPTH, 16, 512))
    sh['wada'] = np.stack([_pack(inp['w_ada'][l]) for l in range(DEPTH)])
    sh['win'] = np.stack([_pack_in(inp['w_in'][l]) for l in range(DEPTH)])
    def pk_cols(W, cols):
        K = W.shape[0]
        return np.concatenate([W[:, c0:c0 + n].reshape(K // 128, 128, n).transpose(1, 0, 2).reshape(128, -1)
                               for c0, n in cols], axis=1)
    qcols = []
    for h in range(4):
        qcols += [(h * 192, 128), (h * 192 + 128, 64)]
    sh['wqb'] = np.stack([pk_cols(inp['mla_wq_b'][l], qcols) for l in range(DEPTH)])
    sh['wkvb'] = np.stack([_pack(inp['mla_wkv_b'][l]) for l in range(DEPTH)])
    sh['wbr'] = np.stack([np.concatenate([_pack(inp['w_branch'][l, k]) for k in range(4)], axis=1)
                          for l in range(DEPTH)])
    sh['wout'] = np.stack([_pack(inp['w_out'][l]) for l in range(DEPTH)])
    sh['w1'] = np.stack([_pack(inp['ffn_w1'][l]) for l in range(DEPTH)])
    sh['w3'] = np.stack([_pack(inp['ffn_w3'][l]) for l in range(DEPTH)])
    sh['w2'] = np.stack([_pack(inp['ffn_w2'][l]) for l in range(DEPTH)])
    maps = []
    for core in range(ncores):
        b = core % 2
        m = dict(sh)
        m['xp'] = inp['x_prompt'][2 * core:2 * core + 2].reshape(2 * LP, D)
        m['xs'] = inp['x_sample'][b]
        m['st_gdn'] = inp['state_gdn'][b]
        m['st_gla'] = inp['state_gla'][b]
        m['st_hg'] = inp['state_hgrn'][b]
        m['cx_ckv'] = inp['cache_mla_ckv'][b]
        m['cx_kpe'] = inp['cache_mla_kpe'][b]
        m['cs'], m['rope'] = _consts(np.stack([inp['c_ctx'], inp['c'][b]]))
        maps.append(m)
    return maps


_NC = None


def kernel(**inputs):
    global _NC
    if _NC is None:
        _NC = build()
    maps = make_inputs(inputs, 8)
    res = run_bass_kernel_spmd(_NC, maps, core_ids=list(range(8)))
    r = res.results
    y_prompt = np.concatenate([r[c]['y_p'].reshape(2, LP, D) for c in range(8)], axis=0)
    y_sample = np.stack([r[0]['y_s'], r[1]['y_s']], axis=0)
    cat = lambda k: np.concatenate([r[c][k] for c in range(8)], axis=0)
    return (y_prompt.astype(np.float32), y_sample.astype(np.float32), cat('o_gdn'), cat('o_gla'), cat('o_hg'),
            cat('o_ckv'), cat('o_kpe'))
```
